# Optimizing a Trainium2 kernel written in Bass

```python
import math
import jax
import jax.numpy as jnp
from jax import lax
import numpy as np


D_MODEL = 1024
BATCH = 8
SEQ = 2048
DEPTH = 4

PLE_DIM = 256
N_EVEN = (DEPTH + 1) // 2
N_ODD = DEPTH // 2
EPS = 1e-6
NEG = -1e30

S5_WIDTH = D_MODEL // 2
S5_GROUP = 16
S5_GROUPS = S5_WIDTH // S5_GROUP
S5_STATE = 64
CONV_WIDTH = D_MODEL // 2
CONV_K = 31
NSA_HEADS = D_MODEL // 128
NSA_KV_HEADS = 2
NSA_GQA = NSA_HEADS // NSA_KV_HEADS
NSA_HEAD_DIM = 64
NSA_WIDTH = NSA_HEADS * NSA_HEAD_DIM
CMP_LEN = 32
CMP_STRIDE = 16
CMP_HIDDEN = 256
SEL_LEN = 64
SEL_TOPN = 8
WINDOW = 512
Q_BLOCK = 128
MLA_HEADS = D_MODEL // 128
MLA_NOPE = 64
MLA_ROPE = 32
MLA_V = 64
MLA_Q_RANK = 256
MLA_KV_RANK = 128
MLA_WIDTH = MLA_HEADS * MLA_V
ROPE_BASE = 10000.0

EVEN_SPLITS = (S5_WIDTH, S5_WIDTH, 2 * CONV_WIDTH, CONV_WIDTH)
ODD_SPLITS = (NSA_WIDTH, 6 * NSA_KV_HEADS * NSA_HEAD_DIM, 3 * NSA_HEADS, NSA_WIDTH,
              MLA_Q_RANK, MLA_KV_RANK, MLA_ROPE, MLA_WIDTH)
EVEN_IN = sum(EVEN_SPLITS)
ODD_IN = sum(ODD_SPLITS)
EVEN_MIX = S5_WIDTH + CONV_WIDTH
ODD_MIX = NSA_WIDTH + MLA_WIDTH

kernel_name = 'hybrid_s5conv_nsamla_trunk'


def rms_norm(x, g):
    xf = x.astype(jnp.float32)
    y = xf * lax.rsqrt(jnp.mean(xf * xf, axis=-1, keepdims=True) + EPS)
    return (y * g.astype(jnp.float32)).astype(x.dtype)


def layer_norm(x, g, b):
    xf = x.astype(jnp.float32)
    mu = jnp.mean(xf, axis=-1, keepdims=True)
    var = jnp.mean(jnp.square(xf - mu), axis=-1, keepdims=True)
    return ((xf - mu) * lax.rsqrt(var + EPS) * g + b).astype(x.dtype)


def split_cols(h, sizes):
    return jnp.split(h, np.cumsum(sizes)[:-1].tolist(), axis=-1)


def masked_softmax(s, mask):
    s = jnp.where(mask, s.astype(jnp.float32), NEG)
    m = jnp.max(s, axis=-1, keepdims=True)
    e = jnp.exp(s - m) * mask
    return e / jnp.maximum(jnp.sum(e, axis=-1, keepdims=True), 1e-30)


def alibi_slopes(n):
    return np.array([2.0 ** (-8.0 * (i + 1) / n) for i in range(n)], np.float32)


def rope(x, positions):
    half = MLA_ROPE // 2
    freqs = ROPE_BASE ** (-jnp.arange(half, dtype=jnp.float32) / half)
    ang = positions.astype(jnp.float32)[..., None] * freqs
    cos = jnp.cos(ang)[:, :, None, :]
    sin = jnp.sin(ang)[:, :, None, :]
    x1, x2 = x[..., :half], x[..., half:]
    return jnp.concatenate([x1 * cos - x2 * sin, x1 * sin + x2 * cos], axis=-1).astype(x.dtype)


def s5_mixer(u, lam_re, lam_im, b_re, b_im, c_re, c_im, d_skip, log_step, w_glu, b_glu):
    f32 = jnp.float32
    bsz, s_len, _ = u.shape
    step = jnp.exp(log_step.astype(f32))[:, None]
    lr, li = lam_re.astype(f32), lam_im.astype(f32)
    mag = jnp.exp(lr * step)
    ab_re, ab_im = mag * jnp.cos(li * step), mag * jnp.sin(li * step)
    den = lr * lr + li * li
    nr, ni = ab_re - 1.0, ab_im
    f_re, f_im = (nr * lr + ni * li) / den, (ni * lr - nr * li) / den
    br, bi = b_re.astype(f32), b_im.astype(f32)
    bb_re = f_re[..., None] * br - f_im[..., None] * bi
    bb_im = f_re[..., None] * bi + f_im[..., None] * br
    ut = jnp.swapaxes(u.astype(f32).reshape(bsz, s_len, S5_GROUPS, S5_GROUP), 0, 1)
    bu_re = jnp.einsum('sbgc,gnc->sbgn', ut, bb_re)
    bu_im = jnp.einsum('sbgc,gnc->sbgn', ut, bb_im)
    a_re = jnp.broadcast_to(ab_re[None, None], (s_len, 1, S5_GROUPS, S5_STATE))
    a_im = jnp.broadcast_to(ab_im[None, None], (s_len, 1, S5_GROUPS, S5_STATE))

    def combine(e1, e2):
        a1r, a1i, b1r, b1i = e1
        a2r, a2i, b2r, b2i = e2
        return (a2r * a1r - a2i * a1i, a2r * a1i + a2i * a1r,
                a2r * b1r - a2i * b1i + b2r, a2r * b1i + a2i * b1r + b2i)

    _, _, x_re, x_im = lax.associative_scan(combine, (a_re, a_im, bu_re, bu_im), axis=0)
    y = (jnp.einsum('sbgn,gcn->sbgc', x_re, c_re.astype(f32))
         - jnp.einsum('sbgn,gcn->sbgc', x_im, c_im.astype(f32)))
    y = jnp.swapaxes(y, 0, 1).reshape(bsz, s_len, S5_WIDTH) + d_skip * u
    y = jax.nn.gelu(y)
    return (y * jax.nn.sigmoid(y @ w_glu + b_glu)).astype(u.dtype)


def conv_mixer(v, w_dw, b_dw, ln_g, ln_b, w_pw):
    a, g = jnp.split(v, 2, axis=-1)
    h = a * jax.nn.sigmoid(g)
    h = lax.conv_general_dilated(h, w_dw[:, None, :].astype(h.dtype), window_strides=(1,),
                                 padding=[(CONV_K - 1, 0)],
                                 dimension_numbers=('NWC', 'WIO', 'NWC'),
                                 feature_group_count=CONV_WIDTH) + b_dw
    h = jax.nn.silu(layer_norm(h, ln_g, ln_b))
    return h @ w_pw


def even_mixer(h, w_in, lam_re, lam_im, b_re, b_im, c_re, c_im, d_skip, log_step, w_glu, b_glu,
               w_dw, b_dw, ln_g, ln_b, w_pw, w_out):
    u_a, g_a, v_b, g_b = split_cols(h @ w_in, EVEN_SPLITS)
    y_a = s5_mixer(u_a, lam_re, lam_im, b_re, b_im, c_re, c_im, d_skip, log_step, w_glu, b_glu) * jax.nn.silu(g_a)
    y_b = conv_mixer(v_b, w_dw, b_dw, ln_g, ln_b, w_pw) * jax.nn.silu(g_b)
    return jnp.concatenate([y_a, y_b], axis=-1) @ w_out


def compress_blocks(k, cmp_idx, pos, w1, w2):
    bsz, nc = k.shape[0], cmp_idx.shape[0]
    blk = k[:, cmp_idx] + pos[:, None, :]
    blk = jnp.moveaxis(blk, 3, 2).reshape(bsz, nc, NSA_KV_HEADS, CMP_LEN * NSA_HEAD_DIM)
    return jax.nn.gelu(blk @ w1) @ w2


def nsa_mixer(q, kv, gate_logits, cmp_pos_k, cmp_pos_v, ck_w1, ck_w2, cv_w1, cv_w2):
    f32 = jnp.float32
    bsz, s_len = q.shape[:2]
    hkv, grp, dh = NSA_KV_HEADS, NSA_GQA, NSA_HEAD_DIM
    k_cmp, v_cmp, k_sel, v_sel, k_win, v_win = [kv[:, :, i] for i in range(6)]
    qg = q.reshape(bsz, s_len, hkv, grp, dh)
    scale = dh ** -0.5
    slopes = jnp.asarray(alibi_slopes(NSA_HEADS)).reshape(hkv, grp)
    t = np.arange(s_len)

    nc = (s_len - CMP_LEN) // CMP_STRIDE + 1
    cmp_idx = np.arange(nc)[:, None] * CMP_STRIDE + np.arange(CMP_LEN)[None]
    kc = compress_blocks(k_cmp, cmp_idx, cmp_pos_k, ck_w1, ck_w2)
    vc = compress_blocks(v_cmp, cmp_idx, cmp_pos_v, cv_w1, cv_w2)
    dist_c = (t[:, None] - cmp_idx[:, -1][None]).astype(np.float32)
    s_c = jnp.einsum('bsygd,bcyd->bygsc', qg, kc).astype(f32) * scale - slopes[:, :, None, None] * dist_c
    p_c = masked_softmax(s_c, jnp.asarray(dist_c >= 0))
    o_cmp = jnp.einsum('bygsc,bcyd->bsygd', p_c, vc)

    ns = s_len // SEL_LEN
    n_top = min(SEL_TOPN, ns)
    sb = np.arange(ns)
    cs = np.arange(nc)[:, None] * CMP_STRIDE
    overlap = ((cs < (sb[None] + 1) * SEL_LEN) & (cs + CMP_LEN > sb[None] * SEL_LEN)).astype(np.float32)
    p_slc = jnp.einsum('bygsc,cj->bysj', p_c, jnp.asarray(overlap))
    cur = t[:, None] // SEL_LEN
    forced = (sb[None] == 0) | (sb[None] == cur) | (sb[None] == cur - 1)
    causal_blk = sb[None] * SEL_LEN <= t[:, None]
    sel_score = jnp.where(forced, p_slc + 1e4, jnp.where(causal_blk, p_slc, -1e4))
    _, sel_idx = lax.top_k(sel_score, n_top)

    k_blk = jnp.moveaxis(k_sel.reshape(bsz, ns, SEL_LEN, hkv, dh), 3, 1)
    v_blk = jnp.moveaxis(v_sel.reshape(bsz, ns, SEL_LEN, hkv, dh), 3, 1)
    k_pad = jnp.pad(k_win, ((0, 0), (WINDOW, 0), (0, 0), (0, 0)))
    v_pad = jnp.pad(v_win, ((0, 0), (WINDOW, 0), (0, 0), (0, 0)))
    bi = jnp.arange(bsz)[:, None, None, None]
    hi = jnp.arange(hkv)[None, :, None, None]
    span = WINDOW + Q_BLOCK
    n_sel_keys = n_top * SEL_LEN

    def one_block(i):
        t0 = i * Q_BLOCK
        tq = t0 + jnp.arange(Q_BLOCK)
        qb = lax.dynamic_slice_in_dim(qg, t0, Q_BLOCK, axis=1)
        ib = lax.dynamic_slice_in_dim(sel_idx, t0, Q_BLOCK, axis=2)
        ks = k_blk[bi, hi, ib].reshape(bsz, hkv, Q_BLOCK, n_sel_keys, dh)
        vs = v_blk[bi, hi, ib].reshape(bsz, hkv, Q_BLOCK, n_sel_keys, dh)
        kpos = (ib[..., None] * SEL_LEN + jnp.arange(SEL_LEN)).reshape(bsz, hkv, Q_BLOCK, n_sel_keys)
        dist = (tq[None, None, :, None] - kpos).astype(f32)[:, :, None]
        s_s = jnp.einsum('bqygd,byqkd->bygqk', qb, ks).astype(f32) * scale - slopes[None, :, :, None, None] * dist
        p_s = masked_softmax(s_s, dist >= 0)
        o_sel = jnp.einsum('bygqk,byqkd->bqygd', p_s, vs)
        kw = lax.dynamic_slice_in_dim(k_pad, t0, span, axis=1)
        vw = lax.dynamic_slice_in_dim(v_pad, t0, span, axis=1)
        kpos_w = t0 - WINDOW + jnp.arange(span)
        dist_w = (tq[:, None] - kpos_w[None]).astype(f32)
        mask_w = (dist_w >= 0) & (dist_w < WINDOW) & (kpos_w[None] >= 0)
        s_w = jnp.einsum('bqygd,bkyd->bygqk', qb, kw).astype(f32) * scale - slopes[:, :, None, None] * dist_w
        p_w = masked_softmax(s_w, mask_w)
        o_win = jnp.einsum('bygqk,bkyd->bqygd', p_w, vw)
        return o_sel, o_win

    o_sel, o_win = lax.map(one_block, jnp.arange(s_len // Q_BLOCK))
    o_sel = jnp.moveaxis(o_sel, 0, 1).reshape(bsz, s_len, hkv, grp, dh)
    o_win = jnp.moveaxis(o_win, 0, 1).reshape(bsz, s_len, hkv, grp, dh)
    g = jax.nn.sigmoid(gate_logits.astype(f32)).reshape(bsz, s_len, 3, hkv, grp)[..., None]
    o = g[:, :, 0] * o_cmp + g[:, :, 1] * o_sel + g[:, :, 2] * o_win
    return o.reshape(bsz, s_len, NSA_WIDTH)


def mla_mixer(c_q, c_kv, k_rope, positions, q_norm, kv_norm, w_uq, w_ukv):
    f32 = jnp.float32
    bsz, s_len = c_q.shape[:2]
    q = (rms_norm(c_q, q_norm) @ w_uq).reshape(bsz, s_len, MLA_HEADS, MLA_NOPE + MLA_ROPE)
    kv = (rms_norm(c_kv, kv_norm) @ w_ukv).reshape(bsz, s_len, MLA_HEADS, MLA_NOPE + MLA_V)
    q = jnp.concatenate([q[..., :MLA_NOPE], rope(q[..., MLA_NOPE:], positions)], axis=-1)
    k_r = jnp.broadcast_to(rope(k_rope[:, :, None, :], positions), (bsz, s_len, MLA_HEADS, MLA_ROPE))
    k = jnp.concatenate([kv[..., :MLA_NOPE], k_r], axis=-1)
    v = kv[..., MLA_NOPE:]
    scale = (MLA_NOPE + MLA_ROPE) ** -0.5
    kpos = jnp.arange(s_len)

    def one_block(i):
        t0 = i * Q_BLOCK
        qb = lax.dynamic_slice_in_dim(q, t0, Q_BLOCK, axis=1)
        s = jnp.einsum('bqhd,bkhd->bhqk', qb, k).astype(f32) * scale
        mask = kpos[None] <= (t0 + jnp.arange(Q_BLOCK))[:, None]
        return jnp.einsum('bhqk,bkhd->bqhd', masked_softmax(s, mask), v)

    o = lax.map(one_block, jnp.arange(s_len // Q_BLOCK))
    return jnp.moveaxis(o, 0, 1).reshape(bsz, s_len, MLA_WIDTH)


def odd_mixer(h, positions, w_in, cmp_pos_k, cmp_pos_v, ck_w1, ck_w2, cv_w1, cv_w2,
              q_norm, kv_norm, w_uq, w_ukv, w_out):
    bsz, s_len = h.shape[:2]
    q_n, kv_n, gl_n, g_n, c_q, c_kv, k_r, g_m = split_cols(h @ w_in, ODD_SPLITS)
    q_n = q_n.reshape(bsz, s_len, NSA_HEADS, NSA_HEAD_DIM)
    kv_n = kv_n.reshape(bsz, s_len, 6, NSA_KV_HEADS, NSA_HEAD_DIM)
    y_c = nsa_mixer(q_n, kv_n, gl_n, cmp_pos_k, cmp_pos_v, ck_w1, ck_w2, cv_w1, cv_w2) * jax.nn.silu(g_n)
    y_d = mla_mixer(c_q, c_kv, k_r, positions, q_norm, kv_norm, w_uq, w_ukv) * jax.nn.silu(g_m)
    return jnp.concatenate([y_c, y_d], axis=-1) @ w_out


def setup_inputs(seed: int = 0) -> dict:
    key = jax.random.key(seed)
    ks = iter(jax.random.split(key, 40))

    def nrm(shape, scale):
        return jax.random.normal(next(ks), shape, jnp.float32) * scale

    def gain(shape):
        return 1.0 + nrm(shape, 0.02)

    ne, no = N_EVEN, N_ODD
    g, n, c = S5_GROUPS, S5_STATE, S5_GROUP
    return {
        'x': nrm((BATCH, SEQ, D_MODEL), 1.0),
        'p': nrm((DEPTH, BATCH, SEQ, PLE_DIM), 1.0),
        'positions': jnp.broadcast_to(jnp.arange(SEQ, dtype=jnp.int32), (BATCH, SEQ)),
        'pre_norm': gain((DEPTH, D_MODEL)),
        'post_norm': gain((DEPTH, D_MODEL)),
        'ple_gate': nrm((DEPTH, D_MODEL, D_MODEL), D_MODEL ** -0.5),
        'ple_proj': nrm((DEPTH, PLE_DIM, D_MODEL), PLE_DIM ** -0.5),
        'ev_w_in': nrm((ne, D_MODEL, EVEN_IN), D_MODEL ** -0.5),
        's5_lam_re': -0.5 * jnp.exp(nrm((ne, g, n), 0.01)),
        's5_lam_im': jnp.pi * jnp.arange(n, dtype=jnp.float32) + nrm((ne, g, n), 0.01),
        's5_b_re': nrm((ne, g, n, c), (2 * c) ** -0.5),
        's5_b_im': nrm((ne, g, n, c), (2 * c) ** -0.5),
        's5_c_re': nrm((ne, g, c, n), (2 * n) ** -0.5),
        's5_c_im': nrm((ne, g, c, n), (2 * n) ** -0.5),
        's5_d': nrm((ne, S5_WIDTH), 0.5),
        's5_log_step': jax.random.uniform(next(ks), (ne, g), jnp.float32, math.log(1e-3), math.log(1e-1)),
        's5_w_glu': nrm((ne, S5_WIDTH, S5_WIDTH), S5_WIDTH ** -0.5),
        's5_b_glu': nrm((ne, S5_WIDTH), 0.01),
        'cv_w_dw': nrm((ne, CONV_K, CONV_WIDTH), CONV_K ** -0.5),
        'cv_b_dw': nrm((ne, CONV_WIDTH), 0.01),
        'cv_ln_g': gain((ne, CONV_WIDTH)),
        'cv_ln_b': nrm((ne, CONV_WIDTH), 0.01),
        'cv_w_pw': nrm((ne, CONV_WIDTH, CONV_WIDTH), CONV_WIDTH ** -0.5),
        'ev_w_out': nrm((ne, EVEN_MIX, D_MODEL), EVEN_MIX ** -0.5),
        'od_w_in': nrm((no, D_MODEL, ODD_IN), D_MODEL ** -0.5),
        'nsa_pos_k': nrm((no, CMP_LEN, NSA_HEAD_DIM), 0.1),
        'nsa_pos_v': nrm((no, CMP_LEN, NSA_HEAD_DIM), 0.1),
        'nsa_ck_w1': nrm((no, CMP_LEN * NSA_HEAD_DIM, CMP_HIDDEN), (CMP_LEN * NSA_HEAD_DIM) ** -0.5),
        'nsa_ck_w2': nrm((no, CMP_HIDDEN, NSA_HEAD_DIM), CMP_HIDDEN ** -0.5),
        'nsa_cv_w1': nrm((no, CMP_LEN * NSA_HEAD_DIM, CMP_HIDDEN), (CMP_LEN * NSA_HEAD_DIM) ** -0.5),
        'nsa_cv_w2': nrm((no, CMP_HIDDEN, NSA_HEAD_DIM), CMP_HIDDEN ** -0.5),
        'mla_q_norm': gain((no, MLA_Q_RANK)),
        'mla_kv_norm': gain((no, MLA_KV_RANK)),
        'mla_w_uq': nrm((no, MLA_Q_RANK, MLA_HEADS * (MLA_NOPE + MLA_ROPE)), MLA_Q_RANK ** -0.5),
        'mla_w_ukv': nrm((no, MLA_KV_RANK, MLA_HEADS * (MLA_NOPE + MLA_V)), MLA_KV_RANK ** -0.5),
        'od_w_out': nrm((no, ODD_MIX, D_MODEL), ODD_MIX ** -0.5),
    }


def reference(x, p, positions, pre_norm, post_norm, ple_gate, ple_proj,
              ev_w_in, s5_lam_re, s5_lam_im, s5_b_re, s5_b_im, s5_c_re, s5_c_im, s5_d, s5_log_step,
              s5_w_glu, s5_b_glu, cv_w_dw, cv_b_dw, cv_ln_g, cv_ln_b, cv_w_pw, ev_w_out,
              od_w_in, nsa_pos_k, nsa_pos_v, nsa_ck_w1, nsa_ck_w2, nsa_cv_w1, nsa_cv_w2,
              mla_q_norm, mla_kv_norm, mla_w_uq, mla_w_ukv, od_w_out):
    h = x
    for i in range(DEPTH):
        hn = rms_norm(h, pre_norm[i])
        j = i // 2
        if i % 2 == 0:
            y = even_mixer(hn, ev_w_in[j], s5_lam_re[j], s5_lam_im[j], s5_b_re[j], s5_b_im[j],
                           s5_c_re[j], s5_c_im[j], s5_d[j], s5_log_step[j], s5_w_glu[j], s5_b_glu[j],
                           cv_w_dw[j], cv_b_dw[j], cv_ln_g[j], cv_ln_b[j], cv_w_pw[j], ev_w_out[j])
        else:
            y = odd_mixer(hn, positions, od_w_in[j], nsa_pos_k[j], nsa_pos_v[j], nsa_ck_w1[j], nsa_ck_w2[j],
                          nsa_cv_w1[j], nsa_cv_w2[j], mla_q_norm[j], mla_kv_norm[j], mla_w_uq[j],
                          mla_w_ukv[j], od_w_out[j])
        h = h + rms_norm(y, post_norm[i])
        h = h + jax.nn.sigmoid(h @ ple_gate[i]) * (p[i] @ ple_proj[i])
    return h
```

```python
import math
from contextlib import ExitStack

import numpy as np
import concourse.bass as bass
import concourse.mybir as mybir
from concourse.bass_utils import run_bass_kernel_spmd

F32 = mybir.dt.float32
BF16 = mybir.dt.bfloat16
I32 = mybir.dt.int32
ALU = mybir.AluOpType
AF = mybir.ActivationFunctionType

S = 2048
D = 1024
NT = S // 128
EPS = 1e-6
ENGS = ("pe", "act", "dve", "pool", "sp")
CENG = ("pe", "act", "dve", "pool")
N_DMA_SEMS = 84
TWO_PI = 2.0 * math.pi


class Prog:
    def __init__(self, nc):
        self.nc = nc
        self.gstack = ExitStack()
        self.engsem = {e: self.gstack.enter_context(nc.semaphore("es_" + e)) for e in CENG}
        self.dsems = [self.gstack.enter_context(nc.semaphore("ds%d" % i)) for i in range(N_DMA_SEMS)]
        self.engcnt = {e: 0 for e in CENG}
        self.dcnt = [0] * N_DMA_SEMS
        self.scopes = []
        self.uid = 0
        self.n_instr = 0

    def close(self):
        self.gstack.close()

    def gsb(self, name, shape, dt):
        return self.gstack.enter_context(self.nc.sbuf_tensor(name, list(shape), dt))

    def gps(self, name, shape, dt=F32):
        return self.gstack.enter_context(self.nc.psum_tensor(name, list(shape), dt))

    def sb(self, name, shape, dt):
        self.uid += 1
        return self.scopes[-1].enter_context(self.nc.sbuf_tensor("%s_%d" % (name, self.uid), list(shape), dt))

    def push_scope(self):
        self.scopes.append(ExitStack())

    def pop_scope(self):
        self.scopes.pop().close()

    def begin_phase(self):
        self.push_scope()
        self.ins = []
        self.last_w = {}
        self.readers = {}
        self.eng_seq = {e: [] for e in ENGS}
        self.semmap = {}

    def _add(self, eng, fn, reads, writes, kind, semkey=None):
        idx = len(self.ins)
        deps = set()
        for k in reads:
            if k in self.last_w:
                deps.add((self.last_w[k], 0))
            if isinstance(k, tuple) and k[0] == "ps":
                for r in self.readers.get(k, ()):
                    if self.ins[r][0] != eng:
                        deps.add((r, 1))
        for k in writes:
            if k in self.last_w:
                deps.add((self.last_w[k], 1))
            for r in self.readers.get(k, ()):
                deps.add((r, 2))
        for k in reads:
            self.readers.setdefault(k, []).append(idx)
        for k in writes:
            self.last_w[k] = idx
            self.readers[k] = []
        self.ins.append((eng, fn, kind, semkey, deps))
        self.eng_seq[eng].append(idx)
        return idx

    def op(self, eng, fn, reads=(), writes=()):
        return self._add(eng, fn, list(reads), list(writes), "c")

    def dma(self, out, in_, reads=(), writes=(), q="sp", **kw):
        semkey = (q, tuple(writes))
        if semkey not in self.semmap:
            assert len(self.semmap) < N_DMA_SEMS, "too many dma sem keys"
            self.semmap[semkey] = len(self.semmap)
        fn = lambda e: e.dma_start(out=out, in_=in_, **kw)
        return self._add(q, fn, list(reads), list(writes), "d", semkey)

    def act(self, out, in_, func, r, w, **kw):
        self.op("act", lambda e: e.activation(out=out, in_=in_, func=func, **kw), r, w)

    def tt(self, eng, out, in0, in1, op, r, w):
        self.op(eng, lambda e: e.tensor_tensor(out=out, in0=in0, in1=in1, op=op), r, w)

    def ts(self, eng, out, in0, s1, s2, op0, op1, r, w, **kw):
        if s2 is None:
            self.op(eng, lambda e: e.tensor_scalar(out=out, in0=in0, scalar1=s1, scalar2=None, op0=op0, **kw), r, w)
        else:
            self.op(eng, lambda e: e.tensor_scalar(out=out, in0=in0, scalar1=s1, scalar2=s2, op0=op0, op1=op1, **kw), r, w)

    def stt(self, eng, out, in0, scalar, in1, op0, op1, r, w):
        self.op(eng, lambda e: e.scalar_tensor_tensor(out=out, in0=in0, scalar=scalar, in1=in1, op0=op0, op1=op1), r, w)

    def copy(self, eng, out, in_, r, w):
        if eng == "act":
            self.op(eng, lambda e: e.copy(out=out, in_=in_), r, w)
        else:
            self.op(eng, lambda e: e.tensor_copy(out=out, in_=in_), r, w)

    def memset(self, eng, ap, val, w):
        self.op(eng, lambda e: e.memset(ap, val), (), w)

    def mm(self, out, lhsT, rhs, start, stop, r, w, skip=False):
        if skip:
            self.op("pe", lambda e: e.matmul(out, lhsT=lhsT, rhs=rhs, start=start, stop=stop, skip_group_check=True), r, w)
        else:
            self.op("pe", lambda e: e.matmul(out, lhsT=lhsT, rhs=rhs, start=start, stop=stop), r, w)

    def tr(self, out, in_, ident, r, w):
        self.op("pe", lambda e: e.transpose(out=out, in_=in_, identity=ident), r, w)

    def recip(self, out, in_, r, w):
        self.op("dve", lambda e: e.reciprocal(out=out, in_=in_), r, w)

    def end_phase(self):
        nc = self.nc
        ins = self.ins
        pos = {}
        for e in ENGS:
            for p, idx in enumerate(self.eng_seq[e]):
                pos[idx] = p
        WIN = 3

        def edge_needed(idx, d, typ):
            eng = ins[idx][0]
            deng, _, dkind, _, _ = ins[d]
            if dkind == "d" or ins[idx][2] == "d":
                return True
            if deng == eng:
                return eng != "pe"
            return True

        pruned = []
        for idx, (eng, fn, kind, semkey, deps) in enumerate(ins):
            best = {}
            keep = set()
            for (d, typ) in deps:
                if not edge_needed(idx, d, typ):
                    continue
                if ins[d][2] == "d":
                    keep.add((d, typ))
                    continue
                pe_ = ins[d][0]
                if pe_ not in best or pos[d] > pos[best[pe_][0]]:
                    best[pe_] = (d, typ)
            keep.update(best.values())
            pruned.append(keep)
        needed = set()
        for idx in range(len(ins)):
            for (d, typ) in pruned[idx]:
                needed.add(d)
        for e in CENG:
            if self.eng_seq[e]:
                needed.add(self.eng_seq[e][-1])
        token = {}
        finals = {}
        for idx, (eng, fn, kind, semkey, deps) in enumerate(ins):
            if kind == "d":
                si = self.semmap[semkey]
                self.dcnt[si] += 16
                token[idx] = (("d", si), self.dcnt[si])
                finals[("d", si)] = self.dcnt[si]
            elif idx in needed:
                self.engcnt[eng] += 1
                token[idx] = (("e", eng), self.engcnt[eng])
                finals[("e", eng)] = self.engcnt[eng]
        progs = {e: [] for e in ENGS}
        waited = {e: {} for e in ENGS}
        for idx, (eng, fn, kind, semkey, deps) in enumerate(ins):
            waits = {}
            for (d, typ) in pruned[idx]:
                sn, val = token[d]
                if waited[eng].get(sn, 0) >= val:
                    continue
                waits[sn] = max(waits.get(sn, 0), val)
            for sn, val in waits.items():
                waited[eng][sn] = val
            progs[eng].append((waits, fn, token.get(idx)))
        self.n_instr += len(ins)

        def sem_of(sn):
            return self.dsems[sn[1]] if sn[0] == "d" else self.engsem[sn[1]]

        def run_engine(e, name):
            for waits, fn, inc in progs[name]:
                for sn, val in waits.items():
                    e.wait_ge(sem_of(sn), val)
                r = fn(e)
                if inc is not None:
                    r.then_inc(sem_of(inc[0]), 16 if inc[0][0] == "d" else 1)

        with nc.Block() as block:
            @block.tensor
            def _(e):
                run_engine(e, "pe")

            @block.scalar
            def _(e):
                run_engine(e, "act")

            @block.vector
            def _(e):
                run_engine(e, "dve")

            @block.gpsimd
            def _(e):
                run_engine(e, "pool")
                for sn, val in finals.items():
                    if sn[0] == "d" and any(k[0] == "pool" and self.semmap[k] == sn[1] for k in self.semmap):
                        e.wait_ge(sem_of(sn), val)

            @block.sync
            def _(e):
                run_engine(e, "sp")
                for sn, val in finals.items():
                    e.wait_ge(sem_of(sn), val)
        nc.all_engine_barrier()
        self.pop_scope()


class Ctx:
    pass


def load_w_bf16(P, C, dst, src, nkf, cols, tag, conv_engs=("pool", "act")):
    stage = [P.sb("wst_%s%d" % (tag, i), [128, cols], F32) for i in range(2)]
    for kf in range(nkf):
        s = kf % 2
        P.dma(stage[s][:], src[kf * 128:(kf + 1) * 128, :], writes=[("wst", tag, s)])
        P.copy(conv_engs[kf % len(conv_engs)], dst[:, kf, :], stage[s][:], [("wst", tag, s)], [("w", tag, kf)])


def rstd_from_ssq(P, C, st, n, key):
    P.act(st[:, 1:2], st[:, 0:1], AF.Sqrt, [key + (0,)], [key + (1,)], scale=1.0 / n, bias=C.epsb[:])
    P.recip(st[:, 2:3], st[:, 1:2], [key + (1,)], [key + (2,)])


def phase_prenorm(P, C, L, src, hnT):
    P.begin_phase()
    gb = P.sb("gpre", [128, D], F32)
    P.dma(gb[:], C.pre_norm[L:L + 1, :].partition_broadcast(128), writes=["gpre"])
    ht = [P.sb("ht%d" % i, [128, D], F32) for i in range(2)]
    junk = P.sb("junk", [128, D], BF16)
    hnb = [P.sb("hnb%d" % i, [128, D], BF16) for i in range(2)]
    st = [P.sb("st%d" % i, [128, 4], F32) for i in range(2)]
    psT = C.ps[0][:].bitcast(BF16)
    psT2 = C.ps[1][:].bitcast(BF16)
    def stage1(tt):
        s = tt % 2
        P.dma(ht[s][:], src[tt * 128:(tt + 1) * 128, :], writes=[("ht", s)])
        P.act(junk[:], ht[s][:], AF.Square, [("ht", s)], ["junk", ("st", s, 0)], accum_out=st[s][:, 0:1])
        rstd_from_ssq(P, C, st[s], D, ("st", s))
        P.stt("dve", hnb[s][:], ht[s][:], st[s][:, 2:3], gb[:], ALU.mult, ALU.mult, [("ht", s), ("st", s, 2), "gpre"], [("hnb", s)])

    def stage2(tt):
        s = tt % 2
        pst = psT if s == 0 else psT2
        for kf in range(8):
            P.tr(pst[:, kf * 128:(kf + 1) * 128], hnb[s][:, kf * 128:(kf + 1) * 128], C.ident[:], [("hnb", s), "ident"], [("psT", s)])
        P.copy("act" if s == 0 else "dve", hnT[:, :, tt * 128:(tt + 1) * 128], pst[:, 0:1024].rearrange("p (k t) -> p k t", k=8), [("psT", s)], [("hnT", tt)])

    stage1(0)
    for tt in range(NT):
        if tt + 1 < NT:
            stage1(tt + 1)
        stage2(tt)
    P.end_phase()


def phase_out(P, C, L, ymT, src, dst):
    P.begin_phase()
    wout = P.sb("wout", [128, 8, D], BF16)
    wpg = P.sb("wpg", [128, 8, D], BF16)
    wpp = P.sb("wpp", [128, 2, D], BF16)
    load_w_bf16(P, C, wout, C.w_out[L], 8, D, "wout")
    load_w_bf16(P, C, wpg, C.ple_gate[L], 8, D, "wpg")
    load_w_bf16(P, C, wpp, C.ple_proj[L], 2, D, "wpp")
    gb = P.sb("gpost", [128, D], F32)
    P.dma(gb[:], C.post_norm[L:L + 1, :].partition_broadcast(128), writes=["gpost"])
    ht = [P.sb("ht%d" % i, [128, D], F32) for i in range(2)]
    pt = [P.sb("pt%d" % i, [128, 256], F32) for i in range(2)]
    ptb = [P.sb("ptb%d" % i, [128, 256], BF16) for i in range(2)]
    pT = [P.sb("pT%d" % i, [128, 2, 128], BF16) for i in range(2)]
    junk = P.sb("junk", [128, D], BF16)
    t1 = [P.sb("t1%d" % i, [128, D], F32) for i in range(2)]
    hm = [P.sb("hm%d" % i, [128, D], F32) for i in range(2)]
    hmb = [P.sb("hmb%d" % i, [128, D], BF16) for i in range(2)]
    hmT = [P.sb("hmT%d" % i, [128, 8, 128], BF16) for i in range(2)]
    sg = [P.sb("sg%d" % i, [128, D], F32) for i in range(2)]
    hn = [P.sb("hnw%d" % i, [128, D], F32) for i in range(2)]
    st = [P.sb("st%d" % i, [128, 4], F32) for i in range(2)]
    wkeys_out = [("w", "wout", k) for k in range(8)]
    wkeys_pg = [("w", "wpg", k) for k in range(8)]
    wkeys_pp = [("w", "wpp", k) for k in range(2)]
    def stage1(tt):
        s = tt % 2
        tsl = slice(tt * 128, (tt + 1) * 128)
        P.dma(ht[s][:], src[tsl, :], writes=[("ht", s)])
        P.dma(pt[s][:], C.p[L, tsl, :], writes=[("pt", s)])
        for hf in range(2):
            for kf in range(8):
                P.mm(C.ps[hf][:], ymT[:, kf, tsl], wout[:, kf, hf * 512:(hf + 1) * 512], kf == 0, kf == 7,
                     [("ymT", kf)] + wkeys_out, [("ps", hf)])
        for hf in range(2):
            P.act(junk[:, hf * 512:(hf + 1) * 512], C.ps[hf][:], AF.Square, [("ps", hf)], ["junk", ("st", s, 0, hf)],
                  accum_out=st[s][:, hf:hf + 1])
        P.tt("dve", st[s][:, 0:1], st[s][:, 0:1], st[s][:, 1:2], ALU.add, [("st", s, 0, 0), ("st", s, 0, 1)], [("st", s, 0)])
        rstd_from_ssq(P, C, st[s], D, ("st", s))
        for hf in range(2):
            hs = slice(hf * 512, (hf + 1) * 512)
            P.stt("dve", t1[s][:, hs], C.ps[hf][:], st[s][:, 2:3], gb[:, hs], ALU.mult, ALU.mult,
                  [("ps", hf), ("st", s, 2), "gpost"], [("t1", s, hf)])
            P.tt("dve", hm[s][:, hs], t1[s][:, hs], ht[s][:, hs], ALU.add, [("t1", s, hf), ("ht", s)], [("hm", s, hf)])
            P.copy("act", hmb[s][:, hs], hm[s][:, hs], [("hm", s, hf)], [("hmb", s, hf)])
        P.copy("act", ptb[s][:], pt[s][:], [("pt", s)], [("ptb", s)])

    def stage2(tt):
        s = tt % 2
        tsl = slice(tt * 128, (tt + 1) * 128)
        psT = C.ps[2][:].bitcast(BF16)
        for kf in range(8):
            P.tr(psT[:, kf * 128:(kf + 1) * 128], hmb[s][:, kf * 128:(kf + 1) * 128], C.ident[:],
                 [("hmb", s, kf // 4), "ident"], [("ps", 2)])
        P.copy("dve", hmT[s][:], psT[:, 0:1024].rearrange("p (k t) -> p k t", k=8), [("ps", 2)], [("hmT", s)])
        psT3 = C.ps[3][:].bitcast(BF16)
        for j in range(2):
            P.tr(psT3[:, j * 128:(j + 1) * 128], ptb[s][:, j * 128:(j + 1) * 128], C.ident[:], [("ptb", s), "ident"], [("ps", 3)])
        P.copy("dve", pT[s][:], psT3[:, 0:256].rearrange("p (k t) -> p k t", k=2), [("ps", 3)], [("pT", s)])
        for hf in range(2):
            hs = slice(hf * 512, (hf + 1) * 512)
            for kf in range(8):
                P.mm(C.ps[4 + hf][:], hmT[s][:, kf, :], wpg[:, kf, hs], kf == 0, kf == 7, [("hmT", s)] + wkeys_pg, [("ps", 4 + hf)])
            for j in range(2):
                P.mm(C.ps[6 + hf][:], pT[s][:, j, :], wpp[:, j, hs], j == 0, j == 1, [("pT", s)] + wkeys_pp, [("ps", 6 + hf)])
            P.act(sg[s][:, hs], C.ps[4 + hf][:], AF.Sigmoid, [("ps", 4 + hf)], [("sg", s, hf)])
            P.tt("dve", sg[s][:, hs], sg[s][:, hs], C.ps[6 + hf][:], ALU.mult, [("sg", s, hf), ("ps", 6 + hf)], [("sg", s, hf)])
            P.tt("dve", hn[s][:, hs], sg[s][:, hs], hm[s][:, hs], ALU.add, [("sg", s, hf), ("hm", s, hf)], [("hn", s, hf)])
        P.dma(dst[tsl, :], hn[s][:], reads=[("hn", s, 0), ("hn", s, 1)], writes=[("dst", tt % 4)], q="pool")

    stage1(0)
    for tt in range(NT):
        if tt + 1 < NT:
            stage1(tt + 1)
        stage2(tt)
    P.end_phase()


def phase_even_proj(P, C, j, hnT, uT, sgaT, sgbT, hcpad):
    P.begin_phase()
    wst = [P.sb("wst%d" % i, [128, 8, 128], F32) for i in range(2)]
    wbf = [P.sb("wbf%d" % i, [128, 8, 128], BF16) for i in range(2)]
    aT = P.sb("aT", [128, 4, S], BF16)
    sig = [P.sb("sig%d" % i, [128, 512], BF16) for i in range(2)]
    w_in = C.ev_w_in[j]
    P.memset("pool", hcpad[:, :, 0:30], 0.0, ["hcpad0"])
    n = 0
    for sl in range(20):
        s = sl % 2
        P.dma(wst[s][:], w_in[:, sl * 128:(sl + 1) * 128].rearrange("(k p) c -> p k c", p=128), writes=[("wst", s)])
        P.copy("pool", wbf[s][:], wst[s][:], [("wst", s)], [("wbf", s)])
        for c in range(4):
            cs = slice(c * 512, (c + 1) * 512)
            b = n % 4
            n += 1
            pb = C.ps[b]
            for kf in range(8):
                P.mm(pb[:], wbf[s][:, kf, :], hnT[:, kf, cs], kf == 0, kf == 7, [("wbf", s)], [("ps", b)])
            if sl < 4:
                P.copy("act", uT[:, sl, cs], pb[:], [("ps", b)], [("uT", sl, c)])
            elif sl < 8:
                P.act(sgaT[:, sl - 4, cs], pb[:], AF.Silu, [("ps", b)], [("sgaT", sl - 4, c)])
            elif sl < 12:
                P.copy("dve", aT[:, sl - 8, cs], pb[:], [("ps", b)], [("aT", sl - 8, c)])
            elif sl < 16:
                q = n % 2
                P.act(sig[q][:], pb[:], AF.Sigmoid, [("ps", b)], [("sig", q)])
                P.tt("dve", hcpad[:, sl - 12, 30 + c * 512:30 + (c + 1) * 512], aT[:, sl - 12, cs], sig[q][:], ALU.mult,
                     [("aT", sl - 12, c), ("sig", q)], [("hcpad", sl - 12, c)])
            else:
                P.act(sgbT[:, sl - 16, cs], pb[:], AF.Silu, [("ps", b)], [("sgbT", sl - 16, c)])
    P.end_phase()


def phase_conv(P, C, j, hcpad, sgbT, ymT):
    P.begin_phase()
    evp = P.sb("evp", [128, 4, 40], F32)
    P.dma(evp[:], C.evp[j], writes=["evp"])
    wpw = P.sb("wpw", [128, 4, 512], BF16)
    load_w_bf16(P, C, wpw, C.cv_w_pw[j], 4, 512, "wpw")
    wpw_keys = [("w", "wpw", k) for k in range(4)]
    dg = P.sb("dg", [128, 4, 31, 128], BF16)
    for ft in range(4):
        for k in range(31):
            P.ts("pool" if (k % 2) else "dve", dg[:, ft, k, :], C.identf[:], evp[:, ft, k:k + 1], None, ALU.mult, None,
                 ["evp", "identf"], [("dg", ft)])
    cv1 = P.sb("cv1", [128, 4, 512], F32)
    sq = P.sb("sq", [128, 4, 512], F32)
    mu = P.sb("mu", [128, 512], F32)
    m2 = P.sb("m2", [128, 512], F32)
    rs = P.sb("rs", [128, 512], F32)
    xn = P.sb("xn", [128, 4, 512], F32)
    cvn = P.sb("cvn", [128, 4, 512], BF16)
    for c in range(4):
        cs = slice(c * 512, (c + 1) * 512)
        for ft in range(4):
            for k in range(31):
                P.mm(C.ps[ft][:], dg[:, ft, k, :], hcpad[:, ft, c * 512 + k:c * 512 + k + 512], k == 0, k == 30,
                     [("dg", ft)], [("ps", ft)])
            P.act(cv1[:, ft, :], C.ps[ft][:], AF.Identity, [("ps", ft), "evp"], [("cv1", ft)], bias=evp[:, ft, 31:32])
            P.act(sq[:, ft, :], cv1[:, ft, :], AF.Square, [("cv1", ft)], [("sq", ft)])
        for ft in range(4):
            P.mm(C.ps[4][:], C.onesf[:], cv1[:, ft, :], ft == 0, ft == 3, [("cv1", ft), "onesf"], [("ps", 4)])
        for ft in range(4):
            P.mm(C.ps[5][:], C.onesf[:], sq[:, ft, :], ft == 0, ft == 3, [("sq", ft), "onesf"], [("ps", 5)])
        P.act(mu[:], C.ps[4][:], AF.Copy, [("ps", 4)], ["mu"], scale=1.0 / 512)
        P.tt("dve", m2[:], mu[:], mu[:], ALU.mult, ["mu"], ["m2"])
        P.stt("dve", m2[:], C.ps[5][:], 1.0 / 512, m2[:], ALU.mult, ALU.subtract, [("ps", 5), "m2"], ["m2"])
        P.act(rs[:], m2[:], AF.Ln, ["m2"], ["rs"], bias=C.epsb[:])
        P.act(rs[:], rs[:], AF.Exp, ["rs"], ["rs"], scale=-0.5)
        for ft in range(4):
            eng = "dve"
            P.tt(eng, xn[:, ft, :], cv1[:, ft, :], mu[:], ALU.subtract, [("cv1", ft), "mu"], [("xn", ft)])
            P.tt(eng, xn[:, ft, :], xn[:, ft, :], rs[:], ALU.mult, [("xn", ft), "rs"], [("xn", ft)])
            P.act(cvn[:, ft, :], xn[:, ft, :], AF.Silu, [("xn", ft), "evp"], [("cvn", ft)],
                  scale=evp[:, ft, 32:33], bias=evp[:, ft, 33:34])
        for ot in range(4):
            b = 6 + (ot % 2)
            for ft in range(4):
                P.mm(C.ps[b][:], wpw[:, ft, ot * 128:(ot + 1) * 128], cvn[:, ft, :], ft == 0, ft == 3,
                     [("cvn", ft)] + wpw_keys, [("ps", b)])
            P.tt("dve", ymT[:, 4 + ot, cs], C.ps[b][:], sgbT[:, ot, cs], ALU.mult, [("ps", b)], [("ymT", 4 + ot, c)])
    P.end_phase()


def phase_s5_setup(P, C, j, BtR, BtI, CtR, CtI, sc2):
    P.begin_phase()
    sc = P.sb("sc", [128, 16, 3], F32)
    P.dma(sc[:], C.s5sc[j], writes=["sc"])
    w = P.sb("w", [128, 16, 16], F32)

    def col(i):
        return w[:, :, i]
    lr, li, ls = sc[:, :, 0], sc[:, :, 1], sc[:, :, 2]
    k = ["w%d" % i for i in range(16)]
    P.act(col(0), ls, AF.Exp, ["sc"], [k[0]])
    P.tt("dve", col(1), lr, col(0), ALU.mult, ["sc", k[0]], [k[1]])
    P.tt("dve", sc2[:, :, 0], li, col(0), ALU.mult, ["sc", k[0]], ["th"])
    P.ts("dve", sc2[:, :, 1], sc2[:, :, 0], 1.0 / TWO_PI, None, ALU.mult, None, ["th"], ["thq"])
    P.act(sc2[:, :, 2], col(1), AF.Exp, [k[1]], ["rho"])
    ki = P.sb("ki", [128, 16], I32)
    P.copy("dve", ki[:], sc2[:, :, 1], ["thq"], ["ki"])
    P.stt("dve", col(2), ki[:], -TWO_PI, sc2[:, :, 0], ALU.mult, ALU.add, ["ki", "th"], [k[2]])
    P.ts("dve", col(2), col(2), math.pi, -math.pi, ALU.min, ALU.max, [k[2]], [k[2]])
    P.act(col(3), col(2), AF.Abs, [k[2]], [k[3]])
    P.act(col(4), col(2), AF.Sin, [k[2]], [k[4]])
    P.act(col(5), col(3), AF.Sin, [k[3]], [k[5]], scale=-1.0, bias=C.hpib[:])
    P.tt("dve", col(6), sc2[:, :, 2], col(5), ALU.mult, ["rho", k[5]], [k[6]])
    P.tt("dve", col(7), sc2[:, :, 2], col(4), ALU.mult, ["rho", k[4]], [k[7]])
    P.ts("dve", col(6), col(6), -1.0, None, ALU.add, None, [k[6]], [k[6]])
    P.tt("dve", col(8), lr, lr, ALU.mult, ["sc"], [k[8]])
    P.tt("dve", col(9), li, li, ALU.mult, ["sc"], [k[9]])
    P.tt("dve", col(8), col(8), col(9), ALU.add, [k[8], k[9]], [k[8]])
    P.recip(col(8), col(8), [k[8]], [k[8]])
    P.tt("dve", col(9), col(6), lr, ALU.mult, [k[6], "sc"], [k[9]])
    P.tt("dve", col(10), col(7), li, ALU.mult, [k[7], "sc"], [k[10]])
    P.tt("dve", col(9), col(9), col(10), ALU.add, [k[9], k[10]], [k[9]])
    P.tt("dve", col(11), col(9), col(8), ALU.mult, [k[9], k[8]], [k[11]])
    P.tt("dve", col(9), col(7), lr, ALU.mult, [k[7], "sc"], [k[9]])
    P.tt("dve", col(10), col(6), li, ALU.mult, [k[6], "sc"], [k[10]])
    P.tt("dve", col(9), col(9), col(10), ALU.subtract, [k[9], k[10]], [k[9]])
    P.tt("dve", col(12), col(9), col(8), ALU.mult, [k[9], k[8]], [k[12]])
    for ri, dst in ((0, BtR), (1, BtI)):
        stg = P.sb("bst%d" % ri, [128, 16, 128], F32)
        P.dma(stg[:], C.s5bT[j, ri], writes=[("bst", ri)])
        P.copy("pool", dst[:], stg[:], [("bst", ri)], [("Bt", ri)])
    cre = P.sb("cre", [128, 16, 128], F32)
    cim = P.sb("cim", [128, 16, 128], F32)
    t1 = P.sb("t1", [128, 16, 128], F32)
    t2 = P.sb("t2", [128, 16, 128], F32)
    P.dma(cre[:], C.s5cP[j, 0], writes=["cre"])
    P.dma(cim[:], C.s5cP[j, 1], writes=["cim"])
    fre = w[:, :, 11:12].to_broadcast([128, 16, 128])
    fim = w[:, :, 12:13].to_broadcast([128, 16, 128])
    P.tt("dve", t1[:], cre[:], fre, ALU.mult, ["cre", k[11]], ["t1"])
    P.tt("pool", t2[:], cim[:], fim, ALU.mult, ["cim", k[12]], ["t2"])
    P.tt("dve", CtR[:], t1[:], t2[:], ALU.subtract, ["t1", "t2"], ["CtR"])
    P.tt("dve", t1[:], cre[:], fim, ALU.mult, ["cre", k[12]], ["t1"])
    P.tt("pool", t2[:], cim[:], fre, ALU.mult, ["cim", k[11]], ["t2"])
    P.stt("dve", CtI[:], t1[:], -1.0, t2[:], ALU.mult, ALU.subtract, ["t1", "t2"], ["CtI"])
    P.end_phase()


def phase_s5(P, C, j, uT, sgaT, ymT, BtR, BtI, CtR, CtI, sc2):
    P.begin_phase()
    TH = 1024
    evp = P.sb("evp", [128, 4, 40], F32)
    P.dma(evp[:], C.evp[j], writes=["evp"])
    wglu = P.sb("wglu", [128, 4, 512], BF16)
    load_w_bf16(P, C, wglu, C.s5_w_glu[j], 4, 512, "wglu")
    wglu_keys = [("w", "wglu", k) for k in range(4)]
    iot = P.sb("iot", [128, S], F32)
    P.op("pool", lambda e: e.iota(iot[:], pattern=[[1, S]], base=0, channel_multiplier=0,
                                  allow_small_or_imprecise_dtypes=True), (), ["iot"])
    ki = P.sb("ki", [128, TH], I32)
    rr = P.sb("rr", [128, TH], F32)
    ra = P.sb("ra", [128, TH], F32)
    cs_ = P.sb("cos", [128, TH], BF16)
    sn_ = P.sb("sin", [128, TH], BF16)
    bpR = P.sb("bpR", [128, TH], BF16)
    bpI = P.sb("bpI", [128, TH], BF16)
    wR = P.sb("wR", [128, TH], BF16)
    wI = P.sb("wI", [128, TH], BF16)
    xR = P.sb("xR", [128, TH], BF16)
    xI = P.sb("xI", [128, TH], BF16)
    buR = [P.sb("buR%d" % q, [128, 512], BF16) for q in range(2)]
    buI = [P.sb("buI%d" % q, [128, 512], BF16) for q in range(2)]
    mt = [[P.sb("m%d_%d" % (i, q), [128, 512], BF16) for i in range(4)] for q in range(2)]
    mo = [P.sb("mo%d" % i, [128, TH], BF16) for i in range(4)]
    carry = P.sb("carry", [128, 16, 2], F32)
    P.memset("pool", carry[:], 0.0, ["carry"])
    ygT = P.sb("ygT", [128, 4, S], BF16)
    yv = P.sb("yv", [128, 512], F32)
    y2 = P.sb("y2", [128, 512], F32)
    sgm = P.sb("sgm", [128, 512], F32)
    it = 0
    for ft in range(4):
        for hf in range(2):
            t0 = hf * TH
            for pl in range(4):
                pr = 4 * ft + pl
                th = sc2[:, pr, 0:1]
                thq = sc2[:, pr, 1:2]
                P.ts("dve", ki[:], iot[:, t0:t0 + TH], thq, None, ALU.mult, None, ["iot"], ["ki"])
                P.act(ra[:], iot[:, t0:t0 + TH], AF.Copy, ["iot"], ["ra"], scale=th)
                P.stt("dve", rr[:], ki[:], -TWO_PI, ra[:], ALU.mult, ALU.add, ["ki", "ra"], ["rr"])
                P.ts("dve", rr[:], rr[:], math.pi, -math.pi, ALU.min, ALU.max, ["rr"], ["rr"])
                P.act(sn_[:], rr[:], AF.Sin, ["rr"], ["sin"])
                P.act(ra[:], rr[:], AF.Abs, ["rr"], ["ra"])
                P.act(cs_[:], ra[:], AF.Sin, ["ra"], ["cos"], scale=-1.0, bias=C.hpib[:])
                for c in range(2):
                    cl = slice(c * 512, (c + 1) * 512)
                    cg = slice(t0 + c * 512, t0 + (c + 1) * 512)
                    q = it % 2
                    it += 1
                    bA, bB = C.ps[2 * q], C.ps[2 * q + 1]
                    P.mm(bA[:], BtR[:, pr, :], uT[:, ft, cg], True, True, [], [("ps", 2 * q)])
                    P.mm(bB[:], BtI[:, pr, :], uT[:, ft, cg], True, True, [], [("ps", 2 * q + 1)])
                    P.copy("act", buR[q][:], bA[:], [("ps", 2 * q)], [("buR", q)])
                    P.copy("act", buI[q][:], bB[:], [("ps", 2 * q + 1)], [("buI", q)])
                    m = mt[q]
                    P.tt("dve", m[0][:], buR[q][:], cs_[:, cl], ALU.mult, [("buR", q), "cos"], [("m", q, 0)])
                    P.tt("dve", m[1][:], buI[q][:], sn_[:, cl], ALU.mult, [("buI", q), "sin"], [("m", q, 1)])
                    P.tt("dve", m[2][:], buI[q][:], cs_[:, cl], ALU.mult, [("buI", q), "cos"], [("m", q, 2)])
                    P.tt("dve", m[3][:], buR[q][:], sn_[:, cl], ALU.mult, [("buR", q), "sin"], [("m", q, 3)])
                    P.tt("dve", bpR[:, cl], m[0][:], m[1][:], ALU.add, [("m", q, 0), ("m", q, 1)], [("bpR", c)])
                    P.tt("dve", bpI[:, cl], m[2][:], m[3][:], ALU.subtract, [("m", q, 2), ("m", q, 3)], [("bpI", c)])
                P.op("dve", lambda e, pr=pr: e.tensor_tensor_scan(out=wR[:], data0=sc2[:, pr, 2:3].to_broadcast([128, TH]), data1=bpR[:],
                                                                 initial=carry[:, pr, 0:1], op0=ALU.mult, op1=ALU.add),
                     [("bpR", 0), ("bpR", 1), "carry"], ["wR"])
                P.op("dve", lambda e, pr=pr: e.tensor_tensor_scan(out=wI[:], data0=sc2[:, pr, 2:3].to_broadcast([128, TH]), data1=bpI[:],
                                                                 initial=carry[:, pr, 1:2], op0=ALU.mult, op1=ALU.add),
                     [("bpI", 0), ("bpI", 1), "carry"], ["wI"])
                if hf == 0:
                    P.copy("dve", carry[:, pr, 0:1], wR[:, TH - 1:TH], ["wR"], ["carry"])
                    P.copy("dve", carry[:, pr, 1:2], wI[:, TH - 1:TH], ["wI"], ["carry"])
                P.tt("dve", mo[0][:], wR[:], cs_[:], ALU.mult, ["wR", "cos"], ["mo0"])
                P.tt("dve", mo[1][:], wI[:], sn_[:], ALU.mult, ["wI", "sin"], ["mo1"])
                P.tt("dve", mo[2][:], wI[:], cs_[:], ALU.mult, ["wI", "cos"], ["mo2"])
                P.tt("dve", mo[3][:], wR[:], sn_[:], ALU.mult, ["wR", "sin"], ["mo3"])
                P.tt("dve", xR[:], mo[0][:], mo[1][:], ALU.subtract, ["mo0", "mo1"], ["xR"])
                P.tt("dve", xI[:], mo[2][:], mo[3][:], ALU.add, ["mo2", "mo3"], ["xI"])
                for c in range(2):
                    cl = slice(c * 512, (c + 1) * 512)
                    P.mm(C.ps[4 + c][:], CtR[:, pr, :], xR[:, cl], pl == 0, False, ["xR"], [("ps", 4 + c)])
                    P.mm(C.ps[4 + c][:], CtI[:, pr, :], xI[:, cl], False, pl == 3, ["xI"], [("ps", 4 + c)])
            for c in range(2):
                cg = slice(t0 + c * 512, t0 + (c + 1) * 512)
                P.stt("dve", yv[:], uT[:, ft, cg], evp[:, ft, 34:35], C.ps[4 + c][:], ALU.mult, ALU.add,
                      [("ps", 4 + c), "evp"], ["yv"])
                P.act(y2[:], yv[:], AF.Square, ["yv"], ["y2"])
                P.ts("dve", y2[:], y2[:], 0.044715, 1.0, ALU.mult, ALU.add, ["y2"], ["y2"])
                P.tt("dve", y2[:], y2[:], yv[:], ALU.mult, ["y2", "yv"], ["y2"])
                P.act(sgm[:], y2[:], AF.Sigmoid, ["y2"], ["sgm"], scale=1.5957691216057308)
                P.tt("pool", ygT[:, ft, cg], sgm[:], yv[:], ALU.mult, ["sgm", "yv"], [("ygT", ft, hf, c)])
    gk = [("ygT", ft, hf, c) for ft in range(4) for hf in range(2) for c in range(2)]
    n = 0
    for ot in range(4):
        for c in range(4):
            cs = slice(c * 512, (c + 1) * 512)
            b = n % 4
            n += 1
            for ft in range(4):
                P.mm(C.ps[b][:], wglu[:, ft, ot * 128:(ot + 1) * 128], ygT[:, ft, cs], ft == 0, ft == 3, gk + wglu_keys, [("ps", b)])
            P.act(sgm[:], C.ps[b][:], AF.Sigmoid, [("ps", b), "evp"], ["sgm"], bias=evp[:, ot, 35:36])
            P.tt("dve", sgm[:], sgm[:], ygT[:, ot, cs], ALU.mult, ["sgm"] + gk, ["sgm"])
            P.tt("dve", ymT[:, ot, cs], sgm[:], sgaT[:, ot, cs], ALU.mult, ["sgm"], [("ymT", ot, c)])
    P.end_phase()


def even_layer(P, C, L, src, dst):
    j = L // 2
    P.push_scope()
    ymT = P.sb("ymT", [128, 8, S], BF16)
    P.push_scope()
    uT = P.sb("uT", [128, 4, S], BF16)
    sgaT = P.sb("sgaT", [128, 4, S], BF16)
    P.push_scope()
    sgbT = P.sb("sgbT", [128, 4, S], BF16)
    hcpad = P.sb("hcpad", [128, 4, S + 30], BF16)
    P.push_scope()
    hnT = P.sb("hnT", [128, 8, S], BF16)
    phase_prenorm(P, C, L, src, hnT)
    phase_even_proj(P, C, j, hnT, uT, sgaT, sgbT, hcpad)
    P.pop_scope()
    phase_conv(P, C, j, hcpad, sgbT, ymT)
    P.pop_scope()
    P.push_scope()
    BtR = P.sb("BtR", [128, 16, 128], BF16)
    BtI = P.sb("BtI", [128, 16, 128], BF16)
    CtR = P.sb("CtR", [128, 16, 128], BF16)
    CtI = P.sb("CtI", [128, 16, 128], BF16)
    sc2 = P.sb("sc2", [128, 16, 3], F32)
    phase_s5_setup(P, C, j, BtR, BtI, CtR, CtI, sc2)
    phase_s5(P, C, j, uT, sgaT, ymT, BtR, BtI, CtR, CtI, sc2)
    P.pop_scope()
    P.pop_scope()
    phase_out(P, C, L, ymT, src, dst)
    P.pop_scope()


W_SHAPES = {
    "pre_norm": [4, D], "post_norm": [4, D], "ple_gate": [4, D, D], "ple_proj": [4, 256, D],
    "ev_w_in": [2, D, 2560], "s5_w_glu": [2, 512, 512], "cv_w_pw": [2, 512, 512], "w_out": [4, D, D],
    "evp": [2, 128, 4, 40], "s5sc": [2, 128, 16, 3], "s5bT": [2, 2, 128, 16, 128], "s5cP": [2, 2, 128, 16, 128],
    "od_w_in": [2, D, 2744], "mla_w_uq": [2, 256, 768], "mla_w_ukv": [2, 128, 1024], "mlap": [2, 128, 3],
    "ropef": [128, 2],
    "nsa_ck_w1": [2, 2048, 256], "nsa_ck_w2": [2, 256, 64], "nsa_cv_w1": [2, 2048, 256], "nsa_cv_w2": [2, 256, 64],
    "nsapos2": [2, 128, 2, 16], "selc": [128, NT, 2, 32],
}
W_BF16 = {"qaug": [4, 8, S], "kaugc": [36, S], "kaugcmp": [4, 128], "addmask": [128, S], "ovl": [128, 64], "gsel": [32, 3, 8, 128]}


def build(n_layers=4, dbg=False, odd_kw=None):
    nc = bass.Bass("TRN2", target_bir_lowering=False)
    C = Ctx()
    odd_kw = odd_kw or {}

    def din(name, shape, dt=F32):
        return nc.dram_tensor(name, list(shape), dt, kind="ExternalInput").ap()
    C.x = din("x", [S, D])
    C.p = din("p", [4, S, 256])
    C.positions = din("positions", [1, S], I32)
    C.dbg = nc.dram_tensor("dbg", [8, 128, S], F32, kind="ExternalOutput").ap() if dbg else None
    for k, shp in W_SHAPES.items():
        setattr(C, k, din(k, shp))
    for k, shp in W_BF16.items():
        setattr(C, k, din(k, shp, BF16))
    out = nc.dram_tensor("out", [S, D], F32, kind="ExternalOutput").ap()
    hbuf = nc.dram_tensor("hbuf", [S, D], F32, kind="Internal").ap()
    P = Prog(nc)
    C.ps = [P.gps("ps%d" % i, [128, 512]) for i in range(8)]
    C.ident = P.gsb("ident", [128, 128], BF16)
    C.identf = P.gsb("identf", [128, 128], F32)
    C.onesf = P.gsb("onesf", [128, 128], F32)
    C.epsb = P.gsb("epsb", [128, 1], F32)
    C.tri = P.gsb("tri", [128, 128], BF16)
    C.wmask = P.gsb("wmask", [128, 128], BF16)
    C.ropef_sb = P.gsb("ropef_sb", [128, 2], F32)
    C.hpib = P.gsb("hpib", [128, 1], F32)
    P.begin_phase()
    io = P.sb("io", [128, 128], F32)
    P.op("pool", lambda e: e.iota(io[:], pattern=[[1, 128]], base=0, channel_multiplier=-1,
                                  allow_small_or_imprecise_dtypes=True), (), ["io"])
    P.op("dve", lambda e: e.tensor_single_scalar(out=C.identf[:], in_=io[:], scalar=0.0, op=ALU.is_equal), ["io"], ["identf"])
    P.copy("dve", C.ident[:], C.identf[:], ["identf"], ["ident"])
    P.memset("pool", C.onesf[:], 1.0, ["onesf"])
    P.op("dve", lambda e: e.tensor_single_scalar(out=C.tri[:], in_=io[:], scalar=0.0, op=ALU.is_ge), ["io"], ["tri"])
    P.op("dve", lambda e: e.tensor_single_scalar(out=C.wmask[:], in_=io[:], scalar=0.0, op=ALU.is_lt), ["io"], ["wmask"])
    P.dma(C.ropef_sb[:], C.ropef[:, :], writes=["ropef_sb"])
    P.memset("pool", C.epsb[:], EPS, ["epsb"])
    P.memset("pool", C.hpib[:], math.pi / 2, ["hpib"])
    P.end_phase()
    C.ropef_dram = C.ropef
    C.ropef = C.ropef_sb
    for L in range(n_layers):
        src = C.x if L == 0 else hbuf
        dst = out if L == n_layers - 1 else hbuf
        if L % 2 == 0:
            even_layer(P, C, L, src, dst)
        else:
            odd_layer(P, C, L, src, dst, **odd_kw)
    C.ropef = C.ropef_dram
    P.close()
    return nc, P


def nsa_constants():
    import ml_dtypes
    bf = ml_dtypes.bfloat16
    t = np.arange(S)
    a_t, b_t = (t // 64).astype(np.float32), (t % 64).astype(np.float32)
    slopes = np.array([2.0 ** (-(i + 1)) for i in range(8)], np.float32)
    qaug = np.zeros((4, 8, S), np.float32)
    for h in range(8):
        qaug[0, h] = -slopes[h] * 64.0 * a_t
        qaug[1, h] = -slopes[h] * b_t
        qaug[2, h] = slopes[h] * 64.0
        qaug[3, h] = slopes[h]
    kaugc = np.zeros((36, S), np.float32)
    kaugc[t // 64, t] = 1.0
    kaugc[32] = 1.0
    kaugc[33] = 1.0
    kaugc[34] = a_t
    kaugc[35] = b_t
    c = np.arange(128)
    pc = 16 * c + 31
    kaugcmp = np.stack([np.ones(128), np.ones(128), pc // 64, pc % 64]).astype(np.float32)
    addmask = np.where((t[None, :] >= pc[:, None]) & (c[:, None] <= 126), 0.0, -30000.0).astype(np.float32)
    sb = np.arange(32)
    cs_ = c[:, None] * 16
    overlap = ((cs_ < (sb[None] + 1) * 64) & (cs_ + 32 > sb[None] * 64) & (c[:, None] <= 126)).astype(np.float32)
    ovl = np.concatenate([overlap, np.ones((128, 32), np.float32)], axis=1)
    cur = t[:, None] // 64
    forced = (sb[None] == 0) | (sb[None] == cur) | (sb[None] == cur - 1)
    causal = sb[None] * 64 <= t[:, None]
    mul = (forced | causal).astype(np.float32)
    add = np.where(forced, 1e4, np.where(causal, 0.0, -1e4)).astype(np.float32)
    selc = np.stack([mul, add], axis=1).reshape(NT, 128, 2, 32).transpose(1, 0, 2, 3)
    gsel = np.zeros((32, 3, 8, 128), np.float32)
    for b in range(3):
        for h in range(8):
            gsel[b * 8 + h, b, h, (h % 2) * 64:(h % 2) * 64 + 64] = 1.0
    return {"qaug": qaug.astype(bf), "kaugc": kaugc.astype(bf), "kaugcmp": kaugcmp.astype(bf), "addmask": addmask.astype(bf),
            "ovl": ovl.astype(bf), "gsel": gsel.astype(bf), "selc": np.ascontiguousarray(selc)}


def host_layout(inputs):
    f = lambda k: np.asarray(inputs[k], np.float32)
    ne = 2
    evp = np.zeros((ne, 128, 4, 40), np.float32)
    wdw = f("cv_w_dw")
    evp[:, :, :, 0:31] = wdw.reshape(ne, 31, 4, 128).transpose(0, 3, 2, 1)
    for col, key in ((31, "cv_b_dw"), (32, "cv_ln_g"), (33, "cv_ln_b"), (34, "s5_d"), (35, "s5_b_glu")):
        evp[:, :, :, col] = f(key).reshape(ne, 4, 128).transpose(0, 2, 1)
    s5sc = np.zeros((ne, 128, 16, 3), np.float32)
    for i, key in enumerate(("s5_lam_re", "s5_lam_im")):
        a = f(key).reshape(ne, 16, 2, 64)
        s5sc[:, :, :, i] = a.transpose(0, 2, 3, 1).reshape(ne, 128, 16)
    ls = f("s5_log_step").reshape(ne, 16, 2)
    s5sc[:, :, :, 2] = np.repeat(ls.transpose(0, 2, 1)[:, :, None, :], 64, axis=2).reshape(ne, 128, 16)
    s5bT = np.zeros((ne, 2, 128, 16, 128), np.float32)
    s5cP = np.zeros((ne, 2, 128, 16, 128), np.float32)
    for ri, (kb, kc) in enumerate((("s5_b_re", "s5_c_re"), ("s5_b_im", "s5_c_im"))):
        b = f(kb)
        c = f(kc)
        for pr in range(16):
            for gl in range(2):
                g = 2 * pr + gl
                k0 = (pr % 4) * 32 + gl * 16
                s5bT[:, ri, k0:k0 + 16, pr, gl * 64:(gl + 1) * 64] = b[:, g].transpose(0, 2, 1)
                s5cP[:, ri, gl * 64:(gl + 1) * 64, pr, k0:k0 + 16] = c[:, g].transpose(0, 2, 1)
    w_out = np.stack([f("ev_w_out")[0], f("od_w_out")[0], f("ev_w_out")[1], f("od_w_out")[1]])
    no = 2
    mlap = np.zeros((no, 128, 3), np.float32)
    mlap[:, :, 0:2] = f("mla_q_norm").reshape(no, 2, 128).transpose(0, 2, 1)
    mlap[:, :, 2] = f("mla_kv_norm")
    ropef = np.zeros((128, 2), np.float32)
    fr = (10000.0 ** (-np.arange(16, dtype=np.float32) / 16)).astype(np.float32)
    ropef[64:96, 0] = np.tile(fr, 2)
    ropef[:, 1] = ropef[:, 0] / np.float32(TWO_PI)
    rep = {"evp": evp, "s5sc": s5sc, "s5bT": s5bT, "s5cP": s5cP, "w_out": w_out, "mlap": mlap, "ropef": ropef}
    rep.update(nsa_constants())
    pos2 = np.zeros((no, 128, 2, 16), np.float32)
    for kv, key in enumerate(("nsa_pos_k", "nsa_pos_v")):
        a = f(key).reshape(no, 16, 2, 64)
        pos2[:, :, kv, :] = a.transpose(0, 2, 3, 1).reshape(no, 128, 16)
    rep["nsapos2"] = pos2
    for k in ("nsa_ck_w1", "nsa_ck_w2", "nsa_cv_w1", "nsa_cv_w2"):
        rep[k] = np.ascontiguousarray(f(k))
    for k in ("pre_norm", "post_norm", "ple_gate", "ple_proj", "ev_w_in", "s5_w_glu", "cv_w_pw", "od_w_in", "mla_w_uq", "mla_w_ukv"):
        rep[k] = np.ascontiguousarray(f(k))
    return rep


def kernel(**inputs):
    n = 8
    rep = host_layout(inputs)
    x = np.asarray(inputs["x"], np.float32)
    p = np.asarray(inputs["p"], np.float32)
    nc, _ = build(4)
    in_maps = []
    for b in range(n):
        m = dict(rep)
        m["x"] = np.ascontiguousarray(x[b])
        m["p"] = np.ascontiguousarray(p[:, b])
        m["positions"] = np.ascontiguousarray(np.asarray(inputs["positions"])[b:b + 1]).astype(np.int32)
        in_maps.append(m)
    res = run_bass_kernel_spmd(nc, in_maps, core_ids=list(range(n)))
    return np.stack([r["out"] for r in res.results], axis=0).astype(np.float32)


OD = {"q": 0, "kv": 512, "gl": 1280, "gn": 1304, "cq": 1816, "ckv": 2072, "kr": 2200, "gm": 2232}


def load_slab(P, C, W, c0, n, tag, slot, eng="pool"):
    stg = P.sb("sl_st_%s" % tag, [128, 8, n], F32)
    dst = P.sb("sl_bf_%s" % tag, [128, 8, n], BF16)
    P.dma(stg[:], W[:, c0:c0 + n].rearrange("(k p) c -> p k c", p=128), writes=[("slst", tag)])
    P.copy(eng, dst[:], stg[:], [("slst", tag)], [("slab", tag)])
    return dst


def angle_tables(P, C, ang_in, fq, f, cosT, sinT, n, tag):
    ki = P.sb("ki_" + tag, [128, n], I32)
    rr = P.sb("rr_" + tag, [128, n], F32)
    ra = P.sb("ra_" + tag, [128, n], F32)
    P.ts("dve", ki[:], ang_in, fq, None, ALU.mult, None, [tag + "in"], [tag + "ki"])
    P.ts("pool", rr[:], ki[:], -TWO_PI, None, ALU.mult, None, [tag + "ki"], [tag + "rr"])
    P.ts("pool", ra[:], ang_in, f, None, ALU.mult, None, [tag + "in"], [tag + "ra"])
    P.tt("pool", rr[:], rr[:], ra[:], ALU.add, [tag + "rr", tag + "ra"], [tag + "rr"])
    P.ts("pool", rr[:], rr[:], math.pi, -math.pi, ALU.min, ALU.max, [tag + "rr"], [tag + "rr"])
    P.act(sinT, rr[:], AF.Sin, [tag + "rr"], [tag + "sin"])
    P.act(ra[:], rr[:], AF.Abs, [tag + "rr"], [tag + "ra"])
    P.act(cosT, ra[:], AF.Sin, [tag + "ra"], [tag + "cos"], scale=-1.0, bias=C.hpib[:])


def softmax_pv_finish(P, C, ob, par, dst_rows, rec, tmpf, extra_mul, ekey, okey, wkey, first, clamp=False, gate_ps=None, gkey=None):
    lo = slice(par * 64, par * 64 + 64)
    hi = slice((1 - par) * 64, (1 - par) * 64 + 64)
    if clamp:
        P.ts("dve", rec[lo, :], ob[hi, :], 1e-18, None, ALU.max, None, [okey], [("rec", par)])
        P.act(rec[lo, :], rec[lo, :], AF.Ln, [("rec", par)], [("rec", par)])
    else:
        P.act(rec[lo, :], ob[hi, :], AF.Ln, [okey], [("rec", par)])
    P.act(rec[lo, :], rec[lo, :], AF.Exp, [("rec", par)], [("rec", par)], scale=-1.0)
    P.tt("dve", tmpf[lo, :], ob[lo, :], rec[lo, :], ALU.mult, [okey, ("rec", par)], [("tmpf", par)])
    if gate_ps is not None:
        P.tt("dve", tmpf[lo, :], tmpf[lo, :], gate_ps[lo, :], ALU.mult, [("tmpf", par), gkey], [("tmpf", par)])
    if not first:
        P.tt("dve", tmpf[lo, :], tmpf[lo, :], dst_rows, ALU.add, [("tmpf", par), wkey], [("tmpf", par)])
    if extra_mul is not None:
        P.tt("dve", dst_rows, tmpf[lo, :], extra_mul, ALU.mult, [("tmpf", par), ekey], [wkey])
    else:
        P.copy("act", dst_rows, tmpf[lo, :], [("tmpf", par)], [wkey])


def run_attention(P, C, groups, PT, depth=3, mask_eng="dve", sbanks=(0, 1), tick=None):
    flat = [(g, i) for g, (steps, fin) in enumerate(groups) for i in range(len(steps))]
    issued = 0
    for idx in range(len(flat)):
        while issued < min(len(flat), idx + depth):
            g2, i2 = flat[issued]
            st2 = groups[g2][0][i2]
            bank = sbanks[issued % len(sbanks)]
            P.mm(C.ps[bank][:, st2["c0"]:st2["c1"]], st2["lhsK"], st2["rhsQ"], True, True, st2.get("rk", []), [("ps", bank)])
            issued += 1
        g, i = flat[idx]
        steps, fin = groups[g]
        st = steps[i]
        bank = sbanks[idx % len(sbanks)]
        c0, c1 = st["c0"], st["c1"]
        pt = PT[idx % len(PT)]
        pk = ("PT", idx % len(PT))
        if st.get("scale") is not None:
            P.act(pt[:, c0:c1], C.ps[bank][:, c0:c1], AF.Exp, [("ps", bank)], [pk], scale=st["scale"])
        else:
            P.act(pt[:, c0:c1], C.ps[bank][:, c0:c1], AF.Exp, [("ps", bank)], [pk])
        for (a, b, m) in st["masks"]:
            P.tt(mask_eng, pt[:, a:b], pt[:, a:b], m, ALU.mult, [pk], [pk])
        P.mm(C.ps[st["ob"]][:, c0:c1], st["lhsV"], pt[:, c0:c1], i == 0, i == len(steps) - 1, [pk] + st.get("rv", []), [("ps", st["ob"])],
             skip=True)
        if i == len(steps) - 1:
            fin()
        if tick is not None:
            tick(idx)


def phase_mla_prep(P, C, L, hnT, cosT, sinT, cqnT, ckvnT, kropeT):
    j = L // 2
    W = C.od_w_in[j]
    P.begin_phase()
    mlap = P.sb("mlap", [128, 3], F32)
    P.dma(mlap[:], C.mlap[j], writes=["mlap"])
    posi = [P.sb("posi%d" % i, [128, 512], I32) for i in range(2)]
    posf = [P.sb("posf%d" % i, [128, 512], F32) for i in range(2)]
    ki = P.sb("rki", [128, 512], I32)
    rr = P.sb("rrr", [128, 512], F32)
    ra = P.sb("rra", [128, 512], F32)
    for c in range(4):
        cs = slice(c * 512, (c + 1) * 512)
        s = c % 2
        P.dma(posi[s][:], C.positions[0:1, cs].partition_broadcast(128), writes=[("posi", s)])
        P.copy("dve", posf[s][:], posi[s][:], [("posi", s)], [("posf", s)])
        P.ts("dve", ki[:], posf[s][:], C.ropef[:, 1:2], None, ALU.mult, None, [("posf", s)], ["ki"])
        P.ts("pool", rr[:], ki[:], -TWO_PI, None, ALU.mult, None, ["ki"], ["rr"])
        P.ts("pool", ra[:], posf[s][:], C.ropef[:, 0:1], None, ALU.mult, None, [("posf", s)], ["ra"])
        P.tt("pool", rr[:], rr[:], ra[:], ALU.add, ["rr", "ra"], ["rr"])
        P.ts("pool", rr[:], rr[:], math.pi, -math.pi, ALU.min, ALU.max, ["rr"], ["rr"])
        P.act(sinT[:, cs], rr[:], AF.Sin, ["rr"], [("sinT", c)])
        P.act(ra[:], rr[:], AF.Abs, ["rr"], ["ra"])
        P.act(cosT[:, cs], ra[:], AF.Sin, ["ra"], [("cosT", c)], scale=-1.0, bias=C.hpib[:])
    wcq = load_slab(P, C, W, OD["cq"], 256, "cq", 0)
    wckv = load_slab(P, C, W, OD["ckv"], 128, "ckv", 0, eng="dve")
    wkr = load_slab(P, C, W, OD["kr"], 32, "kr", 0, eng="dve")
    wkrA = P.sb("wkrA", [128, 8, 96], BF16)
    wkrR = P.sb("wkrR", [128, 8, 96], BF16)
    P.memset("pool", wkrA[:], 0.0, ["wkrA"])
    P.memset("pool", wkrR[:], 0.0, ["wkrR"])
    P.copy("dve", wkrA[:, :, 64:96], wkr[:], [("slab", "kr"), "wkrA"], ["wkrA"])
    P.ts("dve", wkrR[:, :, 64:80], wkr[:, :, 16:32], -1.0, None, ALU.mult, None, [("slab", "kr"), "wkrR"], ["wkrR"])
    P.copy("dve", wkrR[:, :, 80:96], wkr[:, :, 0:16], [("slab", "kr"), "wkrR"], ["wkrR"])
    cqf = [P.sb("cqf%d" % i, [128, 512], F32) for i in range(3)]
    sq = [P.sb("sqm%d" % i, [128, 512], F32) for i in range(3)]
    rs = P.sb("rsm", [128, 512], F32)
    m1 = P.sb("m1", [128, 512], F32)
    m2 = P.sb("m2", [128, 512], F32)
    for c in range(4):
        cs = slice(c * 512, (c + 1) * 512)
        for t in range(3):
            for kf in range(8):
                lhs = wcq[:, kf, t * 128:(t + 1) * 128] if t < 2 else wckv[:, kf, :]
                P.mm(C.ps[4 + t][:], lhs, hnT[:, kf, cs], kf == 0, kf == 7, [("slab", "cq"), ("slab", "ckv")], [("ps", 4 + t)])
            P.copy("act", cqf[t][:], C.ps[4 + t][:], [("ps", 4 + t)], [("cqf", t)])
            P.act(sq[t][:], C.ps[4 + t][:], AF.Square, [("ps", 4 + t)], [("sq", t)])
        for tiles, nfeat in (((0, 1), 256), ((2,), 128)):
            for i, t in enumerate(tiles):
                P.mm(C.ps[7][:], C.onesf[:], sq[t][:], i == 0, i == len(tiles) - 1, [("sq", t)], [("ps", 7)])
            P.act(rs[:], C.ps[7][:], AF.Ln, [("ps", 7)], ["rs"], scale=1.0 / nfeat, bias=C.epsb[:])
            P.act(rs[:], rs[:], AF.Exp, ["rs"], ["rs"], scale=-0.5)
            for t in tiles:
                dst = cqnT[:, t, cs] if t < 2 else ckvnT[:, cs]
                P.stt("dve", dst, cqf[t][:], mlap[:, t:t + 1], rs[:], ALU.mult, ALU.mult, [("cqf", t), "rs", "mlap"], [("cn", t, c)])
        for kf in range(8):
            P.mm(C.ps[0][:96, :], wkrA[:, kf, :], hnT[:, kf, cs], kf == 0, kf == 7, ["wkrA"], [("ps", 0)])
        for kf in range(8):
            P.mm(C.ps[1][:96, :], wkrR[:, kf, :], hnT[:, kf, cs], kf == 0, kf == 7, ["wkrR"], [("ps", 1)])
        P.tt("dve", m1[64:96, :], C.ps[0][64:96, :], cosT[64:96, cs], ALU.mult, [("ps", 0), ("cosT", c)], ["m1"])
        P.tt("dve", m2[64:96, :], C.ps[1][64:96, :], sinT[64:96, cs], ALU.mult, [("ps", 1), ("sinT", c)], ["m2"])
        P.tt("pool", kropeT[64:96, cs], m1[64:96, :], m2[64:96, :], ALU.add, ["m1", "m2"], [("krope", c)])
    P.end_phase()


def phase_mla(P, C, L, hnT, ymT, cosT, sinT, cqnT, ckvnT, kropeT):
    j = L // 2
    W = C.od_w_in[j]
    SC = 96 ** -0.5
    P.begin_phase()
    wuq = P.sb("wuq", [128, 2, 768], BF16)
    load_w_bf16(P, C, wuq, C.mla_w_uq[j], 2, 768, "wuq")
    wuq_keys = [("w", "wuq", k) for k in range(2)]
    wuqR = P.sb("wuqR", [128, 2, 8, 96], BF16)
    P.memset("pool", wuqR[:], 0.0, ["wuqR"])
    wuq4 = wuq[:].rearrange("p k (h c) -> p k h c", h=8)
    for t in range(2):
        P.ts("dve", wuqR[:, t, :, 64:80], wuq4[:, t, :, 80:96], -1.0, None, ALU.mult, None, wuq_keys + ["wuqR"], ["wuqR"])
        P.copy("dve", wuqR[:, t, :, 80:96], wuq4[:, t, :, 64:80], wuq_keys + ["wuqR"], ["wuqR"])
    wukv = P.sb("wukv", [128, 1, 1024], BF16)
    load_w_bf16(P, C, wukv, C.mla_w_ukv[j], 1, 1024, "wukv")
    wukv_k = [("w", "wukv", 0)]
    m1s = [P.sb("m1_%d" % i, [128, 512], F32) for i in range(2)]
    m2s = [P.sb("m2_%d" % i, [128, 512], F32) for i in range(2)]
    QT = [P.sb("QT%d" % i, [128, 2, S], BF16) for i in range(2)]
    KT = [P.sb("KT%d" % i, [128, 2, S], BF16) for i in range(2)]
    VA = [P.sb("VA%d" % i, [128, NT, 2, 128], BF16) for i in range(2)]
    sgm = [P.sb("sgmT%d" % i, [128, S], BF16) for i in range(2)]
    PT = [P.sb("PT%d" % i, [128, 512], BF16) for i in range(4)]
    rec = P.sb("rec", [128, 512], F32)
    tmpf = P.sb("tmpf", [128, 512], F32)
    gst = [P.sb("gst%d" % i, [128, 8, 128], F32) for i in range(2)]
    gbf = [P.sb("gbf%d" % i, [128, 8, 128], BF16) for i in range(2)]
    wukv3 = wukv[:, 0, :].rearrange("p (h c) -> p h c", h=8)
    for i in range(2):
        P.memset("dve" if i == 0 else "pool", VA[i][:], 1.0, [("VA", i)])
    cnt = {"ps": 0, "m": 0}

    def pbank():
        b = (4, 5, 7)[cnt["ps"] % 3]
        cnt["ps"] += 1
        return b

    def proj_units(hp, sl):
        for par in range(2):
            h = 2 * hp + par
            for c in range(4):
                cs = slice(c * 512, (c + 1) * 512)
                b = pbank()
                P.mm(C.ps[b][:64, :], wukv[:, 0, h * 128:h * 128 + 64], ckvnT[:, cs], True, True, wukv_k, [("ps", b)])
                P.copy("dve", KT[sl][0:64, par, cs], C.ps[b][:64, :], [("ps", b)], [("KT", sl, par)])
                bA = pbank()
                bR = pbank()
                for t in range(2):
                    P.mm(C.ps[bA][:96, :], wuq[:, t, h * 96:(h + 1) * 96], cqnT[:, t, cs], t == 0, t == 1, wuq_keys, [("ps", bA)])
                for t in range(2):
                    P.mm(C.ps[bR][:96, :], wuqR[:, t, h, :], cqnT[:, t, cs], t == 0, t == 1, ["wuqR"], [("ps", bR)])
                P.copy("dve", QT[sl][0:64, par, cs], C.ps[bA][0:64, :], [("ps", bA)], [("QT", sl, par)])
                mi = cnt["m"] % 2
                cnt["m"] += 1
                m1, m2 = m1s[mi], m2s[mi]
                P.tt("dve", m1[64:96, :], C.ps[bA][64:96, :], cosT[64:96, cs], ALU.mult, [("ps", bA)], [("m1", mi)])
                P.tt("dve", m2[64:96, :], C.ps[bR][64:96, :], sinT[64:96, cs], ALU.mult, [("ps", bR)], [("m2", mi)])
                P.tt("dve", QT[sl][64:96, par, cs], m1[64:96, :], m2[64:96, :], ALU.add, [("m1", mi), ("m2", mi)], [("QT", sl, par)])
                yield
            P.copy("dve", KT[sl][64:96, par, :], kropeT[64:96, :], [], [("KT", sl, par)])
        for kt in range(NT):
            b = pbank()
            P.mm(C.ps[b][:, 0:128], ckvnT[:, kt * 128:(kt + 1) * 128], wukv3[:, 2 * hp:2 * hp + 2, 64:128], True, True,
                 wukv_k, [("ps", b)])
            P.copy("dve", VA[sl][:, kt, 0, 0:64], C.ps[b][:, 0:64], [("ps", b), ("VA", sl)], [("VA", sl)])
            P.copy("dve", VA[sl][:, kt, 1, 64:128], C.ps[b][:, 64:128], [("ps", b), ("VA", sl)], [("VA", sl)])
            if kt % 2 == 1:
                yield
        c0 = OD["gm"] + hp * 128
        P.dma(gst[sl][:], W[:, c0:c0 + 128].rearrange("(k p) c -> p k c", p=128), writes=[("gst", sl)])
        P.copy("pool", gbf[sl][:], gst[sl][:], [("gst", sl)], [("gbf", sl)])
        for c in range(4):
            cs = slice(c * 512, (c + 1) * 512)
            b = pbank()
            for kf in range(8):
                P.mm(C.ps[b][:], gbf[sl][:, kf, :], hnT[:, kf, cs], kf == 0, kf == 7, [("gbf", sl)], [("ps", b)])
            P.act(sgm[sl][:, cs], C.ps[b][:], AF.Silu, [("ps", b)], [("sgm", sl, c)])
            yield

    for _ in proj_units(0, 0):
        pass
    for hp in range(4):
        sl = hp % 2
        tile_i = 4 + hp
        gen = proj_units(hp + 1, 1 - sl) if hp < 3 else None
        groups = []
        for par in range(2):
            for qc in range(4):
                ob = 2 + (qc % 2)
                nk = 4 * qc + 4
                steps = []
                for kt in range(nk):
                    o = max(0, kt - 4 * qc)
                    q0 = qc * 512 + o * 128
                    ncol = 512 - o * 128
                    steps.append(dict(c0=o * 128, c1=512, lhsK=KT[sl][0:96, par, kt * 128:(kt + 1) * 128], rhsQ=QT[sl][0:96, par, q0:q0 + ncol],
                                      rk=[("KT", sl, par), ("QT", sl, par)], scale=SC,
                                      masks=[(o * 128, (o + 1) * 128, C.tri[:])] if kt >= 4 * qc else [],
                                      lhsV=VA[sl][:, kt, par, :], ob=ob, rv=[("VA", sl)]))

                def fin(par=par, qc=qc, ob=ob, tile_i=tile_i, sl=sl):
                    lo = slice(par * 64, par * 64 + 64)
                    qs = slice(qc * 512, (qc + 1) * 512)
                    softmax_pv_finish(P, C, C.ps[ob], par, ymT[lo, tile_i, qs], rec, tmpf, sgm[sl][lo, qs], ("sgm", sl, qc),
                                      ("ps", ob), ("ymT", tile_i, par, qc), True)
                groups.append((steps, fin))

        def tick(idx, gen=gen):
            if gen is not None and idx % 2 == 1:
                next(gen, None)
        run_attention(P, C, groups, PT, sbanks=(0, 1, 6), tick=tick, mask_eng="pool")
        if gen is not None:
            for _ in gen:
                pass
    P.end_phase()


def odd_layer(P, C, L, src, dst, do_nsa=True, do_mla=True):
    P.push_scope()
    ymT = P.sb("ymT", [128, 8, S], BF16)
    hnT = P.sb("hnT", [128, 8, S], BF16)
    phase_prenorm(P, C, L, src, hnT)
    if not (do_nsa and do_mla):
        P.begin_phase()
        P.memset("pool", ymT[:], 0.0, ["ymT"])
        P.end_phase()
    if do_nsa:
        nsa_mixer(P, C, L, hnT, ymT)
    if do_mla:
        P.push_scope()
        cosT = P.sb("cosT", [128, S], F32)
        sinT = P.sb("sinT", [128, S], F32)
        cqnT = P.sb("cqnT", [128, 2, S], BF16)
        ckvnT = P.sb("ckvnT", [128, S], BF16)
        kropeT = P.sb("kropeT", [128, S], BF16)
        phase_mla_prep(P, C, L, hnT, cosT, sinT, cqnT, ckvnT, kropeT)
        phase_mla(P, C, L, hnT, ymT, cosT, sinT, cqnT, ckvnT, kropeT)
        P.pop_scope()
    if C.dbg is not None:
        P.begin_phase()
        cv = [P.sb("dbgc%d" % i, [128, S], F32) for i in range(2)]
        for t in range(8):
            P.copy("dve", cv[t % 2][:], ymT[:, t, :], [], [("cv", t % 2)])
            P.dma(C.dbg[t], cv[t % 2][:], reads=[("cv", t % 2)], writes=[("dbg", t % 2)])
        P.end_phase()
    phase_out(P, C, L, ymT, src, dst)
    P.pop_scope()


class SlabLoader:
    def __init__(self, P, tag):
        self.P = P
        self.tag = tag
        self.st = [P.sb("sls_%s%d" % (tag, i), [128, 8, 128], F32) for i in range(2)]
        self.bf = [P.sb("slb_%s%d" % (tag, i), [128, 8, 128], BF16) for i in range(2)]
        self.n = 0

    def load(self, W, c0, n, eng="pool"):
        s = self.n % 2
        self.n += 1
        P = self.P
        P.dma(self.st[s][:, :, 0:n], W[:, c0:c0 + n].rearrange("(k p) c -> p k c", p=128), writes=[("sls", self.tag, s)])
        P.copy(eng, self.bf[s][:, :, 0:n], self.st[s][:, :, 0:n], [("sls", self.tag, s)], [("slb", self.tag, s)])
        return self.bf[s], ("slb", self.tag, s)


def proj_fm(P, C, slab, skey, n, hnT, consume, banks=(6, 7)):
    for c in range(4):
        cs = slice(c * 512, (c + 1) * 512)
        b = banks[c % len(banks)]
        for kf in range(8):
            P.mm(C.ps[b][0:n, :], slab[:, kf, 0:n], hnT[:, kf, cs], kf == 0, kf == 7, [skey], [("ps", b)])
        consume(c, cs, C.ps[b], ("ps", b))


def gelu_tanh(P, x_ps, xkey, out, okey, y2, t1, sg, tag):
    P.act(y2, x_ps, AF.Square, [xkey], [tag + "y2"])
    P.ts("dve", y2, y2, 0.044715, 1.0, ALU.mult, ALU.add, [tag + "y2"], [tag + "y2"])
    P.tt("dve", t1, y2, x_ps, ALU.mult, [tag + "y2", xkey], [tag + "t1"])
    P.act(sg, t1, AF.Sigmoid, [tag + "t1"], [tag + "sg"], scale=1.5957691216057308)
    P.tt("dve", out, sg, x_ps, ALU.mult, [tag + "sg", xkey], [okey])


def nsa_mixer(P, C, L, hnT, ymT):
    j = L // 2
    W = C.od_w_in[j]
    P.push_scope()
    QA = P.sb("QaugT", [128, 8, S], BF16)
    KS = P.sb("KselA", [128, 2, S], BF16)
    KW = P.sb("KwinA", [128, 2, S], BF16)
    SgT = P.sb("SgT", [32, S], BF16)
    KcA = P.sb("KcA", [128, 2, 128], BF16)
    VcA = P.sb("VcA", [128, 2, 2, 128], BF16)
    gsel = P.sb("gsel", [32, 3, 8, 128], BF16)

    P.begin_phase()
    SL = SlabLoader(P, "a")
    P.memset("pool", QA[64:96, :, :], 0.0, ["QAmask"])
    P.dma(QA[96:100, :, :], C.qaug[:, :, :], writes=["QAaug"])
    P.dma(gsel[:], C.gsel[:, :, :, :], writes=["gsel"])
    for y in range(2):
        P.dma(KS[64:96, y, :], C.kaugc[0:32, :], writes=[("KSe", y)])
        P.dma(KS[96:100, y, :], C.kaugc[32:36, :], writes=[("KSa", y)])
        P.dma(KW[96:100, y, :], C.kaugc[32:36, :], writes=[("KWa", y)])
    P.memset("pool", KW[64:96, :, :], 0.0, ["KWz"])
    P.memset("pool", SgT[:], 0.0, ["SgT0"])
    import os
    part = int(os.environ.get("NSA_PART", "9"))
    for hp in range(4 if part >= 1 else 0):
        slab, sk = SL.load(W, OD["q"] + hp * 128, 128)

        def cons_q(c, cs, ps, pk, hp=hp):
            for par in range(2):
                P.act(QA[0:64, 2 * hp + par, cs], ps[par * 64:(par + 1) * 64, :], AF.Copy, [pk], [("QA", 2 * hp + par, c)], scale=0.125)
        proj_fm(P, C, slab, sk, 128, hnT, cons_q)
    for (i, dst, nm) in (((2, KS, "KS"), (4, KW, "KW")) if part >= 2 else ()):
        slab, sk = SL.load(W, OD["kv"] + i * 128, 128)

        def cons_k(c, cs, ps, pk, dst=dst, nm=nm):
            for y in range(2):
                P.copy("act" if y == 0 else "dve", dst[0:64, y, cs], ps[y * 64:(y + 1) * 64, :], [pk], [(nm, y, c)])
        proj_fm(P, C, slab, sk, 128, hnT, cons_k)
    def cons_g(c, cs, ps, pk):
        P.act(SgT[0:32, cs], ps[0:32, :], AF.Sigmoid, [pk, "SgT0"], [("SgT", c)])
    if part >= 3:
        slab, sk = SL.load(W, OD["gl"], 32)
        proj_fm(P, C, slab, sk, 32, hnT, cons_g)
    P.end_phase()
    stop = os.environ.get("NSA_STOP", "")
    if stop == "a":
        P.pop_scope()
        return

    P.begin_phase()
    SL = SlabLoader(P, "b")
    P.memset("pool", VcA[:], 1.0, ["VcA"])
    P.memset("pool", KcA[64:96, :, :], 0.0, ["KcAz"])
    for y in range(2):
        P.dma(KcA[96:100, y, :], C.kaugcmp[:, :], writes=[("KcAa", y)])
    K2 = [P.sb("K2_%d" % y, [128, S], BF16) for y in range(2)]
    G = P.sb("Gc", [128, 16, 128], BF16)
    GT = P.sb("GTc", [128, 2, 128], BF16)
    w1st = P.sb("w1st", [128, 16, 256], F32)
    w1b = P.sb("w1b", [128, 16, 256], BF16)
    w2st = P.sb("w2st", [128, 2, 64], F32)
    w2b = P.sb("w2b", [128, 2, 64], BF16)
    pos2 = P.sb("pos2", [128, 2, 16], F32)
    P.dma(pos2[:], C.nsapos2[j], writes=["pos2"])
    y2 = P.sb("gy2", [128, 128], F32)
    t1 = P.sb("gt1", [128, 128], F32)
    sg = P.sb("gsg", [128, 128], F32)
    P.memset("pool", GT[:], 0.0, ["GT0"])
    for kv in range(2):
        w1 = (C.nsa_ck_w1 if kv == 0 else C.nsa_cv_w1)[j]
        w2 = (C.nsa_ck_w2 if kv == 0 else C.nsa_cv_w2)[j]
        P.dma(w1st[:], w1.rearrange("(j p) c -> p j c", p=128), writes=["w1st"])
        P.copy("pool", w1b[:], w1st[:], ["w1st"], ["w1b"])
        P.dma(w2st[:], w2.rearrange("(k p) c -> p k c", p=128), writes=["w2st"])
        P.copy("dve", w2b[:], w2st[:], ["w2st"], ["w2b"])
        slab, sk = SL.load(W, OD["kv"] + kv * 128, 128, eng="dve")
        for y in range(2):
            P.memset("pool", K2[y][64:128, S - 1:S], 0.0, [("K2z", y)])

        def cons_c(c, cs, ps, pk):
            for y in range(2):
                src = ps[y * 64:(y + 1) * 64, :]
                eng = "act" if y == 0 else "dve"
                P.copy(eng, K2[y][0:64, cs], src, [pk], [("K2a", y, c)])
                if c == 0:
                    P.copy(eng, K2[y][64:128, 0:511], ps[y * 64:(y + 1) * 64, 1:512], [pk], [("K2b", y, c)])
                else:
                    P.copy(eng, K2[y][64:128, c * 512 - 1:c * 512 + 511], src, [pk, ("K2z", y)], [("K2b", y, c)])
        if part >= 1:
            proj_fm(P, C, slab, sk, 128, hnT, cons_c)
        k2keys = [[("K2a", y, c) for c in range(4)] + [("K2b", y, c) for c in range(4)] + [("K2z", y)] for y in range(2)]
        for y in range(2 if part >= 2 else 0):
            for jj in range(16):
                P.ts("dve" if jj % 2 == 0 else "pool", G[:, jj, 0:127], K2[y][:, 2 * jj:2 * jj + 2017:16], pos2[:, kv, jj:jj + 1], None, ALU.add, None,
                     k2keys[y] + ["pos2"], [("G", jj)])
            if part < 3:
                continue
            for ht in range(2):
                b = 4 + ht
                for jj in range(16):
                    P.mm(C.ps[b][:, 0:127], w1b[:, jj, ht * 128:(ht + 1) * 128], G[:, jj, 0:127], jj == 0, jj == 15, [("G", jj), "w1b"], [("ps", b)])
                gelu_tanh(P, C.ps[b][:, 0:127], ("ps", b), GT[:, ht, 0:127], ("GT", ht), y2[:, 0:127], t1[:, 0:127], sg[:, 0:127], "g")
            if part < 4:
                continue
            if kv == 0:
                for ht in range(2):
                    P.mm(C.ps[2][0:64, 0:128], w2b[:, ht, :], GT[:, ht, :], ht == 0, ht == 1, [("GT", ht), "GT0", "w2b"], [("ps", 2)])
                P.copy("act", KcA[0:64, y, :], C.ps[2][0:64, 0:128], [("ps", 2)], [("KcA", y)])
            elif part >= 5:
                for ht in range(2):
                    P.mm(C.ps[3][:, 0:64], GT[:, ht, :], w2b[:, ht, :], ht == 0, ht == 1, [("GT", ht), "GT0", "w2b"], [("ps", 3)])
                if part >= 6:
                    P.copy("dve", VcA[:, y, 0, 0:64], C.ps[3][:, 0:64], [("ps", 3), "VcA"], [("VcAw", y, 0)])
                    P.copy("dve", VcA[:, y, 1, 64:128], C.ps[3][:, 0:64], [("ps", 3), "VcA"], [("VcAw", y, 1)])
    P.end_phase()
    if stop == "b":
        P.pop_scope()
        return

    P.begin_phase()
    addm = P.sb("addm", [128, S], BF16)
    P.dma(addm[:], C.addmask[:, :], writes=["addm"])
    ovl = P.sb("ovl", [128, 64], BF16)
    P.dma(ovl[:], C.ovl[:, :], writes=["ovl"])
    selc = P.sb("selc", [128, NT, 2, 32], F32)
    P.dma(selc[:], C.selc[:, :, :, :], writes=["selc"])
    pslc = P.sb("pslcT", [32, 2, S], F32)
    sm = [P.sb("smc%d" % i, [128, 512], F32) for i in range(2)]
    PT = [P.sb("PTc%d" % i, [128, 512], BF16) for i in range(2)]
    rec = P.sb("rec", [128, 512], F32)
    tmpf = P.sb("tmpf", [128, 512], F32)
    rec2 = P.sb("rec2", [32, 512], F32)
    tmp2 = P.sb("tmp2", [32, 512], F32)
    items = [(h, qc) for h in range(8) for qc in range(4)]

    def cmpA(i):
        h, qc = items[i]
        y = h // 4
        qs = slice(qc * 512, (qc + 1) * 512)
        s_ = i % 2
        P.mm(C.ps[s_][:], KcA[0:100, y, :], QA[0:100, h, qs], True, True, [], [("ps", s_)])
        P.tt("dve", sm[s_][:], C.ps[s_][:], addm[:, qs], ALU.add, [("ps", s_), "addm"], [("sm", s_)])
        P.act(PT[s_][:], sm[s_][:], AF.Exp, [("sm", s_)], [("PT", s_)])

    def cmpB(i):
        h, qc = items[i]
        y, par, tile_i, hh = h // 4, h % 2, h // 2, h % 4
        lo = slice(par * 64, par * 64 + 64)
        qs = slice(qc * 512, (qc + 1) * 512)
        s_ = i % 2
        ob = 2 + s_
        P.mm(C.ps[ob][:], VcA[:, y, par, :], PT[s_][:], True, True, [("PT", s_)], [("ps", ob)])
        P.mm(C.ps[5][0:64, :], ovl[:], PT[s_][:], True, True, [("PT", s_), "ovl"], [("ps", 5)])
        P.mm(C.ps[4][:], gsel[0:32, 0, h, :], SgT[0:32, qs], True, True, ["gsel"], [("ps", 4)])
        softmax_pv_finish(P, C, C.ps[ob], par, ymT[lo, tile_i, qs], rec, tmpf, None, None, ("ps", ob), ("ymT", tile_i, par, qc), True,
                          clamp=True, gate_ps=C.ps[4], gkey=("ps", 4))
        P.ts("dve", rec2[:], C.ps[5][32:64, :], 1e-18, None, ALU.max, None, [("ps", 5)], ["rec2"])
        P.act(rec2[:], rec2[:], AF.Ln, ["rec2"], ["rec2"])
        P.act(rec2[:], rec2[:], AF.Exp, ["rec2"], ["rec2"], scale=-1.0)
        if hh == 0:
            P.tt("dve", pslc[:, y, qs], C.ps[5][0:32, :], rec2[:], ALU.mult, [("ps", 5), "rec2"], [("pslc", y, qc)])
        else:
            P.tt("dve", tmp2[:], C.ps[5][0:32, :], rec2[:], ALU.mult, [("ps", 5), "rec2"], ["tmp2"])
            P.tt("dve", pslc[:, y, qs], pslc[:, y, qs], tmp2[:], ALU.add, ["tmp2", ("pslc", y, qc)], [("pslc", y, qc)])

    cmpA(0)
    for i in range(len(items)):
        if i + 1 < len(items):
            cmpA(i + 1)
        cmpB(i)
    sc = [P.sb("scs%d" % i, [128, 32], F32) for i in range(2)]
    m8 = [P.sb("m8s%d" % i, [128, 8], F32) for i in range(2)]
    ng = [P.sb("ngs%d" % i, [128, 32], F32) for i in range(2)]
    sitems = [(y, qt) for y in range(2) for qt in range(NT)]

    def selA(i):
        y, qt = sitems[i]
        s_ = i % 2
        ts_ = slice(qt * 128, (qt + 1) * 128)
        P.tr(C.ps[6 + s_][:, 0:32], pslc[:, y, ts_], C.identf[0:32, 0:32], [("pslc", y, qt // 4)], [("ps", 6 + s_)])
        P.tt("dve", sc[s_][:], C.ps[6 + s_][:, 0:32], selc[:, qt, 0, :], ALU.mult, [("ps", 6 + s_), "selc"], [("sc", s_)])
        P.tt("dve", sc[s_][:], sc[s_][:], selc[:, qt, 1, :], ALU.add, [("sc", s_), "selc"], [("sc", s_)])
        P.op("dve", lambda e, s_=s_: e.max(out=m8[s_][:], in_=sc[s_][:]), [("sc", s_)], [("m8", s_)])
        P.ts("dve", ng[s_][:], sc[s_][:], m8[s_][:, 7:8], 30000.0, ALU.is_ge, ALU.mult, [("sc", s_), ("m8", s_)], [("ng", s_)])
        P.ts("dve", ng[s_][:], ng[s_][:], -30000.0, None, ALU.add, None, [("ng", s_)], [("ng", s_)])

    def selB(i):
        y, qt = sitems[i]
        s_ = i % 2
        ts_ = slice(qt * 128, (qt + 1) * 128)
        P.tr(C.ps[4 + s_][0:32, 0:128], ng[s_][:], C.identf[:], [("ng", s_)], [("ps", 4 + s_)])
        for hh in range(4):
            eng = "act" if s_ == 0 else "dve"
            P.copy(eng, QA[64:96, 4 * y + hh, ts_], C.ps[4 + s_][0:32, 0:128], [("ps", 4 + s_), "QAmask"], [("QAm", 4 * y + hh, qt)])

    selA(0)
    for i in range(len(sitems)):
        if i + 1 < len(sitems):
            selA(i + 1)
        selB(i)
    P.end_phase()
    if stop == "c":
        P.pop_scope()
        return

    for br in (1, 2):
        P.begin_phase()
        SL = SlabLoader(P, "v%d" % br)
        VA = P.sb("VAn", [128, NT, 2, 2, 128], BF16)
        P.memset("dve", VA[:], 1.0, ["VA"])
        slab, sk = SL.load(W, OD["kv"] + (3 if br == 1 else 5) * 128, 128)
        for kt in range(NT):
            b = 6 + (kt % 2)
            for kf in range(8):
                P.mm(C.ps[b][:, 0:128], hnT[:, kf, kt * 128:(kt + 1) * 128], slab[:, kf, :], kf == 0, kf == 7, [sk], [("ps", b)])
            pv = C.ps[b][:, 0:128].rearrange("p (y c) -> p y c", y=2)
            eng = "act" if kt % 2 == 0 else "dve"
            P.copy(eng, VA[:, kt, :, 0, 0:64], pv, [("ps", b), "VA"], [("VAw", kt, 0)])
            P.copy(eng, VA[:, kt, :, 1, 64:128], pv, [("ps", b), "VA"], [("VAw", kt, 1)])
        sgn = None
        if br == 2:
            sgn = P.sb("sgn", [128, 4, S], BF16)
            for hp in range(4):
                slab2, sk2 = SL.load(W, OD["gn"] + hp * 128, 128, eng="dve")

                def cons_gn(c, cs, ps, pk, hp=hp):
                    P.act(sgn[:, hp, cs], ps[:], AF.Silu, [pk], [("sgn", hp, c)])
                proj_fm(P, C, slab2, sk2, 128, hnT, cons_gn)
        KA = KS if br == 1 else KW
        PT = [P.sb("PTn%d" % i, [128, 512], BF16) for i in range(6)]
        rec = P.sb("rec", [128, 512], F32)
        tmpf = P.sb("tmpf", [128, 512], F32)
        groups = []
        for h in range(8):
            y, par, tile_i = h // 4, h % 2, h // 2
            for qc in range(4):
                ob = 2 + (qc % 2)
                kts = list(range(0, 4 * qc + 4)) if br == 1 else list(range(max(0, 4 * qc - 4), 4 * qc + 4))
                steps = []
                for kt in kts:
                    o = kt - 4 * qc
                    rlo = max(o, 0)
                    rhi = 3 if br == 1 else min(o + 4, 3)
                    c0, c1 = rlo * 128, (rhi + 1) * 128
                    masks = []
                    if o >= 0:
                        masks.append((o * 128, (o + 1) * 128, C.tri[:]))
                    if br == 2 and o <= -1:
                        masks.append(((o + 4) * 128, (o + 5) * 128, C.wmask[:]))
                    steps.append(dict(c0=c0, c1=c1, lhsK=KA[0:100, y, kt * 128:(kt + 1) * 128],
                                      rhsQ=QA[0:100, h, qc * 512 + c0:qc * 512 + c1], scale=None, masks=masks,
                                      lhsV=VA[:, kt, y, par, :], ob=ob, rv=[("VAw", kt, par)]))

                def fin(h=h, par=par, qc=qc, ob=ob, tile_i=tile_i):
                    lo = slice(par * 64, par * 64 + 64)
                    qs = slice(qc * 512, (qc + 1) * 512)
                    P.mm(C.ps[4][:], gsel[0:32, br, h, :], SgT[0:32, qs], True, True, [], [("ps", 4)])
                    softmax_pv_finish(P, C, C.ps[ob], par, ymT[lo, tile_i, qs], rec, tmpf,
                                      sgn[lo, tile_i, qs] if br == 2 else None, ("sgn", tile_i, qc) if br == 2 else None,
                                      ("ps", ob), ("ymT", tile_i, par, qc), False, clamp=False, gate_ps=C.ps[4], gkey=("ps", 4))
                groups.append((steps, fin))
        run_attention(P, C, groups, PT, sbanks=(0, 1, 5, 7), depth=4)
        P.end_phase()
    P.pop_scope()
```

```python
import math
from contextlib import ExitStack

import numpy as np
import concourse.bass as bass
import concourse.mybir as mybir
from concourse.bass_utils import run_bass_kernel_spmd

F32 = mybir.dt.float32
BF16 = mybir.dt.bfloat16
I32 = mybir.dt.int32
ALU = mybir.AluOpType
AF = mybir.ActivationFunctionType

S = 2048
D = 1024
NT = S // 128
EPS = 1e-6
ENGS = ("pe", "act", "dve", "pool", "sp")
CENG = ("pe", "act", "dve", "pool")
N_DMA_SEMS = 84
TWO_PI = 2.0 * math.pi


class Prog:
    def __init__(self, nc):
        self.nc = nc
        self.gstack = ExitStack()
        self.engsem = {e: self.gstack.enter_context(nc.semaphore("es_" + e)) for e in CENG}
        self.dsems = [self.gstack.enter_context(nc.semaphore("ds%d" % i)) for i in range(N_DMA_SEMS)]
        self.engcnt = {e: 0 for e in CENG}
        self.dcnt = [0] * N_DMA_SEMS
        self.scopes = []
        self.uid = 0
        self.n_instr = 0

    def close(self):
        self.gstack.close()

    def gsb(self, name, shape, dt):
        return self.gstack.enter_context(self.nc.sbuf_tensor(name, list(shape), dt))

    def gps(self, name, shape, dt=F32):
        return self.gstack.enter_context(self.nc.psum_tensor(name, list(shape), dt))

    def sb(self, name, shape, dt):
        self.uid += 1
        return self.scopes[-1].enter_context(self.nc.sbuf_tensor("%s_%d" % (name, self.uid), list(shape), dt))

    def push_scope(self):
        self.scopes.append(ExitStack())

    def pop_scope(self):
        self.scopes.pop().close()

    def begin_phase(self):
        self.push_scope()
        self.ins = []
        self.last_w = {}
        self.readers = {}
        self.eng_seq = {e: [] for e in ENGS}
        self.semmap = {}

    def _add(self, eng, fn, reads, writes, kind, semkey=None):
        idx = len(self.ins)
        deps = set()
        for k in reads:
            if k in self.last_w:
                deps.add((self.last_w[k], 0))
            if isinstance(k, tuple) and k[0] == "ps":
                for r in self.readers.get(k, ()):
                    if self.ins[r][0] != eng:
                        deps.add((r, 1))
        for k in writes:
            if k in self.last_w:
                deps.add((self.last_w[k], 1))
            for r in self.readers.get(k, ()):
                deps.add((r, 2))
        for k in reads:
            self.readers.setdefault(k, []).append(idx)
        for k in writes:
            self.last_w[k] = idx
            self.readers[k] = []
        self.ins.append((eng, fn, kind, semkey, deps))
        self.eng_seq[eng].append(idx)
        return idx

    def op(self, eng, fn, reads=(), writes=()):
        return self._add(eng, fn, list(reads), list(writes), "c")

    def dma(self, out, in_, reads=(), writes=(), q="sp", **kw):
        semkey = (q, tuple(writes))
        if semkey not in self.semmap:
            assert len(self.semmap) < N_DMA_SEMS, "too many dma sem keys"
            self.semmap[semkey] = len(self.semmap)
        fn = lambda e: e.dma_start(out=out, in_=in_, **kw)
        return self._add(q, fn, list(reads), list(writes), "d", semkey)

    def act(self, out, in_, func, r, w, **kw):
        self.op("act", lambda e: e.activation(out=out, in_=in_, func=func, **kw), r, w)

    def tt(self, eng, out, in0, in1, op, r, w):
        self.op(eng, lambda e: e.tensor_tensor(out=out, in0=in0, in1=in1, op=op), r, w)

    def ts(self, eng, out, in0, s1, s2, op0, op1, r, w, **kw):
        if s2 is None:
            self.op(eng, lambda e: e.tensor_scalar(out=out, in0=in0, scalar1=s1, scalar2=None, op0=op0, **kw), r, w)
        else:
            self.op(eng, lambda e: e.tensor_scalar(out=out, in0=in0, scalar1=s1, scalar2=s2, op0=op0, op1=op1, **kw), r, w)

    def stt(self, eng, out, in0, scalar, in1, op0, op1, r, w):
        self.op(eng, lambda e: e.scalar_tensor_tensor(out=out, in0=in0, scalar=scalar, in1=in1, op0=op0, op1=op1), r, w)

    def copy(self, eng, out, in_, r, w):
        if eng == "act":
            self.op(eng, lambda e: e.copy(out=out, in_=in_), r, w)
        else:
            self.op(eng, lambda e: e.tensor_copy(out=out, in_=in_), r, w)

    def memset(self, eng, ap, val, w):
        self.op(eng, lambda e: e.memset(ap, val), (), w)

    def mm(self, out, lhsT, rhs, start, stop, r, w, skip=False):
        if skip:
            self.op("pe", lambda e: e.matmul(out, lhsT=lhsT, rhs=rhs, start=start, stop=stop, skip_group_check=True), r, w)
        else:
            self.op("pe", lambda e: e.matmul(out, lhsT=lhsT, rhs=rhs, start=start, stop=stop), r, w)

    def tr(self, out, in_, ident, r, w):
        self.op("pe", lambda e: e.transpose(out=out, in_=in_, identity=ident), r, w)

    def recip(self, out, in_, r, w):
        self.op("dve", lambda e: e.reciprocal(out=out, in_=in_), r, w)

    def end_phase(self):
        nc = self.nc
        ins = self.ins
        pos = {}
        for e in ENGS:
            for p, idx in enumerate(self.eng_seq[e]):
                pos[idx] = p
        WIN = 3

        def edge_needed(idx, d, typ):
            eng = ins[idx][0]
            deng, _, dkind, _, _ = ins[d]
            if dkind == "d" or ins[idx][2] == "d":
                return True
            if deng == eng:
                return eng != "pe"
            return True

        pruned = []
        for idx, (eng, fn, kind, semkey, deps) in enumerate(ins):
            best = {}
            keep = set()
            for (d, typ) in deps:
                if not edge_needed(idx, d, typ):
                    continue
                if ins[d][2] == "d":
                    keep.add((d, typ))
                    continue
                pe_ = ins[d][0]
                if pe_ not in best or pos[d] > pos[best[pe_][0]]:
                    best[pe_] = (d, typ)
            keep.update(best.values())
            pruned.append(keep)
        needed = set()
        for idx in range(len(ins)):
            for (d, typ) in pruned[idx]:
                needed.add(d)
        for e in CENG:
            if self.eng_seq[e]:
                needed.add(self.eng_seq[e][-1])
        token = {}
        finals = {}
        for idx, (eng, fn, kind, semkey, deps) in enumerate(ins):
            if kind == "d":
                si = self.semmap[semkey]
                self.dcnt[si] += 16
                token[idx] = (("d", si), self.dcnt[si])
                finals[("d", si)] = self.dcnt[si]
            elif idx in needed:
                self.engcnt[eng] += 1
                token[idx] = (("e", eng), self.engcnt[eng])
                finals[("e", eng)] = self.engcnt[eng]
        progs = {e: [] for e in ENGS}
        waited = {e: {} for e in ENGS}
        for idx, (eng, fn, kind, semkey, deps) in enumerate(ins):
            waits = {}
            for (d, typ) in pruned[idx]:
                sn, val = token[d]
                if waited[eng].get(sn, 0) >= val:
                    continue
                waits[sn] = max(waits.get(sn, 0), val)
            for sn, val in waits.items():
                waited[eng][sn] = val
            progs[eng].append((waits, fn, token.get(idx)))
        self.n_instr += len(ins)

        def sem_of(sn):
            return self.dsems[sn[1]] if sn[0] == "d" else self.engsem[sn[1]]

        def run_engine(e, name):
            for waits, fn, inc in progs[name]:
                for sn, val in waits.items():
                    e.wait_ge(sem_of(sn), val)
                r = fn(e)
                if inc is not None:
                    r.then_inc(sem_of(inc[0]), 16 if inc[0][0] == "d" else 1)

        with nc.Block() as block:
            @block.tensor
            def _(e):
                run_engine(e, "pe")

            @block.scalar
            def _(e):
                run_engine(e, "act")

            @block.vector
            def _(e):
                run_engine(e, "dve")

            @block.gpsimd
            def _(e):
                run_engine(e, "pool")
                for sn, val in finals.items():
                    if sn[0] == "d" and any(k[0] == "pool" and self.semmap[k] == sn[1] for k in self.semmap):
                        e.wait_ge(sem_of(sn), val)

            @block.sync
            def _(e):
                run_engine(e, "sp")
                for sn, val in finals.items():
                    e.wait_ge(sem_of(sn), val)
        nc.all_engine_barrier()
        self.pop_scope()


class Ctx:
    pass


def load_w_bf16(P, C, dst, src, nkf, cols, tag, conv_engs=("dve", "act")):
    stage = [P.sb("wst_%s%d" % (tag, i), [128, cols], F32) for i in range(2)]
    for kf in range(nkf):
        s = kf % 2
        P.dma(stage[s][:], src[kf * 128:(kf + 1) * 128, :], writes=[("wst", tag, s)])
        P.copy(conv_engs[kf % len(conv_engs)], dst[:, kf, :], stage[s][:], [("wst", tag, s)], [("w", tag, kf)])


def rstd_from_ssq(P, C, st, n, key):
    P.act(st[:, 1:2], st[:, 0:1], AF.Sqrt, [key + (0,)], [key + (1,)], scale=1.0 / n, bias=C.epsb[:])
    P.recip(st[:, 2:3], st[:, 1:2], [key + (1,)], [key + (2,)])


def phase_prenorm(P, C, L, src, hnT):
    P.begin_phase()
    gb = P.sb("gpre", [128, D], F32)
    P.dma(gb[:], C.pre_norm[L:L + 1, :].partition_broadcast(128), writes=["gpre"])
    ht = [P.sb("ht%d" % i, [128, D], F32) for i in range(2)]
    junk = P.sb("junk", [128, D], BF16)
    hnb = [P.sb("hnb%d" % i, [128, D], BF16) for i in range(2)]
    st = [P.sb("st%d" % i, [128, 4], F32) for i in range(2)]
    psT = C.ps[0][:].bitcast(BF16)
    psT2 = C.ps[1][:].bitcast(BF16)
    def stage1(tt):
        s = tt % 2
        P.dma(ht[s][:], src[tt * 128:(tt + 1) * 128, :], writes=[("ht", s)])
        P.act(junk[:], ht[s][:], AF.Square, [("ht", s)], ["junk", ("st", s, 0)], accum_out=st[s][:, 0:1])
        rstd_from_ssq(P, C, st[s], D, ("st", s))
        P.stt("dve", hnb[s][:], ht[s][:], st[s][:, 2:3], gb[:], ALU.mult, ALU.mult, [("ht", s), ("st", s, 2), "gpre"], [("hnb", s)])

    def stage2(tt):
        s = tt % 2
        pst = psT if s == 0 else psT2
        for kf in range(8):
            P.tr(pst[:, kf * 128:(kf + 1) * 128], hnb[s][:, kf * 128:(kf + 1) * 128], C.ident[:], [("hnb", s), "ident"], [("psT", s)])
        P.copy("act" if s == 0 else "dve", hnT[:, :, tt * 128:(tt + 1) * 128], pst[:, 0:1024].rearrange("p (k t) -> p k t", k=8), [("psT", s)], [("hnT", tt)])

    stage1(0)
    for tt in range(NT):
        if tt + 1 < NT:
            stage1(tt + 1)
        stage2(tt)
    P.end_phase()


def phase_out(P, C, L, ymT, src, dst):
    P.begin_phase()
    wout = P.sb("wout", [128, 8, D], BF16)
    wpg = P.sb("wpg", [128, 8, D], BF16)
    wpp = P.sb("wpp", [128, 2, D], BF16)
    load_w_bf16(P, C, wout, C.w_out[L], 8, D, "wout")
    load_w_bf16(P, C, wpg, C.ple_gate[L], 8, D, "wpg")
    load_w_bf16(P, C, wpp, C.ple_proj[L], 2, D, "wpp")
    gb = P.sb("gpost", [128, D], F32)
    P.dma(gb[:], C.post_norm[L:L + 1, :].partition_broadcast(128), writes=["gpost"])
    ht = [P.sb("ht%d" % i, [128, D], F32) for i in range(2)]
    pt = [P.sb("pt%d" % i, [128, 256], F32) for i in range(2)]
    ptb = [P.sb("ptb%d" % i, [128, 256], BF16) for i in range(2)]
    pT = [P.sb("pT%d" % i, [128, 2, 128], BF16) for i in range(2)]
    junk = P.sb("junk", [128, D], BF16)
    t1 = [P.sb("t1%d" % i, [128, D], F32) for i in range(2)]
    hm = [P.sb("hm%d" % i, [128, D], F32) for i in range(2)]
    hmb = [P.sb("hmb%d" % i, [128, D], BF16) for i in range(2)]
    hmT = [P.sb("hmT%d" % i, [128, 8, 128], BF16) for i in range(2)]
    sg = [P.sb("sg%d" % i, [128, D], F32) for i in range(2)]
    hn = [P.sb("hnw%d" % i, [128, D], F32) for i in range(2)]
    st = [P.sb("st%d" % i, [128, 4], F32) for i in range(2)]
    wkeys_out = [("w", "wout", k) for k in range(8)]
    wkeys_pg = [("w", "wpg", k) for k in range(8)]
    wkeys_pp = [("w", "wpp", k) for k in range(2)]
    def stage1(tt):
        s = tt % 2
        tsl = slice(tt * 128, (tt + 1) * 128)
        P.dma(ht[s][:], src[tsl, :], writes=[("ht", s)])
        P.dma(pt[s][:], C.p[L, tsl, :], writes=[("pt", s)])
        for hf in range(2):
            for kf in range(8):
                P.mm(C.ps[hf][:], ymT[:, kf, tsl], wout[:, kf, hf * 512:(hf + 1) * 512], kf == 0, kf == 7,
                     [("ymT", kf)] + wkeys_out, [("ps", hf)])
        for hf in range(2):
            P.act(junk[:, hf * 512:(hf + 1) * 512], C.ps[hf][:], AF.Square, [("ps", hf)], ["junk", ("st", s, 0, hf)],
                  accum_out=st[s][:, hf:hf + 1])
        P.tt("dve", st[s][:, 0:1], st[s][:, 0:1], st[s][:, 1:2], ALU.add, [("st", s, 0, 0), ("st", s, 0, 1)], [("st", s, 0)])
        rstd_from_ssq(P, C, st[s], D, ("st", s))
        for hf in range(2):
            hs = slice(hf * 512, (hf + 1) * 512)
            P.stt("dve", t1[s][:, hs], C.ps[hf][:], st[s][:, 2:3], gb[:, hs], ALU.mult, ALU.mult,
                  [("ps", hf), ("st", s, 2), "gpost"], [("t1", s, hf)])
            P.tt("dve", hm[s][:, hs], t1[s][:, hs], ht[s][:, hs], ALU.add, [("t1", s, hf), ("ht", s)], [("hm", s, hf)])
            P.copy("act", hmb[s][:, hs], hm[s][:, hs], [("hm", s, hf)], [("hmb", s, hf)])
        P.copy("act", ptb[s][:], pt[s][:], [("pt", s)], [("ptb", s)])

    def stage2(tt):
        s = tt % 2
        tsl = slice(tt * 128, (tt + 1) * 128)
        psT = C.ps[2][:].bitcast(BF16)
        for kf in range(8):
            P.tr(psT[:, kf * 128:(kf + 1) * 128], hmb[s][:, kf * 128:(kf + 1) * 128], C.ident[:],
                 [("hmb", s, kf // 4), "ident"], [("ps", 2)])
        P.copy("dve", hmT[s][:], psT[:, 0:1024].rearrange("p (k t) -> p k t", k=8), [("ps", 2)], [("hmT", s)])
        psT3 = C.ps[3][:].bitcast(BF16)
        for j in range(2):
            P.tr(psT3[:, j * 128:(j + 1) * 128], ptb[s][:, j * 128:(j + 1) * 128], C.ident[:], [("ptb", s), "ident"], [("ps", 3)])
        P.copy("dve", pT[s][:], psT3[:, 0:256].rearrange("p (k t) -> p k t", k=2), [("ps", 3)], [("pT", s)])
        for hf in range(2):
            hs = slice(hf * 512, (hf + 1) * 512)
            for kf in range(8):
                P.mm(C.ps[4 + hf][:], hmT[s][:, kf, :], wpg[:, kf, hs], kf == 0, kf == 7, [("hmT", s)] + wkeys_pg, [("ps", 4 + hf)])
            for j in range(2):
                P.mm(C.ps[6 + hf][:], pT[s][:, j, :], wpp[:, j, hs], j == 0, j == 1, [("pT", s)] + wkeys_pp, [("ps", 6 + hf)])
            P.act(sg[s][:, hs], C.ps[4 + hf][:], AF.Sigmoid, [("ps", 4 + hf)], [("sg", s, hf)])
            P.tt("dve", sg[s][:, hs], sg[s][:, hs], C.ps[6 + hf][:], ALU.mult, [("sg", s, hf), ("ps", 6 + hf)], [("sg", s, hf)])
            P.tt("dve", hn[s][:, hs], sg[s][:, hs], hm[s][:, hs], ALU.add, [("sg", s, hf), ("hm", s, hf)], [("hn", s, hf)])
        P.dma(dst[tsl, :], hn[s][:], reads=[("hn", s, 0), ("hn", s, 1)], writes=[("dst", tt % 4)], q="pool")

    stage1(0)
    for tt in range(NT):
        if tt + 1 < NT:
            stage1(tt + 1)
        stage2(tt)
    P.end_phase()


def phase_even_proj(P, C, j, hnT, uT, sgaT, sgbT, hcpad):
    P.begin_phase()
    wst = [P.sb("wst%d" % i, [128, 8, 128], F32) for i in range(2)]
    wbf = [P.sb("wbf%d" % i, [128, 8, 128], BF16) for i in range(2)]
    aT = P.sb("aT", [128, 4, S], BF16)
    sig = [P.sb("sig%d" % i, [128, 512], BF16) for i in range(2)]
    w_in = C.ev_w_in[j]
    P.memset("pool", hcpad[:, :, 0:30], 0.0, ["hcpad0"])
    n = 0
    for sl in range(20):
        s = sl % 2
        P.dma(wst[s][:], w_in[:, sl * 128:(sl + 1) * 128].rearrange("(k p) c -> p k c", p=128), writes=[("wst", s)])
        P.copy("pool", wbf[s][:], wst[s][:], [("wst", s)], [("wbf", s)])
        for c in range(4):
            cs = slice(c * 512, (c + 1) * 512)
            b = n % 4
            n += 1
            pb = C.ps[b]
            for kf in range(8):
                P.mm(pb[:], wbf[s][:, kf, :], hnT[:, kf, cs], kf == 0, kf == 7, [("wbf", s)], [("ps", b)])
            if sl < 4:
                P.copy("act", uT[:, sl, cs], pb[:], [("ps", b)], [("uT", sl, c)])
            elif sl < 8:
                P.act(sgaT[:, sl - 4, cs], pb[:], AF.Silu, [("ps", b)], [("sgaT", sl - 4, c)])
            elif sl < 12:
                P.copy("dve", aT[:, sl - 8, cs], pb[:], [("ps", b)], [("aT", sl - 8, c)])
            elif sl < 16:
                q = n % 2
                P.act(sig[q][:], pb[:], AF.Sigmoid, [("ps", b)], [("sig", q)])
                P.tt("dve", hcpad[:, sl - 12, 30 + c * 512:30 + (c + 1) * 512], aT[:, sl - 12, cs], sig[q][:], ALU.mult,
                     [("aT", sl - 12, c), ("sig", q)], [("hcpad", sl - 12, c)])
            else:
                P.act(sgbT[:, sl - 16, cs], pb[:], AF.Silu, [("ps", b)], [("sgbT", sl - 16, c)])
    P.end_phase()


def phase_conv(P, C, j, hcpad, sgbT, ymT):
    P.begin_phase()
    evp = P.sb("evp", [128, 4, 40], F32)
    P.dma(evp[:], C.evp[j], writes=["evp"])
    wpw = P.sb("wpw", [128, 4, 512], BF16)
    load_w_bf16(P, C, wpw, C.cv_w_pw[j], 4, 512, "wpw")
    wpw_keys = [("w", "wpw", k) for k in range(4)]
    dg = P.sb("dg", [128, 4, 31, 128], BF16)
    for ft in range(4):
        for k in range(31):
            P.ts("pool" if (k % 2) else "dve", dg[:, ft, k, :], C.identf[:], evp[:, ft, k:k + 1], None, ALU.mult, None,
                 ["evp", "identf"], [("dg", ft)])
    cv1 = P.sb("cv1", [128, 4, 512], F32)
    sq = P.sb("sq", [128, 4, 512], F32)
    mu = P.sb("mu", [128, 512], F32)
    m2 = P.sb("m2", [128, 512], F32)
    rs = P.sb("rs", [128, 512], F32)
    xn = P.sb("xn", [128, 4, 512], F32)
    cvn = P.sb("cvn", [128, 4, 512], BF16)
    for c in range(4):
        cs = slice(c * 512, (c + 1) * 512)
        for ft in range(4):
            for k in range(31):
                P.mm(C.ps[ft][:], dg[:, ft, k, :], hcpad[:, ft, c * 512 + k:c * 512 + k + 512], k == 0, k == 30,
                     [("dg", ft)], [("ps", ft)])
            P.act(cv1[:, ft, :], C.ps[ft][:], AF.Identity, [("ps", ft), "evp"], [("cv1", ft)], bias=evp[:, ft, 31:32])
            P.act(sq[:, ft, :], cv1[:, ft, :], AF.Square, [("cv1", ft)], [("sq", ft)])
        for ft in range(4):
            P.mm(C.ps[4][:], C.onesf[:], cv1[:, ft, :], ft == 0, ft == 3, [("cv1", ft), "onesf"], [("ps", 4)])
        for ft in range(4):
            P.mm(C.ps[5][:], C.onesf[:], sq[:, ft, :], ft == 0, ft == 3, [("sq", ft), "onesf"], [("ps", 5)])
        P.act(mu[:], C.ps[4][:], AF.Copy, [("ps", 4)], ["mu"], scale=1.0 / 512)
        P.tt("dve", m2[:], mu[:], mu[:], ALU.mult, ["mu"], ["m2"])
        P.stt("dve", m2[:], C.ps[5][:], 1.0 / 512, m2[:], ALU.mult, ALU.subtract, [("ps", 5), "m2"], ["m2"])
        P.act(rs[:], m2[:], AF.Ln, ["m2"], ["rs"], bias=C.epsb[:])
        P.act(rs[:], rs[:], AF.Exp, ["rs"], ["rs"], scale=-0.5)
        for ft in range(4):
            eng = "dve"
            P.tt(eng, xn[:, ft, :], cv1[:, ft, :], mu[:], ALU.subtract, [("cv1", ft), "mu"], [("xn", ft)])
            P.tt(eng, xn[:, ft, :], xn[:, ft, :], rs[:], ALU.mult, [("xn", ft), "rs"], [("xn", ft)])
            P.act(cvn[:, ft, :], xn[:, ft, :], AF.Silu, [("xn", ft), "evp"], [("cvn", ft)],
                  scale=evp[:, ft, 32:33], bias=evp[:, ft, 33:34])
        for ot in range(4):
            b = 6 + (ot % 2)
            for ft in range(4):
                P.mm(C.ps[b][:], wpw[:, ft, ot * 128:(ot + 1) * 128], cvn[:, ft, :], ft == 0, ft == 3,
                     [("cvn", ft)] + wpw_keys, [("ps", b)])
            P.tt("dve", ymT[:, 4 + ot, cs], C.ps[b][:], sgbT[:, ot, cs], ALU.mult, [("ps", b)], [("ymT", 4 + ot, c)])
    P.end_phase()


def phase_s5_setup(P, C, j, BtR, BtI, CtR, CtI, sc2):
    P.begin_phase()
    sc = P.sb("sc", [128, 16, 3], F32)
    P.dma(sc[:], C.s5sc[j], writes=["sc"])
    w = P.sb("w", [128, 16, 16], F32)

    def col(i):
        return w[:, :, i]
    lr, li, ls = sc[:, :, 0], sc[:, :, 1], sc[:, :, 2]
    k = ["w%d" % i for i in range(16)]
    P.act(col(0), ls, AF.Exp, ["sc"], [k[0]])
    P.tt("dve", col(1), lr, col(0), ALU.mult, ["sc", k[0]], [k[1]])
    P.tt("dve", sc2[:, :, 0], li, col(0), ALU.mult, ["sc", k[0]], ["th"])
    P.ts("dve", sc2[:, :, 1], sc2[:, :, 0], 1.0 / TWO_PI, None, ALU.mult, None, ["th"], ["thq"])
    P.act(sc2[:, :, 2], col(1), AF.Exp, [k[1]], ["rho"])
    ki = P.sb("ki", [128, 16], I32)
    P.copy("dve", ki[:], sc2[:, :, 1], ["thq"], ["ki"])
    P.stt("dve", col(2), ki[:], -TWO_PI, sc2[:, :, 0], ALU.mult, ALU.add, ["ki", "th"], [k[2]])
    P.ts("dve", col(2), col(2), math.pi, -math.pi, ALU.min, ALU.max, [k[2]], [k[2]])
    P.act(col(3), col(2), AF.Abs, [k[2]], [k[3]])
    P.act(col(4), col(2), AF.Sin, [k[2]], [k[4]])
    P.act(col(5), col(3), AF.Sin, [k[3]], [k[5]], scale=-1.0, bias=C.hpib[:])
    P.tt("dve", col(6), sc2[:, :, 2], col(5), ALU.mult, ["rho", k[5]], [k[6]])
    P.tt("dve", col(7), sc2[:, :, 2], col(4), ALU.mult, ["rho", k[4]], [k[7]])
    P.ts("dve", col(6), col(6), -1.0, None, ALU.add, None, [k[6]], [k[6]])
    P.tt("dve", col(8), lr, lr, ALU.mult, ["sc"], [k[8]])
    P.tt("dve", col(9), li, li, ALU.mult, ["sc"], [k[9]])
    P.tt("dve", col(8), col(8), col(9), ALU.add, [k[8], k[9]], [k[8]])
    P.recip(col(8), col(8), [k[8]], [k[8]])
    P.tt("dve", col(9), col(6), lr, ALU.mult, [k[6], "sc"], [k[9]])
    P.tt("dve", col(10), col(7), li, ALU.mult, [k[7], "sc"], [k[10]])
    P.tt("dve", col(9), col(9), col(10), ALU.add, [k[9], k[10]], [k[9]])
    P.tt("dve", col(11), col(9), col(8), ALU.mult, [k[9], k[8]], [k[11]])
    P.tt("dve", col(9), col(7), lr, ALU.mult, [k[7], "sc"], [k[9]])
    P.tt("dve", col(10), col(6), li, ALU.mult, [k[6], "sc"], [k[10]])
    P.tt("dve", col(9), col(9), col(10), ALU.subtract, [k[9], k[10]], [k[9]])
    P.tt("dve", col(12), col(9), col(8), ALU.mult, [k[9], k[8]], [k[12]])
    for ri, dst in ((0, BtR), (1, BtI)):
        stg = P.sb("bst%d" % ri, [128, 16, 128], F32)
        P.dma(stg[:], C.s5bT[j, ri], writes=[("bst", ri)])
        P.copy("pool", dst[:], stg[:], [("bst", ri)], [("Bt", ri)])
    cre = P.sb("cre", [128, 16, 128], F32)
    cim = P.sb("cim", [128, 16, 128], F32)
    t1 = P.sb("t1", [128, 16, 128], F32)
    t2 = P.sb("t2", [128, 16, 128], F32)
    P.dma(cre[:], C.s5cP[j, 0], writes=["cre"])
    P.dma(cim[:], C.s5cP[j, 1], writes=["cim"])
    fre = w[:, :, 11:12].to_broadcast([128, 16, 128])
    fim = w[:, :, 12:13].to_broadcast([128, 16, 128])
    P.tt("dve", t1[:], cre[:], fre, ALU.mult, ["cre", k[11]], ["t1"])
    P.tt("pool", t2[:], cim[:], fim, ALU.mult, ["cim", k[12]], ["t2"])
    P.tt("dve", CtR[:], t1[:], t2[:], ALU.subtract, ["t1", "t2"], ["CtR"])
    P.tt("dve", t1[:], cre[:], fim, ALU.mult, ["cre", k[12]], ["t1"])
    P.tt("pool", t2[:], cim[:], fre, ALU.mult, ["cim", k[11]], ["t2"])
    P.stt("dve", CtI[:], t1[:], -1.0, t2[:], ALU.mult, ALU.subtract, ["t1", "t2"], ["CtI"])
    P.end_phase()


def phase_s5(P, C, j, uT, sgaT, ymT, BtR, BtI, CtR, CtI, sc2):
    P.begin_phase()
    TH = 1024
    evp = P.sb("evp", [128, 4, 40], F32)
    P.dma(evp[:], C.evp[j], writes=["evp"])
    wglu = P.sb("wglu", [128, 4, 512], BF16)
    load_w_bf16(P, C, wglu, C.s5_w_glu[j], 4, 512, "wglu")
    wglu_keys = [("w", "wglu", k) for k in range(4)]
    iot = P.sb("iot", [128, S], F32)
    P.op("pool", lambda e: e.iota(iot[:], pattern=[[1, S]], base=0, channel_multiplier=0,
                                  allow_small_or_imprecise_dtypes=True), (), ["iot"])
    ki = P.sb("ki", [128, TH], I32)
    rr = P.sb("rr", [128, TH], F32)
    ra = P.sb("ra", [128, TH], F32)
    cs_ = P.sb("cos", [128, TH], BF16)
    sn_ = P.sb("sin", [128, TH], BF16)
    bpR = P.sb("bpR", [128, TH], BF16)
    bpI = P.sb("bpI", [128, TH], BF16)
    wR = P.sb("wR", [128, TH], BF16)
    wI = P.sb("wI", [128, TH], BF16)
    xR = P.sb("xR", [128, TH], BF16)
    xI = P.sb("xI", [128, TH], BF16)
    buR = [P.sb("buR%d" % q, [128, 512], BF16) for q in range(2)]
    buI = [P.sb("buI%d" % q, [128, 512], BF16) for q in range(2)]
    mt = [[P.sb("m%d_%d" % (i, q), [128, 512], BF16) for i in range(4)] for q in range(2)]
    mo = [P.sb("mo%d" % i, [128, TH], BF16) for i in range(4)]
    carry = P.sb("carry", [128, 16, 2], F32)
    P.memset("pool", carry[:], 0.0, ["carry"])
    ygT = P.sb("ygT", [128, 4, S], BF16)
    yv = P.sb("yv", [128, 512], F32)
    y2 = P.sb("y2", [128, 512], F32)
    sgm = P.sb("sgm", [128, 512], F32)
    it = 0
    for ft in range(4):
        for hf in range(2):
            t0 = hf * TH
            for pl in range(4):
                pr = 4 * ft + pl
                th = sc2[:, pr, 0:1]
                thq = sc2[:, pr, 1:2]
                P.ts("dve", ki[:], iot[:, t0:t0 + TH], thq, None, ALU.mult, None, ["iot"], ["ki"])
                P.act(ra[:], iot[:, t0:t0 + TH], AF.Copy, ["iot"], ["ra"], scale=th)
                P.stt("dve", rr[:], ki[:], -TWO_PI, ra[:], ALU.mult, ALU.add, ["ki", "ra"], ["rr"])
                P.ts("dve", rr[:], rr[:], math.pi, -math.pi, ALU.min, ALU.max, ["rr"], ["rr"])
                P.act(sn_[:], rr[:], AF.Sin, ["rr"], ["sin"])
                P.act(ra[:], rr[:], AF.Abs, ["rr"], ["ra"])
                P.act(cs_[:], ra[:], AF.Sin, ["ra"], ["cos"], scale=-1.0, bias=C.hpib[:])
                for c in range(2):
                    cl = slice(c * 512, (c + 1) * 512)
                    cg = slice(t0 + c * 512, t0 + (c + 1) * 512)
                    q = it % 2
                    it += 1
                    bA, bB = C.ps[2 * q], C.ps[2 * q + 1]
                    P.mm(bA[:], BtR[:, pr, :], uT[:, ft, cg], True, True, [], [("ps", 2 * q)])
                    P.mm(bB[:], BtI[:, pr, :], uT[:, ft, cg], True, True, [], [("ps", 2 * q + 1)])
                    P.copy("act", buR[q][:], bA[:], [("ps", 2 * q)], [("buR", q)])
                    P.copy("act", buI[q][:], bB[:], [("ps", 2 * q + 1)], [("buI", q)])
                    m = mt[q]
                    P.tt("dve", m[0][:], buR[q][:], cs_[:, cl], ALU.mult, [("buR", q), "cos"], [("m", q, 0)])
                    P.tt("dve", m[1][:], buI[q][:], sn_[:, cl], ALU.mult, [("buI", q), "sin"], [("m", q, 1)])
                    P.tt("dve", m[2][:], buI[q][:], cs_[:, cl], ALU.mult, [("buI", q), "cos"], [("m", q, 2)])
                    P.tt("dve", m[3][:], buR[q][:], sn_[:, cl], ALU.mult, [("buR", q), "sin"], [("m", q, 3)])
                    P.tt("dve", bpR[:, cl], m[0][:], m[1][:], ALU.add, [("m", q, 0), ("m", q, 1)], [("bpR", c)])
                    P.tt("dve", bpI[:, cl], m[2][:], m[3][:], ALU.subtract, [("m", q, 2), ("m", q, 3)], [("bpI", c)])
                P.op("dve", lambda e, pr=pr: e.tensor_tensor_scan(out=wR[:], data0=sc2[:, pr, 2:3].to_broadcast([128, TH]), data1=bpR[:],
                                                                 initial=carry[:, pr, 0:1], op0=ALU.mult, op1=ALU.add),
                     [("bpR", 0), ("bpR", 1), "carry"], ["wR"])
                P.op("dve", lambda e, pr=pr: e.tensor_tensor_scan(out=wI[:], data0=sc2[:, pr, 2:3].to_broadcast([128, TH]), data1=bpI[:],
                                                                 initial=carry[:, pr, 1:2], op0=ALU.mult, op1=ALU.add),
                     [("bpI", 0), ("bpI", 1), "carry"], ["wI"])
                if hf == 0:
                    P.copy("dve", carry[:, pr, 0:1], wR[:, TH - 1:TH], ["wR"], ["carry"])
                    P.copy("dve", carry[:, pr, 1:2], wI[:, TH - 1:TH], ["wI"], ["carry"])
                P.tt("dve", mo[0][:], wR[:], cs_[:], ALU.mult, ["wR", "cos"], ["mo0"])
                P.tt("dve", mo[1][:], wI[:], sn_[:], ALU.mult, ["wI", "sin"], ["mo1"])
                P.tt("dve", mo[2][:], wI[:], cs_[:], ALU.mult, ["wI", "cos"], ["mo2"])
                P.tt("dve", mo[3][:], wR[:], sn_[:], ALU.mult, ["wR", "sin"], ["mo3"])
                P.tt("dve", xR[:], mo[0][:], mo[1][:], ALU.subtract, ["mo0", "mo1"], ["xR"])
                P.tt("dve", xI[:], mo[2][:], mo[3][:], ALU.add, ["mo2", "mo3"], ["xI"])
                for c in range(2):
                    cl = slice(c * 512, (c + 1) * 512)
                    P.mm(C.ps[4 + c][:], CtR[:, pr, :], xR[:, cl], pl == 0, False, ["xR"], [("ps", 4 + c)])
                    P.mm(C.ps[4 + c][:], CtI[:, pr, :], xI[:, cl], False, pl == 3, ["xI"], [("ps", 4 + c)])
            for c in range(2):
                cg = slice(t0 + c * 512, t0 + (c + 1) * 512)
                P.stt("dve", yv[:], uT[:, ft, cg], evp[:, ft, 34:35], C.ps[4 + c][:], ALU.mult, ALU.add,
                      [("ps", 4 + c), "evp"], ["yv"])
                P.act(y2[:], yv[:], AF.Square, ["yv"], ["y2"])
                P.ts("dve", y2[:], y2[:], 0.044715, 1.0, ALU.mult, ALU.add, ["y2"], ["y2"])
                P.tt("dve", y2[:], y2[:], yv[:], ALU.mult, ["y2", "yv"], ["y2"])
                P.act(sgm[:], y2[:], AF.Sigmoid, ["y2"], ["sgm"], scale=1.5957691216057308)
                P.tt("pool", ygT[:, ft, cg], sgm[:], yv[:], ALU.mult, ["sgm", "yv"], [("ygT", ft, hf, c)])
    gk = [("ygT", ft, hf, c) for ft in range(4) for hf in range(2) for c in range(2)]
    n = 0
    for ot in range(4):
        for c in range(4):
            cs = slice(c * 512, (c + 1) * 512)
            b = n % 4
            n += 1
            for ft in range(4):
                P.mm(C.ps[b][:], wglu[:, ft, ot * 128:(ot + 1) * 128], ygT[:, ft, cs], ft == 0, ft == 3, gk + wglu_keys, [("ps", b)])
            P.act(sgm[:], C.ps[b][:], AF.Sigmoid, [("ps", b), "evp"], ["sgm"], bias=evp[:, ot, 35:36])
            P.tt("dve", sgm[:], sgm[:], ygT[:, ot, cs], ALU.mult, ["sgm"] + gk, ["sgm"])
            P.tt("dve", ymT[:, ot, cs], sgm[:], sgaT[:, ot, cs], ALU.mult, ["sgm"], [("ymT", ot, c)])
    P.end_phase()


def even_layer(P, C, L, src, dst):
    j = L // 2
    P.push_scope()
    ymT = P.sb("ymT", [128, 8, S], BF16)
    P.push_scope()
    uT = P.sb("uT", [128, 4, S], BF16)
    sgaT = P.sb("sgaT", [128, 4, S], BF16)
    P.push_scope()
    sgbT = P.sb("sgbT", [128, 4, S], BF16)
    hcpad = P.sb("hcpad", [128, 4, S + 30], BF16)
    P.push_scope()
    hnT = P.sb("hnT", [128, 8, S], BF16)
    phase_prenorm(P, C, L, src, hnT)
    phase_even_proj(P, C, j, hnT, uT, sgaT, sgbT, hcpad)
    P.pop_scope()
    phase_conv(P, C, j, hcpad, sgbT, ymT)
    P.pop_scope()
    P.push_scope()
    BtR = P.sb("BtR", [128, 16, 128], BF16)
    BtI = P.sb("BtI", [128, 16, 128], BF16)
    CtR = P.sb("CtR", [128, 16, 128], BF16)
    CtI = P.sb("CtI", [128, 16, 128], BF16)
    sc2 = P.sb("sc2", [128, 16, 3], F32)
    phase_s5_setup(P, C, j, BtR, BtI, CtR, CtI, sc2)
    phase_s5(P, C, j, uT, sgaT, ymT, BtR, BtI, CtR, CtI, sc2)
    P.pop_scope()
    P.pop_scope()
    phase_out(P, C, L, ymT, src, dst)
    P.pop_scope()


W_SHAPES = {
    "pre_norm": [4, D], "post_norm": [4, D], "ple_gate": [4, D, D], "ple_proj": [4, 256, D],
    "ev_w_in": [2, D, 2560], "s5_w_glu": [2, 512, 512], "cv_w_pw": [2, 512, 512], "w_out": [4, D, D],
    "evp": [2, 128, 4, 40], "s5sc": [2, 128, 16, 3], "s5bT": [2, 2, 128, 16, 128], "s5cP": [2, 2, 128, 16, 128],
    "od_w_in": [2, D, 2744], "mla_w_uq": [2, 256, 768], "mla_w_ukv": [2, 128, 1024], "mlap": [2, 128, 3],
    "ropef": [128, 2],
    "nsa_ck_w1": [2, 2048, 256], "nsa_ck_w2": [2, 256, 64], "nsa_cv_w1": [2, 2048, 256], "nsa_cv_w2": [2, 256, 64],
    "nsapos2": [2, 128, 2, 16], "selc": [128, NT, 2, 32],
}
W_BF16 = {"qaug": [4, 8, S], "kaugc": [36, S], "kaugcmp": [4, 128], "addmask": [128, S], "ovl": [128, 64], "gsel": [32, 3, 8, 128]}


def build(n_layers=4, dbg=False, odd_kw=None):
    nc = bass.Bass("TRN2", target_bir_lowering=False)
    C = Ctx()
    odd_kw = odd_kw or {}

    def din(name, shape, dt=F32):
        return nc.dram_tensor(name, list(shape), dt, kind="ExternalInput").ap()
    C.x = din("x", [S, D])
    C.p = din("p", [4, S, 256])
    C.positions = din("positions", [1, S], I32)
    C.dbg = nc.dram_tensor("dbg", [8, 128, S], F32, kind="ExternalOutput").ap() if dbg else None
    for k, shp in W_SHAPES.items():
        setattr(C, k, din(k, shp))
    for k, shp in W_BF16.items():
        setattr(C, k, din(k, shp, BF16))
    out = nc.dram_tensor("out", [S, D], F32, kind="ExternalOutput").ap()
    hbuf = nc.dram_tensor("hbuf", [S, D], F32, kind="Internal").ap()
    P = Prog(nc)
    C.ps = [P.gps("ps%d" % i, [128, 512]) for i in range(8)]
    C.ident = P.gsb("ident", [128, 128], BF16)
    C.identf = P.gsb("identf", [128, 128], F32)
    C.onesf = P.gsb("onesf", [128, 128], F32)
    C.epsb = P.gsb("epsb", [128, 1], F32)
    C.tri = P.gsb("tri", [128, 128], BF16)
    C.wmask = P.gsb("wmask", [128, 128], BF16)
    C.ropef_sb = P.gsb("ropef_sb", [128, 2], F32)
    C.hpib = P.gsb("hpib", [128, 1], F32)
    P.begin_phase()
    io = P.sb("io", [128, 128], F32)
    P.op("pool", lambda e: e.iota(io[:], pattern=[[1, 128]], base=0, channel_multiplier=-1,
                                  allow_small_or_imprecise_dtypes=True), (), ["io"])
    P.op("dve", lambda e: e.tensor_single_scalar(out=C.identf[:], in_=io[:], scalar=0.0, op=ALU.is_equal), ["io"], ["identf"])
    P.copy("dve", C.ident[:], C.identf[:], ["identf"], ["ident"])
    P.memset("pool", C.onesf[:], 1.0, ["onesf"])
    P.op("dve", lambda e: e.tensor_single_scalar(out=C.tri[:], in_=io[:], scalar=0.0, op=ALU.is_ge), ["io"], ["tri"])
    P.op("dve", lambda e: e.tensor_single_scalar(out=C.wmask[:], in_=io[:], scalar=0.0, op=ALU.is_lt), ["io"], ["wmask"])
    P.dma(C.ropef_sb[:], C.ropef[:, :], writes=["ropef_sb"])
    P.memset("pool", C.epsb[:], EPS, ["epsb"])
    P.memset("pool", C.hpib[:], math.pi / 2, ["hpib"])
    P.end_phase()
    C.ropef_dram = C.ropef
    C.ropef = C.ropef_sb
    for L in range(n_layers):
        src = C.x if L == 0 else hbuf
        dst = out if L == n_layers - 1 else hbuf
        if L % 2 == 0:
            even_layer(P, C, L, src, dst)
        else:
            odd_layer(P, C, L, src, dst, **odd_kw)
    C.ropef = C.ropef_dram
    P.close()
    return nc, P


def nsa_constants():
    import ml_dtypes
    bf = ml_dtypes.bfloat16
    t = np.arange(S)
    a_t, b_t = (t // 64).astype(np.float32), (t % 64).astype(np.float32)
    slopes = np.array([2.0 ** (-(i + 1)) for i in range(8)], np.float32)
    qaug = np.zeros((4, 8, S), np.float32)
    for h in range(8):
        qaug[0, h] = -slopes[h] * 64.0 * a_t
        qaug[1, h] = -slopes[h] * b_t
        qaug[2, h] = slopes[h] * 64.0
        qaug[3, h] = slopes[h]
    kaugc = np.zeros((36, S), np.float32)
    kaugc[t // 64, t] = 1.0
    kaugc[32] = 1.0
    kaugc[33] = 1.0
    kaugc[34] = a_t
    kaugc[35] = b_t
    c = np.arange(128)
    pc = 16 * c + 31
    kaugcmp = np.stack([np.ones(128), np.ones(128), pc // 64, pc % 64]).astype(np.float32)
    addmask = np.where((t[None, :] >= pc[:, None]) & (c[:, None] <= 126), 0.0, -30000.0).astype(np.float32)
    sb = np.arange(32)
    cs_ = c[:, None] * 16
    overlap = ((cs_ < (sb[None] + 1) * 64) & (cs_ + 32 > sb[None] * 64) & (c[:, None] <= 126)).astype(np.float32)
    ovl = np.concatenate([overlap, np.ones((128, 32), np.float32)], axis=1)
    cur = t[:, None] // 64
    forced = (sb[None] == 0) | (sb[None] == cur) | (sb[None] == cur - 1)
    causal = sb[None] * 64 <= t[:, None]
    mul = (forced | causal).astype(np.float32)
    add = np.where(forced, 1e4, np.where(causal, 0.0, -1e4)).astype(np.float32)
    selc = np.stack([mul, add], axis=1).reshape(NT, 128, 2, 32).transpose(1, 0, 2, 3)
    gsel = np.zeros((32, 3, 8, 128), np.float32)
    for b in range(3):
        for h in range(8):
            gsel[b * 8 + h, b, h, (h % 2) * 64:(h % 2) * 64 + 64] = 1.0
    return {"qaug": qaug.astype(bf), "kaugc": kaugc.astype(bf), "kaugcmp": kaugcmp.astype(bf), "addmask": addmask.astype(bf),
            "ovl": ovl.astype(bf), "gsel": gsel.astype(bf), "selc": np.ascontiguousarray(selc)}


def host_layout(inputs):
    f = lambda k: np.asarray(inputs[k], np.float32)
    ne = 2
    evp = np.zeros((ne, 128, 4, 40), np.float32)
    wdw = f("cv_w_dw")
    evp[:, :, :, 0:31] = wdw.reshape(ne, 31, 4, 128).transpose(0, 3, 2, 1)
    for col, key in ((31, "cv_b_dw"), (32, "cv_ln_g"), (33, "cv_ln_b"), (34, "s5_d"), (35, "s5_b_glu")):
        evp[:, :, :, col] = f(key).reshape(ne, 4, 128).transpose(0, 2, 1)
    s5sc = np.zeros((ne, 128, 16, 3), np.float32)
    for i, key in enumerate(("s5_lam_re", "s5_lam_im")):
        a = f(key).reshape(ne, 16, 2, 64)
        s5sc[:, :, :, i] = a.transpose(0, 2, 3, 1).reshape(ne, 128, 16)
    ls = f("s5_log_step").reshape(ne, 16, 2)
    s5sc[:, :, :, 2] = np.repeat(ls.transpose(0, 2, 1)[:, :, None, :], 64, axis=2).reshape(ne, 128, 16)
    s5bT = np.zeros((ne, 2, 128, 16, 128), np.float32)
    s5cP = np.zeros((ne, 2, 128, 16, 128), np.float32)
    for ri, (kb, kc) in enumerate((("s5_b_re", "s5_c_re"), ("s5_b_im", "s5_c_im"))):
        b = f(kb)
        c = f(kc)
        for pr in range(16):
            for gl in range(2):
                g = 2 * pr + gl
                k0 = (pr % 4) * 32 + gl * 16
                s5bT[:, ri, k0:k0 + 16, pr, gl * 64:(gl + 1) * 64] = b[:, g].transpose(0, 2, 1)
                s5cP[:, ri, gl * 64:(gl + 1) * 64, pr, k0:k0 + 16] = c[:, g].transpose(0, 2, 1)
    w_out = np.stack([f("ev_w_out")[0], f("od_w_out")[0], f("ev_w_out")[1], f("od_w_out")[1]])
    no = 2
    mlap = np.zeros((no, 128, 3), np.float32)
    mlap[:, :, 0:2] = f("mla_q_norm").reshape(no, 2, 128).transpose(0, 2, 1)
    mlap[:, :, 2] = f("mla_kv_norm")
    ropef = np.zeros((128, 2), np.float32)
    fr = (10000.0 ** (-np.arange(16, dtype=np.float32) / 16)).astype(np.float32)
    ropef[64:96, 0] = np.tile(fr, 2)
    ropef[:, 1] = ropef[:, 0] / np.float32(TWO_PI)
    rep = {"evp": evp, "s5sc": s5sc, "s5bT": s5bT, "s5cP": s5cP, "w_out": w_out, "mlap": mlap, "ropef": ropef}
    rep.update(nsa_constants())
    pos2 = np.zeros((no, 128, 2, 16), np.float32)
    for kv, key in enumerate(("nsa_pos_k", "nsa_pos_v")):
        a = f(key).reshape(no, 16, 2, 64)
        pos2[:, :, kv, :] = a.transpose(0, 2, 3, 1).reshape(no, 128, 16)
    rep["nsapos2"] = pos2
    for k in ("nsa_ck_w1", "nsa_ck_w2", "nsa_cv_w1", "nsa_cv_w2"):
        rep[k] = np.ascontiguousarray(f(k))
    for k in ("pre_norm", "post_norm", "ple_gate", "ple_proj", "ev_w_in", "s5_w_glu", "cv_w_pw", "od_w_in", "mla_w_uq", "mla_w_ukv"):
        rep[k] = np.ascontiguousarray(f(k))
    return rep


def kernel(**inputs):
    n = 8
    rep = host_layout(inputs)
    x = np.asarray(inputs["x"], np.float32)
    p = np.asarray(inputs["p"], np.float32)
    nc, _ = build(4)
    in_maps = []
    for b in range(n):
        m = dict(rep)
        m["x"] = np.ascontiguousarray(x[b])
        m["p"] = np.ascontiguousarray(p[:, b])
        m["positions"] = np.ascontiguousarray(np.asarray(inputs["positions"])[b:b + 1]).astype(np.int32)
        in_maps.append(m)
    res = run_bass_kernel_spmd(nc, in_maps, core_ids=list(range(n)))
    return np.stack([r["out"] for r in res.results], axis=0).astype(np.float32)


OD = {"q": 0, "kv": 512, "gl": 1280, "gn": 1304, "cq": 1816, "ckv": 2072, "kr": 2200, "gm": 2232}


def load_slab(P, C, W, c0, n, tag, slot, eng="pool"):
    stg = P.sb("sl_st_%s" % tag, [128, 8, n], F32)
    dst = P.sb("sl_bf_%s" % tag, [128, 8, n], BF16)
    P.dma(stg[:], W[:, c0:c0 + n].rearrange("(k p) c -> p k c", p=128), writes=[("slst", tag)])
    P.copy(eng, dst[:], stg[:], [("slst", tag)], [("slab", tag)])
    return dst


def angle_tables(P, C, ang_in, fq, f, cosT, sinT, n, tag):
    ki = P.sb("ki_" + tag, [128, n], I32)
    rr = P.sb("rr_" + tag, [128, n], F32)
    ra = P.sb("ra_" + tag, [128, n], F32)
    P.ts("dve", ki[:], ang_in, fq, None, ALU.mult, None, [tag + "in"], [tag + "ki"])
    P.ts("pool", rr[:], ki[:], -TWO_PI, None, ALU.mult, None, [tag + "ki"], [tag + "rr"])
    P.ts("pool", ra[:], ang_in, f, None, ALU.mult, None, [tag + "in"], [tag + "ra"])
    P.tt("pool", rr[:], rr[:], ra[:], ALU.add, [tag + "rr", tag + "ra"], [tag + "rr"])
    P.ts("pool", rr[:], rr[:], math.pi, -math.pi, ALU.min, ALU.max, [tag + "rr"], [tag + "rr"])
    P.act(sinT, rr[:], AF.Sin, [tag + "rr"], [tag + "sin"])
    P.act(ra[:], rr[:], AF.Abs, [tag + "rr"], [tag + "ra"])
    P.act(cosT, ra[:], AF.Sin, [tag + "ra"], [tag + "cos"], scale=-1.0, bias=C.hpib[:])


def softmax_pv_finish(P, C, ob, par, dst_rows, rec, tmpf, extra_mul, ekey, okey, wkey, first, clamp=False, gate_ps=None, gkey=None):
    lo = slice(par * 64, par * 64 + 64)
    hi = slice((1 - par) * 64, (1 - par) * 64 + 64)
    if clamp:
        P.ts("dve", rec[lo, :], ob[hi, :], 1e-18, None, ALU.max, None, [okey], [("rec", par)])
        P.act(rec[lo, :], rec[lo, :], AF.Ln, [("rec", par)], [("rec", par)])
    else:
        P.act(rec[lo, :], ob[hi, :], AF.Ln, [okey], [("rec", par)])
    P.act(rec[lo, :], rec[lo, :], AF.Exp, [("rec", par)], [("rec", par)], scale=-1.0)
    P.tt("dve", tmpf[lo, :], ob[lo, :], rec[lo, :], ALU.mult, [okey, ("rec", par)], [("tmpf", par)])
    if gate_ps is not None:
        P.tt("dve", tmpf[lo, :], tmpf[lo, :], gate_ps[lo, :], ALU.mult, [("tmpf", par), gkey], [("tmpf", par)])
    if not first:
        P.tt("dve", tmpf[lo, :], tmpf[lo, :], dst_rows, ALU.add, [("tmpf", par), wkey], [("tmpf", par)])
    if extra_mul is not None:
        P.tt("dve", dst_rows, tmpf[lo, :], extra_mul, ALU.mult, [("tmpf", par), ekey], [wkey])
    else:
        P.copy("act", dst_rows, tmpf[lo, :], [("tmpf", par)], [wkey])


def run_attention(P, C, groups, PT, depth=3, mask_eng="dve", sbanks=(0, 1), tick=None):
    flat = [(g, i) for g, (steps, fin) in enumerate(groups) for i in range(len(steps))]
    issued = 0
    for idx in range(len(flat)):
        while issued < min(len(flat), idx + depth):
            g2, i2 = flat[issued]
            st2 = groups[g2][0][i2]
            bank = sbanks[issued % len(sbanks)]
            P.mm(C.ps[bank][:, st2["c0"]:st2["c1"]], st2["lhsK"], st2["rhsQ"], True, True, st2.get("rk", []), [("ps", bank)])
            issued += 1
        g, i = flat[idx]
        steps, fin = groups[g]
        st = steps[i]
        bank = sbanks[idx % len(sbanks)]
        c0, c1 = st["c0"], st["c1"]
        pt = PT[idx % len(PT)]
        pk = ("PT", idx % len(PT))
        if st.get("scale") is not None:
            P.act(pt[:, c0:c1], C.ps[bank][:, c0:c1], AF.Exp, [("ps", bank)], [pk], scale=st["scale"])
        else:
            P.act(pt[:, c0:c1], C.ps[bank][:, c0:c1], AF.Exp, [("ps", bank)], [pk])
        for (a, b, m) in st["masks"]:
            P.tt(mask_eng, pt[:, a:b], pt[:, a:b], m, ALU.mult, [pk], [pk])
        P.mm(C.ps[st["ob"]][:, c0:c1], st["lhsV"], pt[:, c0:c1], i == 0, i == len(steps) - 1, [pk] + st.get("rv", []), [("ps", st["ob"])],
             skip=True)
        if i == len(steps) - 1:
            fin()
        if tick is not None:
            tick(idx)


def phase_mla_prep(P, C, L, hnT, cosT, sinT, cqnT, ckvnT, kropeT):
    j = L // 2
    W = C.od_w_in[j]
    P.begin_phase()
    mlap = P.sb("mlap", [128, 3], F32)
    P.dma(mlap[:], C.mlap[j], writes=["mlap"])
    posi = [P.sb("posi%d" % i, [128, 512], I32) for i in range(2)]
    posf = [P.sb("posf%d" % i, [128, 512], F32) for i in range(2)]
    ki = P.sb("rki", [128, 512], I32)
    rr = P.sb("rrr", [128, 512], F32)
    ra = P.sb("rra", [128, 512], F32)
    for c in range(4):
        cs = slice(c * 512, (c + 1) * 512)
        s = c % 2
        P.dma(posi[s][:], C.positions[0:1, cs].partition_broadcast(128), writes=[("posi", s)])
        P.copy("dve", posf[s][:], posi[s][:], [("posi", s)], [("posf", s)])
        P.ts("dve", ki[:], posf[s][:], C.ropef[:, 1:2], None, ALU.mult, None, [("posf", s)], ["ki"])
        P.ts("pool", rr[:], ki[:], -TWO_PI, None, ALU.mult, None, ["ki"], ["rr"])
        P.ts("pool", ra[:], posf[s][:], C.ropef[:, 0:1], None, ALU.mult, None, [("posf", s)], ["ra"])
        P.tt("pool", rr[:], rr[:], ra[:], ALU.add, ["rr", "ra"], ["rr"])
        P.ts("pool", rr[:], rr[:], math.pi, -math.pi, ALU.min, ALU.max, ["rr"], ["rr"])
        P.act(sinT[:, cs], rr[:], AF.Sin, ["rr"], [("sinT", c)])
        P.act(ra[:], rr[:], AF.Abs, ["rr"], ["ra"])
        P.act(cosT[:, cs], ra[:], AF.Sin, ["ra"], [("cosT", c)], scale=-1.0, bias=C.hpib[:])
    wcq = load_slab(P, C, W, OD["cq"], 256, "cq", 0)
    wckv = load_slab(P, C, W, OD["ckv"], 128, "ckv", 0, eng="dve")
    wkr = load_slab(P, C, W, OD["kr"], 32, "kr", 0, eng="dve")
    wkrA = P.sb("wkrA", [128, 8, 96], BF16)
    wkrR = P.sb("wkrR", [128, 8, 96], BF16)
    P.memset("pool", wkrA[:], 0.0, ["wkrA"])
    P.memset("pool", wkrR[:], 0.0, ["wkrR"])
    P.copy("dve", wkrA[:, :, 64:96], wkr[:], [("slab", "kr"), "wkrA"], ["wkrA"])
    P.ts("dve", wkrR[:, :, 64:80], wkr[:, :, 16:32], -1.0, None, ALU.mult, None, [("slab", "kr"), "wkrR"], ["wkrR"])
    P.copy("dve", wkrR[:, :, 80:96], wkr[:, :, 0:16], [("slab", "kr"), "wkrR"], ["wkrR"])
    cqf = [P.sb("cqf%d" % i, [128, 512], F32) for i in range(3)]
    sq = [P.sb("sqm%d" % i, [128, 512], F32) for i in range(3)]
    rs = P.sb("rsm", [128, 512], F32)
    m1 = P.sb("m1", [128, 512], F32)
    m2 = P.sb("m2", [128, 512], F32)
    for c in range(4):
        cs = slice(c * 512, (c + 1) * 512)
        for t in range(3):
            for kf in range(8):
                lhs = wcq[:, kf, t * 128:(t + 1) * 128] if t < 2 else wckv[:, kf, :]
                P.mm(C.ps[4 + t][:], lhs, hnT[:, kf, cs], kf == 0, kf == 7, [("slab", "cq"), ("slab", "ckv")], [("ps", 4 + t)])
            P.copy("act", cqf[t][:], C.ps[4 + t][:], [("ps", 4 + t)], [("cqf", t)])
            P.act(sq[t][:], C.ps[4 + t][:], AF.Square, [("ps", 4 + t)], [("sq", t)])
        for tiles, nfeat in (((0, 1), 256), ((2,), 128)):
            for i, t in enumerate(tiles):
                P.mm(C.ps[7][:], C.onesf[:], sq[t][:], i == 0, i == len(tiles) - 1, [("sq", t)], [("ps", 7)])
            P.act(rs[:], C.ps[7][:], AF.Ln, [("ps", 7)], ["rs"], scale=1.0 / nfeat, bias=C.epsb[:])
            P.act(rs[:], rs[:], AF.Exp, ["rs"], ["rs"], scale=-0.5)
            for t in tiles:
                dst = cqnT[:, t, cs] if t < 2 else ckvnT[:, cs]
                P.stt("dve", dst, cqf[t][:], mlap[:, t:t + 1], rs[:], ALU.mult, ALU.mult, [("cqf", t), "rs", "mlap"], [("cn", t, c)])
        for kf in range(8):
            P.mm(C.ps[0][:96, :], wkrA[:, kf, :], hnT[:, kf, cs], kf == 0, kf == 7, ["wkrA"], [("ps", 0)])
        for kf in range(8):
            P.mm(C.ps[1][:96, :], wkrR[:, kf, :], hnT[:, kf, cs], kf == 0, kf == 7, ["wkrR"], [("ps", 1)])
        P.tt("dve", m1[64:96, :], C.ps[0][64:96, :], cosT[64:96, cs], ALU.mult, [("ps", 0), ("cosT", c)], ["m1"])
        P.tt("dve", m2[64:96, :], C.ps[1][64:96, :], sinT[64:96, cs], ALU.mult, [("ps", 1), ("sinT", c)], ["m2"])
        P.tt("pool", kropeT[64:96, cs], m1[64:96, :], m2[64:96, :], ALU.add, ["m1", "m2"], [("krope", c)])
    P.end_phase()


def phase_mla(P, C, L, hnT, ymT, cosT, sinT, cqnT, ckvnT, kropeT):
    j = L // 2
    W = C.od_w_in[j]
    SC = 96 ** -0.5
    P.begin_phase()
    wuq = P.sb("wuq", [128, 2, 768], BF16)
    load_w_bf16(P, C, wuq, C.mla_w_uq[j], 2, 768, "wuq")
    wuq_keys = [("w", "wuq", k) for k in range(2)]
    wuqR = P.sb("wuqR", [128, 2, 8, 96], BF16)
    P.memset("pool", wuqR[:], 0.0, ["wuqR"])
    wuq4 = wuq[:].rearrange("p k (h c) -> p k h c", h=8)
    for t in range(2):
        P.ts("dve", wuqR[:, t, :, 64:80], wuq4[:, t, :, 80:96], -1.0, None, ALU.mult, None, wuq_keys + ["wuqR"], ["wuqR"])
        P.copy("dve", wuqR[:, t, :, 80:96], wuq4[:, t, :, 64:80], wuq_keys + ["wuqR"], ["wuqR"])
    wukv = P.sb("wukv", [128, 1, 1024], BF16)
    load_w_bf16(P, C, wukv, C.mla_w_ukv[j], 1, 1024, "wukv")
    wukv_k = [("w", "wukv", 0)]
    m1s = [P.sb("m1_%d" % i, [128, 512], F32) for i in range(2)]
    m2s = [P.sb("m2_%d" % i, [128, 512], F32) for i in range(2)]
    QT = [P.sb("QT%d" % i, [128, 2, S], BF16) for i in range(2)]
    KT = [P.sb("KT%d" % i, [128, 2, S], BF16) for i in range(2)]
    VA = [P.sb("VA%d" % i, [128, NT, 2, 128], BF16) for i in range(2)]
    sgm = [P.sb("sgmT%d" % i, [128, S], BF16) for i in range(2)]
    PT = [P.sb("PT%d" % i, [128, 512], BF16) for i in range(4)]
    rec = P.sb("rec", [128, 512], F32)
    tmpf = P.sb("tmpf", [128, 512], F32)
    gst = [P.sb("gst%d" % i, [128, 8, 128], F32) for i in range(2)]
    gbf = [P.sb("gbf%d" % i, [128, 8, 128], BF16) for i in range(2)]
    wukv3 = wukv[:, 0, :].rearrange("p (h c) -> p h c", h=8)
    for i in range(2):
        P.memset("dve" if i == 0 else "pool", VA[i][:], 1.0, [("VA", i)])
    cnt = {"ps": 0, "m": 0}

    def pbank():
        b = (4, 5, 7)[cnt["ps"] % 3]
        cnt["ps"] += 1
        return b

    def proj_units(hp, sl):
        for par in range(2):
            h = 2 * hp + par
            for c in range(4):
                cs = slice(c * 512, (c + 1) * 512)
                b = pbank()
                P.mm(C.ps[b][:64, :], wukv[:, 0, h * 128:h * 128 + 64], ckvnT[:, cs], True, True, wukv_k, [("ps", b)])
                P.copy("dve", KT[sl][0:64, par, cs], C.ps[b][:64, :], [("ps", b)], [("KT", sl, par)])
                bA = pbank()
                bR = pbank()
                for t in range(2):
                    P.mm(C.ps[bA][:96, :], wuq[:, t, h * 96:(h + 1) * 96], cqnT[:, t, cs], t == 0, t == 1, wuq_keys, [("ps", bA)])
                for t in range(2):
                    P.mm(C.ps[bR][:96, :], wuqR[:, t, h, :], cqnT[:, t, cs], t == 0, t == 1, ["wuqR"], [("ps", bR)])
                P.copy("dve", QT[sl][0:64, par, cs], C.ps[bA][0:64, :], [("ps", bA)], [("QT", sl, par)])
                mi = cnt["m"] % 2
                cnt["m"] += 1
                m1, m2 = m1s[mi], m2s[mi]
                P.tt("dve", m1[64:96, :], C.ps[bA][64:96, :], cosT[64:96, cs], ALU.mult, [("ps", bA)], [("m1", mi)])
                P.tt("dve", m2[64:96, :], C.ps[bR][64:96, :], sinT[64:96, cs], ALU.mult, [("ps", bR)], [("m2", mi)])
                P.tt("dve", QT[sl][64:96, par, cs], m1[64:96, :], m2[64:96, :], ALU.add, [("m1", mi), ("m2", mi)], [("QT", sl, par)])
                yield
            P.copy("dve", KT[sl][64:96, par, :], kropeT[64:96, :], [], [("KT", sl, par)])
        for kt in range(NT):
            b = pbank()
            P.mm(C.ps[b][:, 0:128], ckvnT[:, kt * 128:(kt + 1) * 128], wukv3[:, 2 * hp:2 * hp + 2, 64:128], True, True,
                 wukv_k, [("ps", b)])
            P.copy("dve", VA[sl][:, kt, 0, 0:64], C.ps[b][:, 0:64], [("ps", b), ("VA", sl)], [("VA", sl)])
            P.copy("dve", VA[sl][:, kt, 1, 64:128], C.ps[b][:, 64:128], [("ps", b), ("VA", sl)], [("VA", sl)])
            if kt % 2 == 1:
                yield
        c0 = OD["gm"] + hp * 128
        P.dma(gst[sl][:], W[:, c0:c0 + 128].rearrange("(k p) c -> p k c", p=128), writes=[("gst", sl)])
        P.copy("pool", gbf[sl][:], gst[sl][:], [("gst", sl)], [("gbf", sl)])
        for c in range(4):
            cs = slice(c * 512, (c + 1) * 512)
            b = pbank()
            for kf in range(8):
                P.mm(C.ps[b][:], gbf[sl][:, kf, :], hnT[:, kf, cs], kf == 0, kf == 7, [("gbf", sl)], [("ps", b)])
            P.act(sgm[sl][:, cs], C.ps[b][:], AF.Silu, [("ps", b)], [("sgm", sl, c)])
            yield

    for _ in proj_units(0, 0):
        pass
    for hp in range(4):
        sl = hp % 2
        tile_i = 4 + hp
        gen = proj_units(hp + 1, 1 - sl) if hp < 3 else None
        groups = []
        for par in range(2):
            for qc in range(4):
                ob = 2 + (qc % 2)
                nk = 4 * qc + 4
                steps = []
                for kt in range(nk):
                    o = max(0, kt - 4 * qc)
                    q0 = qc * 512 + o * 128
                    ncol = 512 - o * 128
                    steps.append(dict(c0=o * 128, c1=512, lhsK=KT[sl][0:96, par, kt * 128:(kt + 1) * 128], rhsQ=QT[sl][0:96, par, q0:q0 + ncol],
                                      rk=[("KT", sl, par), ("QT", sl, par)], scale=SC,
                                      masks=[(o * 128, (o + 1) * 128, C.tri[:])] if kt >= 4 * qc else [],
                                      lhsV=VA[sl][:, kt, par, :], ob=ob, rv=[("VA", sl)]))

                def fin(par=par, qc=qc, ob=ob, tile_i=tile_i, sl=sl):
                    lo = slice(par * 64, par * 64 + 64)
                    qs = slice(qc * 512, (qc + 1) * 512)
                    softmax_pv_finish(P, C, C.ps[ob], par, ymT[lo, tile_i, qs], rec, tmpf, sgm[sl][lo, qs], ("sgm", sl, qc),
                                      ("ps", ob), ("ymT", tile_i, par, qc), True)
                groups.append((steps, fin))

        def tick(idx, gen=gen):
            if gen is not None and idx % 2 == 1:
                next(gen, None)
        run_attention(P, C, groups, PT, sbanks=(0, 1, 6), tick=tick, mask_eng="pool")
        if gen is not None:
            for _ in gen:
                pass
    P.end_phase()


def odd_layer(P, C, L, src, dst, do_nsa=True, do_mla=True):
    P.push_scope()
    ymT = P.sb("ymT", [128, 8, S], BF16)
    hnT = P.sb("hnT", [128, 8, S], BF16)
    phase_prenorm(P, C, L, src, hnT)
    if not (do_nsa and do_mla):
        P.begin_phase()
        P.memset("pool", ymT[:], 0.0, ["ymT"])
        P.end_phase()
    if do_nsa:
        nsa_mixer(P, C, L, hnT, ymT)
    if do_mla:
        P.push_scope()
        cosT = P.sb("cosT", [128, S], F32)
        sinT = P.sb("sinT", [128, S], F32)
        cqnT = P.sb("cqnT", [128, 2, S], BF16)
        ckvnT = P.sb("ckvnT", [128, S], BF16)
        kropeT = P.sb("kropeT", [128, S], BF16)
        phase_mla_prep(P, C, L, hnT, cosT, sinT, cqnT, ckvnT, kropeT)
        phase_mla(P, C, L, hnT, ymT, cosT, sinT, cqnT, ckvnT, kropeT)
        P.pop_scope()
    if C.dbg is not None:
        P.begin_phase()
        cv = [P.sb("dbgc%d" % i, [128, S], F32) for i in range(2)]
        for t in range(8):
            P.copy("dve", cv[t % 2][:], ymT[:, t, :], [], [("cv", t % 2)])
            P.dma(C.dbg[t], cv[t % 2][:], reads=[("cv", t % 2)], writes=[("dbg", t % 2)])
        P.end_phase()
    phase_out(P, C, L, ymT, src, dst)
    P.pop_scope()


class SlabLoader:
    def __init__(self, P, tag):
        self.P = P
        self.tag = tag
        self.st = [P.sb("sls_%s%d" % (tag, i), [128, 8, 128], F32) for i in range(2)]
        self.bf = [P.sb("slb_%s%d" % (tag, i), [128, 8, 128], BF16) for i in range(2)]
        self.n = 0

    def load(self, W, c0, n, eng="pool"):
        s = self.n % 2
        self.n += 1
        P = self.P
        P.dma(self.st[s][:, :, 0:n], W[:, c0:c0 + n].rearrange("(k p) c -> p k c", p=128), writes=[("sls", self.tag, s)])
        P.copy(eng, self.bf[s][:, :, 0:n], self.st[s][:, :, 0:n], [("sls", self.tag, s)], [("slb", self.tag, s)])
        return self.bf[s], ("slb", self.tag, s)


def proj_fm(P, C, slab, skey, n, hnT, consume, banks=(6, 7)):
    for c in range(4):
        cs = slice(c * 512, (c + 1) * 512)
        b = banks[c % len(banks)]
        for kf in range(8):
            P.mm(C.ps[b][0:n, :], slab[:, kf, 0:n], hnT[:, kf, cs], kf == 0, kf == 7, [skey], [("ps", b)])
        consume(c, cs, C.ps[b], ("ps", b))


def gelu_tanh(P, x_ps, xkey, out, okey, y2, t1, sg, tag):
    P.act(y2, x_ps, AF.Square, [xkey], [tag + "y2"])
    P.ts("dve", y2, y2, 0.044715, 1.0, ALU.mult, ALU.add, [tag + "y2"], [tag + "y2"])
    P.tt("dve", t1, y2, x_ps, ALU.mult, [tag + "y2", xkey], [tag + "t1"])
    P.act(sg, t1, AF.Sigmoid, [tag + "t1"], [tag + "sg"], scale=1.5957691216057308)
    P.tt("dve", out, sg, x_ps, ALU.mult, [tag + "sg", xkey], [okey])


def nsa_mixer(P, C, L, hnT, ymT):
    j = L // 2
    W = C.od_w_in[j]
    P.push_scope()
    QA = P.sb("QaugT", [128, 8, S], BF16)
    KS = P.sb("KselA", [128, 2, S], BF16)
    KW = P.sb("KwinA", [128, 2, S], BF16)
    SgT = P.sb("SgT", [32, S], BF16)
    KcA = P.sb("KcA", [128, 2, 128], BF16)
    VcA = P.sb("VcA", [128, 2, 2, 128], BF16)
    gsel = P.sb("gsel", [32, 3, 8, 128], BF16)

    P.begin_phase()
    SL = SlabLoader(P, "a")
    P.memset("pool", QA[64:96, :, :], 0.0, ["QAmask"])
    P.dma(QA[96:100, :, :], C.qaug[:, :, :], writes=["QAaug"])
    P.dma(gsel[:], C.gsel[:, :, :, :], writes=["gsel"])
    for y in range(2):
        P.dma(KS[64:96, y, :], C.kaugc[0:32, :], writes=[("KSe", y)])
        P.dma(KS[96:100, y, :], C.kaugc[32:36, :], writes=[("KSa", y)])
        P.dma(KW[96:100, y, :], C.kaugc[32:36, :], writes=[("KWa", y)])
    P.memset("pool", KW[64:96, :, :], 0.0, ["KWz"])
    P.memset("pool", SgT[:], 0.0, ["SgT0"])
    import os
    part = int(os.environ.get("NSA_PART", "9"))
    for hp in range(4 if part >= 1 else 0):
        slab, sk = SL.load(W, OD["q"] + hp * 128, 128)

        def cons_q(c, cs, ps, pk, hp=hp):
            for par in range(2):
                P.act(QA[0:64, 2 * hp + par, cs], ps[par * 64:(par + 1) * 64, :], AF.Copy, [pk], [("QA", 2 * hp + par, c)], scale=0.125)
        proj_fm(P, C, slab, sk, 128, hnT, cons_q)
    for (i, dst, nm) in (((2, KS, "KS"), (4, KW, "KW")) if part >= 2 else ()):
        slab, sk = SL.load(W, OD["kv"] + i * 128, 128)

        def cons_k(c, cs, ps, pk, dst=dst, nm=nm):
            for y in range(2):
                P.copy("act" if y == 0 else "dve", dst[0:64, y, cs], ps[y * 64:(y + 1) * 64, :], [pk], [(nm, y, c)])
        proj_fm(P, C, slab, sk, 128, hnT, cons_k)
    def cons_g(c, cs, ps, pk):
        P.act(SgT[0:32, cs], ps[0:32, :], AF.Sigmoid, [pk, "SgT0"], [("SgT", c)])
    if part >= 3:
        slab, sk = SL.load(W, OD["gl"], 32)
        proj_fm(P, C, slab, sk, 32, hnT, cons_g)
    P.end_phase()
    stop = os.environ.get("NSA_STOP", "")
    if stop == "a":
        P.pop_scope()
        return

    P.begin_phase()
    SL = SlabLoader(P, "b")
    P.memset("pool", VcA[:], 1.0, ["VcA"])
    P.memset("pool", KcA[64:96, :, :], 0.0, ["KcAz"])
    for y in range(2):
        P.dma(KcA[96:100, y, :], C.kaugcmp[:, :], writes=[("KcAa", y)])
    K2 = [P.sb("K2_%d" % y, [128, S], BF16) for y in range(2)]
    G = P.sb("Gc", [128, 16, 128], BF16)
    GT = P.sb("GTc", [128, 2, 128], BF16)
    w1st = P.sb("w1st", [128, 16, 256], F32)
    w1b = P.sb("w1b", [128, 16, 256], BF16)
    w2st = P.sb("w2st", [128, 2, 64], F32)
    w2b = P.sb("w2b", [128, 2, 64], BF16)
    pos2 = P.sb("pos2", [128, 2, 16], F32)
    P.dma(pos2[:], C.nsapos2[j], writes=["pos2"])
    y2 = P.sb("gy2", [128, 128], F32)
    t1 = P.sb("gt1", [128, 128], F32)
    sg = P.sb("gsg", [128, 128], F32)
    P.memset("pool", GT[:], 0.0, ["GT0"])
    for kv in range(2):
        w1 = (C.nsa_ck_w1 if kv == 0 else C.nsa_cv_w1)[j]
        w2 = (C.nsa_ck_w2 if kv == 0 else C.nsa_cv_w2)[j]
        P.dma(w1st[:], w1.rearrange("(j p) c -> p j c", p=128), writes=["w1st"])
        P.copy("pool", w1b[:], w1st[:], ["w1st"], ["w1b"])
        P.dma(w2st[:], w2.rearrange("(k p) c -> p k c", p=128), writes=["w2st"])
        P.copy("dve", w2b[:], w2st[:], ["w2st"], ["w2b"])
        slab, sk = SL.load(W, OD["kv"] + kv * 128, 128, eng="dve")
        for y in range(2):
            P.memset("pool", K2[y][64:128, S - 1:S], 0.0, [("K2z", y)])

        def cons_c(c, cs, ps, pk):
            for y in range(2):
                src = ps[y * 64:(y + 1) * 64, :]
                eng = "act" if y == 0 else "dve"
                P.copy(eng, K2[y][0:64, cs], src, [pk], [("K2a", y, c)])
                if c == 0:
                    P.copy(eng, K2[y][64:128, 0:511], ps[y * 64:(y + 1) * 64, 1:512], [pk], [("K2b", y, c)])
                else:
                    P.copy(eng, K2[y][64:128, c * 512 - 1:c * 512 + 511], src, [pk, ("K2z", y)], [("K2b", y, c)])
        if part >= 1:
            proj_fm(P, C, slab, sk, 128, hnT, cons_c)
        k2keys = [[("K2a", y, c) for c in range(4)] + [("K2b", y, c) for c in range(4)] + [("K2z", y)] for y in range(2)]
        for y in range(2 if part >= 2 else 0):
            for jj in range(16):
                P.ts("dve" if jj % 2 == 0 else "pool", G[:, jj, 0:127], K2[y][:, 2 * jj:2 * jj + 2017:16], pos2[:, kv, jj:jj + 1], None, ALU.add, None,
                     k2keys[y] + ["pos2"], [("G", jj)])
            if part < 3:
                continue
            for ht in range(2):
                b = 4 + ht
                for jj in range(16):
                    P.mm(C.ps[b][:, 0:127], w1b[:, jj, ht * 128:(ht + 1) * 128], G[:, jj, 0:127], jj == 0, jj == 15, [("G", jj), "w1b"], [("ps", b)])
                gelu_tanh(P, C.ps[b][:, 0:127], ("ps", b), GT[:, ht, 0:127], ("GT", ht), y2[:, 0:127], t1[:, 0:127], sg[:, 0:127], "g")
            if part < 4:
                continue
            if kv == 0:
                for ht in range(2):
                    P.mm(C.ps[2][0:64, 0:128], w2b[:, ht, :], GT[:, ht, :], ht == 0, ht == 1, [("GT", ht), "GT0", "w2b"], [("ps", 2)])
                P.copy("act", KcA[0:64, y, :], C.ps[2][0:64, 0:128], [("ps", 2)], [("KcA", y)])
            elif part >= 5:
                for ht in range(2):
                    P.mm(C.ps[3][:, 0:64], GT[:, ht, :], w2b[:, ht, :], ht == 0, ht == 1, [("GT", ht), "GT0", "w2b"], [("ps", 3)])
                if part >= 6:
                    P.copy("dve", VcA[:, y, 0, 0:64], C.ps[3][:, 0:64], [("ps", 3), "VcA"], [("VcAw", y, 0)])
                    P.copy("dve", VcA[:, y, 1, 64:128], C.ps[3][:, 0:64], [("ps", 3), "VcA"], [("VcAw", y, 1)])
    P.end_phase()
    if stop == "b":
        P.pop_scope()
        return

    P.begin_phase()
    addm = P.sb("addm", [128, S], BF16)
    P.dma(addm[:], C.addmask[:, :], writes=["addm"])
    ovl = P.sb("ovl", [128, 64], BF16)
    P.dma(ovl[:], C.ovl[:, :], writes=["ovl"])
    selc = P.sb("selc", [128, NT, 2, 32], F32)
    P.dma(selc[:], C.selc[:, :, :, :], writes=["selc"])
    pslc = P.sb("pslcT", [32, 2, S], F32)
    sm = [P.sb("smc%d" % i, [128, 512], F32) for i in range(2)]
    PT = [P.sb("PTc%d" % i, [128, 512], BF16) for i in range(2)]
    rec = P.sb("rec", [128, 512], F32)
    tmpf = P.sb("tmpf", [128, 512], F32)
    rec2 = P.sb("rec2", [32, 512], F32)
    tmp2 = P.sb("tmp2", [32, 512], F32)
    items = [(h, qc) for h in range(8) for qc in range(4)]

    def cmpA(i):
        h, qc = items[i]
        y = h // 4
        qs = slice(qc * 512, (qc + 1) * 512)
        s_ = i % 2
        P.mm(C.ps[s_][:], KcA[0:100, y, :], QA[0:100, h, qs], True, True, [], [("ps", s_)])
        P.tt("dve", sm[s_][:], C.ps[s_][:], addm[:, qs], ALU.add, [("ps", s_), "addm"], [("sm", s_)])
        P.act(PT[s_][:], sm[s_][:], AF.Exp, [("sm", s_)], [("PT", s_)])

    def cmpB(i):
        h, qc = items[i]
        y, par, tile_i, hh = h // 4, h % 2, h // 2, h % 4
        lo = slice(par * 64, par * 64 + 64)
        qs = slice(qc * 512, (qc + 1) * 512)
        s_ = i % 2
        ob = 2 + s_
        P.mm(C.ps[ob][:], VcA[:, y, par, :], PT[s_][:], True, True, [("PT", s_)], [("ps", ob)])
        P.mm(C.ps[5][0:64, :], ovl[:], PT[s_][:], True, True, [("PT", s_), "ovl"], [("ps", 5)])
        P.mm(C.ps[4][:], gsel[0:32, 0, h, :], SgT[0:32, qs], True, True, ["gsel"], [("ps", 4)])
        softmax_pv_finish(P, C, C.ps[ob], par, ymT[lo, tile_i, qs], rec, tmpf, None, None, ("ps", ob), ("ymT", tile_i, par, qc), True,
                          clamp=True, gate_ps=C.ps[4], gkey=("ps", 4))
        P.ts("dve", rec2[:], C.ps[5][32:64, :], 1e-18, None, ALU.max, None, [("ps", 5)], ["rec2"])
        P.act(rec2[:], rec2[:], AF.Ln, ["rec2"], ["rec2"])
        P.act(rec2[:], rec2[:], AF.Exp, ["rec2"], ["rec2"], scale=-1.0)
        if hh == 0:
            P.tt("dve", pslc[:, y, qs], C.ps[5][0:32, :], rec2[:], ALU.mult, [("ps", 5), "rec2"], [("pslc", y, qc)])
        else:
            P.tt("dve", tmp2[:], C.ps[5][0:32, :], rec2[:], ALU.mult, [("ps", 5), "rec2"], ["tmp2"])
            P.tt("dve", pslc[:, y, qs], pslc[:, y, qs], tmp2[:], ALU.add, ["tmp2", ("pslc", y, qc)], [("pslc", y, qc)])

    cmpA(0)
    for i in range(len(items)):
        if i + 1 < len(items):
            cmpA(i + 1)
        cmpB(i)
    sc = [P.sb("scs%d" % i, [128, 32], F32) for i in range(2)]
    m8 = [P.sb("m8s%d" % i, [128, 8], F32) for i in range(2)]
    ng = [P.sb("ngs%d" % i, [128, 32], F32) for i in range(2)]
    sitems = [(y, qt) for y in range(2) for qt in range(NT)]

    def selA(i):
        y, qt = sitems[i]
        s_ = i % 2
        ts_ = slice(qt * 128, (qt + 1) * 128)
        P.tr(C.ps[6 + s_][:, 0:32], pslc[:, y, ts_], C.identf[0:32, 0:32], [("pslc", y, qt // 4)], [("ps", 6 + s_)])
        P.tt("dve", sc[s_][:], C.ps[6 + s_][:, 0:32], selc[:, qt, 0, :], ALU.mult, [("ps", 6 + s_), "selc"], [("sc", s_)])
        P.tt("dve", sc[s_][:], sc[s_][:], selc[:, qt, 1, :], ALU.add, [("sc", s_), "selc"], [("sc", s_)])
        P.op("dve", lambda e, s_=s_: e.max(out=m8[s_][:], in_=sc[s_][:]), [("sc", s_)], [("m8", s_)])
        P.ts("dve", ng[s_][:], sc[s_][:], m8[s_][:, 7:8], 30000.0, ALU.is_ge, ALU.mult, [("sc", s_), ("m8", s_)], [("ng", s_)])
        P.ts("dve", ng[s_][:], ng[s_][:], -30000.0, None, ALU.add, None, [("ng", s_)], [("ng", s_)])

    def selB(i):
        y, qt = sitems[i]
        s_ = i % 2
        ts_ = slice(qt * 128, (qt + 1) * 128)
        P.tr(C.ps[4 + s_][0:32, 0:128], ng[s_][:], C.identf[:], [("ng", s_)], [("ps", 4 + s_)])
        for hh in range(4):
            eng = "act" if s_ == 0 else "dve"
            P.copy(eng, QA[64:96, 4 * y + hh, ts_], C.ps[4 + s_][0:32, 0:128], [("ps", 4 + s_), "QAmask"], [("QAm", 4 * y + hh, qt)])

    selA(0)
    for i in range(len(sitems)):
        if i + 1 < len(sitems):
            selA(i + 1)
        selB(i)
    P.end_phase()
    if stop == "c":
        P.pop_scope()
        return

    for br in (1, 2):
        P.begin_phase()
        SL = SlabLoader(P, "v%d" % br)
        VA = P.sb("VAn", [128, NT, 2, 2, 128], BF16)
        P.memset("dve", VA[:], 1.0, ["VA"])
        slab, sk = SL.load(W, OD["kv"] + (3 if br == 1 else 5) * 128, 128)
        for kt in range(NT):
            b = 6 + (kt % 2)
            for kf in range(8):
                P.mm(C.ps[b][:, 0:128], hnT[:, kf, kt * 128:(kt + 1) * 128], slab[:, kf, :], kf == 0, kf == 7, [sk], [("ps", b)])
            pv = C.ps[b][:, 0:128].rearrange("p (y c) -> p y c", y=2)
            eng = "act" if kt % 2 == 0 else "dve"
            P.copy(eng, VA[:, kt, :, 0, 0:64], pv, [("ps", b), "VA"], [("VAw", kt, 0)])
            P.copy(eng, VA[:, kt, :, 1, 64:128], pv, [("ps", b), "VA"], [("VAw", kt, 1)])
        sgn = None
        if br == 2:
            sgn = P.sb("sgn", [128, 4, S], BF16)
            for hp in range(4):
                slab2, sk2 = SL.load(W, OD["gn"] + hp * 128, 128, eng="dve")

                def cons_gn(c, cs, ps, pk, hp=hp):
                    P.act(sgn[:, hp, cs], ps[:], AF.Silu, [pk], [("sgn", hp, c)])
                proj_fm(P, C, slab2, sk2, 128, hnT, cons_gn)
        KA = KS if br == 1 else KW
        PT = [P.sb("PTn%d" % i, [128, 512], BF16) for i in range(6)]
        rec = P.sb("rec", [128, 512], F32)
        tmpf = P.sb("tmpf", [128, 512], F32)
        groups = []
        for h in range(8):
            y, par, tile_i = h // 4, h % 2, h // 2
            for qc in range(4):
                ob = 2 + (qc % 2)
                kts = list(range(0, 4 * qc + 4)) if br == 1 else list(range(max(0, 4 * qc - 4), 4 * qc + 4))
                steps = []
                for kt in kts:
                    o = kt - 4 * qc
                    rlo = max(o, 0)
                    rhi = 3 if br == 1 else min(o + 4, 3)
                    c0, c1 = rlo * 128, (rhi + 1) * 128
                    masks = []
                    if o >= 0:
                        masks.append((o * 128, (o + 1) * 128, C.tri[:]))
                    if br == 2 and o <= -1:
                        masks.append(((o + 4) * 128, (o + 5) * 128, C.wmask[:]))
                    steps.append(dict(c0=c0, c1=c1, lhsK=KA[0:100, y, kt * 128:(kt + 1) * 128],
                                      rhsQ=QA[0:100, h, qc * 512 + c0:qc * 512 + c1], scale=None, masks=masks,
                                      lhsV=VA[:, kt, y, par, :], ob=ob, rv=[("VAw", kt, par)]))

                def fin(h=h, par=par, qc=qc, ob=ob, tile_i=tile_i):
                    lo = slice(par * 64, par * 64 + 64)
                    qs = slice(qc * 512, (qc + 1) * 512)
                    P.mm(C.ps[4][:], gsel[0:32, br, h, :], SgT[0:32, qs], True, True, [], [("ps", 4)])
                    softmax_pv_finish(P, C, C.ps[ob], par, ymT[lo, tile_i, qs], rec, tmpf,
                                      sgn[lo, tile_i, qs] if br == 2 else None, ("sgn", tile_i, qc) if br == 2 else None,
                                      ("ps", ob), ("ymT", tile_i, par, qc), False, clamp=False, gate_ps=C.ps[4], gkey=("ps", 4))
                groups.append((steps, fin))
        run_attention(P, C, groups, PT, sbanks=(0, 1, 5, 7), depth=4)
        P.end_phase()
    P.pop_scope()
```

```python
import math
from contextlib import ExitStack

import numpy as np
import concourse.bass as bass
import concourse.mybir as mybir
from concourse.bass_utils import run_bass_kernel_spmd

F32 = mybir.dt.float32
BF16 = mybir.dt.bfloat16
I32 = mybir.dt.int32
ALU = mybir.AluOpType
AF = mybir.ActivationFunctionType

S = 2048
D = 1024
NT = S // 128
EPS = 1e-6
ENGS = ("pe", "act", "dve", "pool", "sp")
CENG = ("pe", "act", "dve", "pool")
N_DMA_SEMS = 84
TWO_PI = 2.0 * math.pi


class Prog:
    def __init__(self, nc):
        self.nc = nc
        self.gstack = ExitStack()
        self.engsem = {e: self.gstack.enter_context(nc.semaphore("es_" + e)) for e in CENG}
        self.dsems = [self.gstack.enter_context(nc.semaphore("ds%d" % i)) for i in range(N_DMA_SEMS)]
        self.engcnt = {e: 0 for e in CENG}
        self.dcnt = [0] * N_DMA_SEMS
        self.scopes = []
        self.uid = 0
        self.n_instr = 0

    def close(self):
        self.gstack.close()

    def gsb(self, name, shape, dt):
        return self.gstack.enter_context(self.nc.sbuf_tensor(name, list(shape), dt))

    def gps(self, name, shape, dt=F32):
        return self.gstack.enter_context(self.nc.psum_tensor(name, list(shape), dt))

    def sb(self, name, shape, dt):
        self.uid += 1
        return self.scopes[-1].enter_context(self.nc.sbuf_tensor("%s_%d" % (name, self.uid), list(shape), dt))

    def push_scope(self):
        self.scopes.append(ExitStack())

    def pop_scope(self):
        self.scopes.pop().close()

    def begin_phase(self):
        self.push_scope()
        self.ins = []
        self.last_w = {}
        self.readers = {}
        self.eng_seq = {e: [] for e in ENGS}
        self.semmap = {}

    def _add(self, eng, fn, reads, writes, kind, semkey=None):
        idx = len(self.ins)
        deps = set()
        for k in reads:
            if k in self.last_w:
                deps.add((self.last_w[k], 0))
            if isinstance(k, tuple) and k[0] == "ps":
                for r in self.readers.get(k, ()):
                    if self.ins[r][0] != eng:
                        deps.add((r, 1))
        for k in writes:
            if k in self.last_w:
                deps.add((self.last_w[k], 1))
            for r in self.readers.get(k, ()):
                deps.add((r, 2))
        for k in reads:
            self.readers.setdefault(k, []).append(idx)
        for k in writes:
            self.last_w[k] = idx
            self.readers[k] = []
        self.ins.append((eng, fn, kind, semkey, deps))
        self.eng_seq[eng].append(idx)
        return idx

    def op(self, eng, fn, reads=(), writes=()):
        return self._add(eng, fn, list(reads), list(writes), "c")

    def dma(self, out, in_, reads=(), writes=(), q="sp", **kw):
        semkey = (q, tuple(writes))
        if semkey not in self.semmap:
            assert len(self.semmap) < N_DMA_SEMS, "too many dma sem keys"
            self.semmap[semkey] = len(self.semmap)
        fn = lambda e: e.dma_start(out=out, in_=in_, **kw)
        return self._add(q, fn, list(reads), list(writes), "d", semkey)

    def act(self, out, in_, func, r, w, **kw):
        self.op("act", lambda e: e.activation(out=out, in_=in_, func=func, **kw), r, w)

    def tt(self, eng, out, in0, in1, op, r, w):
        self.op(eng, lambda e: e.tensor_tensor(out=out, in0=in0, in1=in1, op=op), r, w)

    def ts(self, eng, out, in0, s1, s2, op0, op1, r, w, **kw):
        if s2 is None:
            self.op(eng, lambda e: e.tensor_scalar(out=out, in0=in0, scalar1=s1, scalar2=None, op0=op0, **kw), r, w)
        else:
            self.op(eng, lambda e: e.tensor_scalar(out=out, in0=in0, scalar1=s1, scalar2=s2, op0=op0, op1=op1, **kw), r, w)

    def stt(self, eng, out, in0, scalar, in1, op0, op1, r, w):
        self.op(eng, lambda e: e.scalar_tensor_tensor(out=out, in0=in0, scalar=scalar, in1=in1, op0=op0, op1=op1), r, w)

    def copy(self, eng, out, in_, r, w):
        if eng == "act":
            self.op(eng, lambda e: e.copy(out=out, in_=in_), r, w)
        else:
            self.op(eng, lambda e: e.tensor_copy(out=out, in_=in_), r, w)

    def memset(self, eng, ap, val, w):
        self.op(eng, lambda e: e.memset(ap, val), (), w)

    def mm(self, out, lhsT, rhs, start, stop, r, w, skip=False):
        if skip:
            self.op("pe", lambda e: e.matmul(out, lhsT=lhsT, rhs=rhs, start=start, stop=stop, skip_group_check=True), r, w)
        else:
            self.op("pe", lambda e: e.matmul(out, lhsT=lhsT, rhs=rhs, start=start, stop=stop), r, w)

    def tr(self, out, in_, ident, r, w):
        self.op("pe", lambda e: e.transpose(out=out, in_=in_, identity=ident), r, w)

    def recip(self, out, in_, r, w):
        self.op("dve", lambda e: e.reciprocal(out=out, in_=in_), r, w)

    def end_phase(self):
        nc = self.nc
        ins = self.ins
        pos = {}
        for e in ENGS:
            for p, idx in enumerate(self.eng_seq[e]):
                pos[idx] = p
        WIN = 3

        def edge_needed(idx, d, typ):
            eng = ins[idx][0]
            deng, _, dkind, _, _ = ins[d]
            if dkind == "d" or ins[idx][2] == "d":
                return True
            if deng == eng:
                return eng != "pe"
            return True

        pruned = []
        for idx, (eng, fn, kind, semkey, deps) in enumerate(ins):
            best = {}
            keep = set()
            for (d, typ) in deps:
                if not edge_needed(idx, d, typ):
                    continue
                if ins[d][2] == "d":
                    keep.add((d, typ))
                    continue
                pe_ = ins[d][0]
                if pe_ not in best or pos[d] > pos[best[pe_][0]]:
                    best[pe_] = (d, typ)
            keep.update(best.values())
            pruned.append(keep)
        needed = set()
        for idx in range(len(ins)):
            for (d, typ) in pruned[idx]:
                needed.add(d)
        for e in CENG:
            if self.eng_seq[e]:
                needed.add(self.eng_seq[e][-1])
        token = {}
        finals = {}
        for idx, (eng, fn, kind, semkey, deps) in enumerate(ins):
            if kind == "d":
                si = self.semmap[semkey]
                self.dcnt[si] += 16
                token[idx] = (("d", si), self.dcnt[si])
                finals[("d", si)] = self.dcnt[si]
            elif idx in needed:
                self.engcnt[eng] += 1
                token[idx] = (("e", eng), self.engcnt[eng])
                finals[("e", eng)] = self.engcnt[eng]
        progs = {e: [] for e in ENGS}
        waited = {e: {} for e in ENGS}
        for idx, (eng, fn, kind, semkey, deps) in enumerate(ins):
            waits = {}
            for (d, typ) in pruned[idx]:
                sn, val = token[d]
                if waited[eng].get(sn, 0) >= val:
                    continue
                waits[sn] = max(waits.get(sn, 0), val)
            for sn, val in waits.items():
                waited[eng][sn] = val
            progs[eng].append((waits, fn, token.get(idx)))
        self.n_instr += len(ins)

        def sem_of(sn):
            return self.dsems[sn[1]] if sn[0] == "d" else self.engsem[sn[1]]

        def run_engine(e, name):
            for waits, fn, inc in progs[name]:
                for sn, val in waits.items():
                    e.wait_ge(sem_of(sn), val)
                r = fn(e)
                if inc is not None:
                    r.then_inc(sem_of(inc[0]), 16 if inc[0][0] == "d" else 1)

        with nc.Block() as block:
            @block.tensor
            def _(e):
                run_engine(e, "pe")

            @block.scalar
            def _(e):
                run_engine(e, "act")

            @block.vector
            def _(e):
                run_engine(e, "dve")

            @block.gpsimd
            def _(e):
                run_engine(e, "pool")
                for sn, val in finals.items():
                    if sn[0] == "d" and any(k[0] == "pool" and self.semmap[k] == sn[1] for k in self.semmap):
                        e.wait_ge(sem_of(sn), val)

            @block.sync
            def _(e):
                run_engine(e, "sp")
                for sn, val in finals.items():
                    e.wait_ge(sem_of(sn), val)
        nc.all_engine_barrier()
        self.pop_scope()


class Ctx:
    pass


def load_w_bf16(P, C, dst, src, nkf, cols, tag, conv_engs=("dve", "act")):
    stage = [P.sb("wst_%s%d" % (tag, i), [128, cols], F32) for i in range(2)]
    for kf in range(nkf):
        s = kf % 2
        P.dma(stage[s][:], src[kf * 128:(kf + 1) * 128, :], writes=[("wst", tag, s)])
        P.copy(conv_engs[kf % len(conv_engs)], dst[:, kf, :], stage[s][:], [("wst", tag, s)], [("w", tag, kf)])


def rstd_from_ssq(P, C, st, n, key):
    P.act(st[:, 1:2], st[:, 0:1], AF.Sqrt, [key + (0,)], [key + (1,)], scale=1.0 / n, bias=C.epsb[:])
    P.recip(st[:, 2:3], st[:, 1:2], [key + (1,)], [key + (2,)])


def phase_prenorm(P, C, L, src, hnT):
    P.begin_phase()
    gb = P.sb("gpre", [128, D], F32)
    P.dma(gb[:], C.pre_norm[L:L + 1, :].partition_broadcast(128), writes=["gpre"])
    ht = [P.sb("ht%d" % i, [128, D], F32) for i in range(2)]
    junk = P.sb("junk", [128, D], BF16)
    hnb = [P.sb("hnb%d" % i, [128, D], BF16) for i in range(2)]
    st = [P.sb("st%d" % i, [128, 4], F32) for i in range(2)]
    psT = C.ps[0][:].bitcast(BF16)
    psT2 = C.ps[1][:].bitcast(BF16)
    def stage1(tt):
        s = tt % 2
        P.dma(ht[s][:], src[tt * 128:(tt + 1) * 128, :], writes=[("ht", s)])
        P.act(junk[:], ht[s][:], AF.Square, [("ht", s)], ["junk", ("st", s, 0)], accum_out=st[s][:, 0:1])
        rstd_from_ssq(P, C, st[s], D, ("st", s))
        P.stt("dve", hnb[s][:], ht[s][:], st[s][:, 2:3], gb[:], ALU.mult, ALU.mult, [("ht", s), ("st", s, 2), "gpre"], [("hnb", s)])

    def stage2(tt):
        s = tt % 2
        pst = psT if s == 0 else psT2
        for kf in range(8):
            P.tr(pst[:, kf * 128:(kf + 1) * 128], hnb[s][:, kf * 128:(kf + 1) * 128], C.ident[:], [("hnb", s), "ident"], [("psT", s)])
        P.copy("act" if s == 0 else "dve", hnT[:, :, tt * 128:(tt + 1) * 128], pst[:, 0:1024].rearrange("p (k t) -> p k t", k=8), [("psT", s)], [("hnT", tt)])

    stage1(0)
    for tt in range(NT):
        if tt + 1 < NT:
            stage1(tt + 1)
        stage2(tt)
    P.end_phase()


def phase_out(P, C, L, ymT, src, dst):
    P.begin_phase()
    wout = P.sb("wout", [128, 8, D], BF16)
    wpg = P.sb("wpg", [128, 8, D], BF16)
    wpp = P.sb("wpp", [128, 2, D], BF16)
    load_w_bf16(P, C, wout, C.w_out[L], 8, D, "wout")
    load_w_bf16(P, C, wpg, C.ple_gate[L], 8, D, "wpg")
    load_w_bf16(P, C, wpp, C.ple_proj[L], 2, D, "wpp")
    gb = P.sb("gpost", [128, D], F32)
    P.dma(gb[:], C.post_norm[L:L + 1, :].partition_broadcast(128), writes=["gpost"])
    ht = [P.sb("ht%d" % i, [128, D], F32) for i in range(2)]
    pt = [P.sb("pt%d" % i, [128, 256], F32) for i in range(2)]
    ptb = [P.sb("ptb%d" % i, [128, 256], BF16) for i in range(2)]
    pT = [P.sb("pT%d" % i, [128, 2, 128], BF16) for i in range(2)]
    junk = P.sb("junk", [128, D], BF16)
    t1 = [P.sb("t1%d" % i, [128, D], F32) for i in range(2)]
    hm = [P.sb("hm%d" % i, [128, D], F32) for i in range(2)]
    hmb = [P.sb("hmb%d" % i, [128, D], BF16) for i in range(2)]
    hmT = [P.sb("hmT%d" % i, [128, 8, 128], BF16) for i in range(2)]
    sg = [P.sb("sg%d" % i, [128, D], F32) for i in range(2)]
    hn = [P.sb("hnw%d" % i, [128, D], F32) for i in range(2)]
    st = [P.sb("st%d" % i, [128, 4], F32) for i in range(2)]
    wkeys_out = [("w", "wout", k) for k in range(8)]
    wkeys_pg = [("w", "wpg", k) for k in range(8)]
    wkeys_pp = [("w", "wpp", k) for k in range(2)]
    def stage1(tt):
        s = tt % 2
        tsl = slice(tt * 128, (tt + 1) * 128)
        P.dma(ht[s][:], src[tsl, :], writes=[("ht", s)])
        P.dma(pt[s][:], C.p[L, tsl, :], writes=[("pt", s)])
        for hf in range(2):
            for kf in range(8):
                P.mm(C.ps[hf][:], ymT[:, kf, tsl], wout[:, kf, hf * 512:(hf + 1) * 512], kf == 0, kf == 7,
                     [("ymT", kf)] + wkeys_out, [("ps", hf)])
        for hf in range(2):
            P.act(junk[:, hf * 512:(hf + 1) * 512], C.ps[hf][:], AF.Square, [("ps", hf)], ["junk", ("st", s, 0, hf)],
                  accum_out=st[s][:, hf:hf + 1])
        P.tt("dve", st[s][:, 0:1], st[s][:, 0:1], st[s][:, 1:2], ALU.add, [("st", s, 0, 0), ("st", s, 0, 1)], [("st", s, 0)])
        rstd_from_ssq(P, C, st[s], D, ("st", s))
        for hf in range(2):
            hs = slice(hf * 512, (hf + 1) * 512)
            P.stt("dve", t1[s][:, hs], C.ps[hf][:], st[s][:, 2:3], gb[:, hs], ALU.mult, ALU.mult,
                  [("ps", hf), ("st", s, 2), "gpost"], [("t1", s, hf)])
            P.tt("dve", hm[s][:, hs], t1[s][:, hs], ht[s][:, hs], ALU.add, [("t1", s, hf), ("ht", s)], [("hm", s, hf)])
            P.copy("act", hmb[s][:, hs], hm[s][:, hs], [("hm", s, hf)], [("hmb", s, hf)])
        P.copy("act", ptb[s][:], pt[s][:], [("pt", s)], [("ptb", s)])

    def stage2(tt):
        s = tt % 2
        tsl = slice(tt * 128, (tt + 1) * 128)
        psT = C.ps[2][:].bitcast(BF16)
        for kf in range(8):
            P.tr(psT[:, kf * 128:(kf + 1) * 128], hmb[s][:, kf * 128:(kf + 1) * 128], C.ident[:],
                 [("hmb", s, kf // 4), "ident"], [("ps", 2)])
        P.copy("dve", hmT[s][:], psT[:, 0:1024].rearrange("p (k t) -> p k t", k=8), [("ps", 2)], [("hmT", s)])
        psT3 = C.ps[3][:].bitcast(BF16)
        for j in range(2):
            P.tr(psT3[:, j * 128:(j + 1) * 128], ptb[s][:, j * 128:(j + 1) * 128], C.ident[:], [("ptb", s), "ident"], [("ps", 3)])
        P.copy("dve", pT[s][:], psT3[:, 0:256].rearrange("p (k t) -> p k t", k=2), [("ps", 3)], [("pT", s)])
        for hf in range(2):
            hs = slice(hf * 512, (hf + 1) * 512)
            for kf in range(8):
                P.mm(C.ps[4 + hf][:], hmT[s][:, kf, :], wpg[:, kf, hs], kf == 0, kf == 7, [("hmT", s)] + wkeys_pg, [("ps", 4 + hf)])
            for j in range(2):
                P.mm(C.ps[6 + hf][:], pT[s][:, j, :], wpp[:, j, hs], j == 0, j == 1, [("pT", s)] + wkeys_pp, [("ps", 6 + hf)])
            P.act(sg[s][:, hs], C.ps[4 + hf][:], AF.Sigmoid, [("ps", 4 + hf)], [("sg", s, hf)])
            P.tt("dve", sg[s][:, hs], sg[s][:, hs], C.ps[6 + hf][:], ALU.mult, [("sg", s, hf), ("ps", 6 + hf)], [("sg", s, hf)])
            P.tt("dve", hn[s][:, hs], sg[s][:, hs], hm[s][:, hs], ALU.add, [("sg", s, hf), ("hm", s, hf)], [("hn", s, hf)])
        P.dma(dst[tsl, :], hn[s][:], reads=[("hn", s, 0), ("hn", s, 1)], writes=[("dst", tt % 4)], q="pool")

    stage1(0)
    for tt in range(NT):
        if tt + 1 < NT:
            stage1(tt + 1)
        stage2(tt)
    P.end_phase()


def phase_even_proj(P, C, j, hnT, uT, sgaT, sgbT, hcpad):
    P.begin_phase()
    wst = [P.sb("wst%d" % i, [128, 8, 128], F32) for i in range(2)]
    wbf = [P.sb("wbf%d" % i, [128, 8, 128], BF16) for i in range(2)]
    aT = P.sb("aT", [128, 4, S], BF16)
    sig = [P.sb("sig%d" % i, [128, 512], BF16) for i in range(2)]
    w_in = C.ev_w_in[j]
    P.memset("pool", hcpad[:, :, 0:30], 0.0, ["hcpad0"])
    n = 0
    for sl in range(20):
        s = sl % 2
        P.dma(wst[s][:], w_in[:, sl * 128:(sl + 1) * 128].rearrange("(k p) c -> p k c", p=128), writes=[("wst", s)])
        P.copy("pool", wbf[s][:], wst[s][:], [("wst", s)], [("wbf", s)])
        for c in range(4):
            cs = slice(c * 512, (c + 1) * 512)
            b = n % 4
            n += 1
            pb = C.ps[b]
            for kf in range(8):
                P.mm(pb[:], wbf[s][:, kf, :], hnT[:, kf, cs], kf == 0, kf == 7, [("wbf", s)], [("ps", b)])
            if sl < 4:
                P.copy("act", uT[:, sl, cs], pb[:], [("ps", b)], [("uT", sl, c)])
            elif sl < 8:
                P.act(sgaT[:, sl - 4, cs], pb[:], AF.Silu, [("ps", b)], [("sgaT", sl - 4, c)])
            elif sl < 12:
                P.copy("dve", aT[:, sl - 8, cs], pb[:], [("ps", b)], [("aT", sl - 8, c)])
            elif sl < 16:
                q = n % 2
                P.act(sig[q][:], pb[:], AF.Sigmoid, [("ps", b)], [("sig", q)])
                P.tt("dve", hcpad[:, sl - 12, 30 + c * 512:30 + (c + 1) * 512], aT[:, sl - 12, cs], sig[q][:], ALU.mult,
                     [("aT", sl - 12, c), ("sig", q)], [("hcpad", sl - 12, c)])
            else:
                P.act(sgbT[:, sl - 16, cs], pb[:], AF.Silu, [("ps", b)], [("sgbT", sl - 16, c)])
    P.end_phase()


def phase_conv(P, C, j, hcpad, sgbT, ymT):
    P.begin_phase()
    evp = P.sb("evp", [128, 4, 40], F32)
    P.dma(evp[:], C.evp[j], writes=["evp"])
    wpw = P.sb("wpw", [128, 4, 512], BF16)
    load_w_bf16(P, C, wpw, C.cv_w_pw[j], 4, 512, "wpw")
    wpw_keys = [("w", "wpw", k) for k in range(4)]
    dg = P.sb("dg", [128, 4, 31, 128], BF16)
    for ft in range(4):
        for k in range(31):
            P.ts("dve", dg[:, ft, k, :], C.identf[:], evp[:, ft, k:k + 1], None, ALU.mult, None,
                 ["evp", "identf"], [("dg", ft)])
    cv1 = P.sb("cv1", [128, 4, 512], F32)
    sq = P.sb("sq", [128, 4, 512], F32)
    mu = P.sb("mu", [128, 512], F32)
    m2 = P.sb("m2", [128, 512], F32)
    rs = P.sb("rs", [128, 512], F32)
    xn = P.sb("xn", [128, 4, 512], F32)
    cvn = P.sb("cvn", [128, 4, 512], BF16)
    for c in range(4):
        cs = slice(c * 512, (c + 1) * 512)
        for ft in range(4):
            for k in range(31):
                P.mm(C.ps[ft][:], dg[:, ft, k, :], hcpad[:, ft, c * 512 + k:c * 512 + k + 512], k == 0, k == 30,
                     [("dg", ft)], [("ps", ft)])
            P.act(cv1[:, ft, :], C.ps[ft][:], AF.Identity, [("ps", ft), "evp"], [("cv1", ft)], bias=evp[:, ft, 31:32])
            P.act(sq[:, ft, :], cv1[:, ft, :], AF.Square, [("cv1", ft)], [("sq", ft)])
        for ft in range(4):
            P.mm(C.ps[4][:], C.onesf[:], cv1[:, ft, :], ft == 0, ft == 3, [("cv1", ft), "onesf"], [("ps", 4)])
        for ft in range(4):
            P.mm(C.ps[5][:], C.onesf[:], sq[:, ft, :], ft == 0, ft == 3, [("sq", ft), "onesf"], [("ps", 5)])
        P.act(mu[:], C.ps[4][:], AF.Copy, [("ps", 4)], ["mu"], scale=1.0 / 512)
        P.tt("dve", m2[:], mu[:], mu[:], ALU.mult, ["mu"], ["m2"])
        P.stt("dve", m2[:], C.ps[5][:], 1.0 / 512, m2[:], ALU.mult, ALU.subtract, [("ps", 5), "m2"], ["m2"])
        P.act(rs[:], m2[:], AF.Ln, ["m2"], ["rs"], bias=C.epsb[:])
        P.act(rs[:], rs[:], AF.Exp, ["rs"], ["rs"], scale=-0.5)
        for ft in range(4):
            eng = "dve"
            P.tt(eng, xn[:, ft, :], cv1[:, ft, :], mu[:], ALU.subtract, [("cv1", ft), "mu"], [("xn", ft)])
            P.tt(eng, xn[:, ft, :], xn[:, ft, :], rs[:], ALU.mult, [("xn", ft), "rs"], [("xn", ft)])
            P.act(cvn[:, ft, :], xn[:, ft, :], AF.Silu, [("xn", ft), "evp"], [("cvn", ft)],
                  scale=evp[:, ft, 32:33], bias=evp[:, ft, 33:34])
        for ot in range(4):
            b = 6 + (ot % 2)
            for ft in range(4):
                P.mm(C.ps[b][:], wpw[:, ft, ot * 128:(ot + 1) * 128], cvn[:, ft, :], ft == 0, ft == 3,
                     [("cvn", ft)] + wpw_keys, [("ps", b)])
            P.tt("dve", ymT[:, 4 + ot, cs], C.ps[b][:], sgbT[:, ot, cs], ALU.mult, [("ps", b)], [("ymT", 4 + ot, c)])
    P.end_phase()


def phase_s5_setup(P, C, j, BtR, BtI, CtR, CtI, sc2):
    P.begin_phase()
    sc = P.sb("sc", [128, 16, 3], F32)
    P.dma(sc[:], C.s5sc[j], writes=["sc"])
    w = P.sb("w", [128, 16, 16], F32)

    def col(i):
        return w[:, :, i]
    lr, li, ls = sc[:, :, 0], sc[:, :, 1], sc[:, :, 2]
    k = ["w%d" % i for i in range(16)]
    P.act(col(0), ls, AF.Exp, ["sc"], [k[0]])
    P.tt("dve", col(1), lr, col(0), ALU.mult, ["sc", k[0]], [k[1]])
    P.tt("dve", sc2[:, :, 0], li, col(0), ALU.mult, ["sc", k[0]], ["th"])
    P.ts("dve", sc2[:, :, 1], sc2[:, :, 0], 1.0 / TWO_PI, None, ALU.mult, None, ["th"], ["thq"])
    P.act(sc2[:, :, 2], col(1), AF.Exp, [k[1]], ["rho"])
    ki = P.sb("ki", [128, 16], I32)
    P.copy("dve", ki[:], sc2[:, :, 1], ["thq"], ["ki"])
    P.stt("dve", col(2), ki[:], -TWO_PI, sc2[:, :, 0], ALU.mult, ALU.add, ["ki", "th"], [k[2]])
    P.ts("dve", col(2), col(2), math.pi, -math.pi, ALU.min, ALU.max, [k[2]], [k[2]])
    P.act(col(3), col(2), AF.Abs, [k[2]], [k[3]])
    P.act(col(4), col(2), AF.Sin, [k[2]], [k[4]])
    P.act(col(5), col(3), AF.Sin, [k[3]], [k[5]], scale=-1.0, bias=C.hpib[:])
    P.tt("dve", col(6), sc2[:, :, 2], col(5), ALU.mult, ["rho", k[5]], [k[6]])
    P.tt("dve", col(7), sc2[:, :, 2], col(4), ALU.mult, ["rho", k[4]], [k[7]])
    P.ts("dve", col(6), col(6), -1.0, None, ALU.add, None, [k[6]], [k[6]])
    P.tt("dve", col(8), lr, lr, ALU.mult, ["sc"], [k[8]])
    P.tt("dve", col(9), li, li, ALU.mult, ["sc"], [k[9]])
    P.tt("dve", col(8), col(8), col(9), ALU.add, [k[8], k[9]], [k[8]])
    P.recip(col(8), col(8), [k[8]], [k[8]])
    P.tt("dve", col(9), col(6), lr, ALU.mult, [k[6], "sc"], [k[9]])
    P.tt("dve", col(10), col(7), li, ALU.mult, [k[7], "sc"], [k[10]])
    P.tt("dve", col(9), col(9), col(10), ALU.add, [k[9], k[10]], [k[9]])
    P.tt("dve", col(11), col(9), col(8), ALU.mult, [k[9], k[8]], [k[11]])
    P.tt("dve", col(9), col(7), lr, ALU.mult, [k[7], "sc"], [k[9]])
    P.tt("dve", col(10), col(6), li, ALU.mult, [k[6], "sc"], [k[10]])
    P.tt("dve", col(9), col(9), col(10), ALU.subtract, [k[9], k[10]], [k[9]])
    P.tt("dve", col(12), col(9), col(8), ALU.mult, [k[9], k[8]], [k[12]])
    for ri, dst in ((0, BtR), (1, BtI)):
        stg = P.sb("bst%d" % ri, [128, 16, 128], F32)
        P.dma(stg[:], C.s5bT[j, ri], writes=[("bst", ri)])
        P.copy("pool", dst[:], stg[:], [("bst", ri)], [("Bt", ri)])
    cre = P.sb("cre", [128, 16, 128], F32)
    cim = P.sb("cim", [128, 16, 128], F32)
    t1 = P.sb("t1", [128, 16, 128], F32)
    t2 = P.sb("t2", [128, 16, 128], F32)
    P.dma(cre[:], C.s5cP[j, 0], writes=["cre"])
    P.dma(cim[:], C.s5cP[j, 1], writes=["cim"])
    fre = w[:, :, 11:12].to_broadcast([128, 16, 128])
    fim = w[:, :, 12:13].to_broadcast([128, 16, 128])
    P.tt("dve", t1[:], cre[:], fre, ALU.mult, ["cre", k[11]], ["t1"])
    P.tt("pool", t2[:], cim[:], fim, ALU.mult, ["cim", k[12]], ["t2"])
    P.tt("dve", CtR[:], t1[:], t2[:], ALU.subtract, ["t1", "t2"], ["CtR"])
    P.tt("dve", t1[:], cre[:], fim, ALU.mult, ["cre", k[12]], ["t1"])
    P.tt("pool", t2[:], cim[:], fre, ALU.mult, ["cim", k[11]], ["t2"])
    P.stt("dve", CtI[:], t1[:], -1.0, t2[:], ALU.mult, ALU.subtract, ["t1", "t2"], ["CtI"])
    P.end_phase()


def phase_s5(P, C, j, uT, sgaT, ymT, BtR, BtI, CtR, CtI, sc2):
    P.begin_phase()
    TH = 1024
    evp = P.sb("evp", [128, 4, 40], F32)
    P.dma(evp[:], C.evp[j], writes=["evp"])
    wglu = P.sb("wglu", [128, 4, 512], BF16)
    load_w_bf16(P, C, wglu, C.s5_w_glu[j], 4, 512, "wglu")
    wglu_keys = [("w", "wglu", k) for k in range(4)]
    iot = P.sb("iot", [128, S], F32)
    P.op("pool", lambda e: e.iota(iot[:], pattern=[[1, S]], base=0, channel_multiplier=0,
                                  allow_small_or_imprecise_dtypes=True), (), ["iot"])
    ki = P.sb("ki", [128, TH], I32)
    rr = P.sb("rr", [128, TH], F32)
    ra = P.sb("ra", [128, TH], F32)
    cs_ = P.sb("cos", [128, TH], BF16)
    sn_ = P.sb("sin", [128, TH], BF16)
    bpR = P.sb("bpR", [128, TH], BF16)
    bpI = P.sb("bpI", [128, TH], BF16)
    wR = P.sb("wR", [128, TH], BF16)
    wI = P.sb("wI", [128, TH], BF16)
    xR = P.sb("xR", [128, TH], BF16)
    xI = P.sb("xI", [128, TH], BF16)
    buR = [P.sb("buR%d" % q, [128, 512], BF16) for q in range(2)]
    buI = [P.sb("buI%d" % q, [128, 512], BF16) for q in range(2)]
    mt = [[P.sb("m%d_%d" % (i, q), [128, 512], BF16) for i in range(4)] for q in range(2)]
    mo = [P.sb("mo%d" % i, [128, TH], BF16) for i in range(4)]
    carry = P.sb("carry", [128, 16, 2], F32)
    P.memset("pool", carry[:], 0.0, ["carry"])
    ygT = P.sb("ygT", [128, 4, S], BF16)
    yv = P.sb("yv", [128, 512], F32)
    y2 = P.sb("y2", [128, 512], F32)
    sgm = P.sb("sgm", [128, 512], F32)
    it = 0
    for ft in range(4):
        for hf in range(2):
            t0 = hf * TH
            for pl in range(4):
                pr = 4 * ft + pl
                th = sc2[:, pr, 0:1]
                thq = sc2[:, pr, 1:2]
                P.ts("dve", ki[:], iot[:, t0:t0 + TH], thq, None, ALU.mult, None, ["iot"], ["ki"])
                P.act(ra[:], iot[:, t0:t0 + TH], AF.Copy, ["iot"], ["ra"], scale=th)
                P.stt("dve", rr[:], ki[:], -TWO_PI, ra[:], ALU.mult, ALU.add, ["ki", "ra"], ["rr"])
                P.ts("dve", rr[:], rr[:], math.pi, -math.pi, ALU.min, ALU.max, ["rr"], ["rr"])
                P.act(sn_[:], rr[:], AF.Sin, ["rr"], ["sin"])
                P.act(ra[:], rr[:], AF.Abs, ["rr"], ["ra"])
                P.act(cs_[:], ra[:], AF.Sin, ["ra"], ["cos"], scale=-1.0, bias=C.hpib[:])
                for c in range(2):
                    cl = slice(c * 512, (c + 1) * 512)
                    cg = slice(t0 + c * 512, t0 + (c + 1) * 512)
                    q = it % 2
                    it += 1
                    bA, bB = C.ps[2 * q], C.ps[2 * q + 1]
                    P.mm(bA[:], BtR[:, pr, :], uT[:, ft, cg], True, True, [], [("ps", 2 * q)])
                    P.mm(bB[:], BtI[:, pr, :], uT[:, ft, cg], True, True, [], [("ps", 2 * q + 1)])
                    P.copy("act", buR[q][:], bA[:], [("ps", 2 * q)], [("buR", q)])
                    P.copy("act", buI[q][:], bB[:], [("ps", 2 * q + 1)], [("buI", q)])
                    m = mt[q]
                    P.tt("dve", m[0][:], buR[q][:], cs_[:, cl], ALU.mult, [("buR", q), "cos"], [("m", q, 0)])
                    P.tt("dve", m[1][:], buI[q][:], sn_[:, cl], ALU.mult, [("buI", q), "sin"], [("m", q, 1)])
                    P.tt("dve", m[2][:], buI[q][:], cs_[:, cl], ALU.mult, [("buI", q), "cos"], [("m", q, 2)])
                    P.tt("dve", m[3][:], buR[q][:], sn_[:, cl], ALU.mult, [("buR", q), "sin"], [("m", q, 3)])
                    P.tt("dve", bpR[:, cl], m[0][:], m[1][:], ALU.add, [("m", q, 0), ("m", q, 1)], [("bpR", c)])
                    P.tt("dve", bpI[:, cl], m[2][:], m[3][:], ALU.subtract, [("m", q, 2), ("m", q, 3)], [("bpI", c)])
                P.op("dve", lambda e, pr=pr: e.tensor_tensor_scan(out=wR[:], data0=sc2[:, pr, 2:3].to_broadcast([128, TH]), data1=bpR[:],
                                                                 initial=carry[:, pr, 0:1], op0=ALU.mult, op1=ALU.add),
                     [("bpR", 0), ("bpR", 1), "carry"], ["wR"])
                P.op("dve", lambda e, pr=pr: e.tensor_tensor_scan(out=wI[:], data0=sc2[:, pr, 2:3].to_broadcast([128, TH]), data1=bpI[:],
                                                                 initial=carry[:, pr, 1:2], op0=ALU.mult, op1=ALU.add),
                     [("bpI", 0), ("bpI", 1), "carry"], ["wI"])
                if hf == 0:
                    P.copy("dve", carry[:, pr, 0:1], wR[:, TH - 1:TH], ["wR"], ["carry"])
                    P.copy("dve", carry[:, pr, 1:2], wI[:, TH - 1:TH], ["wI"], ["carry"])
                P.tt("dve", mo[0][:], wR[:], cs_[:], ALU.mult, ["wR", "cos"], ["mo0"])
                P.tt("dve", mo[1][:], wI[:], sn_[:], ALU.mult, ["wI", "sin"], ["mo1"])
                P.tt("dve", mo[2][:], wI[:], cs_[:], ALU.mult, ["wI", "cos"], ["mo2"])
                P.tt("dve", mo[3][:], wR[:], sn_[:], ALU.mult, ["wR", "sin"], ["mo3"])
                P.tt("dve", xR[:], mo[0][:], mo[1][:], ALU.subtract, ["mo0", "mo1"], ["xR"])
                P.tt("dve", xI[:], mo[2][:], mo[3][:], ALU.add, ["mo2", "mo3"], ["xI"])
                for c in range(2):
                    cl = slice(c * 512, (c + 1) * 512)
                    P.mm(C.ps[4 + c][:], CtR[:, pr, :], xR[:, cl], pl == 0, False, ["xR"], [("ps", 4 + c)])
                    P.mm(C.ps[4 + c][:], CtI[:, pr, :], xI[:, cl], False, pl == 3, ["xI"], [("ps", 4 + c)])
            for c in range(2):
                cg = slice(t0 + c * 512, t0 + (c + 1) * 512)
                P.stt("dve", yv[:], uT[:, ft, cg], evp[:, ft, 34:35], C.ps[4 + c][:], ALU.mult, ALU.add,
                      [("ps", 4 + c), "evp"], ["yv"])
                P.act(y2[:], yv[:], AF.Square, ["yv"], ["y2"])
                P.ts("dve", y2[:], y2[:], 0.044715, 1.0, ALU.mult, ALU.add, ["y2"], ["y2"])
                P.tt("dve", y2[:], y2[:], yv[:], ALU.mult, ["y2", "yv"], ["y2"])
                P.act(sgm[:], y2[:], AF.Sigmoid, ["y2"], ["sgm"], scale=1.5957691216057308)
                P.tt("pool", ygT[:, ft, cg], sgm[:], yv[:], ALU.mult, ["sgm", "yv"], [("ygT", ft, hf, c)])
    gk = [("ygT", ft, hf, c) for ft in range(4) for hf in range(2) for c in range(2)]
    n = 0
    for ot in range(4):
        for c in range(4):
            cs = slice(c * 512, (c + 1) * 512)
            b = n % 4
            n += 1
            for ft in range(4):
                P.mm(C.ps[b][:], wglu[:, ft, ot * 128:(ot + 1) * 128], ygT[:, ft, cs], ft == 0, ft == 3, gk + wglu_keys, [("ps", b)])
            P.act(sgm[:], C.ps[b][:], AF.Sigmoid, [("ps", b), "evp"], ["sgm"], bias=evp[:, ot, 35:36])
            P.tt("dve", sgm[:], sgm[:], ygT[:, ot, cs], ALU.mult, ["sgm"] + gk, ["sgm"])
            P.tt("dve", ymT[:, ot, cs], sgm[:], sgaT[:, ot, cs], ALU.mult, ["sgm"], [("ymT", ot, c)])
    P.end_phase()


def even_layer(P, C, L, src, dst):
    j = L // 2
    P.push_scope()
    ymT = P.sb("ymT", [128, 8, S], BF16)
    P.push_scope()
    uT = P.sb("uT", [128, 4, S], BF16)
    sgaT = P.sb("sgaT", [128, 4, S], BF16)
    P.push_scope()
    sgbT = P.sb("sgbT", [128, 4, S], BF16)
    hcpad = P.sb("hcpad", [128, 4, S + 30], BF16)
    P.push_scope()
    hnT = P.sb("hnT", [128, 8, S], BF16)
    phase_prenorm(P, C, L, src, hnT)
    phase_even_proj(P, C, j, hnT, uT, sgaT, sgbT, hcpad)
    P.pop_scope()
    phase_conv(P, C, j, hcpad, sgbT, ymT)
    P.pop_scope()
    P.push_scope()
    BtR = P.sb("BtR", [128, 16, 128], BF16)
    BtI = P.sb("BtI", [128, 16, 128], BF16)
    CtR = P.sb("CtR", [128, 16, 128], BF16)
    CtI = P.sb("CtI", [128, 16, 128], BF16)
    sc2 = P.sb("sc2", [128, 16, 3], F32)
    phase_s5_setup(P, C, j, BtR, BtI, CtR, CtI, sc2)
    phase_s5(P, C, j, uT, sgaT, ymT, BtR, BtI, CtR, CtI, sc2)
    P.pop_scope()
    P.pop_scope()
    phase_out(P, C, L, ymT, src, dst)
    P.pop_scope()


W_SHAPES = {
    "pre_norm": [4, D], "post_norm": [4, D], "ple_gate": [4, D, D], "ple_proj": [4, 256, D],
    "ev_w_in": [2, D, 2560], "s5_w_glu": [2, 512, 512], "cv_w_pw": [2, 512, 512], "w_out": [4, D, D],
    "evp": [2, 128, 4, 40], "s5sc": [2, 128, 16, 3], "s5bT": [2, 2, 128, 16, 128], "s5cP": [2, 2, 128, 16, 128],
    "od_w_in": [2, D, 2744], "mla_w_uq": [2, 256, 768], "mla_w_ukv": [2, 128, 1024], "mlap": [2, 128, 3],
    "ropef": [128, 2],
    "nsa_ck_w1": [2, 2048, 256], "nsa_ck_w2": [2, 256, 64], "nsa_cv_w1": [2, 2048, 256], "nsa_cv_w2": [2, 256, 64],
    "nsapos2": [2, 128, 2, 16], "selc": [128, NT, 2, 32],
}
W_BF16 = {"qaug": [4, 8, S], "kaugc": [36, S], "kaugcmp": [4, 128], "addmask": [128, S], "ovl": [128, 64], "gsel": [32, 3, 8, 128]}


def build(n_layers=4, dbg=False, odd_kw=None):
    nc = bass.Bass("TRN2", target_bir_lowering=False)
    C = Ctx()
    odd_kw = odd_kw or {}

    def din(name, shape, dt=F32):
        return nc.dram_tensor(name, list(shape), dt, kind="ExternalInput").ap()
    C.x = din("x", [S, D])
    C.p = din("p", [4, S, 256])
    C.positions = din("positions", [1, S], I32)
    C.dbg = nc.dram_tensor("dbg", [8, 128, S], F32, kind="ExternalOutput").ap() if dbg else None
    for k, shp in W_SHAPES.items():
        setattr(C, k, din(k, shp))
    for k, shp in W_BF16.items():
        setattr(C, k, din(k, shp, BF16))
    out = nc.dram_tensor("out", [S, D], F32, kind="ExternalOutput").ap()
    hbuf = nc.dram_tensor("hbuf", [S, D], F32, kind="Internal").ap()
    P = Prog(nc)
    C.ps = [P.gps("ps%d" % i, [128, 512]) for i in range(8)]
    C.ident = P.gsb("ident", [128, 128], BF16)
    C.identf = P.gsb("identf", [128, 128], F32)
    C.onesf = P.gsb("onesf", [128, 128], F32)
    C.epsb = P.gsb("epsb", [128, 1], F32)
    C.tri = P.gsb("tri", [128, 128], BF16)
    C.wmask = P.gsb("wmask", [128, 128], BF16)
    C.ropef_sb = P.gsb("ropef_sb", [128, 2], F32)
    C.hpib = P.gsb("hpib", [128, 1], F32)
    P.begin_phase()
    io = P.sb("io", [128, 128], F32)
    P.op("pool", lambda e: e.iota(io[:], pattern=[[1, 128]], base=0, channel_multiplier=-1,
                                  allow_small_or_imprecise_dtypes=True), (), ["io"])
    P.op("dve", lambda e: e.tensor_single_scalar(out=C.identf[:], in_=io[:], scalar=0.0, op=ALU.is_equal), ["io"], ["identf"])
    P.copy("dve", C.ident[:], C.identf[:], ["identf"], ["ident"])
    P.memset("pool", C.onesf[:], 1.0, ["onesf"])
    P.op("dve", lambda e: e.tensor_single_scalar(out=C.tri[:], in_=io[:], scalar=0.0, op=ALU.is_ge), ["io"], ["tri"])
    P.op("dve", lambda e: e.tensor_single_scalar(out=C.wmask[:], in_=io[:], scalar=0.0, op=ALU.is_lt), ["io"], ["wmask"])
    P.dma(C.ropef_sb[:], C.ropef[:, :], writes=["ropef_sb"])
    P.memset("pool", C.epsb[:], EPS, ["epsb"])
    P.memset("pool", C.hpib[:], math.pi / 2, ["hpib"])
    P.end_phase()
    C.ropef_dram = C.ropef
    C.ropef = C.ropef_sb
    for L in range(n_layers):
        src = C.x if L == 0 else hbuf
        dst = out if L == n_layers - 1 else hbuf
        if L % 2 == 0:
            even_layer(P, C, L, src, dst)
        else:
            odd_layer(P, C, L, src, dst, **odd_kw)
    C.ropef = C.ropef_dram
    P.close()
    return nc, P


def nsa_constants():
    import ml_dtypes
    bf = ml_dtypes.bfloat16
    t = np.arange(S)
    a_t, b_t = (t // 64).astype(np.float32), (t % 64).astype(np.float32)
    slopes = np.array([2.0 ** (-(i + 1)) for i in range(8)], np.float32)
    qaug = np.zeros((4, 8, S), np.float32)
    for h in range(8):
        qaug[0, h] = -slopes[h] * 64.0 * a_t
        qaug[1, h] = -slopes[h] * b_t
        qaug[2, h] = slopes[h] * 64.0
        qaug[3, h] = slopes[h]
    kaugc = np.zeros((36, S), np.float32)
    kaugc[t // 64, t] = 1.0
    kaugc[32] = 1.0
    kaugc[33] = 1.0
    kaugc[34] = a_t
    kaugc[35] = b_t
    c = np.arange(128)
    pc = 16 * c + 31
    kaugcmp = np.stack([np.ones(128), np.ones(128), pc // 64, pc % 64]).astype(np.float32)
    addmask = np.where((t[None, :] >= pc[:, None]) & (c[:, None] <= 126), 0.0, -30000.0).astype(np.float32)
    sb = np.arange(32)
    cs_ = c[:, None] * 16
    overlap = ((cs_ < (sb[None] + 1) * 64) & (cs_ + 32 > sb[None] * 64) & (c[:, None] <= 126)).astype(np.float32)
    ovl = np.concatenate([overlap, np.ones((128, 32), np.float32)], axis=1)
    cur = t[:, None] // 64
    forced = (sb[None] == 0) | (sb[None] == cur) | (sb[None] == cur - 1)
    causal = sb[None] * 64 <= t[:, None]
    mul = (forced | causal).astype(np.float32)
    add = np.where(forced, 1e4, np.where(causal, 0.0, -1e4)).astype(np.float32)
    selc = np.stack([mul, add], axis=1).reshape(NT, 128, 2, 32).transpose(1, 0, 2, 3)
    gsel = np.zeros((32, 3, 8, 128), np.float32)
    for b in range(3):
        for h in range(8):
            gsel[b * 8 + h, b, h, (h % 2) * 64:(h % 2) * 64 + 64] = 1.0
    return {"qaug": qaug.astype(bf), "kaugc": kaugc.astype(bf), "kaugcmp": kaugcmp.astype(bf), "addmask": addmask.astype(bf),
            "ovl": ovl.astype(bf), "gsel": gsel.astype(bf), "selc": np.ascontiguousarray(selc)}


def host_layout(inputs):
    f = lambda k: np.asarray(inputs[k], np.float32)
    ne = 2
    evp = np.zeros((ne, 128, 4, 40), np.float32)
    wdw = f("cv_w_dw")
    evp[:, :, :, 0:31] = wdw.reshape(ne, 31, 4, 128).transpose(0, 3, 2, 1)
    for col, key in ((31, "cv_b_dw"), (32, "cv_ln_g"), (33, "cv_ln_b"), (34, "s5_d"), (35, "s5_b_glu")):
        evp[:, :, :, col] = f(key).reshape(ne, 4, 128).transpose(0, 2, 1)
    s5sc = np.zeros((ne, 128, 16, 3), np.float32)
    for i, key in enumerate(("s5_lam_re", "s5_lam_im")):
        a = f(key).reshape(ne, 16, 2, 64)
        s5sc[:, :, :, i] = a.transpose(0, 2, 3, 1).reshape(ne, 128, 16)
    ls = f("s5_log_step").reshape(ne, 16, 2)
    s5sc[:, :, :, 2] = np.repeat(ls.transpose(0, 2, 1)[:, :, None, :], 64, axis=2).reshape(ne, 128, 16)
    s5bT = np.zeros((ne, 2, 128, 16, 128), np.float32)
    s5cP = np.zeros((ne, 2, 128, 16, 128), np.float32)
    for ri, (kb, kc) in enumerate((("s5_b_re", "s5_c_re"), ("s5_b_im", "s5_c_im"))):
        b = f(kb)
        c = f(kc)
        for pr in range(16):
            for gl in range(2):
                g = 2 * pr + gl
                k0 = (pr % 4) * 32 + gl * 16
                s5bT[:, ri, k0:k0 + 16, pr, gl * 64:(gl + 1) * 64] = b[:, g].transpose(0, 2, 1)
                s5cP[:, ri, gl * 64:(gl + 1) * 64, pr, k0:k0 + 16] = c[:, g].transpose(0, 2, 1)
    w_out = np.stack([f("ev_w_out")[0], f("od_w_out")[0], f("ev_w_out")[1], f("od_w_out")[1]])
    no = 2
    mlap = np.zeros((no, 128, 3), np.float32)
    mlap[:, :, 0:2] = f("mla_q_norm").reshape(no, 2, 128).transpose(0, 2, 1)
    mlap[:, :, 2] = f("mla_kv_norm")
    ropef = np.zeros((128, 2), np.float32)
    fr = (10000.0 ** (-np.arange(16, dtype=np.float32) / 16)).astype(np.float32)
    ropef[64:96, 0] = np.tile(fr, 2)
    ropef[:, 1] = ropef[:, 0] / np.float32(TWO_PI)
    rep = {"evp": evp, "s5sc": s5sc, "s5bT": s5bT, "s5cP": s5cP, "w_out": w_out, "mlap": mlap, "ropef": ropef}
    rep.update(nsa_constants())
    pos2 = np.zeros((no, 128, 2, 16), np.float32)
    for kv, key in enumerate(("nsa_pos_k", "nsa_pos_v")):
        a = f(key).reshape(no, 16, 2, 64)
        pos2[:, :, kv, :] = a.transpose(0, 2, 3, 1).reshape(no, 128, 16)
    rep["nsapos2"] = pos2
    for k in ("nsa_ck_w1", "nsa_ck_w2", "nsa_cv_w1", "nsa_cv_w2"):
        rep[k] = np.ascontiguousarray(f(k))
    for k in ("pre_norm", "post_norm", "ple_gate", "ple_proj", "ev_w_in", "s5_w_glu", "cv_w_pw", "od_w_in", "mla_w_uq", "mla_w_ukv"):
        rep[k] = np.ascontiguousarray(f(k))
    return rep


def kernel(**inputs):
    n = 8
    rep = host_layout(inputs)
    x = np.asarray(inputs["x"], np.float32)
    p = np.asarray(inputs["p"], np.float32)
    nc, _ = build(4)
    in_maps = []
    for b in range(n):
        m = dict(rep)
        m["x"] = np.ascontiguousarray(x[b])
        m["p"] = np.ascontiguousarray(p[:, b])
        m["positions"] = np.ascontiguousarray(np.asarray(inputs["positions"])[b:b + 1]).astype(np.int32)
        in_maps.append(m)
    res = run_bass_kernel_spmd(nc, in_maps, core_ids=list(range(n)))
    return np.stack([r["out"] for r in res.results], axis=0).astype(np.float32)


OD = {"q": 0, "kv": 512, "gl": 1280, "gn": 1304, "cq": 1816, "ckv": 2072, "kr": 2200, "gm": 2232}


def load_slab(P, C, W, c0, n, tag, slot, eng="pool"):
    stg = P.sb("sl_st_%s" % tag, [128, 8, n], F32)
    dst = P.sb("sl_bf_%s" % tag, [128, 8, n], BF16)
    P.dma(stg[:], W[:, c0:c0 + n].rearrange("(k p) c -> p k c", p=128), writes=[("slst", tag)])
    P.copy(eng, dst[:], stg[:], [("slst", tag)], [("slab", tag)])
    return dst


def angle_tables(P, C, ang_in, fq, f, cosT, sinT, n, tag):
    ki = P.sb("ki_" + tag, [128, n], I32)
    rr = P.sb("rr_" + tag, [128, n], F32)
    ra = P.sb("ra_" + tag, [128, n], F32)
    P.ts("dve", ki[:], ang_in, fq, None, ALU.mult, None, [tag + "in"], [tag + "ki"])
    P.ts("pool", rr[:], ki[:], -TWO_PI, None, ALU.mult, None, [tag + "ki"], [tag + "rr"])
    P.ts("pool", ra[:], ang_in, f, None, ALU.mult, None, [tag + "in"], [tag + "ra"])
    P.tt("pool", rr[:], rr[:], ra[:], ALU.add, [tag + "rr", tag + "ra"], [tag + "rr"])
    P.ts("pool", rr[:], rr[:], math.pi, -math.pi, ALU.min, ALU.max, [tag + "rr"], [tag + "rr"])
    P.act(sinT, rr[:], AF.Sin, [tag + "rr"], [tag + "sin"])
    P.act(ra[:], rr[:], AF.Abs, [tag + "rr"], [tag + "ra"])
    P.act(cosT, ra[:], AF.Sin, [tag + "ra"], [tag + "cos"], scale=-1.0, bias=C.hpib[:])


def softmax_pv_finish(P, C, ob, par, dst_rows, rec, tmpf, extra_mul, ekey, okey, wkey, first, clamp=False, gate_ps=None, gkey=None):
    lo = slice(par * 64, par * 64 + 64)
    hi = slice((1 - par) * 64, (1 - par) * 64 + 64)
    if clamp:
        P.ts("dve", rec[lo, :], ob[hi, :], 1e-18, None, ALU.max, None, [okey], [("rec", par)])
        P.act(rec[lo, :], rec[lo, :], AF.Ln, [("rec", par)], [("rec", par)])
    else:
        P.act(rec[lo, :], ob[hi, :], AF.Ln, [okey], [("rec", par)])
    P.act(rec[lo, :], rec[lo, :], AF.Exp, [("rec", par)], [("rec", par)], scale=-1.0)
    P.tt("dve", tmpf[lo, :], ob[lo, :], rec[lo, :], ALU.mult, [okey, ("rec", par)], [("tmpf", par)])
    if gate_ps is not None:
        P.tt("dve", tmpf[lo, :], tmpf[lo, :], gate_ps[lo, :], ALU.mult, [("tmpf", par), gkey], [("tmpf", par)])
    if not first:
        P.tt("dve", tmpf[lo, :], tmpf[lo, :], dst_rows, ALU.add, [("tmpf", par), wkey], [("tmpf", par)])
    if extra_mul is not None:
        P.tt("dve", dst_rows, tmpf[lo, :], extra_mul, ALU.mult, [("tmpf", par), ekey], [wkey])
    else:
        P.copy("act", dst_rows, tmpf[lo, :], [("tmpf", par)], [wkey])


def run_attention(P, C, groups, PT, depth=3, mask_eng="dve", sbanks=(0, 1), tick=None):
    flat = [(g, i) for g, (steps, fin) in enumerate(groups) for i in range(len(steps))]
    issued = 0
    for idx in range(len(flat)):
        while issued < min(len(flat), idx + depth):
            g2, i2 = flat[issued]
            st2 = groups[g2][0][i2]
            bank = sbanks[issued % len(sbanks)]
            P.mm(C.ps[bank][:, st2["c0"]:st2["c1"]], st2["lhsK"], st2["rhsQ"], True, True, st2.get("rk", []), [("ps", bank)])
            issued += 1
        g, i = flat[idx]
        steps, fin = groups[g]
        st = steps[i]
        bank = sbanks[idx % len(sbanks)]
        c0, c1 = st["c0"], st["c1"]
        pt = PT[idx % len(PT)]
        pk = ("PT", idx % len(PT))
        if st.get("scale") is not None:
            P.act(pt[:, c0:c1], C.ps[bank][:, c0:c1], AF.Exp, [("ps", bank)], [pk], scale=st["scale"])
        else:
            P.act(pt[:, c0:c1], C.ps[bank][:, c0:c1], AF.Exp, [("ps", bank)], [pk])
        for (a, b, m) in st["masks"]:
            P.tt(mask_eng, pt[:, a:b], pt[:, a:b], m, ALU.mult, [pk], [pk])
        P.mm(C.ps[st["ob"]][:, c0:c1], st["lhsV"], pt[:, c0:c1], i == 0, i == len(steps) - 1, [pk] + st.get("rv", []), [("ps", st["ob"])],
             skip=True)
        if i == len(steps) - 1:
            fin()
        if tick is not None:
            tick(idx)


def phase_mla_prep(P, C, L, hnT, cosT, sinT, cqnT, ckvnT, kropeT):
    j = L // 2
    W = C.od_w_in[j]
    P.begin_phase()
    mlap = P.sb("mlap", [128, 3], F32)
    P.dma(mlap[:], C.mlap[j], writes=["mlap"])
    posi = [P.sb("posi%d" % i, [128, 512], I32) for i in range(2)]
    posf = [P.sb("posf%d" % i, [128, 512], F32) for i in range(2)]
    ki = P.sb("rki", [128, 512], I32)
    rr = P.sb("rrr", [128, 512], F32)
    ra = P.sb("rra", [128, 512], F32)
    for c in range(4):
        cs = slice(c * 512, (c + 1) * 512)
        s = c % 2
        P.dma(posi[s][:], C.positions[0:1, cs].partition_broadcast(128), writes=[("posi", s)])
        P.copy("dve", posf[s][:], posi[s][:], [("posi", s)], [("posf", s)])
        P.ts("dve", ki[:], posf[s][:], C.ropef[:, 1:2], None, ALU.mult, None, [("posf", s)], ["ki"])
        P.ts("pool", rr[:], ki[:], -TWO_PI, None, ALU.mult, None, ["ki"], ["rr"])
        P.ts("pool", ra[:], posf[s][:], C.ropef[:, 0:1], None, ALU.mult, None, [("posf", s)], ["ra"])
        P.tt("pool", rr[:], rr[:], ra[:], ALU.add, ["rr", "ra"], ["rr"])
        P.ts("pool", rr[:], rr[:], math.pi, -math.pi, ALU.min, ALU.max, ["rr"], ["rr"])
        P.act(sinT[:, cs], rr[:], AF.Sin, ["rr"], [("sinT", c)])
        P.act(ra[:], rr[:], AF.Abs, ["rr"], ["ra"])
        P.act(cosT[:, cs], ra[:], AF.Sin, ["ra"], [("cosT", c)], scale=-1.0, bias=C.hpib[:])
    wcq = load_slab(P, C, W, OD["cq"], 256, "cq", 0)
    wckv = load_slab(P, C, W, OD["ckv"], 128, "ckv", 0, eng="dve")
    wkr = load_slab(P, C, W, OD["kr"], 32, "kr", 0, eng="dve")
    wkrA = P.sb("wkrA", [128, 8, 96], BF16)
    wkrR = P.sb("wkrR", [128, 8, 96], BF16)
    P.memset("pool", wkrA[:], 0.0, ["wkrA"])
    P.memset("pool", wkrR[:], 0.0, ["wkrR"])
    P.copy("dve", wkrA[:, :, 64:96], wkr[:], [("slab", "kr"), "wkrA"], ["wkrA"])
    P.ts("dve", wkrR[:, :, 64:80], wkr[:, :, 16:32], -1.0, None, ALU.mult, None, [("slab", "kr"), "wkrR"], ["wkrR"])
    P.copy("dve", wkrR[:, :, 80:96], wkr[:, :, 0:16], [("slab", "kr"), "wkrR"], ["wkrR"])
    cqf = [P.sb("cqf%d" % i, [128, 512], F32) for i in range(3)]
    sq = [P.sb("sqm%d" % i, [128, 512], F32) for i in range(3)]
    rs = P.sb("rsm", [128, 512], F32)
    m1 = P.sb("m1", [128, 512], F32)
    m2 = P.sb("m2", [128, 512], F32)
    for c in range(4):
        cs = slice(c * 512, (c + 1) * 512)
        for t in range(3):
            for kf in range(8):
                lhs = wcq[:, kf, t * 128:(t + 1) * 128] if t < 2 else wckv[:, kf, :]
                P.mm(C.ps[4 + t][:], lhs, hnT[:, kf, cs], kf == 0, kf == 7, [("slab", "cq"), ("slab", "ckv")], [("ps", 4 + t)])
            P.copy("act", cqf[t][:], C.ps[4 + t][:], [("ps", 4 + t)], [("cqf", t)])
            P.act(sq[t][:], C.ps[4 + t][:], AF.Square, [("ps", 4 + t)], [("sq", t)])
        for tiles, nfeat in (((0, 1), 256), ((2,), 128)):
            for i, t in enumerate(tiles):
                P.mm(C.ps[7][:], C.onesf[:], sq[t][:], i == 0, i == len(tiles) - 1, [("sq", t)], [("ps", 7)])
            P.act(rs[:], C.ps[7][:], AF.Ln, [("ps", 7)], ["rs"], scale=1.0 / nfeat, bias=C.epsb[:])
            P.act(rs[:], rs[:], AF.Exp, ["rs"], ["rs"], scale=-0.5)
            for t in tiles:
                dst = cqnT[:, t, cs] if t < 2 else ckvnT[:, cs]
                P.stt("dve", dst, cqf[t][:], mlap[:, t:t + 1], rs[:], ALU.mult, ALU.mult, [("cqf", t), "rs", "mlap"], [("cn", t, c)])
        for kf in range(8):
            P.mm(C.ps[0][:96, :], wkrA[:, kf, :], hnT[:, kf, cs], kf == 0, kf == 7, ["wkrA"], [("ps", 0)])
        for kf in range(8):
            P.mm(C.ps[1][:96, :], wkrR[:, kf, :], hnT[:, kf, cs], kf == 0, kf == 7, ["wkrR"], [("ps", 1)])
        P.tt("dve", m1[64:96, :], C.ps[0][64:96, :], cosT[64:96, cs], ALU.mult, [("ps", 0), ("cosT", c)], ["m1"])
        P.tt("dve", m2[64:96, :], C.ps[1][64:96, :], sinT[64:96, cs], ALU.mult, [("ps", 1), ("sinT", c)], ["m2"])
        P.tt("pool", kropeT[64:96, cs], m1[64:96, :], m2[64:96, :], ALU.add, ["m1", "m2"], [("krope", c)])
    P.end_phase()


def phase_mla(P, C, L, hnT, ymT, cosT, sinT, cqnT, ckvnT, kropeT):
    j = L // 2
    W = C.od_w_in[j]
    SC = 96 ** -0.5
    P.begin_phase()
    wuq = P.sb("wuq", [128, 2, 768], BF16)
    load_w_bf16(P, C, wuq, C.mla_w_uq[j], 2, 768, "wuq")
    wuq_keys = [("w", "wuq", k) for k in range(2)]
    wuqR = P.sb("wuqR", [128, 2, 8, 96], BF16)
    P.memset("pool", wuqR[:], 0.0, ["wuqR"])
    wuq4 = wuq[:].rearrange("p k (h c) -> p k h c", h=8)
    for t in range(2):
        P.ts("dve", wuqR[:, t, :, 64:80], wuq4[:, t, :, 80:96], -1.0, None, ALU.mult, None, wuq_keys + ["wuqR"], ["wuqR"])
        P.copy("dve", wuqR[:, t, :, 80:96], wuq4[:, t, :, 64:80], wuq_keys + ["wuqR"], ["wuqR"])
    wukv = P.sb("wukv", [128, 1, 1024], BF16)
    load_w_bf16(P, C, wukv, C.mla_w_ukv[j], 1, 1024, "wukv")
    wukv_k = [("w", "wukv", 0)]
    m1s = [P.sb("m1_%d" % i, [128, 512], F32) for i in range(2)]
    m2s = [P.sb("m2_%d" % i, [128, 512], F32) for i in range(2)]
    QT = [P.sb("QT%d" % i, [128, 2, S], BF16) for i in range(2)]
    KT = [P.sb("KT%d" % i, [128, 2, S], BF16) for i in range(2)]
    VA = [P.sb("VA%d" % i, [128, NT, 2, 128], BF16) for i in range(2)]
    sgm = [P.sb("sgmT%d" % i, [128, S], BF16) for i in range(2)]
    PT = [P.sb("PT%d" % i, [128, 512], BF16) for i in range(4)]
    rec = P.sb("rec", [128, 512], F32)
    tmpf = P.sb("tmpf", [128, 512], F32)
    gst = [P.sb("gst%d" % i, [128, 8, 128], F32) for i in range(2)]
    gbf = [P.sb("gbf%d" % i, [128, 8, 128], BF16) for i in range(2)]
    wukv3 = wukv[:, 0, :].rearrange("p (h c) -> p h c", h=8)
    for i in range(2):
        P.memset("dve" if i == 0 else "pool", VA[i][:], 1.0, [("VA", i)])
    cnt = {"ps": 0, "m": 0}

    def pbank():
        b = (4, 5, 7)[cnt["ps"] % 3]
        cnt["ps"] += 1
        return b

    def proj_units(hp, sl):
        for par in range(2):
            h = 2 * hp + par
            for c in range(4):
                cs = slice(c * 512, (c + 1) * 512)
                b = pbank()
                P.mm(C.ps[b][:64, :], wukv[:, 0, h * 128:h * 128 + 64], ckvnT[:, cs], True, True, wukv_k, [("ps", b)])
                P.copy("dve", KT[sl][0:64, par, cs], C.ps[b][:64, :], [("ps", b)], [("KT", sl, par)])
                bA = pbank()
                bR = pbank()
                for t in range(2):
                    P.mm(C.ps[bA][:96, :], wuq[:, t, h * 96:(h + 1) * 96], cqnT[:, t, cs], t == 0, t == 1, wuq_keys, [("ps", bA)])
                for t in range(2):
                    P.mm(C.ps[bR][:96, :], wuqR[:, t, h, :], cqnT[:, t, cs], t == 0, t == 1, ["wuqR"], [("ps", bR)])
                P.copy("dve", QT[sl][0:64, par, cs], C.ps[bA][0:64, :], [("ps", bA)], [("QT", sl, par)])
                mi = cnt["m"] % 2
                cnt["m"] += 1
                m1, m2 = m1s[mi], m2s[mi]
                P.tt("dve", m1[64:96, :], C.ps[bA][64:96, :], cosT[64:96, cs], ALU.mult, [("ps", bA)], [("m1", mi)])
                P.tt("dve", m2[64:96, :], C.ps[bR][64:96, :], sinT[64:96, cs], ALU.mult, [("ps", bR)], [("m2", mi)])
                P.tt("dve", QT[sl][64:96, par, cs], m1[64:96, :], m2[64:96, :], ALU.add, [("m1", mi), ("m2", mi)], [("QT", sl, par)])
                yield
            P.copy("dve", KT[sl][64:96, par, :], kropeT[64:96, :], [], [("KT", sl, par)])
        for kt in range(NT):
            b = pbank()
            P.mm(C.ps[b][:, 0:128], ckvnT[:, kt * 128:(kt + 1) * 128], wukv3[:, 2 * hp:2 * hp + 2, 64:128], True, True,
                 wukv_k, [("ps", b)])
            P.copy("dve", VA[sl][:, kt, 0, 0:64], C.ps[b][:, 0:64], [("ps", b), ("VA", sl)], [("VA", sl)])
            P.copy("dve", VA[sl][:, kt, 1, 64:128], C.ps[b][:, 64:128], [("ps", b), ("VA", sl)], [("VA", sl)])
            if kt % 2 == 1:
                yield
        c0 = OD["gm"] + hp * 128
        P.dma(gst[sl][:], W[:, c0:c0 + 128].rearrange("(k p) c -> p k c", p=128), writes=[("gst", sl)])
        P.copy("pool", gbf[sl][:], gst[sl][:], [("gst", sl)], [("gbf", sl)])
        for c in range(4):
            cs = slice(c * 512, (c + 1) * 512)
            b = pbank()
            for kf in range(8):
                P.mm(C.ps[b][:], gbf[sl][:, kf, :], hnT[:, kf, cs], kf == 0, kf == 7, [("gbf", sl)], [("ps", b)])
            P.act(sgm[sl][:, cs], C.ps[b][:], AF.Silu, [("ps", b)], [("sgm", sl, c)])
            yield

    for _ in proj_units(0, 0):
        pass
    for hp in range(4):
        sl = hp % 2
        tile_i = 4 + hp
        gen = proj_units(hp + 1, 1 - sl) if hp < 3 else None
        groups = []
        for par in range(2):
            for qc in range(4):
                ob = 2 + (qc % 2)
                nk = 4 * qc + 4
                steps = []
                for kt in range(nk):
                    o = max(0, kt - 4 * qc)
                    q0 = qc * 512 + o * 128
                    ncol = 512 - o * 128
                    steps.append(dict(c0=o * 128, c1=512, lhsK=KT[sl][0:96, par, kt * 128:(kt + 1) * 128], rhsQ=QT[sl][0:96, par, q0:q0 + ncol],
                                      rk=[("KT", sl, par), ("QT", sl, par)], scale=SC,
                                      masks=[(o * 128, (o + 1) * 128, C.tri[:])] if kt >= 4 * qc else [],
                                      lhsV=VA[sl][:, kt, par, :], ob=ob, rv=[("VA", sl)]))

                def fin(par=par, qc=qc, ob=ob, tile_i=tile_i, sl=sl):
                    lo = slice(par * 64, par * 64 + 64)
                    qs = slice(qc * 512, (qc + 1) * 512)
                    softmax_pv_finish(P, C, C.ps[ob], par, ymT[lo, tile_i, qs], rec, tmpf, sgm[sl][lo, qs], ("sgm", sl, qc),
                                      ("ps", ob), ("ymT", tile_i, par, qc), True)
                groups.append((steps, fin))

        def tick(idx, gen=gen):
            if gen is not None and idx % 2 == 1:
                next(gen, None)
        run_attention(P, C, groups, PT, sbanks=(0, 1, 6), tick=tick, mask_eng="pool")
        if gen is not None:
            for _ in gen:
                pass
    P.end_phase()


def odd_layer(P, C, L, src, dst, do_nsa=True, do_mla=True):
    P.push_scope()
    ymT = P.sb("ymT", [128, 8, S], BF16)
    hnT = P.sb("hnT", [128, 8, S], BF16)
    phase_prenorm(P, C, L, src, hnT)
    if not (do_nsa and do_mla):
        P.begin_phase()
        P.memset("pool", ymT[:], 0.0, ["ymT"])
        P.end_phase()
    if do_nsa:
        nsa_mixer(P, C, L, hnT, ymT)
    if do_mla:
        P.push_scope()
        cosT = P.sb("cosT", [128, S], F32)
        sinT = P.sb("sinT", [128, S], F32)
        cqnT = P.sb("cqnT", [128, 2, S], BF16)
        ckvnT = P.sb("ckvnT", [128, S], BF16)
        kropeT = P.sb("kropeT", [128, S], BF16)
        phase_mla_prep(P, C, L, hnT, cosT, sinT, cqnT, ckvnT, kropeT)
        phase_mla(P, C, L, hnT, ymT, cosT, sinT, cqnT, ckvnT, kropeT)
        P.pop_scope()
    if C.dbg is not None:
        P.begin_phase()
        cv = [P.sb("dbgc%d" % i, [128, S], F32) for i in range(2)]
        for t in range(8):
            P.copy("dve", cv[t % 2][:], ymT[:, t, :], [], [("cv", t % 2)])
            P.dma(C.dbg[t], cv[t % 2][:], reads=[("cv", t % 2)], writes=[("dbg", t % 2)])
        P.end_phase()
    phase_out(P, C, L, ymT, src, dst)
    P.pop_scope()


class SlabLoader:
    def __init__(self, P, tag):
        self.P = P
        self.tag = tag
        self.st = [P.sb("sls_%s%d" % (tag, i), [128, 8, 128], F32) for i in range(2)]
        self.bf = [P.sb("slb_%s%d" % (tag, i), [128, 8, 128], BF16) for i in range(2)]
        self.n = 0

    def load(self, W, c0, n, eng="pool"):
        s = self.n % 2
        self.n += 1
        P = self.P
        P.dma(self.st[s][:, :, 0:n], W[:, c0:c0 + n].rearrange("(k p) c -> p k c", p=128), writes=[("sls", self.tag, s)])
        P.copy(eng, self.bf[s][:, :, 0:n], self.st[s][:, :, 0:n], [("sls", self.tag, s)], [("slb", self.tag, s)])
        return self.bf[s], ("slb", self.tag, s)


def proj_fm(P, C, slab, skey, n, hnT, consume, banks=(6, 7)):
    for c in range(4):
        cs = slice(c * 512, (c + 1) * 512)
        b = banks[c % len(banks)]
        for kf in range(8):
            P.mm(C.ps[b][0:n, :], slab[:, kf, 0:n], hnT[:, kf, cs], kf == 0, kf == 7, [skey], [("ps", b)])
        consume(c, cs, C.ps[b], ("ps", b))


def gelu_tanh(P, x_ps, xkey, out, okey, y2, t1, sg, tag):
    P.act(y2, x_ps, AF.Square, [xkey], [tag + "y2"])
    P.ts("dve", y2, y2, 0.044715, 1.0, ALU.mult, ALU.add, [tag + "y2"], [tag + "y2"])
    P.tt("dve", t1, y2, x_ps, ALU.mult, [tag + "y2", xkey], [tag + "t1"])
    P.act(sg, t1, AF.Sigmoid, [tag + "t1"], [tag + "sg"], scale=1.5957691216057308)
    P.tt("dve", out, sg, x_ps, ALU.mult, [tag + "sg", xkey], [okey])


def nsa_mixer(P, C, L, hnT, ymT):
    j = L // 2
    W = C.od_w_in[j]
    P.push_scope()
    QA = P.sb("QaugT", [128, 8, S], BF16)
    KS = P.sb("KselA", [128, 2, S], BF16)
    KW = P.sb("KwinA", [128, 2, S], BF16)
    SgT = P.sb("SgT", [32, S], BF16)
    KcA = P.sb("KcA", [128, 2, 128], BF16)
    VcA = P.sb("VcA", [128, 2, 2, 128], BF16)
    gsel = P.sb("gsel", [32, 3, 8, 128], BF16)

    P.begin_phase()
    SL = SlabLoader(P, "a")
    P.memset("pool", QA[64:96, :, :], 0.0, ["QAmask"])
    P.dma(QA[96:100, :, :], C.qaug[:, :, :], writes=["QAaug"])
    P.dma(gsel[:], C.gsel[:, :, :, :], writes=["gsel"])
    for y in range(2):
        P.dma(KS[64:96, y, :], C.kaugc[0:32, :], writes=[("KSe", y)])
        P.dma(KS[96:100, y, :], C.kaugc[32:36, :], writes=[("KSa", y)])
        P.dma(KW[96:100, y, :], C.kaugc[32:36, :], writes=[("KWa", y)])
    P.memset("pool", KW[64:96, :, :], 0.0, ["KWz"])
    P.memset("pool", SgT[:], 0.0, ["SgT0"])
    import os
    part = int(os.environ.get("NSA_PART", "9"))
    for hp in range(4 if part >= 1 else 0):
        slab, sk = SL.load(W, OD["q"] + hp * 128, 128)

        def cons_q(c, cs, ps, pk, hp=hp):
            for par in range(2):
                P.act(QA[0:64, 2 * hp + par, cs], ps[par * 64:(par + 1) * 64, :], AF.Copy, [pk], [("QA", 2 * hp + par, c)], scale=0.125)
        proj_fm(P, C, slab, sk, 128, hnT, cons_q)
    for (i, dst, nm) in (((2, KS, "KS"), (4, KW, "KW")) if part >= 2 else ()):
        slab, sk = SL.load(W, OD["kv"] + i * 128, 128)

        def cons_k(c, cs, ps, pk, dst=dst, nm=nm):
            for y in range(2):
                P.copy("act" if y == 0 else "dve", dst[0:64, y, cs], ps[y * 64:(y + 1) * 64, :], [pk], [(nm, y, c)])
        proj_fm(P, C, slab, sk, 128, hnT, cons_k)
    def cons_g(c, cs, ps, pk):
        P.act(SgT[0:32, cs], ps[0:32, :], AF.Sigmoid, [pk, "SgT0"], [("SgT", c)])
    if part >= 3:
        slab, sk = SL.load(W, OD["gl"], 32)
        proj_fm(P, C, slab, sk, 32, hnT, cons_g)
    P.end_phase()
    stop = os.environ.get("NSA_STOP", "")
    if stop == "a":
        P.pop_scope()
        return

    P.begin_phase()
    SL = SlabLoader(P, "b")
    P.memset("pool", VcA[:], 1.0, ["VcA"])
    P.memset("pool", KcA[64:96, :, :], 0.0, ["KcAz"])
    for y in range(2):
        P.dma(KcA[96:100, y, :], C.kaugcmp[:, :], writes=[("KcAa", y)])
    K2 = [P.sb("K2_%d" % y, [128, S], BF16) for y in range(2)]
    G = P.sb("Gc", [128, 16, 128], BF16)
    GT = P.sb("GTc", [128, 2, 128], BF16)
    w1st = P.sb("w1st", [128, 16, 256], F32)
    w1b = P.sb("w1b", [128, 16, 256], BF16)
    w2st = P.sb("w2st", [128, 2, 64], F32)
    w2b = P.sb("w2b", [128, 2, 64], BF16)
    pos2 = P.sb("pos2", [128, 2, 16], F32)
    P.dma(pos2[:], C.nsapos2[j], writes=["pos2"])
    y2 = P.sb("gy2", [128, 128], F32)
    t1 = P.sb("gt1", [128, 128], F32)
    sg = P.sb("gsg", [128, 128], F32)
    P.memset("pool", GT[:], 0.0, ["GT0"])
    for kv in range(2):
        w1 = (C.nsa_ck_w1 if kv == 0 else C.nsa_cv_w1)[j]
        w2 = (C.nsa_ck_w2 if kv == 0 else C.nsa_cv_w2)[j]
        P.dma(w1st[:], w1.rearrange("(j p) c -> p j c", p=128), writes=["w1st"])
        P.copy("pool", w1b[:], w1st[:], ["w1st"], ["w1b"])
        P.dma(w2st[:], w2.rearrange("(k p) c -> p k c", p=128), writes=["w2st"])
        P.copy("dve", w2b[:], w2st[:], ["w2st"], ["w2b"])
        slab, sk = SL.load(W, OD["kv"] + kv * 128, 128, eng="dve")
        for y in range(2):
            P.memset("pool", K2[y][64:128, S - 1:S], 0.0, [("K2z", y)])

        def cons_c(c, cs, ps, pk):
            for y in range(2):
                src = ps[y * 64:(y + 1) * 64, :]
                eng = "act" if y == 0 else "dve"
                P.copy(eng, K2[y][0:64, cs], src, [pk], [("K2a", y, c)])
                if c == 0:
                    P.copy(eng, K2[y][64:128, 0:511], ps[y * 64:(y + 1) * 64, 1:512], [pk], [("K2b", y, c)])
                else:
                    P.copy(eng, K2[y][64:128, c * 512 - 1:c * 512 + 511], src, [pk, ("K2z", y)], [("K2b", y, c)])
        if part >= 1:
            proj_fm(P, C, slab, sk, 128, hnT, cons_c)
        k2keys = [[("K2a", y, c) for c in range(4)] + [("K2b", y, c) for c in range(4)] + [("K2z", y)] for y in range(2)]
        for y in range(2 if part >= 2 else 0):
            for jj in range(16):
                P.ts("dve" if jj % 2 == 0 else "pool", G[:, jj, 0:127], K2[y][:, 2 * jj:2 * jj + 2017:16], pos2[:, kv, jj:jj + 1], None, ALU.add, None,
                     k2keys[y] + ["pos2"], [("G", jj)])
            if part < 3:
                continue
            for ht in range(2):
                b = 4 + ht
                for jj in range(16):
                    P.mm(C.ps[b][:, 0:127], w1b[:, jj, ht * 128:(ht + 1) * 128], G[:, jj, 0:127], jj == 0, jj == 15, [("G", jj), "w1b"], [("ps", b)])
                gelu_tanh(P, C.ps[b][:, 0:127], ("ps", b), GT[:, ht, 0:127], ("GT", ht), y2[:, 0:127], t1[:, 0:127], sg[:, 0:127], "g")
            if part < 4:
                continue
            if kv == 0:
                for ht in range(2):
                    P.mm(C.ps[2][0:64, 0:128], w2b[:, ht, :], GT[:, ht, :], ht == 0, ht == 1, [("GT", ht), "GT0", "w2b"], [("ps", 2)])
                P.copy("act", KcA[0:64, y, :], C.ps[2][0:64, 0:128], [("ps", 2)], [("KcA", y)])
            elif part >= 5:
                for ht in range(2):
                    P.mm(C.ps[3][:, 0:64], GT[:, ht, :], w2b[:, ht, :], ht == 0, ht == 1, [("GT", ht), "GT0", "w2b"], [("ps", 3)])
                if part >= 6:
                    P.copy("dve", VcA[:, y, 0, 0:64], C.ps[3][:, 0:64], [("ps", 3), "VcA"], [("VcAw", y, 0)])
                    P.copy("dve", VcA[:, y, 1, 64:128], C.ps[3][:, 0:64], [("ps", 3), "VcA"], [("VcAw", y, 1)])
    P.end_phase()
    if stop == "b":
        P.pop_scope()
        return

    P.begin_phase()
    addm = P.sb("addm", [128, S], BF16)
    P.dma(addm[:], C.addmask[:, :], writes=["addm"])
    ovl = P.sb("ovl", [128, 64], BF16)
    P.dma(ovl[:], C.ovl[:, :], writes=["ovl"])
    selc = P.sb("selc", [128, NT, 2, 32], F32)
    P.dma(selc[:], C.selc[:, :, :, :], writes=["selc"])
    pslc = P.sb("pslcT", [32, 2, S], F32)
    sm = [P.sb("smc%d" % i, [128, 512], F32) for i in range(2)]
    PT = [P.sb("PTc%d" % i, [128, 512], BF16) for i in range(2)]
    rec = P.sb("rec", [128, 512], F32)
    tmpf = P.sb("tmpf", [128, 512], F32)
    rec2 = P.sb("rec2", [32, 512], F32)
    tmp2 = P.sb("tmp2", [32, 512], F32)
    items = [(h, qc) for h in range(8) for qc in range(4)]

    def cmpA(i):
        h, qc = items[i]
        y = h // 4
        qs = slice(qc * 512, (qc + 1) * 512)
        s_ = i % 2
        P.mm(C.ps[s_][:], KcA[0:100, y, :], QA[0:100, h, qs], True, True, [], [("ps", s_)])
        P.tt("dve", sm[s_][:], C.ps[s_][:], addm[:, qs], ALU.add, [("ps", s_), "addm"], [("sm", s_)])
        P.act(PT[s_][:], sm[s_][:], AF.Exp, [("sm", s_)], [("PT", s_)])

    def cmpB(i):
        h, qc = items[i]
        y, par, tile_i, hh = h // 4, h % 2, h // 2, h % 4
        lo = slice(par * 64, par * 64 + 64)
        qs = slice(qc * 512, (qc + 1) * 512)
        s_ = i % 2
        ob = 2 + s_
        P.mm(C.ps[ob][:], VcA[:, y, par, :], PT[s_][:], True, True, [("PT", s_)], [("ps", ob)])
        P.mm(C.ps[5][0:64, :], ovl[:], PT[s_][:], True, True, [("PT", s_), "ovl"], [("ps", 5)])
        P.mm(C.ps[4][:], gsel[0:32, 0, h, :], SgT[0:32, qs], True, True, ["gsel"], [("ps", 4)])
        softmax_pv_finish(P, C, C.ps[ob], par, ymT[lo, tile_i, qs], rec, tmpf, None, None, ("ps", ob), ("ymT", tile_i, par, qc), True,
                          clamp=True, gate_ps=C.ps[4], gkey=("ps", 4))
        P.ts("dve", rec2[:], C.ps[5][32:64, :], 1e-18, None, ALU.max, None, [("ps", 5)], ["rec2"])
        P.act(rec2[:], rec2[:], AF.Ln, ["rec2"], ["rec2"])
        P.act(rec2[:], rec2[:], AF.Exp, ["rec2"], ["rec2"], scale=-1.0)
        if hh == 0:
            P.tt("dve", pslc[:, y, qs], C.ps[5][0:32, :], rec2[:], ALU.mult, [("ps", 5), "rec2"], [("pslc", y, qc)])
        else:
            P.tt("dve", tmp2[:], C.ps[5][0:32, :], rec2[:], ALU.mult, [("ps", 5), "rec2"], ["tmp2"])
            P.tt("dve", pslc[:, y, qs], pslc[:, y, qs], tmp2[:], ALU.add, ["tmp2", ("pslc", y, qc)], [("pslc", y, qc)])

    cmpA(0)
    for i in range(len(items)):
        if i + 1 < len(items):
            cmpA(i + 1)
        cmpB(i)
    sc = [P.sb("scs%d" % i, [128, 32], F32) for i in range(2)]
    m8 = [P.sb("m8s%d" % i, [128, 8], F32) for i in range(2)]
    ng = [P.sb("ngs%d" % i, [128, 32], F32) for i in range(2)]
    sitems = [(y, qt) for y in range(2) for qt in range(NT)]

    def selA(i):
        y, qt = sitems[i]
        s_ = i % 2
        ts_ = slice(qt * 128, (qt + 1) * 128)
        P.tr(C.ps[6 + s_][:, 0:32], pslc[:, y, ts_], C.identf[0:32, 0:32], [("pslc", y, qt // 4)], [("ps", 6 + s_)])
        P.tt("dve", sc[s_][:], C.ps[6 + s_][:, 0:32], selc[:, qt, 0, :], ALU.mult, [("ps", 6 + s_), "selc"], [("sc", s_)])
        P.tt("dve", sc[s_][:], sc[s_][:], selc[:, qt, 1, :], ALU.add, [("sc", s_), "selc"], [("sc", s_)])
        P.op("dve", lambda e, s_=s_: e.max(out=m8[s_][:], in_=sc[s_][:]), [("sc", s_)], [("m8", s_)])
        P.ts("dve", ng[s_][:], sc[s_][:], m8[s_][:, 7:8], 30000.0, ALU.is_ge, ALU.mult, [("sc", s_), ("m8", s_)], [("ng", s_)])
        P.ts("dve", ng[s_][:], ng[s_][:], -30000.0, None, ALU.add, None, [("ng", s_)], [("ng", s_)])

    def selB(i):
        y, qt = sitems[i]
        s_ = i % 2
        ts_ = slice(qt * 128, (qt + 1) * 128)
        P.tr(C.ps[4 + s_][0:32, 0:128], ng[s_][:], C.identf[:], [("ng", s_)], [("ps", 4 + s_)])
        for hh in range(4):
            eng = "act" if s_ == 0 else "dve"
            P.copy(eng, QA[64:96, 4 * y + hh, ts_], C.ps[4 + s_][0:32, 0:128], [("ps", 4 + s_), "QAmask"], [("QAm", 4 * y + hh, qt)])

    selA(0)
    for i in range(len(sitems)):
        if i + 1 < len(sitems):
            selA(i + 1)
        selB(i)
    P.end_phase()
    if stop == "c":
        P.pop_scope()
        return

    for br in (1, 2):
        P.begin_phase()
        SL = SlabLoader(P, "v%d" % br)
        VA = P.sb("VAn", [128, NT, 2, 2, 128], BF16)
        P.memset("dve", VA[:], 1.0, ["VA"])
        slab, sk = SL.load(W, OD["kv"] + (3 if br == 1 else 5) * 128, 128)
        for kt in range(NT):
            b = 6 + (kt % 2)
            for kf in range(8):
                P.mm(C.ps[b][:, 0:128], hnT[:, kf, kt * 128:(kt + 1) * 128], slab[:, kf, :], kf == 0, kf == 7, [sk], [("ps", b)])
            pv = C.ps[b][:, 0:128].rearrange("p (y c) -> p y c", y=2)
            eng = "act" if kt % 2 == 0 else "dve"
            P.copy(eng, VA[:, kt, :, 0, 0:64], pv, [("ps", b), "VA"], [("VAw", kt, 0)])
            P.copy(eng, VA[:, kt, :, 1, 64:128], pv, [("ps", b), "VA"], [("VAw", kt, 1)])
        sgn = None
        if br == 2:
            sgn = P.sb("sgn", [128, 4, S], BF16)
            for hp in range(4):
                slab2, sk2 = SL.load(W, OD["gn"] + hp * 128, 128, eng="dve")

                def cons_gn(c, cs, ps, pk, hp=hp):
                    P.act(sgn[:, hp, cs], ps[:], AF.Silu, [pk], [("sgn", hp, c)])
                proj_fm(P, C, slab2, sk2, 128, hnT, cons_gn)
        KA = KS if br == 1 else KW
        PT = [P.sb("PTn%d" % i, [128, 512], BF16) for i in range(6)]
        rec = P.sb("rec", [128, 512], F32)
        tmpf = P.sb("tmpf", [128, 512], F32)
        groups = []
        for h in range(8):
            y, par, tile_i = h // 4, h % 2, h // 2
            for qc in range(4):
                ob = 2 + (qc % 2)
                kts = list(range(0, 4 * qc + 4)) if br == 1 else list(range(max(0, 4 * qc - 4), 4 * qc + 4))
                steps = []
                for kt in kts:
                    o = kt - 4 * qc
                    rlo = max(o, 0)
                    rhi = 3 if br == 1 else min(o + 4, 3)
                    c0, c1 = rlo * 128, (rhi + 1) * 128
                    masks = []
                    if o >= 0:
                        masks.append((o * 128, (o + 1) * 128, C.tri[:]))
                    if br == 2 and o <= -1:
                        masks.append(((o + 4) * 128, (o + 5) * 128, C.wmask[:]))
                    steps.append(dict(c0=c0, c1=c1, lhsK=KA[0:100, y, kt * 128:(kt + 1) * 128],
                                      rhsQ=QA[0:100, h, qc * 512 + c0:qc * 512 + c1], scale=None, masks=masks,
                                      lhsV=VA[:, kt, y, par, :], ob=ob, rv=[("VAw", kt, par)]))

                def fin(h=h, par=par, qc=qc, ob=ob, tile_i=tile_i):
                    lo = slice(par * 64, par * 64 + 64)
                    qs = slice(qc * 512, (qc + 1) * 512)
                    P.mm(C.ps[4][:], gsel[0:32, br, h, :], SgT[0:32, qs], True, True, [], [("ps", 4)])
                    softmax_pv_finish(P, C, C.ps[ob], par, ymT[lo, tile_i, qs], rec, tmpf,
                                      sgn[lo, tile_i, qs] if br == 2 else None, ("sgn", tile_i, qc) if br == 2 else None,
                                      ("ps", ob), ("ymT", tile_i, par, qc), False, clamp=False, gate_ps=C.ps[4], gkey=("ps", 4))
                groups.append((steps, fin))
        run_attention(P, C, groups, PT, sbanks=(0, 1, 5, 7), depth=4)
        P.end_phase()
    P.pop_scope()
```

```python
import math
from contextlib import ExitStack

import numpy as np
import concourse.bass as bass
import concourse.mybir as mybir
from concourse.bass_utils import run_bass_kernel_spmd

F32 = mybir.dt.float32
BF16 = mybir.dt.bfloat16
I32 = mybir.dt.int32
ALU = mybir.AluOpType
AF = mybir.ActivationFunctionType

S = 2048
D = 1024
NT = S // 128
EPS = 1e-6
ENGS = ("pe", "act", "dve", "pool", "sp")
CENG = ("pe", "act", "dve", "pool")
N_DMA_SEMS = 84
TWO_PI = 2.0 * math.pi


class Prog:
    def __init__(self, nc):
        self.nc = nc
        self.gstack = ExitStack()
        self.engsem = {e: self.gstack.enter_context(nc.semaphore("es_" + e)) for e in CENG}
        self.dsems = [self.gstack.enter_context(nc.semaphore("ds%d" % i)) for i in range(N_DMA_SEMS)]
        self.engcnt = {e: 0 for e in CENG}
        self.dcnt = [0] * N_DMA_SEMS
        self.scopes = []
        self.uid = 0
        self.n_instr = 0

    def close(self):
        self.gstack.close()

    def gsb(self, name, shape, dt):
        return self.gstack.enter_context(self.nc.sbuf_tensor(name, list(shape), dt))

    def gps(self, name, shape, dt=F32):
        return self.gstack.enter_context(self.nc.psum_tensor(name, list(shape), dt))

    def sb(self, name, shape, dt):
        self.uid += 1
        return self.scopes[-1].enter_context(self.nc.sbuf_tensor("%s_%d" % (name, self.uid), list(shape), dt))

    def push_scope(self):
        self.scopes.append(ExitStack())

    def pop_scope(self):
        self.scopes.pop().close()

    def begin_phase(self):
        self.push_scope()
        self.ins = []
        self.last_w = {}
        self.readers = {}
        self.eng_seq = {e: [] for e in ENGS}
        self.semmap = {}

    def _add(self, eng, fn, reads, writes, kind, semkey=None):
        idx = len(self.ins)
        deps = set()
        for k in reads:
            if k in self.last_w:
                deps.add((self.last_w[k], 0))
            if isinstance(k, tuple) and k[0] == "ps":
                for r in self.readers.get(k, ()):
                    if self.ins[r][0] != eng:
                        deps.add((r, 1))
        for k in writes:
            if k in self.last_w:
                deps.add((self.last_w[k], 1))
            for r in self.readers.get(k, ()):
                deps.add((r, 2))
        for k in reads:
            self.readers.setdefault(k, []).append(idx)
        for k in writes:
            self.last_w[k] = idx
            self.readers[k] = []
        self.ins.append((eng, fn, kind, semkey, deps))
        self.eng_seq[eng].append(idx)
        return idx

    def op(self, eng, fn, reads=(), writes=()):
        return self._add(eng, fn, list(reads), list(writes), "c")

    def dma(self, out, in_, reads=(), writes=(), q="sp", **kw):
        semkey = (q, tuple(writes))
        if semkey not in self.semmap:
            assert len(self.semmap) < N_DMA_SEMS, "too many dma sem keys"
            self.semmap[semkey] = len(self.semmap)
        fn = lambda e: e.dma_start(out=out, in_=in_, **kw)
        return self._add(q, fn, list(reads), list(writes), "d", semkey)

    def act(self, out, in_, func, r, w, **kw):
        self.op("act", lambda e: e.activation(out=out, in_=in_, func=func, **kw), r, w)

    def tt(self, eng, out, in0, in1, op, r, w):
        self.op(eng, lambda e: e.tensor_tensor(out=out, in0=in0, in1=in1, op=op), r, w)

    def ts(self, eng, out, in0, s1, s2, op0, op1, r, w, **kw):
        if s2 is None:
            self.op(eng, lambda e: e.tensor_scalar(out=out, in0=in0, scalar1=s1, scalar2=None, op0=op0, **kw), r, w)
        else:
            self.op(eng, lambda e: e.tensor_scalar(out=out, in0=in0, scalar1=s1, scalar2=s2, op0=op0, op1=op1, **kw), r, w)

    def stt(self, eng, out, in0, scalar, in1, op0, op1, r, w):
        self.op(eng, lambda e: e.scalar_tensor_tensor(out=out, in0=in0, scalar=scalar, in1=in1, op0=op0, op1=op1), r, w)

    def copy(self, eng, out, in_, r, w):
        if eng == "act":
            self.op(eng, lambda e: e.copy(out=out, in_=in_), r, w)
        else:
            self.op(eng, lambda e: e.tensor_copy(out=out, in_=in_), r, w)

    def memset(self, eng, ap, val, w):
        self.op(eng, lambda e: e.memset(ap, val), (), w)

    def mm(self, out, lhsT, rhs, start, stop, r, w, skip=False):
        if skip:
            self.op("pe", lambda e: e.matmul(out, lhsT=lhsT, rhs=rhs, start=start, stop=stop, skip_group_check=True), r, w)
        else:
            self.op("pe", lambda e: e.matmul(out, lhsT=lhsT, rhs=rhs, start=start, stop=stop), r, w)

    def tr(self, out, in_, ident, r, w):
        self.op("pe", lambda e: e.transpose(out=out, in_=in_, identity=ident), r, w)

    def recip(self, out, in_, r, w):
        self.op("dve", lambda e: e.reciprocal(out=out, in_=in_), r, w)

    def end_phase(self):
        nc = self.nc
        ins = self.ins
        pos = {}
        for e in ENGS:
            for p, idx in enumerate(self.eng_seq[e]):
                pos[idx] = p
        WIN = 3

        def edge_needed(idx, d, typ):
            eng = ins[idx][0]
            deng, _, dkind, _, _ = ins[d]
            if dkind == "d" or ins[idx][2] == "d":
                return True
            if deng == eng:
                return eng != "pe"
            return True

        pruned = []
        for idx, (eng, fn, kind, semkey, deps) in enumerate(ins):
            best = {}
            keep = set()
            for (d, typ) in deps:
                if not edge_needed(idx, d, typ):
                    continue
                if ins[d][2] == "d":
                    keep.add((d, typ))
                    continue
                pe_ = ins[d][0]
                if pe_ not in best or pos[d] > pos[best[pe_][0]]:
                    best[pe_] = (d, typ)
            keep.update(best.values())
            pruned.append(keep)
        needed = set()
        for idx in range(len(ins)):
            for (d, typ) in pruned[idx]:
                needed.add(d)
        for e in CENG:
            if self.eng_seq[e]:
                needed.add(self.eng_seq[e][-1])
        token = {}
        finals = {}
        for idx, (eng, fn, kind, semkey, deps) in enumerate(ins):
            if kind == "d":
                si = self.semmap[semkey]
                self.dcnt[si] += 16
                token[idx] = (("d", si), self.dcnt[si])
                finals[("d", si)] = self.dcnt[si]
            elif idx in needed:
                self.engcnt[eng] += 1
                token[idx] = (("e", eng), self.engcnt[eng])
                finals[("e", eng)] = self.engcnt[eng]
        progs = {e: [] for e in ENGS}
        waited = {e: {} for e in ENGS}
        for idx, (eng, fn, kind, semkey, deps) in enumerate(ins):
            waits = {}
            for (d, typ) in pruned[idx]:
                sn, val = token[d]
                if waited[eng].get(sn, 0) >= val:
                    continue
                waits[sn] = max(waits.get(sn, 0), val)
            for sn, val in waits.items():
                waited[eng][sn] = val
            progs[eng].append((waits, fn, token.get(idx)))
        self.n_instr += len(ins)

        def sem_of(sn):
            return self.dsems[sn[1]] if sn[0] == "d" else self.engsem[sn[1]]

        def run_engine(e, name):
            for waits, fn, inc in progs[name]:
                for sn, val in waits.items():
                    e.wait_ge(sem_of(sn), val)
                r = fn(e)
                if inc is not None:
                    r.then_inc(sem_of(inc[0]), 16 if inc[0][0] == "d" else 1)

        with nc.Block() as block:
            @block.tensor
            def _(e):
                run_engine(e, "pe")

            @block.scalar
            def _(e):
                run_engine(e, "act")

            @block.vector
            def _(e):
                run_engine(e, "dve")

            @block.gpsimd
            def _(e):
                run_engine(e, "pool")
                for sn, val in finals.items():
                    if sn[0] == "d" and any(k[0] == "pool" and self.semmap[k] == sn[1] for k in self.semmap):
                        e.wait_ge(sem_of(sn), val)

            @block.sync
            def _(e):
                run_engine(e, "sp")
                for sn, val in finals.items():
                    e.wait_ge(sem_of(sn), val)
        nc.all_engine_barrier()
        self.pop_scope()


class Ctx:
    pass


def load_w_bf16(P, C, dst, src, nkf, cols, tag, conv_engs=("dve", "act")):
    stage = [P.sb("wst_%s%d" % (tag, i), [128, cols], F32) for i in range(2)]
    for kf in range(nkf):
        s = kf % 2
        P.dma(stage[s][:], src[kf * 128:(kf + 1) * 128, :], writes=[("wst", tag, s)])
        P.copy(conv_engs[kf % len(conv_engs)], dst[:, kf, :], stage[s][:], [("wst", tag, s)], [("w", tag, kf)])


def rstd_from_ssq(P, C, st, n, key):
    P.act(st[:, 1:2], st[:, 0:1], AF.Sqrt, [key + (0,)], [key + (1,)], scale=1.0 / n, bias=C.epsb[:])
    P.recip(st[:, 2:3], st[:, 1:2], [key + (1,)], [key + (2,)])


def phase_prenorm(P, C, L, src, hnT):
    P.begin_phase()
    gb = P.sb("gpre", [128, D], F32)
    P.dma(gb[:], C.pre_norm[L:L + 1, :].partition_broadcast(128), writes=["gpre"])
    ht = [P.sb("ht%d" % i, [128, D], F32) for i in range(2)]
    junk = P.sb("junk", [128, D], BF16)
    hnb = [P.sb("hnb%d" % i, [128, D], BF16) for i in range(2)]
    st = [P.sb("st%d" % i, [128, 4], F32) for i in range(2)]
    psT = C.ps[0][:].bitcast(BF16)
    psT2 = C.ps[1][:].bitcast(BF16)
    def stage1(tt):
        s = tt % 2
        P.dma(ht[s][:], src[tt * 128:(tt + 1) * 128, :], writes=[("ht", s)])
        P.act(junk[:], ht[s][:], AF.Square, [("ht", s)], ["junk", ("st", s, 0)], accum_out=st[s][:, 0:1])
        rstd_from_ssq(P, C, st[s], D, ("st", s))
        P.stt("dve", hnb[s][:], ht[s][:], st[s][:, 2:3], gb[:], ALU.mult, ALU.mult, [("ht", s), ("st", s, 2), "gpre"], [("hnb", s)])

    def stage2(tt):
        s = tt % 2
        pst = psT if s == 0 else psT2
        for kf in range(8):
            P.tr(pst[:, kf * 128:(kf + 1) * 128], hnb[s][:, kf * 128:(kf + 1) * 128], C.ident[:], [("hnb", s), "ident"], [("psT", s)])
        P.copy("act" if s == 0 else "dve", hnT[:, :, tt * 128:(tt + 1) * 128], pst[:, 0:1024].rearrange("p (k t) -> p k t", k=8), [("psT", s)], [("hnT", tt)])

    stage1(0)
    for tt in range(NT):
        if tt + 1 < NT:
            stage1(tt + 1)
        stage2(tt)
    P.end_phase()


def phase_out(P, C, L, ymT, src, dst):
    P.begin_phase()
    wout = P.sb("wout", [128, 8, D], BF16)
    wpg = P.sb("wpg", [128, 8, D], BF16)
    wpp = P.sb("wpp", [128, 2, D], BF16)
    load_w_bf16(P, C, wout, C.w_out[L], 8, D, "wout")
    load_w_bf16(P, C, wpg, C.ple_gate[L], 8, D, "wpg")
    load_w_bf16(P, C, wpp, C.ple_proj[L], 2, D, "wpp")
    gb = P.sb("gpost", [128, D], F32)
    P.dma(gb[:], C.post_norm[L:L + 1, :].partition_broadcast(128), writes=["gpost"])
    ht = [P.sb("ht%d" % i, [128, D], F32) for i in range(2)]
    pt = [P.sb("pt%d" % i, [128, 256], F32) for i in range(2)]
    ptb = [P.sb("ptb%d" % i, [128, 256], BF16) for i in range(2)]
    pT = [P.sb("pT%d" % i, [128, 2, 128], BF16) for i in range(2)]
    junk = P.sb("junk", [128, D], BF16)
    t1 = [P.sb("t1%d" % i, [128, D], F32) for i in range(2)]
    hm = [P.sb("hm%d" % i, [128, D], F32) for i in range(2)]
    hmb = [P.sb("hmb%d" % i, [128, D], BF16) for i in range(2)]
    hmT = [P.sb("hmT%d" % i, [128, 8, 128], BF16) for i in range(2)]
    sg = [P.sb("sg%d" % i, [128, D], F32) for i in range(2)]
    hn = [P.sb("hnw%d" % i, [128, D], F32) for i in range(2)]
    st = [P.sb("st%d" % i, [128, 4], F32) for i in range(2)]
    wkeys_out = [("w", "wout", k) for k in range(8)]
    wkeys_pg = [("w", "wpg", k) for k in range(8)]
    wkeys_pp = [("w", "wpp", k) for k in range(2)]
    def stage1(tt):
        s = tt % 2
        tsl = slice(tt * 128, (tt + 1) * 128)
        P.dma(ht[s][:], src[tsl, :], writes=[("ht", s)])
        P.dma(pt[s][:], C.p[L, tsl, :], writes=[("pt", s)])
        for hf in range(2):
            for kf in range(8):
                P.mm(C.ps[hf][:], ymT[:, kf, tsl], wout[:, kf, hf * 512:(hf + 1) * 512], kf == 0, kf == 7,
                     [("ymT", kf)] + wkeys_out, [("ps", hf)])
        for hf in range(2):
            P.act(junk[:, hf * 512:(hf + 1) * 512], C.ps[hf][:], AF.Square, [("ps", hf)], ["junk", ("st", s, 0, hf)],
                  accum_out=st[s][:, hf:hf + 1])
        P.tt("dve", st[s][:, 0:1], st[s][:, 0:1], st[s][:, 1:2], ALU.add, [("st", s, 0, 0), ("st", s, 0, 1)], [("st", s, 0)])
        rstd_from_ssq(P, C, st[s], D, ("st", s))
        for hf in range(2):
            hs = slice(hf * 512, (hf + 1) * 512)
            P.stt("dve", t1[s][:, hs], C.ps[hf][:], st[s][:, 2:3], gb[:, hs], ALU.mult, ALU.mult,
                  [("ps", hf), ("st", s, 2), "gpost"], [("t1", s, hf)])
            P.tt("dve", hm[s][:, hs], t1[s][:, hs], ht[s][:, hs], ALU.add, [("t1", s, hf), ("ht", s)], [("hm", s, hf)])
            P.copy("act", hmb[s][:, hs], hm[s][:, hs], [("hm", s, hf)], [("hmb", s, hf)])
        P.copy("act", ptb[s][:], pt[s][:], [("pt", s)], [("ptb", s)])

    def stage2(tt):
        s = tt % 2
        tsl = slice(tt * 128, (tt + 1) * 128)
        psT = C.ps[2][:].bitcast(BF16)
        for kf in range(8):
            P.tr(psT[:, kf * 128:(kf + 1) * 128], hmb[s][:, kf * 128:(kf + 1) * 128], C.ident[:],
                 [("hmb", s, kf // 4), "ident"], [("ps", 2)])
        P.copy("dve", hmT[s][:], psT[:, 0:1024].rearrange("p (k t) -> p k t", k=8), [("ps", 2)], [("hmT", s)])
        psT3 = C.ps[3][:].bitcast(BF16)
        for j in range(2):
            P.tr(psT3[:, j * 128:(j + 1) * 128], ptb[s][:, j * 128:(j + 1) * 128], C.ident[:], [("ptb", s), "ident"], [("ps", 3)])
        P.copy("dve", pT[s][:], psT3[:, 0:256].rearrange("p (k t) -> p k t", k=2), [("ps", 3)], [("pT", s)])
        for hf in range(2):
            hs = slice(hf * 512, (hf + 1) * 512)
            for kf in range(8):
                P.mm(C.ps[4 + hf][:], hmT[s][:, kf, :], wpg[:, kf, hs], kf == 0, kf == 7, [("hmT", s)] + wkeys_pg, [("ps", 4 + hf)])
            for j in range(2):
                P.mm(C.ps[6 + hf][:], pT[s][:, j, :], wpp[:, j, hs], j == 0, j == 1, [("pT", s)] + wkeys_pp, [("ps", 6 + hf)])
            P.act(sg[s][:, hs], C.ps[4 + hf][:], AF.Sigmoid, [("ps", 4 + hf)], [("sg", s, hf)])
            P.tt("dve", sg[s][:, hs], sg[s][:, hs], C.ps[6 + hf][:], ALU.mult, [("sg", s, hf), ("ps", 6 + hf)], [("sg", s, hf)])
            P.tt("dve", hn[s][:, hs], sg[s][:, hs], hm[s][:, hs], ALU.add, [("sg", s, hf), ("hm", s, hf)], [("hn", s, hf)])
        P.dma(dst[tsl, :], hn[s][:], reads=[("hn", s, 0), ("hn", s, 1)], writes=[("dst", tt % 4)], q="pool")

    stage1(0)
    for tt in range(NT):
        if tt + 1 < NT:
            stage1(tt + 1)
        stage2(tt)
    P.end_phase()


def phase_even_proj(P, C, j, hnT, uT, sgaT, sgbT, hcpad):
    P.begin_phase()
    wst = [P.sb("wst%d" % i, [128, 8, 128], F32) for i in range(2)]
    wbf = [P.sb("wbf%d" % i, [128, 8, 128], BF16) for i in range(2)]
    aT = P.sb("aT", [128, 4, S], BF16)
    sig = [P.sb("sig%d" % i, [128, 512], BF16) for i in range(2)]
    w_in = C.ev_w_in[j]
    P.memset("pool", hcpad[:, :, 0:30], 0.0, ["hcpad0"])
    n = 0
    for sl in range(20):
        s = sl % 2
        P.dma(wst[s][:], w_in[:, sl * 128:(sl + 1) * 128].rearrange("(k p) c -> p k c", p=128), writes=[("wst", s)])
        P.copy("pool", wbf[s][:], wst[s][:], [("wst", s)], [("wbf", s)])
        for c in range(4):
            cs = slice(c * 512, (c + 1) * 512)
            b = n % 4
            n += 1
            pb = C.ps[b]
            for kf in range(8):
                P.mm(pb[:], wbf[s][:, kf, :], hnT[:, kf, cs], kf == 0, kf == 7, [("wbf", s)], [("ps", b)])
            if sl < 4:
                P.copy("act", uT[:, sl, cs], pb[:], [("ps", b)], [("uT", sl, c)])
            elif sl < 8:
                P.act(sgaT[:, sl - 4, cs], pb[:], AF.Silu, [("ps", b)], [("sgaT", sl - 4, c)])
            elif sl < 12:
                P.copy("dve", aT[:, sl - 8, cs], pb[:], [("ps", b)], [("aT", sl - 8, c)])
            elif sl < 16:
                q = n % 2
                P.act(sig[q][:], pb[:], AF.Sigmoid, [("ps", b)], [("sig", q)])
                P.tt("dve", hcpad[:, sl - 12, 30 + c * 512:30 + (c + 1) * 512], aT[:, sl - 12, cs], sig[q][:], ALU.mult,
                     [("aT", sl - 12, c), ("sig", q)], [("hcpad", sl - 12, c)])
            else:
                P.act(sgbT[:, sl - 16, cs], pb[:], AF.Silu, [("ps", b)], [("sgbT", sl - 16, c)])
    P.end_phase()


def phase_conv(P, C, j, hcpad, sgbT, ymT):
    P.begin_phase()
    evp = P.sb("evp", [128, 4, 40], F32)
    P.dma(evp[:], C.evp[j], writes=["evp"])
    wpw = P.sb("wpw", [128, 4, 512], BF16)
    load_w_bf16(P, C, wpw, C.cv_w_pw[j], 4, 512, "wpw")
    wpw_keys = [("w", "wpw", k) for k in range(4)]
    dg = P.sb("dg", [128, 4, 31, 128], BF16)
    for ft in range(4):
        for k in range(31):
            P.ts("dve", dg[:, ft, k, :], C.identf[:], evp[:, ft, k:k + 1], None, ALU.mult, None,
                 ["evp", "identf"], [("dg", ft)])
    cv1 = P.sb("cv1", [128, 4, 512], F32)
    sq = P.sb("sq", [128, 4, 512], F32)
    mu = P.sb("mu", [128, 512], F32)
    m2 = P.sb("m2", [128, 512], F32)
    rs = P.sb("rs", [128, 512], F32)
    xn = P.sb("xn", [128, 4, 512], F32)
    cvn = P.sb("cvn", [128, 4, 512], BF16)
    for c in range(4):
        cs = slice(c * 512, (c + 1) * 512)
        for ft in range(4):
            for k in range(31):
                P.mm(C.ps[ft][:], dg[:, ft, k, :], hcpad[:, ft, c * 512 + k:c * 512 + k + 512], k == 0, k == 30,
                     [("dg", ft)], [("ps", ft)])
            P.act(cv1[:, ft, :], C.ps[ft][:], AF.Identity, [("ps", ft), "evp"], [("cv1", ft)], bias=evp[:, ft, 31:32])
            P.act(sq[:, ft, :], cv1[:, ft, :], AF.Square, [("cv1", ft)], [("sq", ft)])
        for ft in range(4):
            P.mm(C.ps[4][:], C.onesf[:], cv1[:, ft, :], ft == 0, ft == 3, [("cv1", ft), "onesf"], [("ps", 4)])
        for ft in range(4):
            P.mm(C.ps[5][:], C.onesf[:], sq[:, ft, :], ft == 0, ft == 3, [("sq", ft), "onesf"], [("ps", 5)])
        P.act(mu[:], C.ps[4][:], AF.Copy, [("ps", 4)], ["mu"], scale=1.0 / 512)
        P.tt("dve", m2[:], mu[:], mu[:], ALU.mult, ["mu"], ["m2"])
        P.stt("dve", m2[:], C.ps[5][:], 1.0 / 512, m2[:], ALU.mult, ALU.subtract, [("ps", 5), "m2"], ["m2"])
        P.act(rs[:], m2[:], AF.Ln, ["m2"], ["rs"], bias=C.epsb[:])
        P.act(rs[:], rs[:], AF.Exp, ["rs"], ["rs"], scale=-0.5)
        for ft in range(4):
            eng = "dve"
            P.tt(eng, xn[:, ft, :], cv1[:, ft, :], mu[:], ALU.subtract, [("cv1", ft), "mu"], [("xn", ft)])
            P.tt(eng, xn[:, ft, :], xn[:, ft, :], rs[:], ALU.mult, [("xn", ft), "rs"], [("xn", ft)])
            P.act(cvn[:, ft, :], xn[:, ft, :], AF.Silu, [("xn", ft), "evp"], [("cvn", ft)],
                  scale=evp[:, ft, 32:33], bias=evp[:, ft, 33:34])
        for ot in range(4):
            b = 6 + (ot % 2)
            for ft in range(4):
                P.mm(C.ps[b][:], wpw[:, ft, ot * 128:(ot + 1) * 128], cvn[:, ft, :], ft == 0, ft == 3,
                     [("cvn", ft)] + wpw_keys, [("ps", b)])
            P.tt("dve", ymT[:, 4 + ot, cs], C.ps[b][:], sgbT[:, ot, cs], ALU.mult, [("ps", b)], [("ymT", 4 + ot, c)])
    P.end_phase()


def phase_s5_setup(P, C, j, BtR, BtI, CtR, CtI, sc2):
    P.begin_phase()
    sc = P.sb("sc", [128, 16, 3], F32)
    P.dma(sc[:], C.s5sc[j], writes=["sc"])
    w = P.sb("w", [128, 16, 16], F32)

    def col(i):
        return w[:, :, i]
    lr, li, ls = sc[:, :, 0], sc[:, :, 1], sc[:, :, 2]
    k = ["w%d" % i for i in range(16)]
    P.act(col(0), ls, AF.Exp, ["sc"], [k[0]])
    P.tt("dve", col(1), lr, col(0), ALU.mult, ["sc", k[0]], [k[1]])
    P.tt("dve", sc2[:, :, 0], li, col(0), ALU.mult, ["sc", k[0]], ["th"])
    P.ts("dve", sc2[:, :, 1], sc2[:, :, 0], 1.0 / TWO_PI, None, ALU.mult, None, ["th"], ["thq"])
    P.act(sc2[:, :, 2], col(1), AF.Exp, [k[1]], ["rho"])
    ki = P.sb("ki", [128, 16], I32)
    P.copy("dve", ki[:], sc2[:, :, 1], ["thq"], ["ki"])
    P.stt("dve", col(2), ki[:], -TWO_PI, sc2[:, :, 0], ALU.mult, ALU.add, ["ki", "th"], [k[2]])
    P.ts("dve", col(2), col(2), math.pi, -math.pi, ALU.min, ALU.max, [k[2]], [k[2]])
    P.act(col(3), col(2), AF.Abs, [k[2]], [k[3]])
    P.act(col(4), col(2), AF.Sin, [k[2]], [k[4]])
    P.act(col(5), col(3), AF.Sin, [k[3]], [k[5]], scale=-1.0, bias=C.hpib[:])
    P.tt("dve", col(6), sc2[:, :, 2], col(5), ALU.mult, ["rho", k[5]], [k[6]])
    P.tt("dve", col(7), sc2[:, :, 2], col(4), ALU.mult, ["rho", k[4]], [k[7]])
    P.ts("dve", col(6), col(6), -1.0, None, ALU.add, None, [k[6]], [k[6]])
    P.tt("dve", col(8), lr, lr, ALU.mult, ["sc"], [k[8]])
    P.tt("dve", col(9), li, li, ALU.mult, ["sc"], [k[9]])
    P.tt("dve", col(8), col(8), col(9), ALU.add, [k[8], k[9]], [k[8]])
    P.recip(col(8), col(8), [k[8]], [k[8]])
    P.tt("dve", col(9), col(6), lr, ALU.mult, [k[6], "sc"], [k[9]])
    P.tt("dve", col(10), col(7), li, ALU.mult, [k[7], "sc"], [k[10]])
    P.tt("dve", col(9), col(9), col(10), ALU.add, [k[9], k[10]], [k[9]])
    P.tt("dve", col(11), col(9), col(8), ALU.mult, [k[9], k[8]], [k[11]])
    P.tt("dve", col(9), col(7), lr, ALU.mult, [k[7], "sc"], [k[9]])
    P.tt("dve", col(10), col(6), li, ALU.mult, [k[6], "sc"], [k[10]])
    P.tt("dve", col(9), col(9), col(10), ALU.subtract, [k[9], k[10]], [k[9]])
    P.tt("dve", col(12), col(9), col(8), ALU.mult, [k[9], k[8]], [k[12]])
    for ri, dst in ((0, BtR), (1, BtI)):
        stg = P.sb("bst%d" % ri, [128, 16, 128], F32)
        P.dma(stg[:], C.s5bT[j, ri], writes=[("bst", ri)])
        P.copy("pool", dst[:], stg[:], [("bst", ri)], [("Bt", ri)])
    cre = P.sb("cre", [128, 16, 128], F32)
    cim = P.sb("cim", [128, 16, 128], F32)
    t1 = P.sb("t1", [128, 16, 128], F32)
    t2 = P.sb("t2", [128, 16, 128], F32)
    P.dma(cre[:], C.s5cP[j, 0], writes=["cre"])
    P.dma(cim[:], C.s5cP[j, 1], writes=["cim"])
    fre = w[:, :, 11:12].to_broadcast([128, 16, 128])
    fim = w[:, :, 12:13].to_broadcast([128, 16, 128])
    P.tt("dve", t1[:], cre[:], fre, ALU.mult, ["cre", k[11]], ["t1"])
    P.tt("pool", t2[:], cim[:], fim, ALU.mult, ["cim", k[12]], ["t2"])
    P.tt("dve", CtR[:], t1[:], t2[:], ALU.subtract, ["t1", "t2"], ["CtR"])
    P.tt("dve", t1[:], cre[:], fim, ALU.mult, ["cre", k[12]], ["t1"])
    P.tt("pool", t2[:], cim[:], fre, ALU.mult, ["cim", k[11]], ["t2"])
    P.stt("dve", CtI[:], t1[:], -1.0, t2[:], ALU.mult, ALU.subtract, ["t1", "t2"], ["CtI"])
    P.end_phase()


def phase_s5(P, C, j, uT, sgaT, ymT, BtR, BtI, CtR, CtI, sc2):
    P.begin_phase()
    TH = 1024
    evp = P.sb("evp", [128, 4, 40], F32)
    P.dma(evp[:], C.evp[j], writes=["evp"])
    wglu = P.sb("wglu", [128, 4, 512], BF16)
    load_w_bf16(P, C, wglu, C.s5_w_glu[j], 4, 512, "wglu")
    wglu_keys = [("w", "wglu", k) for k in range(4)]
    iot = P.sb("iot", [128, S], F32)
    P.op("pool", lambda e: e.iota(iot[:], pattern=[[1, S]], base=0, channel_multiplier=0,
                                  allow_small_or_imprecise_dtypes=True), (), ["iot"])
    ki = P.sb("ki", [128, TH], I32)
    rr = P.sb("rr", [128, TH], F32)
    ra = P.sb("ra", [128, TH], F32)
    cs_ = P.sb("cos", [128, TH], BF16)
    sn_ = P.sb("sin", [128, TH], BF16)
    bpR = P.sb("bpR", [128, TH], BF16)
    bpI = P.sb("bpI", [128, TH], BF16)
    wR = P.sb("wR", [128, TH], BF16)
    wI = P.sb("wI", [128, TH], BF16)
    xR = P.sb("xR", [128, TH], BF16)
    xI = P.sb("xI", [128, TH], BF16)
    buR = [P.sb("buR%d" % q, [128, 512], BF16) for q in range(2)]
    buI = [P.sb("buI%d" % q, [128, 512], BF16) for q in range(2)]
    mt = [[P.sb("m%d_%d" % (i, q), [128, 512], BF16) for i in range(4)] for q in range(2)]
    mo = [P.sb("mo%d" % i, [128, TH], BF16) for i in range(4)]
    carry = P.sb("carry", [128, 16, 2], F32)
    P.memset("pool", carry[:], 0.0, ["carry"])
    ygT = P.sb("ygT", [128, 4, S], BF16)
    yv = P.sb("yv", [128, 512], F32)
    y2 = P.sb("y2", [128, 512], F32)
    sgm = P.sb("sgm", [128, 512], F32)
    it = 0
    for ft in range(4):
        for hf in range(2):
            t0 = hf * TH
            for pl in range(4):
                pr = 4 * ft + pl
                th = sc2[:, pr, 0:1]
                thq = sc2[:, pr, 1:2]
                P.ts("dve", ki[:], iot[:, t0:t0 + TH], thq, None, ALU.mult, None, ["iot"], ["ki"])
                P.act(ra[:], iot[:, t0:t0 + TH], AF.Copy, ["iot"], ["ra"], scale=th)
                P.stt("dve", rr[:], ki[:], -TWO_PI, ra[:], ALU.mult, ALU.add, ["ki", "ra"], ["rr"])
                P.ts("dve", rr[:], rr[:], math.pi, -math.pi, ALU.min, ALU.max, ["rr"], ["rr"])
                P.act(sn_[:], rr[:], AF.Sin, ["rr"], ["sin"])
                P.act(ra[:], rr[:], AF.Abs, ["rr"], ["ra"])
                P.act(cs_[:], ra[:], AF.Sin, ["ra"], ["cos"], scale=-1.0, bias=C.hpib[:])
                for c in range(2):
                    cl = slice(c * 512, (c + 1) * 512)
                    cg = slice(t0 + c * 512, t0 + (c + 1) * 512)
                    q = it % 2
                    it += 1
                    bA, bB = C.ps[2 * q], C.ps[2 * q + 1]
                    P.mm(bA[:], BtR[:, pr, :], uT[:, ft, cg], True, True, [], [("ps", 2 * q)])
                    P.mm(bB[:], BtI[:, pr, :], uT[:, ft, cg], True, True, [], [("ps", 2 * q + 1)])
                    P.copy("act", buR[q][:], bA[:], [("ps", 2 * q)], [("buR", q)])
                    P.copy("act", buI[q][:], bB[:], [("ps", 2 * q + 1)], [("buI", q)])
                    m = mt[q]
                    P.tt("dve", m[0][:], buR[q][:], cs_[:, cl], ALU.mult, [("buR", q), "cos"], [("m", q, 0)])
                    P.tt("dve", m[1][:], buI[q][:], sn_[:, cl], ALU.mult, [("buI", q), "sin"], [("m", q, 1)])
                    P.tt("dve", m[2][:], buI[q][:], cs_[:, cl], ALU.mult, [("buI", q), "cos"], [("m", q, 2)])
                    P.tt("dve", m[3][:], buR[q][:], sn_[:, cl], ALU.mult, [("buR", q), "sin"], [("m", q, 3)])
                    P.tt("dve", bpR[:, cl], m[0][:], m[1][:], ALU.add, [("m", q, 0), ("m", q, 1)], [("bpR", c)])
                    P.tt("dve", bpI[:, cl], m[2][:], m[3][:], ALU.subtract, [("m", q, 2), ("m", q, 3)], [("bpI", c)])
                P.op("dve", lambda e, pr=pr: e.tensor_tensor_scan(out=wR[:], data0=sc2[:, pr, 2:3].to_broadcast([128, TH]), data1=bpR[:],
                                                                 initial=carry[:, pr, 0:1], op0=ALU.mult, op1=ALU.add),
                     [("bpR", 0), ("bpR", 1), "carry"], ["wR"])
                P.op("dve", lambda e, pr=pr: e.tensor_tensor_scan(out=wI[:], data0=sc2[:, pr, 2:3].to_broadcast([128, TH]), data1=bpI[:],
                                                                 initial=carry[:, pr, 1:2], op0=ALU.mult, op1=ALU.add),
                     [("bpI", 0), ("bpI", 1), "carry"], ["wI"])
                if hf == 0:
                    P.copy("dve", carry[:, pr, 0:1], wR[:, TH - 1:TH], ["wR"], ["carry"])
                    P.copy("dve", carry[:, pr, 1:2], wI[:, TH - 1:TH], ["wI"], ["carry"])
                P.tt("dve", mo[0][:], wR[:], cs_[:], ALU.mult, ["wR", "cos"], ["mo0"])
                P.tt("dve", mo[1][:], wI[:], sn_[:], ALU.mult, ["wI", "sin"], ["mo1"])
                P.tt("dve", mo[2][:], wI[:], cs_[:], ALU.mult, ["wI", "cos"], ["mo2"])
                P.tt("dve", mo[3][:], wR[:], sn_[:], ALU.mult, ["wR", "sin"], ["mo3"])
                P.tt("dve", xR[:], mo[0][:], mo[1][:], ALU.subtract, ["mo0", "mo1"], ["xR"])
                P.tt("dve", xI[:], mo[2][:], mo[3][:], ALU.add, ["mo2", "mo3"], ["xI"])
                for c in range(2):
                    cl = slice(c * 512, (c + 1) * 512)
                    P.mm(C.ps[4 + c][:], CtR[:, pr, :], xR[:, cl], pl == 0, False, ["xR"], [("ps", 4 + c)])
                    P.mm(C.ps[4 + c][:], CtI[:, pr, :], xI[:, cl], False, pl == 3, ["xI"], [("ps", 4 + c)])
            for c in range(2):
                cg = slice(t0 + c * 512, t0 + (c + 1) * 512)
                P.stt("dve", yv[:], uT[:, ft, cg], evp[:, ft, 34:35], C.ps[4 + c][:], ALU.mult, ALU.add,
                      [("ps", 4 + c), "evp"], ["yv"])
                P.act(y2[:], yv[:], AF.Square, ["yv"], ["y2"])
                P.ts("dve", y2[:], y2[:], 0.044715, 1.0, ALU.mult, ALU.add, ["y2"], ["y2"])
                P.tt("dve", y2[:], y2[:], yv[:], ALU.mult, ["y2", "yv"], ["y2"])
                P.act(sgm[:], y2[:], AF.Sigmoid, ["y2"], ["sgm"], scale=1.5957691216057308)
                P.tt("pool", ygT[:, ft, cg], sgm[:], yv[:], ALU.mult, ["sgm", "yv"], [("ygT", ft, hf, c)])
    gk = [("ygT", ft, hf, c) for ft in range(4) for hf in range(2) for c in range(2)]
    n = 0
    for ot in range(4):
        for c in range(4):
            cs = slice(c * 512, (c + 1) * 512)
            b = n % 4
            n += 1
            for ft in range(4):
                P.mm(C.ps[b][:], wglu[:, ft, ot * 128:(ot + 1) * 128], ygT[:, ft, cs], ft == 0, ft == 3, gk + wglu_keys, [("ps", b)])
            P.act(sgm[:], C.ps[b][:], AF.Sigmoid, [("ps", b), "evp"], ["sgm"], bias=evp[:, ot, 35:36])
            P.tt("dve", sgm[:], sgm[:], ygT[:, ot, cs], ALU.mult, ["sgm"] + gk, ["sgm"])
            P.tt("dve", ymT[:, ot, cs], sgm[:], sgaT[:, ot, cs], ALU.mult, ["sgm"], [("ymT", ot, c)])
    P.end_phase()


def even_layer(P, C, L, src, dst):
    j = L // 2
    P.push_scope()
    ymT = P.sb("ymT", [128, 8, S], BF16)
    P.push_scope()
    uT = P.sb("uT", [128, 4, S], BF16)
    sgaT = P.sb("sgaT", [128, 4, S], BF16)
    P.push_scope()
    sgbT = P.sb("sgbT", [128, 4, S], BF16)
    hcpad = P.sb("hcpad", [128, 4, S + 30], BF16)
    P.push_scope()
    hnT = P.sb("hnT", [128, 8, S], BF16)
    phase_prenorm(P, C, L, src, hnT)
    phase_even_proj(P, C, j, hnT, uT, sgaT, sgbT, hcpad)
    P.pop_scope()
    phase_conv(P, C, j, hcpad, sgbT, ymT)
    P.pop_scope()
    P.push_scope()
    BtR = P.sb("BtR", [128, 16, 128], BF16)
    BtI = P.sb("BtI", [128, 16, 128], BF16)
    CtR = P.sb("CtR", [128, 16, 128], BF16)
    CtI = P.sb("CtI", [128, 16, 128], BF16)
    sc2 = P.sb("sc2", [128, 16, 3], F32)
    phase_s5_setup(P, C, j, BtR, BtI, CtR, CtI, sc2)
    phase_s5(P, C, j, uT, sgaT, ymT, BtR, BtI, CtR, CtI, sc2)
    P.pop_scope()
    P.pop_scope()
    phase_out(P, C, L, ymT, src, dst)
    P.pop_scope()


W_SHAPES = {
    "pre_norm": [4, D], "post_norm": [4, D], "ple_gate": [4, D, D], "ple_proj": [4, 256, D],
    "ev_w_in": [2, D, 2560], "s5_w_glu": [2, 512, 512], "cv_w_pw": [2, 512, 512], "w_out": [4, D, D],
    "evp": [2, 128, 4, 40], "s5sc": [2, 128, 16, 3], "s5bT": [2, 2, 128, 16, 128], "s5cP": [2, 2, 128, 16, 128],
    "od_w_in": [2, D, 2744], "mla_w_uq": [2, 256, 768], "mla_w_ukv": [2, 128, 1024], "mlap": [2, 128, 3],
    "ropef": [128, 2],
    "nsa_ck_w1": [2, 2048, 256], "nsa_ck_w2": [2, 256, 64], "nsa_cv_w1": [2, 2048, 256], "nsa_cv_w2": [2, 256, 64],
    "nsapos2": [2, 128, 2, 16], "selc": [128, NT, 2, 32],
}
W_BF16 = {"qaug": [4, 8, S], "kaugc": [36, S], "kaugcmp": [4, 128], "addmask": [128, S], "ovl": [128, 64], "gsel": [32, 3, 8, 128]}


def build(n_layers=4, dbg=False, odd_kw=None):
    nc = bass.Bass("TRN2", target_bir_lowering=False)
    C = Ctx()
    odd_kw = odd_kw or {}

    def din(name, shape, dt=F32):
        return nc.dram_tensor(name, list(shape), dt, kind="ExternalInput").ap()
    C.x = din("x", [S, D])
    C.p = din("p", [4, S, 256])
    C.positions = din("positions", [1, S], I32)
    C.dbg = nc.dram_tensor("dbg", [8, 128, S], F32, kind="ExternalOutput").ap() if dbg else None
    for k, shp in W_SHAPES.items():
        setattr(C, k, din(k, shp))
    for k, shp in W_BF16.items():
        setattr(C, k, din(k, shp, BF16))
    out = nc.dram_tensor("out", [S, D], F32, kind="ExternalOutput").ap()
    hbuf = nc.dram_tensor("hbuf", [S, D], F32, kind="Internal").ap()
    P = Prog(nc)
    C.ps = [P.gps("ps%d" % i, [128, 512]) for i in range(8)]
    C.ident = P.gsb("ident", [128, 128], BF16)
    C.identf = P.gsb("identf", [128, 128], F32)
    C.onesf = P.gsb("onesf", [128, 128], F32)
    C.epsb = P.gsb("epsb", [128, 1], F32)
    C.tri = P.gsb("tri", [128, 128], BF16)
    C.wmask = P.gsb("wmask", [128, 128], BF16)
    C.ropef_sb = P.gsb("ropef_sb", [128, 2], F32)
    C.hpib = P.gsb("hpib", [128, 1], F32)
    P.begin_phase()
    io = P.sb("io", [128, 128], F32)
    P.op("pool", lambda e: e.iota(io[:], pattern=[[1, 128]], base=0, channel_multiplier=-1,
                                  allow_small_or_imprecise_dtypes=True), (), ["io"])
    P.op("dve", lambda e: e.tensor_single_scalar(out=C.identf[:], in_=io[:], scalar=0.0, op=ALU.is_equal), ["io"], ["identf"])
    P.copy("dve", C.ident[:], C.identf[:], ["identf"], ["ident"])
    P.memset("pool", C.onesf[:], 1.0, ["onesf"])
    P.op("dve", lambda e: e.tensor_single_scalar(out=C.tri[:], in_=io[:], scalar=0.0, op=ALU.is_ge), ["io"], ["tri"])
    P.op("dve", lambda e: e.tensor_single_scalar(out=C.wmask[:], in_=io[:], scalar=0.0, op=ALU.is_lt), ["io"], ["wmask"])
    P.dma(C.ropef_sb[:], C.ropef[:, :], writes=["ropef_sb"])
    P.memset("pool", C.epsb[:], EPS, ["epsb"])
    P.memset("pool", C.hpib[:], math.pi / 2, ["hpib"])
    P.end_phase()
    C.ropef_dram = C.ropef
    C.ropef = C.ropef_sb
    for L in range(n_layers):
        src = C.x if L == 0 else hbuf
        dst = out if L == n_layers - 1 else hbuf
        if L % 2 == 0:
            even_layer(P, C, L, src, dst)
        else:
            odd_layer(P, C, L, src, dst, **odd_kw)
    C.ropef = C.ropef_dram
    P.close()
    return nc, P


def nsa_constants():
    import ml_dtypes
    bf = ml_dtypes.bfloat16
    t = np.arange(S)
    a_t, b_t = (t // 64).astype(np.float32), (t % 64).astype(np.float32)
    slopes = np.array([2.0 ** (-(i + 1)) for i in range(8)], np.float32)
    qaug = np.zeros((4, 8, S), np.float32)
    for h in range(8):
        qaug[0, h] = -slopes[h] * 64.0 * a_t
        qaug[1, h] = -slopes[h] * b_t
        qaug[2, h] = slopes[h] * 64.0
        qaug[3, h] = slopes[h]
    kaugc = np.zeros((36, S), np.float32)
    kaugc[t // 64, t] = 1.0
    kaugc[32] = 1.0
    kaugc[33] = 1.0
    kaugc[34] = a_t
    kaugc[35] = b_t
    c = np.arange(128)
    pc = 16 * c + 31
    kaugcmp = np.stack([np.ones(128), np.ones(128), pc // 64, pc % 64]).astype(np.float32)
    addmask = np.where((t[None, :] >= pc[:, None]) & (c[:, None] <= 126), 0.0, -30000.0).astype(np.float32)
    sb = np.arange(32)
    cs_ = c[:, None] * 16
    overlap = ((cs_ < (sb[None] + 1) * 64) & (cs_ + 32 > sb[None] * 64) & (c[:, None] <= 126)).astype(np.float32)
    ovl = np.concatenate([overlap, np.ones((128, 32), np.float32)], axis=1)
    cur = t[:, None] // 64
    forced = (sb[None] == 0) | (sb[None] == cur) | (sb[None] == cur - 1)
    causal = sb[None] * 64 <= t[:, None]
    mul = (forced | causal).astype(np.float32)
    add = np.where(forced, 1e4, np.where(causal, 0.0, -1e4)).astype(np.float32)
    selc = np.stack([mul, add], axis=1).reshape(NT, 128, 2, 32).transpose(1, 0, 2, 3)
    gsel = np.zeros((32, 3, 8, 128), np.float32)
    for b in range(3):
        for h in range(8):
            gsel[b * 8 + h, b, h, (h % 2) * 64:(h % 2) * 64 + 64] = 1.0
    return {"qaug": qaug.astype(bf), "kaugc": kaugc.astype(bf), "kaugcmp": kaugcmp.astype(bf), "addmask": addmask.astype(bf),
            "ovl": ovl.astype(bf), "gsel": gsel.astype(bf), "selc": np.ascontiguousarray(selc)}


def host_layout(inputs):
    f = lambda k: np.asarray(inputs[k], np.float32)
    ne = 2
    evp = np.zeros((ne, 128, 4, 40), np.float32)
    wdw = f("cv_w_dw")
    evp[:, :, :, 0:31] = wdw.reshape(ne, 31, 4, 128).transpose(0, 3, 2, 1)
    for col, key in ((31, "cv_b_dw"), (32, "cv_ln_g"), (33, "cv_ln_b"), (34, "s5_d"), (35, "s5_b_glu")):
        evp[:, :, :, col] = f(key).reshape(ne, 4, 128).transpose(0, 2, 1)
    s5sc = np.zeros((ne, 128, 16, 3), np.float32)
    for i, key in enumerate(("s5_lam_re", "s5_lam_im")):
        a = f(key).reshape(ne, 16, 2, 64)
        s5sc[:, :, :, i] = a.transpose(0, 2, 3, 1).reshape(ne, 128, 16)
    ls = f("s5_log_step").reshape(ne, 16, 2)
    s5sc[:, :, :, 2] = np.repeat(ls.transpose(0, 2, 1)[:, :, None, :], 64, axis=2).reshape(ne, 128, 16)
    s5bT = np.zeros((ne, 2, 128, 16, 128), np.float32)
    s5cP = np.zeros((ne, 2, 128, 16, 128), np.float32)
    for ri, (kb, kc) in enumerate((("s5_b_re", "s5_c_re"), ("s5_b_im", "s5_c_im"))):
        b = f(kb)
        c = f(kc)
        for pr in range(16):
            for gl in range(2):
                g = 2 * pr + gl
                k0 = (pr % 4) * 32 + gl * 16
                s5bT[:, ri, k0:k0 + 16, pr, gl * 64:(gl + 1) * 64] = b[:, g].transpose(0, 2, 1)
                s5cP[:, ri, gl * 64:(gl + 1) * 64, pr, k0:k0 + 16] = c[:, g].transpose(0, 2, 1)
    w_out = np.stack([f("ev_w_out")[0], f("od_w_out")[0], f("ev_w_out")[1], f("od_w_out")[1]])
    no = 2
    mlap = np.zeros((no, 128, 3), np.float32)
    mlap[:, :, 0:2] = f("mla_q_norm").reshape(no, 2, 128).transpose(0, 2, 1)
    mlap[:, :, 2] = f("mla_kv_norm")
    ropef = np.zeros((128, 2), np.float32)
    fr = (10000.0 ** (-np.arange(16, dtype=np.float32) / 16)).astype(np.float32)
    ropef[64:96, 0] = np.tile(fr, 2)
    ropef[:, 1] = ropef[:, 0] / np.float32(TWO_PI)
    rep = {"evp": evp, "s5sc": s5sc, "s5bT": s5bT, "s5cP": s5cP, "w_out": w_out, "mlap": mlap, "ropef": ropef}
    rep.update(nsa_constants())
    pos2 = np.zeros((no, 128, 2, 16), np.float32)
    for kv, key in enumerate(("nsa_pos_k", "nsa_pos_v")):
        a = f(key).reshape(no, 16, 2, 64)
        pos2[:, :, kv, :] = a.transpose(0, 2, 3, 1).reshape(no, 128, 16)
    rep["nsapos2"] = pos2
    for k in ("nsa_ck_w1", "nsa_ck_w2", "nsa_cv_w1", "nsa_cv_w2"):
        rep[k] = np.ascontiguousarray(f(k))
    for k in ("pre_norm", "post_norm", "ple_gate", "ple_proj", "ev_w_in", "s5_w_glu", "cv_w_pw", "od_w_in", "mla_w_uq", "mla_w_ukv"):
        rep[k] = np.ascontiguousarray(f(k))
    return rep


def kernel(**inputs):
    n = 8
    rep = host_layout(inputs)
    x = np.asarray(inputs["x"], np.float32)
    p = np.asarray(inputs["p"], np.float32)
    nc, _ = build(4)
    in_maps = []
    for b in range(n):
        m = dict(rep)
        m["x"] = np.ascontiguousarray(x[b])
        m["p"] = np.ascontiguousarray(p[:, b])
        m["positions"] = np.ascontiguousarray(np.asarray(inputs["positions"])[b:b + 1]).astype(np.int32)
        in_maps.append(m)
    res = run_bass_kernel_spmd(nc, in_maps, core_ids=list(range(n)))
    return np.stack([r["out"] for r in res.results], axis=0).astype(np.float32)


OD = {"q": 0, "kv": 512, "gl": 1280, "gn": 1304, "cq": 1816, "ckv": 2072, "kr": 2200, "gm": 2232}


def load_slab(P, C, W, c0, n, tag, slot, eng="pool"):
    stg = P.sb("sl_st_%s" % tag, [128, 8, n], F32)
    dst = P.sb("sl_bf_%s" % tag, [128, 8, n], BF16)
    P.dma(stg[:], W[:, c0:c0 + n].rearrange("(k p) c -> p k c", p=128), writes=[("slst", tag)])
    P.copy(eng, dst[:], stg[:], [("slst", tag)], [("slab", tag)])
    return dst


def angle_tables(P, C, ang_in, fq, f, cosT, sinT, n, tag):
    ki = P.sb("ki_" + tag, [128, n], I32)
    rr = P.sb("rr_" + tag, [128, n], F32)
    ra = P.sb("ra_" + tag, [128, n], F32)
    P.ts("dve", ki[:], ang_in, fq, None, ALU.mult, None, [tag + "in"], [tag + "ki"])
    P.ts("pool", rr[:], ki[:], -TWO_PI, None, ALU.mult, None, [tag + "ki"], [tag + "rr"])
    P.ts("pool", ra[:], ang_in, f, None, ALU.mult, None, [tag + "in"], [tag + "ra"])
    P.tt("pool", rr[:], rr[:], ra[:], ALU.add, [tag + "rr", tag + "ra"], [tag + "rr"])
    P.ts("pool", rr[:], rr[:], math.pi, -math.pi, ALU.min, ALU.max, [tag + "rr"], [tag + "rr"])
    P.act(sinT, rr[:], AF.Sin, [tag + "rr"], [tag + "sin"])
    P.act(ra[:], rr[:], AF.Abs, [tag + "rr"], [tag + "ra"])
    P.act(cosT, ra[:], AF.Sin, [tag + "ra"], [tag + "cos"], scale=-1.0, bias=C.hpib[:])


def softmax_pv_finish(P, C, ob, par, dst_rows, rec, tmpf, extra_mul, ekey, okey, wkey, first, clamp=False, gate_ps=None, gkey=None):
    lo = slice(par * 64, par * 64 + 64)
    hi = slice((1 - par) * 64, (1 - par) * 64 + 64)
    if clamp:
        P.ts("dve", rec[lo, :], ob[hi, :], 1e-18, None, ALU.max, None, [okey], [("rec", par)])
        P.act(rec[lo, :], rec[lo, :], AF.Ln, [("rec", par)], [("rec", par)])
    else:
        P.act(rec[lo, :], ob[hi, :], AF.Ln, [okey], [("rec", par)])
    P.act(rec[lo, :], rec[lo, :], AF.Exp, [("rec", par)], [("rec", par)], scale=-1.0)
    P.tt("dve", tmpf[lo, :], ob[lo, :], rec[lo, :], ALU.mult, [okey, ("rec", par)], [("tmpf", par)])
    if gate_ps is not None:
        P.tt("dve", tmpf[lo, :], tmpf[lo, :], gate_ps[lo, :], ALU.mult, [("tmpf", par), gkey], [("tmpf", par)])
    if not first:
        P.tt("dve", tmpf[lo, :], tmpf[lo, :], dst_rows, ALU.add, [("tmpf", par), wkey], [("tmpf", par)])
    if extra_mul is not None:
        P.tt("dve", dst_rows, tmpf[lo, :], extra_mul, ALU.mult, [("tmpf", par), ekey], [wkey])
    else:
        P.copy("act", dst_rows, tmpf[lo, :], [("tmpf", par)], [wkey])


def run_attention(P, C, groups, PT, depth=3, mask_eng="dve", sbanks=(0, 1), tick=None):
    flat = [(g, i) for g, (steps, fin) in enumerate(groups) for i in range(len(steps))]
    issued = 0
    for idx in range(len(flat)):
        while issued < min(len(flat), idx + depth):
            g2, i2 = flat[issued]
            st2 = groups[g2][0][i2]
            bank = sbanks[issued % len(sbanks)]
            P.mm(C.ps[bank][:, st2["c0"]:st2["c1"]], st2["lhsK"], st2["rhsQ"], True, True, st2.get("rk", []), [("ps", bank)])
            issued += 1
        g, i = flat[idx]
        steps, fin = groups[g]
        st = steps[i]
        bank = sbanks[idx % len(sbanks)]
        c0, c1 = st["c0"], st["c1"]
        pt = PT[idx % len(PT)]
        pk = ("PT", idx % len(PT))
        if st.get("scale") is not None:
            P.act(pt[:, c0:c1], C.ps[bank][:, c0:c1], AF.Exp, [("ps", bank)], [pk], scale=st["scale"])
        else:
            P.act(pt[:, c0:c1], C.ps[bank][:, c0:c1], AF.Exp, [("ps", bank)], [pk])
        for (a, b, m) in st["masks"]:
            P.tt(mask_eng, pt[:, a:b], pt[:, a:b], m, ALU.mult, [pk], [pk])
        P.mm(C.ps[st["ob"]][:, c0:c1], st["lhsV"], pt[:, c0:c1], i == 0, i == len(steps) - 1, [pk] + st.get("rv", []), [("ps", st["ob"])],
             skip=True)
        if i == len(steps) - 1:
            fin()
        if tick is not None:
            tick(idx)


def phase_mla_prep(P, C, L, hnT, cosT, sinT, cqnT, ckvnT, kropeT):
    j = L // 2
    W = C.od_w_in[j]
    P.begin_phase()
    mlap = P.sb("mlap", [128, 3], F32)
    P.dma(mlap[:], C.mlap[j], writes=["mlap"])
    posi = [P.sb("posi%d" % i, [128, 512], I32) for i in range(2)]
    posf = [P.sb("posf%d" % i, [128, 512], F32) for i in range(2)]
    ki = P.sb("rki", [128, 512], I32)
    rr = P.sb("rrr", [128, 512], F32)
    ra = P.sb("rra", [128, 512], F32)
    for c in range(4):
        cs = slice(c * 512, (c + 1) * 512)
        s = c % 2
        P.dma(posi[s][:], C.positions[0:1, cs].partition_broadcast(128), writes=[("posi", s)])
        P.copy("dve", posf[s][:], posi[s][:], [("posi", s)], [("posf", s)])
        P.ts("dve", ki[:], posf[s][:], C.ropef[:, 1:2], None, ALU.mult, None, [("posf", s)], ["ki"])
        P.ts("dve", rr[:], ki[:], -TWO_PI, None, ALU.mult, None, ["ki"], ["rr"])
        P.ts("dve", ra[:], posf[s][:], C.ropef[:, 0:1], None, ALU.mult, None, [("posf", s)], ["ra"])
        P.tt("dve", rr[:], rr[:], ra[:], ALU.add, ["rr", "ra"], ["rr"])
        P.ts("dve", rr[:], rr[:], math.pi, -math.pi, ALU.min, ALU.max, ["rr"], ["rr"])
        P.act(sinT[:, cs], rr[:], AF.Sin, ["rr"], [("sinT", c)])
        P.act(ra[:], rr[:], AF.Abs, ["rr"], ["ra"])
        P.act(cosT[:, cs], ra[:], AF.Sin, ["ra"], [("cosT", c)], scale=-1.0, bias=C.hpib[:])
    wcq = load_slab(P, C, W, OD["cq"], 256, "cq", 0, eng="act")
    wckv = load_slab(P, C, W, OD["ckv"], 128, "ckv", 0, eng="dve")
    wkr = load_slab(P, C, W, OD["kr"], 32, "kr", 0, eng="dve")
    wkrA = P.sb("wkrA", [128, 8, 96], BF16)
    wkrR = P.sb("wkrR", [128, 8, 96], BF16)
    P.memset("pool", wkrA[:], 0.0, ["wkrA"])
    P.memset("pool", wkrR[:], 0.0, ["wkrR"])
    P.copy("dve", wkrA[:, :, 64:96], wkr[:], [("slab", "kr"), "wkrA"], ["wkrA"])
    P.ts("dve", wkrR[:, :, 64:80], wkr[:, :, 16:32], -1.0, None, ALU.mult, None, [("slab", "kr"), "wkrR"], ["wkrR"])
    P.copy("dve", wkrR[:, :, 80:96], wkr[:, :, 0:16], [("slab", "kr"), "wkrR"], ["wkrR"])
    cqf = [P.sb("cqf%d" % i, [128, 512], F32) for i in range(3)]
    sq = [P.sb("sqm%d" % i, [128, 512], F32) for i in range(3)]
    rs = P.sb("rsm", [128, 512], F32)
    m1 = P.sb("m1", [128, 512], F32)
    m2 = P.sb("m2", [128, 512], F32)
    for c in range(4):
        cs = slice(c * 512, (c + 1) * 512)
        for t in range(3):
            for kf in range(8):
                lhs = wcq[:, kf, t * 128:(t + 1) * 128] if t < 2 else wckv[:, kf, :]
                P.mm(C.ps[4 + t][:], lhs, hnT[:, kf, cs], kf == 0, kf == 7, [("slab", "cq"), ("slab", "ckv")], [("ps", 4 + t)])
            P.copy("act", cqf[t][:], C.ps[4 + t][:], [("ps", 4 + t)], [("cqf", t)])
            P.act(sq[t][:], C.ps[4 + t][:], AF.Square, [("ps", 4 + t)], [("sq", t)])
        for tiles, nfeat in (((0, 1), 256), ((2,), 128)):
            for i, t in enumerate(tiles):
                P.mm(C.ps[7][:], C.onesf[:], sq[t][:], i == 0, i == len(tiles) - 1, [("sq", t)], [("ps", 7)])
            P.act(rs[:], C.ps[7][:], AF.Ln, [("ps", 7)], ["rs"], scale=1.0 / nfeat, bias=C.epsb[:])
            P.act(rs[:], rs[:], AF.Exp, ["rs"], ["rs"], scale=-0.5)
            for t in tiles:
                dst = cqnT[:, t, cs] if t < 2 else ckvnT[:, cs]
                P.stt("dve", dst, cqf[t][:], mlap[:, t:t + 1], rs[:], ALU.mult, ALU.mult, [("cqf", t), "rs", "mlap"], [("cn", t, c)])
        for kf in range(8):
            P.mm(C.ps[0][:96, :], wkrA[:, kf, :], hnT[:, kf, cs], kf == 0, kf == 7, ["wkrA"], [("ps", 0)])
        for kf in range(8):
            P.mm(C.ps[1][:96, :], wkrR[:, kf, :], hnT[:, kf, cs], kf == 0, kf == 7, ["wkrR"], [("ps", 1)])
        P.tt("dve", m1[64:96, :], C.ps[0][64:96, :], cosT[64:96, cs], ALU.mult, [("ps", 0), ("cosT", c)], ["m1"])
        P.tt("dve", m2[64:96, :], C.ps[1][64:96, :], sinT[64:96, cs], ALU.mult, [("ps", 1), ("sinT", c)], ["m2"])
        P.tt("dve", kropeT[64:96, cs], m1[64:96, :], m2[64:96, :], ALU.add, ["m1", "m2"], [("krope", c)])
    P.end_phase()


def phase_mla(P, C, L, hnT, ymT, cosT, sinT, cqnT, ckvnT, kropeT):
    j = L // 2
    W = C.od_w_in[j]
    SC = 96 ** -0.5
    P.begin_phase()
    wuq = P.sb("wuq", [128, 2, 768], BF16)
    load_w_bf16(P, C, wuq, C.mla_w_uq[j], 2, 768, "wuq")
    wuq_keys = [("w", "wuq", k) for k in range(2)]
    wuqR = P.sb("wuqR", [128, 2, 8, 96], BF16)
    P.memset("pool", wuqR[:], 0.0, ["wuqR"])
    wuq4 = wuq[:].rearrange("p k (h c) -> p k h c", h=8)
    for t in range(2):
        P.ts("dve", wuqR[:, t, :, 64:80], wuq4[:, t, :, 80:96], -1.0, None, ALU.mult, None, wuq_keys + ["wuqR"], ["wuqR"])
        P.copy("dve", wuqR[:, t, :, 80:96], wuq4[:, t, :, 64:80], wuq_keys + ["wuqR"], ["wuqR"])
    wukv = P.sb("wukv", [128, 1, 1024], BF16)
    load_w_bf16(P, C, wukv, C.mla_w_ukv[j], 1, 1024, "wukv")
    wukv_k = [("w", "wukv", 0)]
    m1s = [P.sb("m1_%d" % i, [128, 512], F32) for i in range(2)]
    m2s = [P.sb("m2_%d" % i, [128, 512], F32) for i in range(2)]
    QT = [P.sb("QT%d" % i, [128, 2, S], BF16) for i in range(2)]
    KT = [P.sb("KT%d" % i, [128, 2, S], BF16) for i in range(2)]
    VA = [P.sb("VA%d" % i, [128, NT, 2, 128], BF16) for i in range(2)]
    sgm = [P.sb("sgmT%d" % i, [128, S], BF16) for i in range(2)]
    PT = [P.sb("PT%d" % i, [128, 512], BF16) for i in range(4)]
    rec = P.sb("rec", [128, 512], F32)
    tmpf = P.sb("tmpf", [128, 512], F32)
    gst = [P.sb("gst%d" % i, [128, 8, 128], F32) for i in range(2)]
    gbf = [P.sb("gbf%d" % i, [128, 8, 128], BF16) for i in range(2)]
    wukv3 = wukv[:, 0, :].rearrange("p (h c) -> p h c", h=8)
    for i in range(2):
        P.memset("dve" if i == 0 else "pool", VA[i][:], 1.0, [("VA", i)])
    cnt = {"ps": 0, "m": 0}

    def pbank():
        b = (4, 5, 7)[cnt["ps"] % 3]
        cnt["ps"] += 1
        return b

    def proj_units(hp, sl):
        for par in range(2):
            h = 2 * hp + par
            for c in range(4):
                cs = slice(c * 512, (c + 1) * 512)
                b = pbank()
                P.mm(C.ps[b][:64, :], wukv[:, 0, h * 128:h * 128 + 64], ckvnT[:, cs], True, True, wukv_k, [("ps", b)])
                P.copy("dve", KT[sl][0:64, par, cs], C.ps[b][:64, :], [("ps", b)], [("KT", sl, par)])
                bA = pbank()
                bR = pbank()
                for t in range(2):
                    P.mm(C.ps[bA][:96, :], wuq[:, t, h * 96:(h + 1) * 96], cqnT[:, t, cs], t == 0, t == 1, wuq_keys, [("ps", bA)])
                for t in range(2):
                    P.mm(C.ps[bR][:96, :], wuqR[:, t, h, :], cqnT[:, t, cs], t == 0, t == 1, ["wuqR"], [("ps", bR)])
                P.copy("dve", QT[sl][0:64, par, cs], C.ps[bA][0:64, :], [("ps", bA)], [("QT", sl, par)])
                mi = cnt["m"] % 2
                cnt["m"] += 1
                m1, m2 = m1s[mi], m2s[mi]
                P.tt("dve", m1[64:96, :], C.ps[bA][64:96, :], cosT[64:96, cs], ALU.mult, [("ps", bA)], [("m1", mi)])
                P.tt("dve", m2[64:96, :], C.ps[bR][64:96, :], sinT[64:96, cs], ALU.mult, [("ps", bR)], [("m2", mi)])
                P.tt("dve", QT[sl][64:96, par, cs], m1[64:96, :], m2[64:96, :], ALU.add, [("m1", mi), ("m2", mi)], [("QT", sl, par)])
                yield
            P.copy("dve", KT[sl][64:96, par, :], kropeT[64:96, :], [], [("KT", sl, par)])
        for kt in range(NT):
            b = pbank()
            P.mm(C.ps[b][:, 0:128], ckvnT[:, kt * 128:(kt + 1) * 128], wukv3[:, 2 * hp:2 * hp + 2, 64:128], True, True,
                 wukv_k, [("ps", b)])
            P.copy("dve", VA[sl][:, kt, 0, 0:64], C.ps[b][:, 0:64], [("ps", b), ("VA", sl)], [("VA", sl)])
            P.copy("dve", VA[sl][:, kt, 1, 64:128], C.ps[b][:, 64:128], [("ps", b), ("VA", sl)], [("VA", sl)])
            if kt % 2 == 1:
                yield
        c0 = OD["gm"] + hp * 128
        P.dma(gst[sl][:], W[:, c0:c0 + 128].rearrange("(k p) c -> p k c", p=128), writes=[("gst", sl)])
        P.copy("pool", gbf[sl][:], gst[sl][:], [("gst", sl)], [("gbf", sl)])
        for c in range(4):
            cs = slice(c * 512, (c + 1) * 512)
            b = pbank()
            for kf in range(8):
                P.mm(C.ps[b][:], gbf[sl][:, kf, :], hnT[:, kf, cs], kf == 0, kf == 7, [("gbf", sl)], [("ps", b)])
            P.act(sgm[sl][:, cs], C.ps[b][:], AF.Silu, [("ps", b)], [("sgm", sl, c)])
            yield

    for _ in proj_units(0, 0):
        pass
    for hp in range(4):
        sl = hp % 2
        tile_i = 4 + hp
        gen = proj_units(hp + 1, 1 - sl) if hp < 3 else None
        groups = []
        for par in range(2):
            for qc in range(4):
                ob = 2 + (qc % 2)
                nk = 4 * qc + 4
                steps = []
                for kt in range(nk):
                    o = max(0, kt - 4 * qc)
                    q0 = qc * 512 + o * 128
                    ncol = 512 - o * 128
                    steps.append(dict(c0=o * 128, c1=512, lhsK=KT[sl][0:96, par, kt * 128:(kt + 1) * 128], rhsQ=QT[sl][0:96, par, q0:q0 + ncol],
                                      rk=[("KT", sl, par), ("QT", sl, par)], scale=SC,
                                      masks=[(o * 128, (o + 1) * 128, C.tri[:])] if kt >= 4 * qc else [],
                                      lhsV=VA[sl][:, kt, par, :], ob=ob, rv=[("VA", sl)]))

                def fin(par=par, qc=qc, ob=ob, tile_i=tile_i, sl=sl):
                    lo = slice(par * 64, par * 64 + 64)
                    qs = slice(qc * 512, (qc + 1) * 512)
                    softmax_pv_finish(P, C, C.ps[ob], par, ymT[lo, tile_i, qs], rec, tmpf, sgm[sl][lo, qs], ("sgm", sl, qc),
                                      ("ps", ob), ("ymT", tile_i, par, qc), True)
                groups.append((steps, fin))

        def tick(idx, gen=gen):
            if gen is not None and idx % 2 == 1:
                next(gen, None)
        run_attention(P, C, groups, PT, sbanks=(0, 1, 6), tick=tick, mask_eng="pool")
        if gen is not None:
            for _ in gen:
                pass
    P.end_phase()


def odd_layer(P, C, L, src, dst, do_nsa=True, do_mla=True):
    P.push_scope()
    ymT = P.sb("ymT", [128, 8, S], BF16)
    hnT = P.sb("hnT", [128, 8, S], BF16)
    phase_prenorm(P, C, L, src, hnT)
    if not (do_nsa and do_mla):
        P.begin_phase()
        P.memset("pool", ymT[:], 0.0, ["ymT"])
        P.end_phase()
    if do_nsa:
        nsa_mixer(P, C, L, hnT, ymT)
    if do_mla:
        P.push_scope()
        cosT = P.sb("cosT", [128, S], F32)
        sinT = P.sb("sinT", [128, S], F32)
        cqnT = P.sb("cqnT", [128, 2, S], BF16)
        ckvnT = P.sb("ckvnT", [128, S], BF16)
        kropeT = P.sb("kropeT", [128, S], BF16)
        phase_mla_prep(P, C, L, hnT, cosT, sinT, cqnT, ckvnT, kropeT)
        phase_mla(P, C, L, hnT, ymT, cosT, sinT, cqnT, ckvnT, kropeT)
        P.pop_scope()
    if C.dbg is not None:
        P.begin_phase()
        cv = [P.sb("dbgc%d" % i, [128, S], F32) for i in range(2)]
        for t in range(8):
            P.copy("dve", cv[t % 2][:], ymT[:, t, :], [], [("cv", t % 2)])
            P.dma(C.dbg[t], cv[t % 2][:], reads=[("cv", t % 2)], writes=[("dbg", t % 2)])
        P.end_phase()
    phase_out(P, C, L, ymT, src, dst)
    P.pop_scope()


class SlabLoader:
    def __init__(self, P, tag):
        self.P = P
        self.tag = tag
        self.st = [P.sb("sls_%s%d" % (tag, i), [128, 8, 128], F32) for i in range(2)]
        self.bf = [P.sb("slb_%s%d" % (tag, i), [128, 8, 128], BF16) for i in range(2)]
        self.n = 0

    def load(self, W, c0, n, eng="pool"):
        s = self.n % 2
        self.n += 1
        P = self.P
        P.dma(self.st[s][:, :, 0:n], W[:, c0:c0 + n].rearrange("(k p) c -> p k c", p=128), writes=[("sls", self.tag, s)])
        P.copy(eng, self.bf[s][:, :, 0:n], self.st[s][:, :, 0:n], [("sls", self.tag, s)], [("slb", self.tag, s)])
        return self.bf[s], ("slb", self.tag, s)


def proj_fm(P, C, slab, skey, n, hnT, consume, banks=(6, 7)):
    for c in range(4):
        cs = slice(c * 512, (c + 1) * 512)
        b = banks[c % len(banks)]
        for kf in range(8):
            P.mm(C.ps[b][0:n, :], slab[:, kf, 0:n], hnT[:, kf, cs], kf == 0, kf == 7, [skey], [("ps", b)])
        consume(c, cs, C.ps[b], ("ps", b))


def gelu_tanh(P, x_ps, xkey, out, okey, y2, t1, sg, tag):
    P.act(y2, x_ps, AF.Square, [xkey], [tag + "y2"])
    P.ts("dve", y2, y2, 0.044715, 1.0, ALU.mult, ALU.add, [tag + "y2"], [tag + "y2"])
    P.tt("dve", t1, y2, x_ps, ALU.mult, [tag + "y2", xkey], [tag + "t1"])
    P.act(sg, t1, AF.Sigmoid, [tag + "t1"], [tag + "sg"], scale=1.5957691216057308)
    P.tt("dve", out, sg, x_ps, ALU.mult, [tag + "sg", xkey], [okey])


def nsa_mixer(P, C, L, hnT, ymT):
    j = L // 2
    W = C.od_w_in[j]
    P.push_scope()
    QA = P.sb("QaugT", [128, 8, S], BF16)
    KS = P.sb("KselA", [128, 2, S], BF16)
    KW = P.sb("KwinA", [128, 2, S], BF16)
    SgT = P.sb("SgT", [32, S], BF16)
    KcA = P.sb("KcA", [128, 2, 128], BF16)
    VcA = P.sb("VcA", [128, 2, 2, 128], BF16)
    gsel = P.sb("gsel", [32, 3, 8, 128], BF16)

    P.begin_phase()
    SL = SlabLoader(P, "a")
    P.memset("pool", QA[64:96, :, :], 0.0, ["QAmask"])
    P.dma(QA[96:100, :, :], C.qaug[:, :, :], writes=["QAaug"])
    P.dma(gsel[:], C.gsel[:, :, :, :], writes=["gsel"])
    for y in range(2):
        P.dma(KS[64:96, y, :], C.kaugc[0:32, :], writes=[("KSe", y)])
        P.dma(KS[96:100, y, :], C.kaugc[32:36, :], writes=[("KSa", y)])
        P.dma(KW[96:100, y, :], C.kaugc[32:36, :], writes=[("KWa", y)])
    P.memset("pool", KW[64:96, :, :], 0.0, ["KWz"])
    P.memset("pool", SgT[:], 0.0, ["SgT0"])
    import os
    part = int(os.environ.get("NSA_PART", "9"))
    for hp in range(4 if part >= 1 else 0):
        slab, sk = SL.load(W, OD["q"] + hp * 128, 128)

        def cons_q(c, cs, ps, pk, hp=hp):
            for par in range(2):
                P.act(QA[0:64, 2 * hp + par, cs], ps[par * 64:(par + 1) * 64, :], AF.Copy, [pk], [("QA", 2 * hp + par, c)], scale=0.125)
        proj_fm(P, C, slab, sk, 128, hnT, cons_q)
    for (i, dst, nm) in (((2, KS, "KS"), (4, KW, "KW")) if part >= 2 else ()):
        slab, sk = SL.load(W, OD["kv"] + i * 128, 128)

        def cons_k(c, cs, ps, pk, dst=dst, nm=nm):
            for y in range(2):
                P.copy("act" if y == 0 else "dve", dst[0:64, y, cs], ps[y * 64:(y + 1) * 64, :], [pk], [(nm, y, c)])
        proj_fm(P, C, slab, sk, 128, hnT, cons_k)
    def cons_g(c, cs, ps, pk):
        P.act(SgT[0:32, cs], ps[0:32, :], AF.Sigmoid, [pk, "SgT0"], [("SgT", c)])
    if part >= 3:
        slab, sk = SL.load(W, OD["gl"], 32)
        proj_fm(P, C, slab, sk, 32, hnT, cons_g)
    P.end_phase()
    stop = os.environ.get("NSA_STOP", "")
    if stop == "a":
        P.pop_scope()
        return

    P.begin_phase()
    SL = SlabLoader(P, "b")
    P.memset("pool", VcA[:], 1.0, ["VcA"])
    P.memset("pool", KcA[64:96, :, :], 0.0, ["KcAz"])
    for y in range(2):
        P.dma(KcA[96:100, y, :], C.kaugcmp[:, :], writes=[("KcAa", y)])
    K2 = [P.sb("K2_%d" % y, [128, S], BF16) for y in range(2)]
    G = P.sb("Gc", [128, 16, 128], BF16)
    GT = P.sb("GTc", [128, 2, 128], BF16)
    w1st = P.sb("w1st", [128, 16, 256], F32)
    w1b = P.sb("w1b", [128, 16, 256], BF16)
    w2st = P.sb("w2st", [128, 2, 64], F32)
    w2b = P.sb("w2b", [128, 2, 64], BF16)
    pos2 = P.sb("pos2", [128, 2, 16], F32)
    P.dma(pos2[:], C.nsapos2[j], writes=["pos2"])
    y2 = P.sb("gy2", [128, 128], F32)
    t1 = P.sb("gt1", [128, 128], F32)
    sg = P.sb("gsg", [128, 128], F32)
    P.memset("pool", GT[:], 0.0, ["GT0"])
    for kv in range(2):
        w1 = (C.nsa_ck_w1 if kv == 0 else C.nsa_cv_w1)[j]
        w2 = (C.nsa_ck_w2 if kv == 0 else C.nsa_cv_w2)[j]
        P.dma(w1st[:], w1.rearrange("(j p) c -> p j c", p=128), writes=["w1st"])
        P.copy("dve", w1b[:, 0:8, :], w1st[:, 0:8, :], ["w1st"], ["w1b_a"])
        P.copy("act", w1b[:, 8:16, :], w1st[:, 8:16, :], ["w1st"], ["w1b_b"])
        P.dma(w2st[:], w2.rearrange("(k p) c -> p k c", p=128), writes=["w2st"])
        P.copy("dve", w2b[:], w2st[:], ["w2st"], ["w2b"])
        slab, sk = SL.load(W, OD["kv"] + kv * 128, 128, eng="dve")
        for y in range(2):
            P.memset("pool", K2[y][64:128, S - 1:S], 0.0, [("K2z", y)])

        def cons_c(c, cs, ps, pk):
            for y in range(2):
                src = ps[y * 64:(y + 1) * 64, :]
                eng = "act" if y == 0 else "dve"
                P.copy(eng, K2[y][0:64, cs], src, [pk], [("K2a", y, c)])
                if c == 0:
                    P.copy(eng, K2[y][64:128, 0:511], ps[y * 64:(y + 1) * 64, 1:512], [pk], [("K2b", y, c)])
                else:
                    P.copy(eng, K2[y][64:128, c * 512 - 1:c * 512 + 511], src, [pk, ("K2z", y)], [("K2b", y, c)])
        if part >= 1:
            proj_fm(P, C, slab, sk, 128, hnT, cons_c)
        k2keys = [[("K2a", y, c) for c in range(4)] + [("K2b", y, c) for c in range(4)] + [("K2z", y)] for y in range(2)]
        for y in range(2 if part >= 2 else 0):
            for jj in range(16):
                P.ts("dve", G[:, jj, 0:127], K2[y][:, 2 * jj:2 * jj + 2017:16], pos2[:, kv, jj:jj + 1], None, ALU.add, None,
                     k2keys[y] + ["pos2"], [("G", jj)])
            if part < 3:
                continue
            for ht in range(2):
                b = 4 + ht
                for jj in range(16):
                    P.mm(C.ps[b][:, 0:127], w1b[:, jj, ht * 128:(ht + 1) * 128], G[:, jj, 0:127], jj == 0, jj == 15, [("G", jj), "w1b_a", "w1b_b"], [("ps", b)])
                gelu_tanh(P, C.ps[b][:, 0:127], ("ps", b), GT[:, ht, 0:127], ("GT", ht), y2[:, 0:127], t1[:, 0:127], sg[:, 0:127], "g")
            if part < 4:
                continue
            if kv == 0:
                for ht in range(2):
                    P.mm(C.ps[2][0:64, 0:128], w2b[:, ht, :], GT[:, ht, :], ht == 0, ht == 1, [("GT", ht), "GT0", "w2b"], [("ps", 2)])
                P.copy("act", KcA[0:64, y, :], C.ps[2][0:64, 0:128], [("ps", 2)], [("KcA", y)])
            elif part >= 5:
                for ht in range(2):
                    P.mm(C.ps[3][:, 0:64], GT[:, ht, :], w2b[:, ht, :], ht == 0, ht == 1, [("GT", ht), "GT0", "w2b"], [("ps", 3)])
                if part >= 6:
                    P.copy("dve", VcA[:, y, 0, 0:64], C.ps[3][:, 0:64], [("ps", 3), "VcA"], [("VcAw", y, 0)])
                    P.copy("dve", VcA[:, y, 1, 64:128], C.ps[3][:, 0:64], [("ps", 3), "VcA"], [("VcAw", y, 1)])
    P.end_phase()
    if stop == "b":
        P.pop_scope()
        return

    P.begin_phase()
    addm = P.sb("addm", [128, S], BF16)
    P.dma(addm[:], C.addmask[:, :], writes=["addm"])
    ovl = P.sb("ovl", [128, 64], BF16)
    P.dma(ovl[:], C.ovl[:, :], writes=["ovl"])
    selc = P.sb("selc", [128, NT, 2, 32], F32)
    P.dma(selc[:], C.selc[:, :, :, :], writes=["selc"])
    pslc = P.sb("pslcT", [32, 2, S], F32)
    sm = [P.sb("smc%d" % i, [128, 512], F32) for i in range(2)]
    PT = [P.sb("PTc%d" % i, [128, 512], BF16) for i in range(2)]
    rec = P.sb("rec", [128, 512], F32)
    tmpf = P.sb("tmpf", [128, 512], F32)
    rec2 = P.sb("rec2", [32, 512], F32)
    tmp2 = P.sb("tmp2", [32, 512], F32)
    items = [(h, qc) for h in range(8) for qc in range(4)]

    def cmpA(i):
        h, qc = items[i]
        y = h // 4
        qs = slice(qc * 512, (qc + 1) * 512)
        s_ = i % 2
        P.mm(C.ps[s_][:], KcA[0:100, y, :], QA[0:100, h, qs], True, True, [], [("ps", s_)])
        P.tt("dve", sm[s_][:], C.ps[s_][:], addm[:, qs], ALU.add, [("ps", s_), "addm"], [("sm", s_)])
        P.act(PT[s_][:], sm[s_][:], AF.Exp, [("sm", s_)], [("PT", s_)])

    def cmpB(i):
        h, qc = items[i]
        y, par, tile_i, hh = h // 4, h % 2, h // 2, h % 4
        lo = slice(par * 64, par * 64 + 64)
        qs = slice(qc * 512, (qc + 1) * 512)
        s_ = i % 2
        ob = 2 + s_
        P.mm(C.ps[ob][:], VcA[:, y, par, :], PT[s_][:], True, True, [("PT", s_)], [("ps", ob)])
        P.mm(C.ps[5][0:64, :], ovl[:], PT[s_][:], True, True, [("PT", s_), "ovl"], [("ps", 5)])
        P.mm(C.ps[4][:], gsel[0:32, 0, h, :], SgT[0:32, qs], True, True, ["gsel"], [("ps", 4)])
        softmax_pv_finish(P, C, C.ps[ob], par, ymT[lo, tile_i, qs], rec, tmpf, None, None, ("ps", ob), ("ymT", tile_i, par, qc), True,
                          clamp=True, gate_ps=C.ps[4], gkey=("ps", 4))
        P.ts("dve", rec2[:], C.ps[5][32:64, :], 1e-18, None, ALU.max, None, [("ps", 5)], ["rec2"])
        P.act(rec2[:], rec2[:], AF.Ln, ["rec2"], ["rec2"])
        P.act(rec2[:], rec2[:], AF.Exp, ["rec2"], ["rec2"], scale=-1.0)
        if hh == 0:
            P.tt("dve", pslc[:, y, qs], C.ps[5][0:32, :], rec2[:], ALU.mult, [("ps", 5), "rec2"], [("pslc", y, qc)])
        else:
            P.tt("dve", tmp2[:], C.ps[5][0:32, :], rec2[:], ALU.mult, [("ps", 5), "rec2"], ["tmp2"])
            P.tt("dve", pslc[:, y, qs], pslc[:, y, qs], tmp2[:], ALU.add, ["tmp2", ("pslc", y, qc)], [("pslc", y, qc)])

    cmpA(0)
    for i in range(len(items)):
        if i + 1 < len(items):
            cmpA(i + 1)
        cmpB(i)
    sc = [P.sb("scs%d" % i, [128, 32], F32) for i in range(2)]
    m8 = [P.sb("m8s%d" % i, [128, 8], F32) for i in range(2)]
    ng = [P.sb("ngs%d" % i, [128, 32], F32) for i in range(2)]
    sitems = [(y, qt) for y in range(2) for qt in range(NT)]

    def selA(i):
        y, qt = sitems[i]
        s_ = i % 2
        ts_ = slice(qt * 128, (qt + 1) * 128)
        P.tr(C.ps[6 + s_][:, 0:32], pslc[:, y, ts_], C.identf[0:32, 0:32], [("pslc", y, qt // 4)], [("ps", 6 + s_)])
        P.tt("dve", sc[s_][:], C.ps[6 + s_][:, 0:32], selc[:, qt, 0, :], ALU.mult, [("ps", 6 + s_), "selc"], [("sc", s_)])
        P.tt("dve", sc[s_][:], sc[s_][:], selc[:, qt, 1, :], ALU.add, [("sc", s_), "selc"], [("sc", s_)])
        P.op("dve", lambda e, s_=s_: e.max(out=m8[s_][:], in_=sc[s_][:]), [("sc", s_)], [("m8", s_)])
        P.ts("dve", ng[s_][:], sc[s_][:], m8[s_][:, 7:8], 30000.0, ALU.is_ge, ALU.mult, [("sc", s_), ("m8", s_)], [("ng", s_)])
        P.ts("dve", ng[s_][:], ng[s_][:], -30000.0, None, ALU.add, None, [("ng", s_)], [("ng", s_)])

    def selB(i):
        y, qt = sitems[i]
        s_ = i % 2
        ts_ = slice(qt * 128, (qt + 1) * 128)
        P.tr(C.ps[4 + s_][0:32, 0:128], ng[s_][:], C.identf[:], [("ng", s_)], [("ps", 4 + s_)])
        for hh in range(4):
            eng = "act" if s_ == 0 else "dve"
            P.copy(eng, QA[64:96, 4 * y + hh, ts_], C.ps[4 + s_][0:32, 0:128], [("ps", 4 + s_), "QAmask"], [("QAm", 4 * y + hh, qt)])

    selA(0)
    for i in range(len(sitems)):
        if i + 1 < len(sitems):
            selA(i + 1)
        selB(i)
    P.end_phase()
    if stop == "c":
        P.pop_scope()
        return

    for br in (1, 2):
        P.begin_phase()
        SL = SlabLoader(P, "v%d" % br)
        VA = P.sb("VAn", [128, NT, 2, 2, 128], BF16)
        P.memset("dve", VA[:], 1.0, ["VA"])
        slab, sk = SL.load(W, OD["kv"] + (3 if br == 1 else 5) * 128, 128)
        for kt in range(NT):
            b = 6 + (kt % 2)
            for kf in range(8):
                P.mm(C.ps[b][:, 0:128], hnT[:, kf, kt * 128:(kt + 1) * 128], slab[:, kf, :], kf == 0, kf == 7, [sk], [("ps", b)])
            pv = C.ps[b][:, 0:128].rearrange("p (y c) -> p y c", y=2)
            eng = "act" if kt % 2 == 0 else "dve"
            P.copy(eng, VA[:, kt, :, 0, 0:64], pv, [("ps", b), "VA"], [("VAw", kt, 0)])
            P.copy(eng, VA[:, kt, :, 1, 64:128], pv, [("ps", b), "VA"], [("VAw", kt, 1)])
        sgn = None
        if br == 2:
            sgn = P.sb("sgn", [128, 4, S], BF16)
            for hp in range(4):
                slab2, sk2 = SL.load(W, OD["gn"] + hp * 128, 128, eng="dve")

                def cons_gn(c, cs, ps, pk, hp=hp):
                    P.act(sgn[:, hp, cs], ps[:], AF.Silu, [pk], [("sgn", hp, c)])
                proj_fm(P, C, slab2, sk2, 128, hnT, cons_gn)
        KA = KS if br == 1 else KW
        PT = [P.sb("PTn%d" % i, [128, 512], BF16) for i in range(6)]
        rec = P.sb("rec", [128, 512], F32)
        tmpf = P.sb("tmpf", [128, 512], F32)
        groups = []
        for h in range(8):
            y, par, tile_i = h // 4, h % 2, h // 2
            for qc in range(4):
                ob = 2 + (qc % 2)
                kts = list(range(0, 4 * qc + 4)) if br == 1 else list(range(max(0, 4 * qc - 4), 4 * qc + 4))
                steps = []
                for kt in kts:
                    o = kt - 4 * qc
                    rlo = max(o, 0)
                    rhi = 3 if br == 1 else min(o + 4, 3)
                    c0, c1 = rlo * 128, (rhi + 1) * 128
                    masks = []
                    if o >= 0:
                        masks.append((o * 128, (o + 1) * 128, C.tri[:]))
                    if br == 2 and o <= -1:
                        masks.append(((o + 4) * 128, (o + 5) * 128, C.wmask[:]))
                    steps.append(dict(c0=c0, c1=c1, lhsK=KA[0:100, y, kt * 128:(kt + 1) * 128],
                                      rhsQ=QA[0:100, h, qc * 512 + c0:qc * 512 + c1], scale=None, masks=masks,
                                      lhsV=VA[:, kt, y, par, :], ob=ob, rv=[("VAw", kt, par)]))

                def fin(h=h, par=par, qc=qc, ob=ob, tile_i=tile_i):
                    lo = slice(par * 64, par * 64 + 64)
                    qs = slice(qc * 512, (qc + 1) * 512)
                    P.mm(C.ps[4][:], gsel[0:32, br, h, :], SgT[0:32, qs], True, True, [], [("ps", 4)])
                    softmax_pv_finish(P, C, C.ps[ob], par, ymT[lo, tile_i, qs], rec, tmpf,
                                      sgn[lo, tile_i, qs] if br == 2 else None, ("sgn", tile_i, qc) if br == 2 else None,
                                      ("ps", ob), ("ymT", tile_i, par, qc), False, clamp=False, gate_ps=C.ps[4], gkey=("ps", 4))
                groups.append((steps, fin))
        run_attention(P, C, groups, PT, sbanks=(0, 1, 5, 7), depth=4)
        P.end_phase()
    P.pop_scope()
```

```python
import math
from contextlib import ExitStack

import numpy as np
import concourse.bass as bass
import concourse.mybir as mybir
from concourse.bass_utils import run_bass_kernel_spmd

F32 = mybir.dt.float32
BF16 = mybir.dt.bfloat16
I32 = mybir.dt.int32
ALU = mybir.AluOpType
AF = mybir.ActivationFunctionType

S = 2048
D = 1024
NT = S // 128
EPS = 1e-6
ENGS = ("pe", "act", "dve", "pool", "sp")
CENG = ("pe", "act", "dve", "pool")
N_DMA_SEMS = 84
TWO_PI = 2.0 * math.pi


class Prog:
    def __init__(self, nc):
        self.nc = nc
        self.gstack = ExitStack()
        self.engsem = {e: self.gstack.enter_context(nc.semaphore("es_" + e)) for e in CENG}
        self.dsems = [self.gstack.enter_context(nc.semaphore("ds%d" % i)) for i in range(N_DMA_SEMS)]
        self.engcnt = {e: 0 for e in CENG}
        self.dcnt = [0] * N_DMA_SEMS
        self.scopes = []
        self.uid = 0
        self.n_instr = 0

    def close(self):
        self.gstack.close()

    def gsb(self, name, shape, dt):
        return self.gstack.enter_context(self.nc.sbuf_tensor(name, list(shape), dt))

    def gps(self, name, shape, dt=F32):
        return self.gstack.enter_context(self.nc.psum_tensor(name, list(shape), dt))

    def sb(self, name, shape, dt):
        self.uid += 1
        return self.scopes[-1].enter_context(self.nc.sbuf_tensor("%s_%d" % (name, self.uid), list(shape), dt))

    def push_scope(self):
        self.scopes.append(ExitStack())

    def pop_scope(self):
        self.scopes.pop().close()

    def begin_phase(self):
        self.push_scope()
        self.ins = []
        self.last_w = {}
        self.readers = {}
        self.eng_seq = {e: [] for e in ENGS}
        self.semmap = {}

    def _add(self, eng, fn, reads, writes, kind, semkey=None):
        idx = len(self.ins)
        deps = set()
        for k in reads:
            if k in self.last_w:
                deps.add((self.last_w[k], 0))
            if isinstance(k, tuple) and k[0] == "ps":
                for r in self.readers.get(k, ()):
                    if self.ins[r][0] != eng:
                        deps.add((r, 1))
        for k in writes:
            if k in self.last_w:
                deps.add((self.last_w[k], 1))
            for r in self.readers.get(k, ()):
                deps.add((r, 2))
        for k in reads:
            self.readers.setdefault(k, []).append(idx)
        for k in writes:
            self.last_w[k] = idx
            self.readers[k] = []
        self.ins.append((eng, fn, kind, semkey, deps))
        self.eng_seq[eng].append(idx)
        return idx

    def op(self, eng, fn, reads=(), writes=()):
        return self._add(eng, fn, list(reads), list(writes), "c")

    def dma(self, out, in_, reads=(), writes=(), q="sp", **kw):
        semkey = (q, tuple(writes))
        if semkey not in self.semmap:
            assert len(self.semmap) < N_DMA_SEMS, "too many dma sem keys"
            self.semmap[semkey] = len(self.semmap)
        fn = lambda e: e.dma_start(out=out, in_=in_, **kw)
        return self._add(q, fn, list(reads), list(writes), "d", semkey)

    def act(self, out, in_, func, r, w, **kw):
        self.op("act", lambda e: e.activation(out=out, in_=in_, func=func, **kw), r, w)

    def tt(self, eng, out, in0, in1, op, r, w):
        self.op(eng, lambda e: e.tensor_tensor(out=out, in0=in0, in1=in1, op=op), r, w)

    def ts(self, eng, out, in0, s1, s2, op0, op1, r, w, **kw):
        if s2 is None:
            self.op(eng, lambda e: e.tensor_scalar(out=out, in0=in0, scalar1=s1, scalar2=None, op0=op0, **kw), r, w)
        else:
            self.op(eng, lambda e: e.tensor_scalar(out=out, in0=in0, scalar1=s1, scalar2=s2, op0=op0, op1=op1, **kw), r, w)

    def stt(self, eng, out, in0, scalar, in1, op0, op1, r, w):
        self.op(eng, lambda e: e.scalar_tensor_tensor(out=out, in0=in0, scalar=scalar, in1=in1, op0=op0, op1=op1), r, w)

    def copy(self, eng, out, in_, r, w):
        if eng == "act":
            self.op(eng, lambda e: e.copy(out=out, in_=in_), r, w)
        else:
            self.op(eng, lambda e: e.tensor_copy(out=out, in_=in_), r, w)

    def memset(self, eng, ap, val, w):
        self.op(eng, lambda e: e.memset(ap, val), (), w)

    def mm(self, out, lhsT, rhs, start, stop, r, w, skip=False):
        if skip:
            self.op("pe", lambda e: e.matmul(out, lhsT=lhsT, rhs=rhs, start=start, stop=stop, skip_group_check=True), r, w)
        else:
            self.op("pe", lambda e: e.matmul(out, lhsT=lhsT, rhs=rhs, start=start, stop=stop), r, w)

    def tr(self, out, in_, ident, r, w):
        self.op("pe", lambda e: e.transpose(out=out, in_=in_, identity=ident), r, w)

    def recip(self, out, in_, r, w):
        self.op("dve", lambda e: e.reciprocal(out=out, in_=in_), r, w)

    def end_phase(self):
        nc = self.nc
        ins = self.ins
        pos = {}
        for e in ENGS:
            for p, idx in enumerate(self.eng_seq[e]):
                pos[idx] = p
        WIN = 3

        def edge_needed(idx, d, typ):
            eng = ins[idx][0]
            deng, _, dkind, _, _ = ins[d]
            if dkind == "d" or ins[idx][2] == "d":
                return True
            if deng == eng:
                return eng != "pe"
            return True

        pruned = []
        for idx, (eng, fn, kind, semkey, deps) in enumerate(ins):
            best = {}
            keep = set()
            for (d, typ) in deps:
                if not edge_needed(idx, d, typ):
                    continue
                if ins[d][2] == "d":
                    keep.add((d, typ))
                    continue
                pe_ = ins[d][0]
                if pe_ not in best or pos[d] > pos[best[pe_][0]]:
                    best[pe_] = (d, typ)
            keep.update(best.values())
            pruned.append(keep)
        needed = set()
        for idx in range(len(ins)):
            for (d, typ) in pruned[idx]:
                needed.add(d)
        for e in CENG:
            if self.eng_seq[e]:
                needed.add(self.eng_seq[e][-1])
        token = {}
        finals = {}
        for idx, (eng, fn, kind, semkey, deps) in enumerate(ins):
            if kind == "d":
                si = self.semmap[semkey]
                self.dcnt[si] += 16
                token[idx] = (("d", si), self.dcnt[si])
                finals[("d", si)] = self.dcnt[si]
            elif idx in needed:
                self.engcnt[eng] += 1
                token[idx] = (("e", eng), self.engcnt[eng])
                finals[("e", eng)] = self.engcnt[eng]
        progs = {e: [] for e in ENGS}
        waited = {e: {} for e in ENGS}
        for idx, (eng, fn, kind, semkey, deps) in enumerate(ins):
            waits = {}
            for (d, typ) in pruned[idx]:
                sn, val = token[d]
                if waited[eng].get(sn, 0) >= val:
                    continue
                waits[sn] = max(waits.get(sn, 0), val)
            for sn, val in waits.items():
                waited[eng][sn] = val
            progs[eng].append((waits, fn, token.get(idx)))
        self.n_instr += len(ins)

        def sem_of(sn):
            return self.dsems[sn[1]] if sn[0] == "d" else self.engsem[sn[1]]

        def run_engine(e, name):
            for waits, fn, inc in progs[name]:
                for sn, val in waits.items():
                    e.wait_ge(sem_of(sn), val)
                r = fn(e)
                if inc is not None:
                    r.then_inc(sem_of(inc[0]), 16 if inc[0][0] == "d" else 1)

        with nc.Block() as block:
            @block.tensor
            def _(e):
                run_engine(e, "pe")

            @block.scalar
            def _(e):
                run_engine(e, "act")

            @block.vector
            def _(e):
                run_engine(e, "dve")

            @block.gpsimd
            def _(e):
                run_engine(e, "pool")
                for sn, val in finals.items():
                    if sn[0] == "d" and any(k[0] == "pool" and self.semmap[k] == sn[1] for k in self.semmap):
                        e.wait_ge(sem_of(sn), val)

            @block.sync
            def _(e):
                run_engine(e, "sp")
                for sn, val in finals.items():
                    e.wait_ge(sem_of(sn), val)
        nc.all_engine_barrier()
        self.pop_scope()


class Ctx:
    pass


def load_w_bf16(P, C, dst, src, nkf, cols, tag, conv_engs=("dve", "act")):
    stage = [P.sb("wst_%s%d" % (tag, i), [128, cols], F32) for i in range(2)]
    for kf in range(nkf):
        s = kf % 2
        P.dma(stage[s][:], src[kf * 128:(kf + 1) * 128, :], writes=[("wst", tag, s)])
        P.copy(conv_engs[kf % len(conv_engs)], dst[:, kf, :], stage[s][:], [("wst", tag, s)], [("w", tag, kf)])


def rstd_from_ssq(P, C, st, n, key):
    P.act(st[:, 1:2], st[:, 0:1], AF.Sqrt, [key + (0,)], [key + (1,)], scale=1.0 / n, bias=C.epsb[:])
    P.recip(st[:, 2:3], st[:, 1:2], [key + (1,)], [key + (2,)])


def phase_prenorm(P, C, L, src, hnT):
    P.begin_phase()
    gb = P.sb("gpre", [128, D], F32)
    P.dma(gb[:], C.pre_norm[L:L + 1, :].partition_broadcast(128), writes=["gpre"])
    ht = [P.sb("ht%d" % i, [128, D], F32) for i in range(2)]
    junk = P.sb("junk", [128, D], BF16)
    hnb = [P.sb("hnb%d" % i, [128, D], BF16) for i in range(2)]
    st = [P.sb("st%d" % i, [128, 4], F32) for i in range(2)]
    psT = C.ps[0][:].bitcast(BF16)
    psT2 = C.ps[1][:].bitcast(BF16)
    def stage1(tt):
        s = tt % 2
        P.dma(ht[s][:], src[tt * 128:(tt + 1) * 128, :], writes=[("ht", s)])
        P.act(junk[:], ht[s][:], AF.Square, [("ht", s)], ["junk", ("st", s, 0)], accum_out=st[s][:, 0:1])
        rstd_from_ssq(P, C, st[s], D, ("st", s))
        P.stt("dve", hnb[s][:], ht[s][:], st[s][:, 2:3], gb[:], ALU.mult, ALU.mult, [("ht", s), ("st", s, 2), "gpre"], [("hnb", s)])

    def stage2(tt):
        s = tt % 2
        pst = psT if s == 0 else psT2
        for kf in range(8):
            P.tr(pst[:, kf * 128:(kf + 1) * 128], hnb[s][:, kf * 128:(kf + 1) * 128], C.ident[:], [("hnb", s), "ident"], [("psT", s)])
        P.copy("act" if s == 0 else "dve", hnT[:, :, tt * 128:(tt + 1) * 128], pst[:, 0:1024].rearrange("p (k t) -> p k t", k=8), [("psT", s)], [("hnT", tt)])

    stage1(0)
    for tt in range(NT):
        if tt + 1 < NT:
            stage1(tt + 1)
        stage2(tt)
    P.end_phase()


def phase_out(P, C, L, ymT, src, dst):
    P.begin_phase()
    wout = P.sb("wout", [128, 8, D], BF16)
    wpg = P.sb("wpg", [128, 8, D], BF16)
    wpp = P.sb("wpp", [128, 2, D], BF16)
    load_w_bf16(P, C, wout, C.w_out[L], 8, D, "wout")
    load_w_bf16(P, C, wpg, C.ple_gate[L], 8, D, "wpg")
    load_w_bf16(P, C, wpp, C.ple_proj[L], 2, D, "wpp")
    gb = P.sb("gpost", [128, D], F32)
    P.dma(gb[:], C.post_norm[L:L + 1, :].partition_broadcast(128), writes=["gpost"])
    ht = [P.sb("ht%d" % i, [128, D], F32) for i in range(2)]
    pt = [P.sb("pt%d" % i, [128, 256], F32) for i in range(2)]
    ptb = [P.sb("ptb%d" % i, [128, 256], BF16) for i in range(2)]
    pT = [P.sb("pT%d" % i, [128, 2, 128], BF16) for i in range(2)]
    junk = P.sb("junk", [128, D], BF16)
    t1 = [P.sb("t1%d" % i, [128, D], F32) for i in range(2)]
    hm = [P.sb("hm%d" % i, [128, D], F32) for i in range(2)]
    hmb = [P.sb("hmb%d" % i, [128, D], BF16) for i in range(2)]
    hmT = [P.sb("hmT%d" % i, [128, 8, 128], BF16) for i in range(2)]
    sg = [P.sb("sg%d" % i, [128, D], F32) for i in range(2)]
    hn = [P.sb("hnw%d" % i, [128, D], F32) for i in range(2)]
    st = [P.sb("st%d" % i, [128, 4], F32) for i in range(2)]
    wkeys_out = [("w", "wout", k) for k in range(8)]
    wkeys_pg = [("w", "wpg", k) for k in range(8)]
    wkeys_pp = [("w", "wpp", k) for k in range(2)]
    def stage1(tt):
        s = tt % 2
        tsl = slice(tt * 128, (tt + 1) * 128)
        P.dma(ht[s][:], src[tsl, :], writes=[("ht", s)])
        P.dma(pt[s][:], C.p[L, tsl, :], writes=[("pt", s)])
        for hf in range(2):
            for kf in range(8):
                P.mm(C.ps[hf][:], ymT[:, kf, tsl], wout[:, kf, hf * 512:(hf + 1) * 512], kf == 0, kf == 7,
                     [("ymT", kf)] + wkeys_out, [("ps", hf)])
        for hf in range(2):
            P.act(junk[:, hf * 512:(hf + 1) * 512], C.ps[hf][:], AF.Square, [("ps", hf)], ["junk", ("st", s, 0, hf)],
                  accum_out=st[s][:, hf:hf + 1])
        P.tt("dve", st[s][:, 0:1], st[s][:, 0:1], st[s][:, 1:2], ALU.add, [("st", s, 0, 0), ("st", s, 0, 1)], [("st", s, 0)])
        rstd_from_ssq(P, C, st[s], D, ("st", s))
        for hf in range(2):
            hs = slice(hf * 512, (hf + 1) * 512)
            P.stt("dve", t1[s][:, hs], C.ps[hf][:], st[s][:, 2:3], gb[:, hs], ALU.mult, ALU.mult,
                  [("ps", hf), ("st", s, 2), "gpost"], [("t1", s, hf)])
            P.tt("dve", hm[s][:, hs], t1[s][:, hs], ht[s][:, hs], ALU.add, [("t1", s, hf), ("ht", s)], [("hm", s, hf)])
            P.copy("act", hmb[s][:, hs], hm[s][:, hs], [("hm", s, hf)], [("hmb", s, hf)])
        P.copy("act", ptb[s][:], pt[s][:], [("pt", s)], [("ptb", s)])

    def stage2(tt):
        s = tt % 2
        tsl = slice(tt * 128, (tt + 1) * 128)
        psT = C.ps[2][:].bitcast(BF16)
        for kf in range(8):
            P.tr(psT[:, kf * 128:(kf + 1) * 128], hmb[s][:, kf * 128:(kf + 1) * 128], C.ident[:],
                 [("hmb", s, kf // 4), "ident"], [("ps", 2)])
        P.copy("dve", hmT[s][:], psT[:, 0:1024].rearrange("p (k t) -> p k t", k=8), [("ps", 2)], [("hmT", s)])
        psT3 = C.ps[3][:].bitcast(BF16)
        for j in range(2):
            P.tr(psT3[:, j * 128:(j + 1) * 128], ptb[s][:, j * 128:(j + 1) * 128], C.ident[:], [("ptb", s), "ident"], [("ps", 3)])
        P.copy("dve", pT[s][:], psT3[:, 0:256].rearrange("p (k t) -> p k t", k=2), [("ps", 3)], [("pT", s)])
        for hf in range(2):
            hs = slice(hf * 512, (hf + 1) * 512)
            for kf in range(8):
                P.mm(C.ps[4 + hf][:], hmT[s][:, kf, :], wpg[:, kf, hs], kf == 0, kf == 7, [("hmT", s)] + wkeys_pg, [("ps", 4 + hf)])
            for j in range(2):
                P.mm(C.ps[6 + hf][:], pT[s][:, j, :], wpp[:, j, hs], j == 0, j == 1, [("pT", s)] + wkeys_pp, [("ps", 6 + hf)])
            P.act(sg[s][:, hs], C.ps[4 + hf][:], AF.Sigmoid, [("ps", 4 + hf)], [("sg", s, hf)])
            P.tt("dve", sg[s][:, hs], sg[s][:, hs], C.ps[6 + hf][:], ALU.mult, [("sg", s, hf), ("ps", 6 + hf)], [("sg", s, hf)])
            P.tt("dve", hn[s][:, hs], sg[s][:, hs], hm[s][:, hs], ALU.add, [("sg", s, hf), ("hm", s, hf)], [("hn", s, hf)])
        P.dma(dst[tsl, :], hn[s][:], reads=[("hn", s, 0), ("hn", s, 1)], writes=[("dst", tt % 4)], q="pool")

    stage1(0)
    for tt in range(NT):
        if tt + 1 < NT:
            stage1(tt + 1)
        stage2(tt)
    P.end_phase()


def phase_even_proj(P, C, j, hnT, uT, sgaT, sgbT, hcpad):
    P.begin_phase()
    wst = [P.sb("wst%d" % i, [128, 8, 128], F32) for i in range(2)]
    wbf = [P.sb("wbf%d" % i, [128, 8, 128], BF16) for i in range(2)]
    aT = P.sb("aT", [128, 4, S], BF16)
    sig = [P.sb("sig%d" % i, [128, 512], BF16) for i in range(2)]
    w_in = C.ev_w_in[j]
    P.memset("pool", hcpad[:, :, 0:30], 0.0, ["hcpad0"])
    n = 0
    for sl in range(20):
        s = sl % 2
        P.dma(wst[s][:], w_in[:, sl * 128:(sl + 1) * 128].rearrange("(k p) c -> p k c", p=128), writes=[("wst", s)])
        P.copy("pool", wbf[s][:], wst[s][:], [("wst", s)], [("wbf", s)])
        for c in range(4):
            cs = slice(c * 512, (c + 1) * 512)
            b = n % 4
            n += 1
            pb = C.ps[b]
            for kf in range(8):
                P.mm(pb[:], wbf[s][:, kf, :], hnT[:, kf, cs], kf == 0, kf == 7, [("wbf", s)], [("ps", b)])
            if sl < 4:
                P.copy("act", uT[:, sl, cs], pb[:], [("ps", b)], [("uT", sl, c)])
            elif sl < 8:
                P.act(sgaT[:, sl - 4, cs], pb[:], AF.Silu, [("ps", b)], [("sgaT", sl - 4, c)])
            elif sl < 12:
                P.copy("dve", aT[:, sl - 8, cs], pb[:], [("ps", b)], [("aT", sl - 8, c)])
            elif sl < 16:
                q = n % 2
                P.act(sig[q][:], pb[:], AF.Sigmoid, [("ps", b)], [("sig", q)])
                P.tt("dve", hcpad[:, sl - 12, 30 + c * 512:30 + (c + 1) * 512], aT[:, sl - 12, cs], sig[q][:], ALU.mult,
                     [("aT", sl - 12, c), ("sig", q)], [("hcpad", sl - 12, c)])
            else:
                P.act(sgbT[:, sl - 16, cs], pb[:], AF.Silu, [("ps", b)], [("sgbT", sl - 16, c)])
    P.end_phase()


def phase_conv(P, C, j, hcpad, sgbT, ymT):
    P.begin_phase()
    evp = P.sb("evp", [128, 4, 40], F32)
    P.dma(evp[:], C.evp[j], writes=["evp"])
    wpw = P.sb("wpw", [128, 4, 512], BF16)
    load_w_bf16(P, C, wpw, C.cv_w_pw[j], 4, 512, "wpw")
    wpw_keys = [("w", "wpw", k) for k in range(4)]
    dg = P.sb("dg", [128, 4, 31, 128], BF16)
    for ft in range(4):
        for k in range(31):
            P.ts("dve", dg[:, ft, k, :], C.identf[:], evp[:, ft, k:k + 1], None, ALU.mult, None,
                 ["evp", "identf"], [("dg", ft)])
    cv1 = P.sb("cv1", [128, 4, 512], F32)
    sq = P.sb("sq", [128, 4, 512], F32)
    mu = P.sb("mu", [128, 512], F32)
    m2 = P.sb("m2", [128, 512], F32)
    rs = P.sb("rs", [128, 512], F32)
    xn = P.sb("xn", [128, 4, 512], F32)
    cvn = P.sb("cvn", [128, 4, 512], BF16)
    for c in range(4):
        cs = slice(c * 512, (c + 1) * 512)
        for ft in range(4):
            for k in range(31):
                P.mm(C.ps[ft][:], dg[:, ft, k, :], hcpad[:, ft, c * 512 + k:c * 512 + k + 512], k == 0, k == 30,
                     [("dg", ft)], [("ps", ft)])
            P.act(cv1[:, ft, :], C.ps[ft][:], AF.Identity, [("ps", ft), "evp"], [("cv1", ft)], bias=evp[:, ft, 31:32])
            P.act(sq[:, ft, :], cv1[:, ft, :], AF.Square, [("cv1", ft)], [("sq", ft)])
        for ft in range(4):
            P.mm(C.ps[4][:], C.onesf[:], cv1[:, ft, :], ft == 0, ft == 3, [("cv1", ft), "onesf"], [("ps", 4)])
        for ft in range(4):
            P.mm(C.ps[5][:], C.onesf[:], sq[:, ft, :], ft == 0, ft == 3, [("sq", ft), "onesf"], [("ps", 5)])
        P.act(mu[:], C.ps[4][:], AF.Copy, [("ps", 4)], ["mu"], scale=1.0 / 512)
        P.tt("dve", m2[:], mu[:], mu[:], ALU.mult, ["mu"], ["m2"])
        P.stt("dve", m2[:], C.ps[5][:], 1.0 / 512, m2[:], ALU.mult, ALU.subtract, [("ps", 5), "m2"], ["m2"])
        P.act(rs[:], m2[:], AF.Ln, ["m2"], ["rs"], bias=C.epsb[:])
        P.act(rs[:], rs[:], AF.Exp, ["rs"], ["rs"], scale=-0.5)
        for ft in range(4):
            eng = "dve"
            P.tt(eng, xn[:, ft, :], cv1[:, ft, :], mu[:], ALU.subtract, [("cv1", ft), "mu"], [("xn", ft)])
            P.tt(eng, xn[:, ft, :], xn[:, ft, :], rs[:], ALU.mult, [("xn", ft), "rs"], [("xn", ft)])
            P.act(cvn[:, ft, :], xn[:, ft, :], AF.Silu, [("xn", ft), "evp"], [("cvn", ft)],
                  scale=evp[:, ft, 32:33], bias=evp[:, ft, 33:34])
        for ot in range(4):
            b = 6 + (ot % 2)
            for ft in range(4):
                P.mm(C.ps[b][:], wpw[:, ft, ot * 128:(ot + 1) * 128], cvn[:, ft, :], ft == 0, ft == 3,
                     [("cvn", ft)] + wpw_keys, [("ps", b)])
            P.tt("dve", ymT[:, 4 + ot, cs], C.ps[b][:], sgbT[:, ot, cs], ALU.mult, [("ps", b)], [("ymT", 4 + ot, c)])
    P.end_phase()


def phase_s5_setup(P, C, j, BtR, BtI, CtR, CtI, sc2):
    P.begin_phase()
    sc = P.sb("sc", [128, 16, 3], F32)
    P.dma(sc[:], C.s5sc[j], writes=["sc"])
    w = P.sb("w", [128, 16, 16], F32)

    def col(i):
        return w[:, :, i]
    lr, li, ls = sc[:, :, 0], sc[:, :, 1], sc[:, :, 2]
    k = ["w%d" % i for i in range(16)]
    P.act(col(0), ls, AF.Exp, ["sc"], [k[0]])
    P.tt("dve", col(1), lr, col(0), ALU.mult, ["sc", k[0]], [k[1]])
    P.tt("dve", sc2[:, :, 0], li, col(0), ALU.mult, ["sc", k[0]], ["th"])
    P.ts("dve", sc2[:, :, 1], sc2[:, :, 0], 1.0 / TWO_PI, None, ALU.mult, None, ["th"], ["thq"])
    P.act(sc2[:, :, 2], col(1), AF.Exp, [k[1]], ["rho"])
    ki = P.sb("ki", [128, 16], I32)
    P.copy("dve", ki[:], sc2[:, :, 1], ["thq"], ["ki"])
    P.stt("dve", col(2), ki[:], -TWO_PI, sc2[:, :, 0], ALU.mult, ALU.add, ["ki", "th"], [k[2]])
    P.ts("dve", col(2), col(2), math.pi, -math.pi, ALU.min, ALU.max, [k[2]], [k[2]])
    P.act(col(3), col(2), AF.Abs, [k[2]], [k[3]])
    P.act(col(4), col(2), AF.Sin, [k[2]], [k[4]])
    P.act(col(5), col(3), AF.Sin, [k[3]], [k[5]], scale=-1.0, bias=C.hpib[:])
    P.tt("dve", col(6), sc2[:, :, 2], col(5), ALU.mult, ["rho", k[5]], [k[6]])
    P.tt("dve", col(7), sc2[:, :, 2], col(4), ALU.mult, ["rho", k[4]], [k[7]])
    P.ts("dve", col(6), col(6), -1.0, None, ALU.add, None, [k[6]], [k[6]])
    P.tt("dve", col(8), lr, lr, ALU.mult, ["sc"], [k[8]])
    P.tt("dve", col(9), li, li, ALU.mult, ["sc"], [k[9]])
    P.tt("dve", col(8), col(8), col(9), ALU.add, [k[8], k[9]], [k[8]])
    P.recip(col(8), col(8), [k[8]], [k[8]])
    P.tt("dve", col(9), col(6), lr, ALU.mult, [k[6], "sc"], [k[9]])
    P.tt("dve", col(10), col(7), li, ALU.mult, [k[7], "sc"], [k[10]])
    P.tt("dve", col(9), col(9), col(10), ALU.add, [k[9], k[10]], [k[9]])
    P.tt("dve", col(11), col(9), col(8), ALU.mult, [k[9], k[8]], [k[11]])
    P.tt("dve", col(9), col(7), lr, ALU.mult, [k[7], "sc"], [k[9]])
    P.tt("dve", col(10), col(6), li, ALU.mult, [k[6], "sc"], [k[10]])
    P.tt("dve", col(9), col(9), col(10), ALU.subtract, [k[9], k[10]], [k[9]])
    P.tt("dve", col(12), col(9), col(8), ALU.mult, [k[9], k[8]], [k[12]])
    for ri, dst in ((0, BtR), (1, BtI)):
        stg = P.sb("bst%d" % ri, [128, 16, 128], F32)
        P.dma(stg[:], C.s5bT[j, ri], writes=[("bst", ri)])
        P.copy("act" if ri == 0 else "dve", dst[:], stg[:], [("bst", ri)], [("Bt", ri)])
    cre = P.sb("cre", [128, 16, 128], F32)
    cim = P.sb("cim", [128, 16, 128], F32)
    t1 = P.sb("t1", [128, 16, 128], F32)
    t2 = P.sb("t2", [128, 16, 128], F32)
    P.dma(cre[:], C.s5cP[j, 0], writes=["cre"])
    P.dma(cim[:], C.s5cP[j, 1], writes=["cim"])
    fre = w[:, :, 11:12].to_broadcast([128, 16, 128])
    fim = w[:, :, 12:13].to_broadcast([128, 16, 128])
    P.tt("dve", t1[:], cre[:], fre, ALU.mult, ["cre", k[11]], ["t1"])
    P.tt("dve", t2[:], cim[:], fim, ALU.mult, ["cim", k[12]], ["t2"])
    P.tt("dve", CtR[:], t1[:], t2[:], ALU.subtract, ["t1", "t2"], ["CtR"])
    P.tt("dve", t1[:], cre[:], fim, ALU.mult, ["cre", k[12]], ["t1"])
    P.tt("dve", t2[:], cim[:], fre, ALU.mult, ["cim", k[11]], ["t2"])
    P.stt("dve", CtI[:], t1[:], -1.0, t2[:], ALU.mult, ALU.subtract, ["t1", "t2"], ["CtI"])
    P.end_phase()


def phase_s5(P, C, j, uT, sgaT, ymT, BtR, BtI, CtR, CtI, sc2):
    P.begin_phase()
    TH = 1024
    evp = P.sb("evp", [128, 4, 40], F32)
    P.dma(evp[:], C.evp[j], writes=["evp"])
    wglu = P.sb("wglu", [128, 4, 512], BF16)
    load_w_bf16(P, C, wglu, C.s5_w_glu[j], 4, 512, "wglu")
    wglu_keys = [("w", "wglu", k) for k in range(4)]
    iot = P.sb("iot", [128, S], F32)
    P.op("pool", lambda e: e.iota(iot[:], pattern=[[1, S]], base=0, channel_multiplier=0,
                                  allow_small_or_imprecise_dtypes=True), (), ["iot"])
    ki = P.sb("ki", [128, TH], I32)
    rr = P.sb("rr", [128, TH], F32)
    ra = P.sb("ra", [128, TH], F32)
    cs_ = P.sb("cos", [128, TH], BF16)
    sn_ = P.sb("sin", [128, TH], BF16)
    bpR = P.sb("bpR", [128, TH], BF16)
    bpI = P.sb("bpI", [128, TH], BF16)
    wR = P.sb("wR", [128, TH], BF16)
    wI = P.sb("wI", [128, TH], BF16)
    xR = P.sb("xR", [128, TH], BF16)
    xI = P.sb("xI", [128, TH], BF16)
    buR = [P.sb("buR%d" % q, [128, 512], BF16) for q in range(2)]
    buI = [P.sb("buI%d" % q, [128, 512], BF16) for q in range(2)]
    mt = [[P.sb("m%d_%d" % (i, q), [128, 512], BF16) for i in range(4)] for q in range(2)]
    mo = [P.sb("mo%d" % i, [128, TH], BF16) for i in range(4)]
    carry = P.sb("carry", [128, 16, 2], F32)
    P.memset("pool", carry[:], 0.0, ["carry"])
    ygT = P.sb("ygT", [128, 4, S], BF16)
    yv = P.sb("yv", [128, 512], F32)
    y2 = P.sb("y2", [128, 512], F32)
    sgm = P.sb("sgm", [128, 512], F32)
    it = 0
    for ft in range(4):
        for hf in range(2):
            t0 = hf * TH
            for pl in range(4):
                pr = 4 * ft + pl
                th = sc2[:, pr, 0:1]
                thq = sc2[:, pr, 1:2]
                P.ts("dve", ki[:], iot[:, t0:t0 + TH], thq, None, ALU.mult, None, ["iot"], ["ki"])
                P.act(ra[:], iot[:, t0:t0 + TH], AF.Copy, ["iot"], ["ra"], scale=th)
                P.stt("dve", rr[:], ki[:], -TWO_PI, ra[:], ALU.mult, ALU.add, ["ki", "ra"], ["rr"])
                P.ts("dve", rr[:], rr[:], math.pi, -math.pi, ALU.min, ALU.max, ["rr"], ["rr"])
                P.act(sn_[:], rr[:], AF.Sin, ["rr"], ["sin"])
                P.act(ra[:], rr[:], AF.Abs, ["rr"], ["ra"])
                P.act(cs_[:], ra[:], AF.Sin, ["ra"], ["cos"], scale=-1.0, bias=C.hpib[:])
                for c in range(2):
                    cl = slice(c * 512, (c + 1) * 512)
                    cg = slice(t0 + c * 512, t0 + (c + 1) * 512)
                    q = it % 2
                    it += 1
                    bA, bB = C.ps[2 * q], C.ps[2 * q + 1]
                    P.mm(bA[:], BtR[:, pr, :], uT[:, ft, cg], True, True, [], [("ps", 2 * q)])
                    P.mm(bB[:], BtI[:, pr, :], uT[:, ft, cg], True, True, [], [("ps", 2 * q + 1)])
                    P.copy("act", buR[q][:], bA[:], [("ps", 2 * q)], [("buR", q)])
                    P.copy("act", buI[q][:], bB[:], [("ps", 2 * q + 1)], [("buI", q)])
                    m = mt[q]
                    P.tt("dve", m[0][:], buR[q][:], cs_[:, cl], ALU.mult, [("buR", q), "cos"], [("m", q, 0)])
                    P.tt("dve", m[1][:], buI[q][:], sn_[:, cl], ALU.mult, [("buI", q), "sin"], [("m", q, 1)])
                    P.tt("dve", m[2][:], buI[q][:], cs_[:, cl], ALU.mult, [("buI", q), "cos"], [("m", q, 2)])
                    P.tt("dve", m[3][:], buR[q][:], sn_[:, cl], ALU.mult, [("buR", q), "sin"], [("m", q, 3)])
                    P.tt("dve", bpR[:, cl], m[0][:], m[1][:], ALU.add, [("m", q, 0), ("m", q, 1)], [("bpR", c)])
                    P.tt("dve", bpI[:, cl], m[2][:], m[3][:], ALU.subtract, [("m", q, 2), ("m", q, 3)], [("bpI", c)])
                P.op("dve", lambda e, pr=pr: e.tensor_tensor_scan(out=wR[:], data0=sc2[:, pr, 2:3].to_broadcast([128, TH]), data1=bpR[:],
                                                                 initial=carry[:, pr, 0:1], op0=ALU.mult, op1=ALU.add),
                     [("bpR", 0), ("bpR", 1), "carry"], ["wR"])
                P.op("dve", lambda e, pr=pr: e.tensor_tensor_scan(out=wI[:], data0=sc2[:, pr, 2:3].to_broadcast([128, TH]), data1=bpI[:],
                                                                 initial=carry[:, pr, 1:2], op0=ALU.mult, op1=ALU.add),
                     [("bpI", 0), ("bpI", 1), "carry"], ["wI"])
                if hf == 0:
                    P.copy("dve", carry[:, pr, 0:1], wR[:, TH - 1:TH], ["wR"], ["carry"])
                    P.copy("dve", carry[:, pr, 1:2], wI[:, TH - 1:TH], ["wI"], ["carry"])
                P.tt("dve", mo[0][:], wR[:], cs_[:], ALU.mult, ["wR", "cos"], ["mo0"])
                P.tt("dve", mo[1][:], wI[:], sn_[:], ALU.mult, ["wI", "sin"], ["mo1"])
                P.tt("dve", mo[2][:], wI[:], cs_[:], ALU.mult, ["wI", "cos"], ["mo2"])
                P.tt("dve", mo[3][:], wR[:], sn_[:], ALU.mult, ["wR", "sin"], ["mo3"])
                P.tt("dve", xR[:], mo[0][:], mo[1][:], ALU.subtract, ["mo0", "mo1"], ["xR"])
                P.tt("dve", xI[:], mo[2][:], mo[3][:], ALU.add, ["mo2", "mo3"], ["xI"])
                for c in range(2):
                    cl = slice(c * 512, (c + 1) * 512)
                    P.mm(C.ps[4 + c][:], CtR[:, pr, :], xR[:, cl], pl == 0, False, ["xR"], [("ps", 4 + c)])
                    P.mm(C.ps[4 + c][:], CtI[:, pr, :], xI[:, cl], False, pl == 3, ["xI"], [("ps", 4 + c)])
            for c in range(2):
                cg = slice(t0 + c * 512, t0 + (c + 1) * 512)
                P.stt("dve", yv[:], uT[:, ft, cg], evp[:, ft, 34:35], C.ps[4 + c][:], ALU.mult, ALU.add,
                      [("ps", 4 + c), "evp"], ["yv"])
                P.act(y2[:], yv[:], AF.Square, ["yv"], ["y2"])
                P.ts("dve", y2[:], y2[:], 0.044715, 1.0, ALU.mult, ALU.add, ["y2"], ["y2"])
                P.tt("dve", y2[:], y2[:], yv[:], ALU.mult, ["y2", "yv"], ["y2"])
                P.act(sgm[:], y2[:], AF.Sigmoid, ["y2"], ["sgm"], scale=1.5957691216057308)
                P.tt("dve", ygT[:, ft, cg], sgm[:], yv[:], ALU.mult, ["sgm", "yv"], [("ygT", ft, hf, c)])
    gk = [("ygT", ft, hf, c) for ft in range(4) for hf in range(2) for c in range(2)]
    n = 0
    for ot in range(4):
        for c in range(4):
            cs = slice(c * 512, (c + 1) * 512)
            b = n % 4
            n += 1
            for ft in range(4):
                P.mm(C.ps[b][:], wglu[:, ft, ot * 128:(ot + 1) * 128], ygT[:, ft, cs], ft == 0, ft == 3, gk + wglu_keys, [("ps", b)])
            P.act(sgm[:], C.ps[b][:], AF.Sigmoid, [("ps", b), "evp"], ["sgm"], bias=evp[:, ot, 35:36])
            P.tt("dve", sgm[:], sgm[:], ygT[:, ot, cs], ALU.mult, ["sgm"] + gk, ["sgm"])
            P.tt("dve", ymT[:, ot, cs], sgm[:], sgaT[:, ot, cs], ALU.mult, ["sgm"], [("ymT", ot, c)])
    P.end_phase()


def even_layer(P, C, L, src, dst):
    j = L // 2
    P.push_scope()
    ymT = P.sb("ymT", [128, 8, S], BF16)
    P.push_scope()
    uT = P.sb("uT", [128, 4, S], BF16)
    sgaT = P.sb("sgaT", [128, 4, S], BF16)
    P.push_scope()
    sgbT = P.sb("sgbT", [128, 4, S], BF16)
    hcpad = P.sb("hcpad", [128, 4, S + 30], BF16)
    P.push_scope()
    hnT = P.sb("hnT", [128, 8, S], BF16)
    phase_prenorm(P, C, L, src, hnT)
    phase_even_proj(P, C, j, hnT, uT, sgaT, sgbT, hcpad)
    P.pop_scope()
    phase_conv(P, C, j, hcpad, sgbT, ymT)
    P.pop_scope()
    P.push_scope()
    BtR = P.sb("BtR", [128, 16, 128], BF16)
    BtI = P.sb("BtI", [128, 16, 128], BF16)
    CtR = P.sb("CtR", [128, 16, 128], BF16)
    CtI = P.sb("CtI", [128, 16, 128], BF16)
    sc2 = P.sb("sc2", [128, 16, 3], F32)
    phase_s5_setup(P, C, j, BtR, BtI, CtR, CtI, sc2)
    phase_s5(P, C, j, uT, sgaT, ymT, BtR, BtI, CtR, CtI, sc2)
    P.pop_scope()
    P.pop_scope()
    phase_out(P, C, L, ymT, src, dst)
    P.pop_scope()


W_SHAPES = {
    "pre_norm": [4, D], "post_norm": [4, D], "ple_gate": [4, D, D], "ple_proj": [4, 256, D],
    "ev_w_in": [2, D, 2560], "s5_w_glu": [2, 512, 512], "cv_w_pw": [2, 512, 512], "w_out": [4, D, D],
    "evp": [2, 128, 4, 40], "s5sc": [2, 128, 16, 3], "s5bT": [2, 2, 128, 16, 128], "s5cP": [2, 2, 128, 16, 128],
    "od_w_in": [2, D, 2744], "mla_w_uq": [2, 256, 768], "mla_w_ukv": [2, 128, 1024], "mlap": [2, 128, 3],
    "ropef": [128, 2],
    "nsa_ck_w1": [2, 2048, 256], "nsa_ck_w2": [2, 256, 64], "nsa_cv_w1": [2, 2048, 256], "nsa_cv_w2": [2, 256, 64],
    "nsapos2": [2, 128, 2, 16], "selc": [128, NT, 2, 32],
}
W_BF16 = {"qaug": [4, 8, S], "kaugc": [36, S], "kaugcmp": [4, 128], "addmask": [128, S], "ovl": [128, 64], "gsel": [32, 3, 8, 128]}


def build(n_layers=4, dbg=False, odd_kw=None):
    nc = bass.Bass("TRN2", target_bir_lowering=False)
    C = Ctx()
    odd_kw = odd_kw or {}

    def din(name, shape, dt=F32):
        return nc.dram_tensor(name, list(shape), dt, kind="ExternalInput").ap()
    C.x = din("x", [S, D])
    C.p = din("p", [4, S, 256])
    C.positions = din("positions", [1, S], I32)
    C.dbg = nc.dram_tensor("dbg", [8, 128, S], F32, kind="ExternalOutput").ap() if dbg else None
    for k, shp in W_SHAPES.items():
        setattr(C, k, din(k, shp))
    for k, shp in W_BF16.items():
        setattr(C, k, din(k, shp, BF16))
    out = nc.dram_tensor("out", [S, D], F32, kind="ExternalOutput").ap()
    hbuf = nc.dram_tensor("hbuf", [S, D], F32, kind="Internal").ap()
    P = Prog(nc)
    C.ps = [P.gps("ps%d" % i, [128, 512]) for i in range(8)]
    C.ident = P.gsb("ident", [128, 128], BF16)
    C.identf = P.gsb("identf", [128, 128], F32)
    C.onesf = P.gsb("onesf", [128, 128], F32)
    C.epsb = P.gsb("epsb", [128, 1], F32)
    C.tri = P.gsb("tri", [128, 128], BF16)
    C.wmask = P.gsb("wmask", [128, 128], BF16)
    C.ropef_sb = P.gsb("ropef_sb", [128, 2], F32)
    C.hpib = P.gsb("hpib", [128, 1], F32)
    P.begin_phase()
    io = P.sb("io", [128, 128], F32)
    P.op("pool", lambda e: e.iota(io[:], pattern=[[1, 128]], base=0, channel_multiplier=-1,
                                  allow_small_or_imprecise_dtypes=True), (), ["io"])
    P.op("dve", lambda e: e.tensor_single_scalar(out=C.identf[:], in_=io[:], scalar=0.0, op=ALU.is_equal), ["io"], ["identf"])
    P.copy("dve", C.ident[:], C.identf[:], ["identf"], ["ident"])
    P.memset("pool", C.onesf[:], 1.0, ["onesf"])
    P.op("dve", lambda e: e.tensor_single_scalar(out=C.tri[:], in_=io[:], scalar=0.0, op=ALU.is_ge), ["io"], ["tri"])
    P.op("dve", lambda e: e.tensor_single_scalar(out=C.wmask[:], in_=io[:], scalar=0.0, op=ALU.is_lt), ["io"], ["wmask"])
    P.dma(C.ropef_sb[:], C.ropef[:, :], writes=["ropef_sb"])
    P.memset("pool", C.epsb[:], EPS, ["epsb"])
    P.memset("pool", C.hpib[:], math.pi / 2, ["hpib"])
    P.end_phase()
    C.ropef_dram = C.ropef
    C.ropef = C.ropef_sb
    for L in range(n_layers):
        src = C.x if L == 0 else hbuf
        dst = out if L == n_layers - 1 else hbuf
        if L % 2 == 0:
            even_layer(P, C, L, src, dst)
        else:
            odd_layer(P, C, L, src, dst, **odd_kw)
    C.ropef = C.ropef_dram
    P.close()
    return nc, P


def nsa_constants():
    import ml_dtypes
    bf = ml_dtypes.bfloat16
    t = np.arange(S)
    a_t, b_t = (t // 64).astype(np.float32), (t % 64).astype(np.float32)
    slopes = np.array([2.0 ** (-(i + 1)) for i in range(8)], np.float32)
    qaug = np.zeros((4, 8, S), np.float32)
    for h in range(8):
        qaug[0, h] = -slopes[h] * 64.0 * a_t
        qaug[1, h] = -slopes[h] * b_t
        qaug[2, h] = slopes[h] * 64.0
        qaug[3, h] = slopes[h]
    kaugc = np.zeros((36, S), np.float32)
    kaugc[t // 64, t] = 1.0
    kaugc[32] = 1.0
    kaugc[33] = 1.0
    kaugc[34] = a_t
    kaugc[35] = b_t
    c = np.arange(128)
    pc = 16 * c + 31
    kaugcmp = np.stack([np.ones(128), np.ones(128), pc // 64, pc % 64]).astype(np.float32)
    addmask = np.where((t[None, :] >= pc[:, None]) & (c[:, None] <= 126), 0.0, -30000.0).astype(np.float32)
    sb = np.arange(32)
    cs_ = c[:, None] * 16
    overlap = ((cs_ < (sb[None] + 1) * 64) & (cs_ + 32 > sb[None] * 64) & (c[:, None] <= 126)).astype(np.float32)
    ovl = np.concatenate([overlap, np.ones((128, 32), np.float32)], axis=1)
    cur = t[:, None] // 64
    forced = (sb[None] == 0) | (sb[None] == cur) | (sb[None] == cur - 1)
    causal = sb[None] * 64 <= t[:, None]
    mul = (forced | causal).astype(np.float32)
    add = np.where(forced, 1e4, np.where(causal, 0.0, -1e4)).astype(np.float32)
    selc = np.stack([mul, add], axis=1).reshape(NT, 128, 2, 32).transpose(1, 0, 2, 3)
    gsel = np.zeros((32, 3, 8, 128), np.float32)
    for b in range(3):
        for h in range(8):
            gsel[b * 8 + h, b, h, (h % 2) * 64:(h % 2) * 64 + 64] = 1.0
    return {"qaug": qaug.astype(bf), "kaugc": kaugc.astype(bf), "kaugcmp": kaugcmp.astype(bf), "addmask": addmask.astype(bf),
            "ovl": ovl.astype(bf), "gsel": gsel.astype(bf), "selc": np.ascontiguousarray(selc)}


def host_layout(inputs):
    f = lambda k: np.asarray(inputs[k], np.float32)
    ne = 2
    evp = np.zeros((ne, 128, 4, 40), np.float32)
    wdw = f("cv_w_dw")
    evp[:, :, :, 0:31] = wdw.reshape(ne, 31, 4, 128).transpose(0, 3, 2, 1)
    for col, key in ((31, "cv_b_dw"), (32, "cv_ln_g"), (33, "cv_ln_b"), (34, "s5_d"), (35, "s5_b_glu")):
        evp[:, :, :, col] = f(key).reshape(ne, 4, 128).transpose(0, 2, 1)
    s5sc = np.zeros((ne, 128, 16, 3), np.float32)
    for i, key in enumerate(("s5_lam_re", "s5_lam_im")):
        a = f(key).reshape(ne, 16, 2, 64)
        s5sc[:, :, :, i] = a.transpose(0, 2, 3, 1).reshape(ne, 128, 16)
    ls = f("s5_log_step").reshape(ne, 16, 2)
    s5sc[:, :, :, 2] = np.repeat(ls.transpose(0, 2, 1)[:, :, None, :], 64, axis=2).reshape(ne, 128, 16)
    s5bT = np.zeros((ne, 2, 128, 16, 128), np.float32)
    s5cP = np.zeros((ne, 2, 128, 16, 128), np.float32)
    for ri, (kb, kc) in enumerate((("s5_b_re", "s5_c_re"), ("s5_b_im", "s5_c_im"))):
        b = f(kb)
        c = f(kc)
        for pr in range(16):
            for gl in range(2):
                g = 2 * pr + gl
                k0 = (pr % 4) * 32 + gl * 16
                s5bT[:, ri, k0:k0 + 16, pr, gl * 64:(gl + 1) * 64] = b[:, g].transpose(0, 2, 1)
                s5cP[:, ri, gl * 64:(gl + 1) * 64, pr, k0:k0 + 16] = c[:, g].transpose(0, 2, 1)
    w_out = np.stack([f("ev_w_out")[0], f("od_w_out")[0], f("ev_w_out")[1], f("od_w_out")[1]])
    no = 2
    mlap = np.zeros((no, 128, 3), np.float32)
    mlap[:, :, 0:2] = f("mla_q_norm").reshape(no, 2, 128).transpose(0, 2, 1)
    mlap[:, :, 2] = f("mla_kv_norm")
    ropef = np.zeros((128, 2), np.float32)
    fr = (10000.0 ** (-np.arange(16, dtype=np.float32) / 16)).astype(np.float32)
    ropef[64:96, 0] = np.tile(fr, 2)
    ropef[:, 1] = ropef[:, 0] / np.float32(TWO_PI)
    rep = {"evp": evp, "s5sc": s5sc, "s5bT": s5bT, "s5cP": s5cP, "w_out": w_out, "mlap": mlap, "ropef": ropef}
    rep.update(nsa_constants())
    pos2 = np.zeros((no, 128, 2, 16), np.float32)
    for kv, key in enumerate(("nsa_pos_k", "nsa_pos_v")):
        a = f(key).reshape(no, 16, 2, 64)
        pos2[:, :, kv, :] = a.transpose(0, 2, 3, 1).reshape(no, 128, 16)
    rep["nsapos2"] = pos2
    for k in ("nsa_ck_w1", "nsa_ck_w2", "nsa_cv_w1", "nsa_cv_w2"):
        rep[k] = np.ascontiguousarray(f(k))
    for k in ("pre_norm", "post_norm", "ple_gate", "ple_proj", "ev_w_in", "s5_w_glu", "cv_w_pw", "od_w_in", "mla_w_uq", "mla_w_ukv"):
        rep[k] = np.ascontiguousarray(f(k))
    return rep


def kernel(**inputs):
    n = 8
    rep = host_layout(inputs)
    x = np.asarray(inputs["x"], np.float32)
    p = np.asarray(inputs["p"], np.float32)
    nc, _ = build(4)
    in_maps = []
    for b in range(n):
        m = dict(rep)
        m["x"] = np.ascontiguousarray(x[b])
        m["p"] = np.ascontiguousarray(p[:, b])
        m["positions"] = np.ascontiguousarray(np.asarray(inputs["positions"])[b:b + 1]).astype(np.int32)
        in_maps.append(m)
    res = run_bass_kernel_spmd(nc, in_maps, core_ids=list(range(n)))
    return np.stack([r["out"] for r in res.results], axis=0).astype(np.float32)


OD = {"q": 0, "kv": 512, "gl": 1280, "gn": 1304, "cq": 1816, "ckv": 2072, "kr": 2200, "gm": 2232}


def load_slab(P, C, W, c0, n, tag, slot, eng="pool"):
    stg = P.sb("sl_st_%s" % tag, [128, 8, n], F32)
    dst = P.sb("sl_bf_%s" % tag, [128, 8, n], BF16)
    P.dma(stg[:], W[:, c0:c0 + n].rearrange("(k p) c -> p k c", p=128), writes=[("slst", tag)])
    P.copy(eng, dst[:], stg[:], [("slst", tag)], [("slab", tag)])
    return dst


def angle_tables(P, C, ang_in, fq, f, cosT, sinT, n, tag):
    ki = P.sb("ki_" + tag, [128, n], I32)
    rr = P.sb("rr_" + tag, [128, n], F32)
    ra = P.sb("ra_" + tag, [128, n], F32)
    P.ts("dve", ki[:], ang_in, fq, None, ALU.mult, None, [tag + "in"], [tag + "ki"])
    P.ts("pool", rr[:], ki[:], -TWO_PI, None, ALU.mult, None, [tag + "ki"], [tag + "rr"])
    P.ts("pool", ra[:], ang_in, f, None, ALU.mult, None, [tag + "in"], [tag + "ra"])
    P.tt("pool", rr[:], rr[:], ra[:], ALU.add, [tag + "rr", tag + "ra"], [tag + "rr"])
    P.ts("pool", rr[:], rr[:], math.pi, -math.pi, ALU.min, ALU.max, [tag + "rr"], [tag + "rr"])
    P.act(sinT, rr[:], AF.Sin, [tag + "rr"], [tag + "sin"])
    P.act(ra[:], rr[:], AF.Abs, [tag + "rr"], [tag + "ra"])
    P.act(cosT, ra[:], AF.Sin, [tag + "ra"], [tag + "cos"], scale=-1.0, bias=C.hpib[:])


def softmax_pv_finish(P, C, ob, par, dst_rows, rec, tmpf, extra_mul, ekey, okey, wkey, first, clamp=False, gate_ps=None, gkey=None):
    lo = slice(par * 64, par * 64 + 64)
    hi = slice((1 - par) * 64, (1 - par) * 64 + 64)
    if clamp:
        P.ts("dve", rec[lo, :], ob[hi, :], 1e-18, None, ALU.max, None, [okey], [("rec", par)])
        P.act(rec[lo, :], rec[lo, :], AF.Ln, [("rec", par)], [("rec", par)])
    else:
        P.act(rec[lo, :], ob[hi, :], AF.Ln, [okey], [("rec", par)])
    P.act(rec[lo, :], rec[lo, :], AF.Exp, [("rec", par)], [("rec", par)], scale=-1.0)
    P.tt("dve", tmpf[lo, :], ob[lo, :], rec[lo, :], ALU.mult, [okey, ("rec", par)], [("tmpf", par)])
    if gate_ps is not None:
        P.tt("dve", tmpf[lo, :], tmpf[lo, :], gate_ps[lo, :], ALU.mult, [("tmpf", par), gkey], [("tmpf", par)])
    if not first:
        P.tt("dve", tmpf[lo, :], tmpf[lo, :], dst_rows, ALU.add, [("tmpf", par), wkey], [("tmpf", par)])
    if extra_mul is not None:
        P.tt("dve", dst_rows, tmpf[lo, :], extra_mul, ALU.mult, [("tmpf", par), ekey], [wkey])
    else:
        P.copy("act", dst_rows, tmpf[lo, :], [("tmpf", par)], [wkey])


def run_attention(P, C, groups, PT, depth=3, mask_eng="dve", sbanks=(0, 1), tick=None):
    flat = [(g, i) for g, (steps, fin) in enumerate(groups) for i in range(len(steps))]
    issued = 0
    for idx in range(len(flat)):
        while issued < min(len(flat), idx + depth):
            g2, i2 = flat[issued]
            st2 = groups[g2][0][i2]
            bank = sbanks[issued % len(sbanks)]
            P.mm(C.ps[bank][:, st2["c0"]:st2["c1"]], st2["lhsK"], st2["rhsQ"], True, True, st2.get("rk", []), [("ps", bank)])
            issued += 1
        g, i = flat[idx]
        steps, fin = groups[g]
        st = steps[i]
        bank = sbanks[idx % len(sbanks)]
        c0, c1 = st["c0"], st["c1"]
        pt = PT[idx % len(PT)]
        pk = ("PT", idx % len(PT))
        if st.get("scale") is not None:
            P.act(pt[:, c0:c1], C.ps[bank][:, c0:c1], AF.Exp, [("ps", bank)], [pk], scale=st["scale"])
        else:
            P.act(pt[:, c0:c1], C.ps[bank][:, c0:c1], AF.Exp, [("ps", bank)], [pk])
        for (a, b, m) in st["masks"]:
            P.tt(mask_eng, pt[:, a:b], pt[:, a:b], m, ALU.mult, [pk], [pk])
        P.mm(C.ps[st["ob"]][:, c0:c1], st["lhsV"], pt[:, c0:c1], i == 0, i == len(steps) - 1, [pk] + st.get("rv", []), [("ps", st["ob"])],
             skip=True)
        if i == len(steps) - 1:
            fin()
        if tick is not None:
            tick(idx)


def phase_mla_prep(P, C, L, hnT, cosT, sinT, cqnT, ckvnT, kropeT):
    j = L // 2
    W = C.od_w_in[j]
    P.begin_phase()
    mlap = P.sb("mlap", [128, 3], F32)
    P.dma(mlap[:], C.mlap[j], writes=["mlap"])
    posi = [P.sb("posi%d" % i, [128, 512], I32) for i in range(2)]
    posf = [P.sb("posf%d" % i, [128, 512], F32) for i in range(2)]
    ki = P.sb("rki", [128, 512], I32)
    rr = P.sb("rrr", [128, 512], F32)
    ra = P.sb("rra", [128, 512], F32)
    for c in range(4):
        cs = slice(c * 512, (c + 1) * 512)
        s = c % 2
        P.dma(posi[s][:], C.positions[0:1, cs].partition_broadcast(128), writes=[("posi", s)])
        P.copy("dve", posf[s][:], posi[s][:], [("posi", s)], [("posf", s)])
        P.ts("dve", ki[:], posf[s][:], C.ropef[:, 1:2], None, ALU.mult, None, [("posf", s)], ["ki"])
        P.ts("dve", rr[:], ki[:], -TWO_PI, None, ALU.mult, None, ["ki"], ["rr"])
        P.ts("dve", ra[:], posf[s][:], C.ropef[:, 0:1], None, ALU.mult, None, [("posf", s)], ["ra"])
        P.tt("dve", rr[:], rr[:], ra[:], ALU.add, ["rr", "ra"], ["rr"])
        P.ts("dve", rr[:], rr[:], math.pi, -math.pi, ALU.min, ALU.max, ["rr"], ["rr"])
        P.act(sinT[:, cs], rr[:], AF.Sin, ["rr"], [("sinT", c)])
        P.act(ra[:], rr[:], AF.Abs, ["rr"], ["ra"])
        P.act(cosT[:, cs], ra[:], AF.Sin, ["ra"], [("cosT", c)], scale=-1.0, bias=C.hpib[:])
    wcq = load_slab(P, C, W, OD["cq"], 256, "cq", 0, eng="act")
    wckv = load_slab(P, C, W, OD["ckv"], 128, "ckv", 0, eng="dve")
    wkr = load_slab(P, C, W, OD["kr"], 32, "kr", 0, eng="dve")
    wkrA = P.sb("wkrA", [128, 8, 96], BF16)
    wkrR = P.sb("wkrR", [128, 8, 96], BF16)
    P.memset("pool", wkrA[:], 0.0, ["wkrA"])
    P.memset("pool", wkrR[:], 0.0, ["wkrR"])
    P.copy("dve", wkrA[:, :, 64:96], wkr[:], [("slab", "kr"), "wkrA"], ["wkrA"])
    P.ts("dve", wkrR[:, :, 64:80], wkr[:, :, 16:32], -1.0, None, ALU.mult, None, [("slab", "kr"), "wkrR"], ["wkrR"])
    P.copy("dve", wkrR[:, :, 80:96], wkr[:, :, 0:16], [("slab", "kr"), "wkrR"], ["wkrR"])
    cqf = [P.sb("cqf%d" % i, [128, 512], F32) for i in range(3)]
    sq = [P.sb("sqm%d" % i, [128, 512], F32) for i in range(3)]
    rs = P.sb("rsm", [128, 512], F32)
    m1 = P.sb("m1", [128, 512], F32)
    m2 = P.sb("m2", [128, 512], F32)
    for c in range(4):
        cs = slice(c * 512, (c + 1) * 512)
        for t in range(3):
            for kf in range(8):
                lhs = wcq[:, kf, t * 128:(t + 1) * 128] if t < 2 else wckv[:, kf, :]
                P.mm(C.ps[4 + t][:], lhs, hnT[:, kf, cs], kf == 0, kf == 7, [("slab", "cq"), ("slab", "ckv")], [("ps", 4 + t)])
            P.copy("act", cqf[t][:], C.ps[4 + t][:], [("ps", 4 + t)], [("cqf", t)])
            P.act(sq[t][:], C.ps[4 + t][:], AF.Square, [("ps", 4 + t)], [("sq", t)])
        for tiles, nfeat in (((0, 1), 256), ((2,), 128)):
            for i, t in enumerate(tiles):
                P.mm(C.ps[7][:], C.onesf[:], sq[t][:], i == 0, i == len(tiles) - 1, [("sq", t)], [("ps", 7)])
            P.act(rs[:], C.ps[7][:], AF.Ln, [("ps", 7)], ["rs"], scale=1.0 / nfeat, bias=C.epsb[:])
            P.act(rs[:], rs[:], AF.Exp, ["rs"], ["rs"], scale=-0.5)
            for t in tiles:
                dst = cqnT[:, t, cs] if t < 2 else ckvnT[:, cs]
                P.stt("dve", dst, cqf[t][:], mlap[:, t:t + 1], rs[:], ALU.mult, ALU.mult, [("cqf", t), "rs", "mlap"], [("cn", t, c)])
        for kf in range(8):
            P.mm(C.ps[0][:96, :], wkrA[:, kf, :], hnT[:, kf, cs], kf == 0, kf == 7, ["wkrA"], [("ps", 0)])
        for kf in range(8):
            P.mm(C.ps[1][:96, :], wkrR[:, kf, :], hnT[:, kf, cs], kf == 0, kf == 7, ["wkrR"], [("ps", 1)])
        P.tt("dve", m1[64:96, :], C.ps[0][64:96, :], cosT[64:96, cs], ALU.mult, [("ps", 0), ("cosT", c)], ["m1"])
        P.tt("dve", m2[64:96, :], C.ps[1][64:96, :], sinT[64:96, cs], ALU.mult, [("ps", 1), ("sinT", c)], ["m2"])
        P.tt("dve", kropeT[64:96, cs], m1[64:96, :], m2[64:96, :], ALU.add, ["m1", "m2"], [("krope", c)])
    P.end_phase()


def phase_mla(P, C, L, hnT, ymT, cosT, sinT, cqnT, ckvnT, kropeT):
    j = L // 2
    W = C.od_w_in[j]
    SC = 96 ** -0.5
    P.begin_phase()
    wuq = P.sb("wuq", [128, 2, 768], BF16)
    load_w_bf16(P, C, wuq, C.mla_w_uq[j], 2, 768, "wuq")
    wuq_keys = [("w", "wuq", k) for k in range(2)]
    wuqR = P.sb("wuqR", [128, 2, 8, 96], BF16)
    P.memset("pool", wuqR[:], 0.0, ["wuqR"])
    wuq4 = wuq[:].rearrange("p k (h c) -> p k h c", h=8)
    for t in range(2):
        P.ts("dve", wuqR[:, t, :, 64:80], wuq4[:, t, :, 80:96], -1.0, None, ALU.mult, None, wuq_keys + ["wuqR"], ["wuqR"])
        P.copy("dve", wuqR[:, t, :, 80:96], wuq4[:, t, :, 64:80], wuq_keys + ["wuqR"], ["wuqR"])
    wukv = P.sb("wukv", [128, 1, 1024], BF16)
    load_w_bf16(P, C, wukv, C.mla_w_ukv[j], 1, 1024, "wukv")
    wukv_k = [("w", "wukv", 0)]
    m1s = [P.sb("m1_%d" % i, [128, 512], F32) for i in range(2)]
    m2s = [P.sb("m2_%d" % i, [128, 512], F32) for i in range(2)]
    QT = [P.sb("QT%d" % i, [128, 2, S], BF16) for i in range(2)]
    KT = [P.sb("KT%d" % i, [128, 2, S], BF16) for i in range(2)]
    VA = [P.sb("VA%d" % i, [128, NT, 2, 128], BF16) for i in range(2)]
    sgm = [P.sb("sgmT%d" % i, [128, S], BF16) for i in range(2)]
    PT = [P.sb("PT%d" % i, [128, 512], BF16) for i in range(4)]
    rec = P.sb("rec", [128, 512], F32)
    tmpf = P.sb("tmpf", [128, 512], F32)
    gst = [P.sb("gst%d" % i, [128, 8, 128], F32) for i in range(2)]
    gbf = [P.sb("gbf%d" % i, [128, 8, 128], BF16) for i in range(2)]
    wukv3 = wukv[:, 0, :].rearrange("p (h c) -> p h c", h=8)
    for i in range(2):
        P.memset("dve" if i == 0 else "pool", VA[i][:], 1.0, [("VA", i)])
    cnt = {"ps": 0, "m": 0}

    def pbank():
        b = (4, 5, 7)[cnt["ps"] % 3]
        cnt["ps"] += 1
        return b

    def proj_units(hp, sl):
        for par in range(2):
            h = 2 * hp + par
            for c in range(4):
                cs = slice(c * 512, (c + 1) * 512)
                b = pbank()
                P.mm(C.ps[b][:64, :], wukv[:, 0, h * 128:h * 128 + 64], ckvnT[:, cs], True, True, wukv_k, [("ps", b)])
                P.copy("dve", KT[sl][0:64, par, cs], C.ps[b][:64, :], [("ps", b)], [("KT", sl, par)])
                bA = pbank()
                bR = pbank()
                for t in range(2):
                    P.mm(C.ps[bA][:96, :], wuq[:, t, h * 96:(h + 1) * 96], cqnT[:, t, cs], t == 0, t == 1, wuq_keys, [("ps", bA)])
                for t in range(2):
                    P.mm(C.ps[bR][:96, :], wuqR[:, t, h, :], cqnT[:, t, cs], t == 0, t == 1, ["wuqR"], [("ps", bR)])
                P.copy("dve", QT[sl][0:64, par, cs], C.ps[bA][0:64, :], [("ps", bA)], [("QT", sl, par)])
                mi = cnt["m"] % 2
                cnt["m"] += 1
                m1, m2 = m1s[mi], m2s[mi]
                P.tt("dve", m1[64:96, :], C.ps[bA][64:96, :], cosT[64:96, cs], ALU.mult, [("ps", bA)], [("m1", mi)])
                P.tt("dve", m2[64:96, :], C.ps[bR][64:96, :], sinT[64:96, cs], ALU.mult, [("ps", bR)], [("m2", mi)])
                P.tt("dve", QT[sl][64:96, par, cs], m1[64:96, :], m2[64:96, :], ALU.add, [("m1", mi), ("m2", mi)], [("QT", sl, par)])
                yield
            P.copy("dve", KT[sl][64:96, par, :], kropeT[64:96, :], [], [("KT", sl, par)])
        for kt in range(NT):
            b = pbank()
            P.mm(C.ps[b][:, 0:128], ckvnT[:, kt * 128:(kt + 1) * 128], wukv3[:, 2 * hp:2 * hp + 2, 64:128], True, True,
                 wukv_k, [("ps", b)])
            P.copy("dve", VA[sl][:, kt, 0, 0:64], C.ps[b][:, 0:64], [("ps", b), ("VA", sl)], [("VA", sl)])
            P.copy("dve", VA[sl][:, kt, 1, 64:128], C.ps[b][:, 64:128], [("ps", b), ("VA", sl)], [("VA", sl)])
            if kt % 2 == 1:
                yield
        c0 = OD["gm"] + hp * 128
        P.dma(gst[sl][:], W[:, c0:c0 + 128].rearrange("(k p) c -> p k c", p=128), writes=[("gst", sl)])
        P.copy("pool", gbf[sl][:], gst[sl][:], [("gst", sl)], [("gbf", sl)])
        for c in range(4):
            cs = slice(c * 512, (c + 1) * 512)
            b = pbank()
            for kf in range(8):
                P.mm(C.ps[b][:], gbf[sl][:, kf, :], hnT[:, kf, cs], kf == 0, kf == 7, [("gbf", sl)], [("ps", b)])
            P.act(sgm[sl][:, cs], C.ps[b][:], AF.Silu, [("ps", b)], [("sgm", sl, c)])
            yield

    for _ in proj_units(0, 0):
        pass
    for hp in range(4):
        sl = hp % 2
        tile_i = 4 + hp
        gen = proj_units(hp + 1, 1 - sl) if hp < 3 else None
        groups = []
        for par in range(2):
            for qc in range(4):
                ob = 2 + (qc % 2)
                nk = 4 * qc + 4
                steps = []
                for kt in range(nk):
                    o = max(0, kt - 4 * qc)
                    q0 = qc * 512 + o * 128
                    ncol = 512 - o * 128
                    steps.append(dict(c0=o * 128, c1=512, lhsK=KT[sl][0:96, par, kt * 128:(kt + 1) * 128], rhsQ=QT[sl][0:96, par, q0:q0 + ncol],
                                      rk=[("KT", sl, par), ("QT", sl, par)], scale=SC,
                                      masks=[(o * 128, (o + 1) * 128, C.tri[:])] if kt >= 4 * qc else [],
                                      lhsV=VA[sl][:, kt, par, :], ob=ob, rv=[("VA", sl)]))

                def fin(par=par, qc=qc, ob=ob, tile_i=tile_i, sl=sl):
                    lo = slice(par * 64, par * 64 + 64)
                    qs = slice(qc * 512, (qc + 1) * 512)
                    softmax_pv_finish(P, C, C.ps[ob], par, ymT[lo, tile_i, qs], rec, tmpf, sgm[sl][lo, qs], ("sgm", sl, qc),
                                      ("ps", ob), ("ymT", tile_i, par, qc), True)
                groups.append((steps, fin))

        def tick(idx, gen=gen):
            if gen is not None and idx % 2 == 1:
                next(gen, None)
        run_attention(P, C, groups, PT, sbanks=(0, 1, 6), tick=tick, mask_eng="pool")
        if gen is not None:
            for _ in gen:
                pass
    P.end_phase()


def odd_layer(P, C, L, src, dst, do_nsa=True, do_mla=True):
    P.push_scope()
    ymT = P.sb("ymT", [128, 8, S], BF16)
    hnT = P.sb("hnT", [128, 8, S], BF16)
    phase_prenorm(P, C, L, src, hnT)
    if not (do_nsa and do_mla):
        P.begin_phase()
        P.memset("pool", ymT[:], 0.0, ["ymT"])
        P.end_phase()
    if do_nsa:
        nsa_mixer(P, C, L, hnT, ymT)
    if do_mla:
        P.push_scope()
        cosT = P.sb("cosT", [128, S], F32)
        sinT = P.sb("sinT", [128, S], F32)
        cqnT = P.sb("cqnT", [128, 2, S], BF16)
        ckvnT = P.sb("ckvnT", [128, S], BF16)
        kropeT = P.sb("kropeT", [128, S], BF16)
        phase_mla_prep(P, C, L, hnT, cosT, sinT, cqnT, ckvnT, kropeT)
        phase_mla(P, C, L, hnT, ymT, cosT, sinT, cqnT, ckvnT, kropeT)
        P.pop_scope()
    if C.dbg is not None:
        P.begin_phase()
        cv = [P.sb("dbgc%d" % i, [128, S], F32) for i in range(2)]
        for t in range(8):
            P.copy("dve", cv[t % 2][:], ymT[:, t, :], [], [("cv", t % 2)])
            P.dma(C.dbg[t], cv[t % 2][:], reads=[("cv", t % 2)], writes=[("dbg", t % 2)])
        P.end_phase()
    phase_out(P, C, L, ymT, src, dst)
    P.pop_scope()


class SlabLoader:
    def __init__(self, P, tag):
        self.P = P
        self.tag = tag
        self.st = [P.sb("sls_%s%d" % (tag, i), [128, 8, 128], F32) for i in range(2)]
        self.bf = [P.sb("slb_%s%d" % (tag, i), [128, 8, 128], BF16) for i in range(2)]
        self.n = 0

    def load(self, W, c0, n, eng="pool"):
        s = self.n % 2
        self.n += 1
        P = self.P
        P.dma(self.st[s][:, :, 0:n], W[:, c0:c0 + n].rearrange("(k p) c -> p k c", p=128), writes=[("sls", self.tag, s)])
        P.copy(eng, self.bf[s][:, :, 0:n], self.st[s][:, :, 0:n], [("sls", self.tag, s)], [("slb", self.tag, s)])
        return self.bf[s], ("slb", self.tag, s)


def proj_fm(P, C, slab, skey, n, hnT, consume, banks=(6, 7)):
    for c in range(4):
        cs = slice(c * 512, (c + 1) * 512)
        b = banks[c % len(banks)]
        for kf in range(8):
            P.mm(C.ps[b][0:n, :], slab[:, kf, 0:n], hnT[:, kf, cs], kf == 0, kf == 7, [skey], [("ps", b)])
        consume(c, cs, C.ps[b], ("ps", b))


def gelu_tanh(P, x_ps, xkey, out, okey, y2, t1, sg, tag):
    P.act(y2, x_ps, AF.Square, [xkey], [tag + "y2"])
    P.ts("dve", y2, y2, 0.044715, 1.0, ALU.mult, ALU.add, [tag + "y2"], [tag + "y2"])
    P.tt("dve", t1, y2, x_ps, ALU.mult, [tag + "y2", xkey], [tag + "t1"])
    P.act(sg, t1, AF.Sigmoid, [tag + "t1"], [tag + "sg"], scale=1.5957691216057308)
    P.tt("dve", out, sg, x_ps, ALU.mult, [tag + "sg", xkey], [okey])


def nsa_mixer(P, C, L, hnT, ymT):
    j = L // 2
    W = C.od_w_in[j]
    P.push_scope()
    QA = P.sb("QaugT", [128, 8, S], BF16)
    KS = P.sb("KselA", [128, 2, S], BF16)
    KW = P.sb("KwinA", [128, 2, S], BF16)
    SgT = P.sb("SgT", [32, S], BF16)
    KcA = P.sb("KcA", [128, 2, 128], BF16)
    VcA = P.sb("VcA", [128, 2, 2, 128], BF16)
    gsel = P.sb("gsel", [32, 3, 8, 128], BF16)

    P.begin_phase()
    SL = SlabLoader(P, "a")
    P.memset("dve", QA[64:96, :, :], 0.0, ["QAmask"])
    P.dma(QA[96:100, :, :], C.qaug[:, :, :], writes=["QAaug"])
    P.dma(gsel[:], C.gsel[:, :, :, :], writes=["gsel"])
    for y in range(2):
        P.dma(KS[64:96, y, :], C.kaugc[0:32, :], writes=[("KSe", y)])
        P.dma(KS[96:100, y, :], C.kaugc[32:36, :], writes=[("KSa", y)])
        P.dma(KW[96:100, y, :], C.kaugc[32:36, :], writes=[("KWa", y)])
    P.memset("dve", KW[64:96, :, :], 0.0, ["KWz"])
    P.memset("pool", SgT[:], 0.0, ["SgT0"])
    import os
    part = int(os.environ.get("NSA_PART", "9"))
    for hp in range(4 if part >= 1 else 0):
        slab, sk = SL.load(W, OD["q"] + hp * 128, 128, eng="dve")

        def cons_q(c, cs, ps, pk, hp=hp):
            for par in range(2):
                P.act(QA[0:64, 2 * hp + par, cs], ps[par * 64:(par + 1) * 64, :], AF.Copy, [pk], [("QA", 2 * hp + par, c)], scale=0.125)
        proj_fm(P, C, slab, sk, 128, hnT, cons_q)
    for (i, dst, nm) in (((2, KS, "KS"), (4, KW, "KW")) if part >= 2 else ()):
        slab, sk = SL.load(W, OD["kv"] + i * 128, 128, eng="dve")

        def cons_k(c, cs, ps, pk, dst=dst, nm=nm):
            for y in range(2):
                P.copy("act" if y == 0 else "dve", dst[0:64, y, cs], ps[y * 64:(y + 1) * 64, :], [pk], [(nm, y, c)])
        proj_fm(P, C, slab, sk, 128, hnT, cons_k)
    def cons_g(c, cs, ps, pk):
        P.act(SgT[0:32, cs], ps[0:32, :], AF.Sigmoid, [pk, "SgT0"], [("SgT", c)])
    if part >= 3:
        slab, sk = SL.load(W, OD["gl"], 32)
        proj_fm(P, C, slab, sk, 32, hnT, cons_g)
    P.end_phase()
    stop = os.environ.get("NSA_STOP", "")
    if stop == "a":
        P.pop_scope()
        return

    P.begin_phase()
    SL = SlabLoader(P, "b")
    P.memset("pool", VcA[:], 1.0, ["VcA"])
    P.memset("pool", KcA[64:96, :, :], 0.0, ["KcAz"])
    for y in range(2):
        P.dma(KcA[96:100, y, :], C.kaugcmp[:, :], writes=[("KcAa", y)])
    K2 = [P.sb("K2_%d" % y, [128, S], BF16) for y in range(2)]
    G = P.sb("Gc", [128, 16, 128], BF16)
    GT = P.sb("GTc", [128, 2, 128], BF16)
    w1st = P.sb("w1st", [128, 16, 256], F32)
    w1b = P.sb("w1b", [128, 16, 256], BF16)
    w2st = P.sb("w2st", [128, 2, 64], F32)
    w2b = P.sb("w2b", [128, 2, 64], BF16)
    pos2 = P.sb("pos2", [128, 2, 16], F32)
    P.dma(pos2[:], C.nsapos2[j], writes=["pos2"])
    y2 = P.sb("gy2", [128, 128], F32)
    t1 = P.sb("gt1", [128, 128], F32)
    sg = P.sb("gsg", [128, 128], F32)
    P.memset("pool", GT[:], 0.0, ["GT0"])
    for kv in range(2):
        w1 = (C.nsa_ck_w1 if kv == 0 else C.nsa_cv_w1)[j]
        w2 = (C.nsa_ck_w2 if kv == 0 else C.nsa_cv_w2)[j]
        P.dma(w1st[:], w1.rearrange("(j p) c -> p j c", p=128), writes=["w1st"])
        P.copy("dve", w1b[:, 0:8, :], w1st[:, 0:8, :], ["w1st"], ["w1b_a"])
        P.copy("act", w1b[:, 8:16, :], w1st[:, 8:16, :], ["w1st"], ["w1b_b"])
        P.dma(w2st[:], w2.rearrange("(k p) c -> p k c", p=128), writes=["w2st"])
        P.copy("dve", w2b[:], w2st[:], ["w2st"], ["w2b"])
        slab, sk = SL.load(W, OD["kv"] + kv * 128, 128, eng="dve")
        for y in range(2):
            P.memset("pool", K2[y][64:128, S - 1:S], 0.0, [("K2z", y)])

        def cons_c(c, cs, ps, pk):
            for y in range(2):
                src = ps[y * 64:(y + 1) * 64, :]
                eng = "act" if y == 0 else "dve"
                P.copy(eng, K2[y][0:64, cs], src, [pk], [("K2a", y, c)])
                if c == 0:
                    P.copy(eng, K2[y][64:128, 0:511], ps[y * 64:(y + 1) * 64, 1:512], [pk], [("K2b", y, c)])
                else:
                    P.copy(eng, K2[y][64:128, c * 512 - 1:c * 512 + 511], src, [pk, ("K2z", y)], [("K2b", y, c)])
        if part >= 1:
            proj_fm(P, C, slab, sk, 128, hnT, cons_c)
        k2keys = [[("K2a", y, c) for c in range(4)] + [("K2b", y, c) for c in range(4)] + [("K2z", y)] for y in range(2)]
        for y in range(2 if part >= 2 else 0):
            for jj in range(16):
                P.ts("dve", G[:, jj, 0:127], K2[y][:, 2 * jj:2 * jj + 2017:16], pos2[:, kv, jj:jj + 1], None, ALU.add, None,
                     k2keys[y] + ["pos2"], [("G", jj)])
            if part < 3:
                continue
            for ht in range(2):
                b = 4 + ht
                for jj in range(16):
                    P.mm(C.ps[b][:, 0:127], w1b[:, jj, ht * 128:(ht + 1) * 128], G[:, jj, 0:127], jj == 0, jj == 15, [("G", jj), "w1b_a", "w1b_b"], [("ps", b)])
                gelu_tanh(P, C.ps[b][:, 0:127], ("ps", b), GT[:, ht, 0:127], ("GT", ht), y2[:, 0:127], t1[:, 0:127], sg[:, 0:127], "g")
            if part < 4:
                continue
            if kv == 0:
                for ht in range(2):
                    P.mm(C.ps[2][0:64, 0:128], w2b[:, ht, :], GT[:, ht, :], ht == 0, ht == 1, [("GT", ht), "GT0", "w2b"], [("ps", 2)])
                P.copy("act", KcA[0:64, y, :], C.ps[2][0:64, 0:128], [("ps", 2)], [("KcA", y)])
            elif part >= 5:
                for ht in range(2):
                    P.mm(C.ps[3][:, 0:64], GT[:, ht, :], w2b[:, ht, :], ht == 0, ht == 1, [("GT", ht), "GT0", "w2b"], [("ps", 3)])
                if part >= 6:
                    P.copy("dve", VcA[:, y, 0, 0:64], C.ps[3][:, 0:64], [("ps", 3), "VcA"], [("VcAw", y, 0)])
                    P.copy("dve", VcA[:, y, 1, 64:128], C.ps[3][:, 0:64], [("ps", 3), "VcA"], [("VcAw", y, 1)])
    P.end_phase()
    if stop == "b":
        P.pop_scope()
        return

    P.begin_phase()
    addm = P.sb("addm", [128, S], BF16)
    P.dma(addm[:], C.addmask[:, :], writes=["addm"])
    ovl = P.sb("ovl", [128, 64], BF16)
    P.dma(ovl[:], C.ovl[:, :], writes=["ovl"])
    selc = P.sb("selc", [128, NT, 2, 32], F32)
    P.dma(selc[:], C.selc[:, :, :, :], writes=["selc"])
    pslc = P.sb("pslcT", [32, 2, S], F32)
    sm = [P.sb("smc%d" % i, [128, 512], F32) for i in range(2)]
    PT = [P.sb("PTc%d" % i, [128, 512], BF16) for i in range(2)]
    rec = P.sb("rec", [128, 512], F32)
    tmpf = P.sb("tmpf", [128, 512], F32)
    rec2 = P.sb("rec2", [32, 512], F32)
    tmp2 = P.sb("tmp2", [32, 512], F32)
    items = [(h, qc) for h in range(8) for qc in range(4)]

    def cmpA(i):
        h, qc = items[i]
        y = h // 4
        qs = slice(qc * 512, (qc + 1) * 512)
        s_ = i % 2
        P.mm(C.ps[s_][:], KcA[0:100, y, :], QA[0:100, h, qs], True, True, [], [("ps", s_)])
        P.tt("dve", sm[s_][:], C.ps[s_][:], addm[:, qs], ALU.add, [("ps", s_), "addm"], [("sm", s_)])
        P.act(PT[s_][:], sm[s_][:], AF.Exp, [("sm", s_)], [("PT", s_)])

    def cmpB(i):
        h, qc = items[i]
        y, par, tile_i, hh = h // 4, h % 2, h // 2, h % 4
        lo = slice(par * 64, par * 64 + 64)
        qs = slice(qc * 512, (qc + 1) * 512)
        s_ = i % 2
        ob = 2 + s_
        P.mm(C.ps[ob][:], VcA[:, y, par, :], PT[s_][:], True, True, [("PT", s_)], [("ps", ob)])
        P.mm(C.ps[5][0:64, :], ovl[:], PT[s_][:], True, True, [("PT", s_), "ovl"], [("ps", 5)])
        P.mm(C.ps[4][:], gsel[0:32, 0, h, :], SgT[0:32, qs], True, True, ["gsel"], [("ps", 4)])
        softmax_pv_finish(P, C, C.ps[ob], par, ymT[lo, tile_i, qs], rec, tmpf, None, None, ("ps", ob), ("ymT", tile_i, par, qc), True,
                          clamp=True, gate_ps=C.ps[4], gkey=("ps", 4))
        P.ts("dve", rec2[:], C.ps[5][32:64, :], 1e-18, None, ALU.max, None, [("ps", 5)], ["rec2"])
        P.act(rec2[:], rec2[:], AF.Ln, ["rec2"], ["rec2"])
        P.act(rec2[:], rec2[:], AF.Exp, ["rec2"], ["rec2"], scale=-1.0)
        if hh == 0:
            P.tt("dve", pslc[:, y, qs], C.ps[5][0:32, :], rec2[:], ALU.mult, [("ps", 5), "rec2"], [("pslc", y, qc)])
        else:
            P.tt("dve", tmp2[:], C.ps[5][0:32, :], rec2[:], ALU.mult, [("ps", 5), "rec2"], ["tmp2"])
            P.tt("dve", pslc[:, y, qs], pslc[:, y, qs], tmp2[:], ALU.add, ["tmp2", ("pslc", y, qc)], [("pslc", y, qc)])

    cmpA(0)
    for i in range(len(items)):
        if i + 1 < len(items):
            cmpA(i + 1)
        cmpB(i)
    sc = [P.sb("scs%d" % i, [128, 32], F32) for i in range(2)]
    m8 = [P.sb("m8s%d" % i, [128, 8], F32) for i in range(2)]
    ng = [P.sb("ngs%d" % i, [128, 32], F32) for i in range(2)]
    sitems = [(y, qt) for y in range(2) for qt in range(NT)]

    def selA(i):
        y, qt = sitems[i]
        s_ = i % 2
        ts_ = slice(qt * 128, (qt + 1) * 128)
        P.tr(C.ps[6 + s_][:, 0:32], pslc[:, y, ts_], C.identf[0:32, 0:32], [("pslc", y, qt // 4)], [("ps", 6 + s_)])
        P.tt("dve", sc[s_][:], C.ps[6 + s_][:, 0:32], selc[:, qt, 0, :], ALU.mult, [("ps", 6 + s_), "selc"], [("sc", s_)])
        P.tt("dve", sc[s_][:], sc[s_][:], selc[:, qt, 1, :], ALU.add, [("sc", s_), "selc"], [("sc", s_)])
        P.op("dve", lambda e, s_=s_: e.max(out=m8[s_][:], in_=sc[s_][:]), [("sc", s_)], [("m8", s_)])
        P.ts("dve", ng[s_][:], sc[s_][:], m8[s_][:, 7:8], 30000.0, ALU.is_ge, ALU.mult, [("sc", s_), ("m8", s_)], [("ng", s_)])
        P.ts("dve", ng[s_][:], ng[s_][:], -30000.0, None, ALU.add, None, [("ng", s_)], [("ng", s_)])

    def selB(i):
        y, qt = sitems[i]
        s_ = i % 2
        ts_ = slice(qt * 128, (qt + 1) * 128)
        P.tr(C.ps[4 + s_][0:32, 0:128], ng[s_][:], C.identf[:], [("ng", s_)], [("ps", 4 + s_)])
        for hh in range(4):
            eng = "act" if s_ == 0 else "dve"
            P.copy(eng, QA[64:96, 4 * y + hh, ts_], C.ps[4 + s_][0:32, 0:128], [("ps", 4 + s_), "QAmask"], [("QAm", 4 * y + hh, qt)])

    selA(0)
    for i in range(len(sitems)):
        if i + 1 < len(sitems):
            selA(i + 1)
        selB(i)
    P.end_phase()
    if stop == "c":
        P.pop_scope()
        return

    for br in (1, 2):
        P.begin_phase()
        SL = SlabLoader(P, "v%d" % br)
        VA = P.sb("VAn", [128, NT, 2, 2, 128], BF16)
        P.memset("dve", VA[:], 1.0, ["VA"])
        slab, sk = SL.load(W, OD["kv"] + (3 if br == 1 else 5) * 128, 128, eng="dve")
        for kt in range(NT):
            b = 6 + (kt % 2)
            for kf in range(8):
                P.mm(C.ps[b][:, 0:128], hnT[:, kf, kt * 128:(kt + 1) * 128], slab[:, kf, :], kf == 0, kf == 7, [sk], [("ps", b)])
            pv = C.ps[b][:, 0:128].rearrange("p (y c) -> p y c", y=2)
            eng = "act" if kt % 2 == 0 else "dve"
            P.copy(eng, VA[:, kt, :, 0, 0:64], pv, [("ps", b), "VA"], [("VAw", kt, 0)])
            P.copy(eng, VA[:, kt, :, 1, 64:128], pv, [("ps", b), "VA"], [("VAw", kt, 1)])
        sgn = None
        if br == 2:
            sgn = P.sb("sgn", [128, 4, S], BF16)
            for hp in range(4):
                slab2, sk2 = SL.load(W, OD["gn"] + hp * 128, 128, eng="dve")

                def cons_gn(c, cs, ps, pk, hp=hp):
                    P.act(sgn[:, hp, cs], ps[:], AF.Silu, [pk], [("sgn", hp, c)])
                proj_fm(P, C, slab2, sk2, 128, hnT, cons_gn)
        KA = KS if br == 1 else KW
        PT = [P.sb("PTn%d" % i, [128, 512], BF16) for i in range(6)]
        rec = P.sb("rec", [128, 512], F32)
        tmpf = P.sb("tmpf", [128, 512], F32)
        groups = []
        for h in range(8):
            y, par, tile_i = h // 4, h % 2, h // 2
            for qc in range(4):
                ob = 2 + (qc % 2)
                kts = list(range(0, 4 * qc + 4)) if br == 1 else list(range(max(0, 4 * qc - 4), 4 * qc + 4))
                steps = []
                for kt in kts:
                    o = kt - 4 * qc
                    rlo = max(o, 0)
                    rhi = 3 if br == 1 else min(o + 4, 3)
                    c0, c1 = rlo * 128, (rhi + 1) * 128
                    masks = []
                    if o >= 0:
                        masks.append((o * 128, (o + 1) * 128, C.tri[:]))
                    if br == 2 and o <= -1:
                        masks.append(((o + 4) * 128, (o + 5) * 128, C.wmask[:]))
                    steps.append(dict(c0=c0, c1=c1, lhsK=KA[0:100, y, kt * 128:(kt + 1) * 128],
                                      rhsQ=QA[0:100, h, qc * 512 + c0:qc * 512 + c1], scale=None, masks=masks,
                                      lhsV=VA[:, kt, y, par, :], ob=ob, rv=[("VAw", kt, par)]))

                def fin(h=h, par=par, qc=qc, ob=ob, tile_i=tile_i):
                    lo = slice(par * 64, par * 64 + 64)
                    qs = slice(qc * 512, (qc + 1) * 512)
                    P.mm(C.ps[4][:], gsel[0:32, br, h, :], SgT[0:32, qs], True, True, [], [("ps", 4)])
                    softmax_pv_finish(P, C, C.ps[ob], par, ymT[lo, tile_i, qs], rec, tmpf,
                                      sgn[lo, tile_i, qs] if br == 2 else None, ("sgn", tile_i, qc) if br == 2 else None,
                                      ("ps", ob), ("ymT", tile_i, par, qc), False, clamp=False, gate_ps=C.ps[4], gkey=("ps", 4))
                groups.append((steps, fin))
        run_attention(P, C, groups, PT, sbanks=(0, 1, 5, 7), depth=4)
        P.end_phase()
    P.pop_scope()
```

```python
import math
from contextlib import ExitStack

import numpy as np
import concourse.bass as bass
import concourse.mybir as mybir
from concourse.bass_utils import run_bass_kernel_spmd

F32 = mybir.dt.float32
BF16 = mybir.dt.bfloat16
I32 = mybir.dt.int32
ALU = mybir.AluOpType
AF = mybir.ActivationFunctionType

S = 2048
D = 1024
NT = S // 128
EPS = 1e-6
ENGS = ("pe", "act", "dve", "pool", "sp")
CENG = ("pe", "act", "dve", "pool")
N_DMA_SEMS = 84
TWO_PI = 2.0 * math.pi


class Prog:
    def __init__(self, nc):
        self.nc = nc
        self.gstack = ExitStack()
        self.engsem = {e: self.gstack.enter_context(nc.semaphore("es_" + e)) for e in CENG}
        self.dsems = [self.gstack.enter_context(nc.semaphore("ds%d" % i)) for i in range(N_DMA_SEMS)]
        self.engcnt = {e: 0 for e in CENG}
        self.dcnt = [0] * N_DMA_SEMS
        self.scopes = []
        self.uid = 0
        self.n_instr = 0

    def close(self):
        self.gstack.close()

    def gsb(self, name, shape, dt):
        return self.gstack.enter_context(self.nc.sbuf_tensor(name, list(shape), dt))

    def gps(self, name, shape, dt=F32):
        return self.gstack.enter_context(self.nc.psum_tensor(name, list(shape), dt))

    def sb(self, name, shape, dt):
        self.uid += 1
        return self.scopes[-1].enter_context(self.nc.sbuf_tensor("%s_%d" % (name, self.uid), list(shape), dt))

    def push_scope(self):
        self.scopes.append(ExitStack())

    def pop_scope(self):
        self.scopes.pop().close()

    def begin_phase(self):
        self.push_scope()
        self.ins = []
        self.last_w = {}
        self.readers = {}
        self.eng_seq = {e: [] for e in ENGS}
        self.semmap = {}

    def _add(self, eng, fn, reads, writes, kind, semkey=None):
        idx = len(self.ins)
        deps = set()
        for k in reads:
            if k in self.last_w:
                deps.add((self.last_w[k], 0))
            if isinstance(k, tuple) and k[0] == "ps":
                for r in self.readers.get(k, ()):
                    if self.ins[r][0] != eng:
                        deps.add((r, 1))
        for k in writes:
            if k in self.last_w:
                deps.add((self.last_w[k], 1))
            for r in self.readers.get(k, ()):
                deps.add((r, 2))
        for k in reads:
            self.readers.setdefault(k, []).append(idx)
        for k in writes:
            self.last_w[k] = idx
            self.readers[k] = []
        self.ins.append((eng, fn, kind, semkey, deps))
        self.eng_seq[eng].append(idx)
        return idx

    def op(self, eng, fn, reads=(), writes=()):
        return self._add(eng, fn, list(reads), list(writes), "c")

    def dma(self, out, in_, reads=(), writes=(), q="sp", **kw):
        semkey = (q, tuple(writes))
        if semkey not in self.semmap:
            assert len(self.semmap) < N_DMA_SEMS, "too many dma sem keys"
            self.semmap[semkey] = len(self.semmap)
        fn = lambda e: e.dma_start(out=out, in_=in_, **kw)
        return self._add(q, fn, list(reads), list(writes), "d", semkey)

    def act(self, out, in_, func, r, w, **kw):
        self.op("act", lambda e: e.activation(out=out, in_=in_, func=func, **kw), r, w)

    def tt(self, eng, out, in0, in1, op, r, w):
        self.op(eng, lambda e: e.tensor_tensor(out=out, in0=in0, in1=in1, op=op), r, w)

    def ts(self, eng, out, in0, s1, s2, op0, op1, r, w, **kw):
        if s2 is None:
            self.op(eng, lambda e: e.tensor_scalar(out=out, in0=in0, scalar1=s1, scalar2=None, op0=op0, **kw), r, w)
        else:
            self.op(eng, lambda e: e.tensor_scalar(out=out, in0=in0, scalar1=s1, scalar2=s2, op0=op0, op1=op1, **kw), r, w)

    def stt(self, eng, out, in0, scalar, in1, op0, op1, r, w):
        self.op(eng, lambda e: e.scalar_tensor_tensor(out=out, in0=in0, scalar=scalar, in1=in1, op0=op0, op1=op1), r, w)

    def copy(self, eng, out, in_, r, w):
        if eng == "act":
            self.op(eng, lambda e: e.copy(out=out, in_=in_), r, w)
        else:
            self.op(eng, lambda e: e.tensor_copy(out=out, in_=in_), r, w)

    def memset(self, eng, ap, val, w):
        self.op(eng, lambda e: e.memset(ap, val), (), w)

    def mm(self, out, lhsT, rhs, start, stop, r, w, skip=False):
        if skip:
            self.op("pe", lambda e: e.matmul(out, lhsT=lhsT, rhs=rhs, start=start, stop=stop, skip_group_check=True), r, w)
        else:
            self.op("pe", lambda e: e.matmul(out, lhsT=lhsT, rhs=rhs, start=start, stop=stop), r, w)

    def tr(self, out, in_, ident, r, w):
        self.op("pe", lambda e: e.transpose(out=out, in_=in_, identity=ident), r, w)

    def recip(self, out, in_, r, w):
        self.op("dve", lambda e: e.reciprocal(out=out, in_=in_), r, w)

    def end_phase(self):
        nc = self.nc
        ins = self.ins
        pos = {}
        for e in ENGS:
            for p, idx in enumerate(self.eng_seq[e]):
                pos[idx] = p
        WIN = 3

        def edge_needed(idx, d, typ):
            eng = ins[idx][0]
            deng, _, dkind, _, _ = ins[d]
            if dkind == "d" or ins[idx][2] == "d":
                return True
            if deng == eng:
                return eng != "pe"
            return True

        pruned = []
        for idx, (eng, fn, kind, semkey, deps) in enumerate(ins):
            best = {}
            keep = set()
            for (d, typ) in deps:
                if not edge_needed(idx, d, typ):
                    continue
                if ins[d][2] == "d":
                    keep.add((d, typ))
                    continue
                pe_ = ins[d][0]
                if pe_ not in best or pos[d] > pos[best[pe_][0]]:
                    best[pe_] = (d, typ)
            keep.update(best.values())
            pruned.append(keep)
        needed = set()
        for idx in range(len(ins)):
            for (d, typ) in pruned[idx]:
                needed.add(d)
        for e in CENG:
            if self.eng_seq[e]:
                needed.add(self.eng_seq[e][-1])
        token = {}
        finals = {}
        for idx, (eng, fn, kind, semkey, deps) in enumerate(ins):
            if kind == "d":
                si = self.semmap[semkey]
                self.dcnt[si] += 16
                token[idx] = (("d", si), self.dcnt[si])
                finals[("d", si)] = self.dcnt[si]
            elif idx in needed:
                self.engcnt[eng] += 1
                token[idx] = (("e", eng), self.engcnt[eng])
                finals[("e", eng)] = self.engcnt[eng]
        progs = {e: [] for e in ENGS}
        waited = {e: {} for e in ENGS}
        for idx, (eng, fn, kind, semkey, deps) in enumerate(ins):
            waits = {}
            for (d, typ) in pruned[idx]:
                sn, val = token[d]
                if waited[eng].get(sn, 0) >= val:
                    continue
                waits[sn] = max(waits.get(sn, 0), val)
            for sn, val in waits.items():
                waited[eng][sn] = val
            progs[eng].append((waits, fn, token.get(idx)))
        self.n_instr += len(ins)

        def sem_of(sn):
            return self.dsems[sn[1]] if sn[0] == "d" else self.engsem[sn[1]]

        def run_engine(e, name):
            for waits, fn, inc in progs[name]:
                for sn, val in waits.items():
                    e.wait_ge(sem_of(sn), val)
                r = fn(e)
                if inc is not None:
                    r.then_inc(sem_of(inc[0]), 16 if inc[0][0] == "d" else 1)

        with nc.Block() as block:
            @block.tensor
            def _(e):
                run_engine(e, "pe")

            @block.scalar
            def _(e):
                run_engine(e, "act")

            @block.vector
            def _(e):
                run_engine(e, "dve")

            @block.gpsimd
            def _(e):
                run_engine(e, "pool")
                for sn, val in finals.items():
                    if sn[0] == "d" and any(k[0] == "pool" and self.semmap[k] == sn[1] for k in self.semmap):
                        e.wait_ge(sem_of(sn), val)

            @block.sync
            def _(e):
                run_engine(e, "sp")
                for sn, val in finals.items():
                    e.wait_ge(sem_of(sn), val)
        nc.all_engine_barrier()
        self.pop_scope()


class Ctx:
    pass


def load_w_bf16(P, C, dst, src, nkf, cols, tag, conv_engs=("dve", "act")):
    stage = [P.sb("wst_%s%d" % (tag, i), [128, cols], F32) for i in range(2)]
    for kf in range(nkf):
        s = kf % 2
        P.dma(stage[s][:], src[kf * 128:(kf + 1) * 128, :], writes=[("wst", tag, s)])
        P.copy(conv_engs[kf % len(conv_engs)], dst[:, kf, :], stage[s][:], [("wst", tag, s)], [("w", tag, kf)])


def rstd_from_ssq(P, C, st, n, key):
    P.act(st[:, 1:2], st[:, 0:1], AF.Sqrt, [key + (0,)], [key + (1,)], scale=1.0 / n, bias=C.epsb[:])
    P.recip(st[:, 2:3], st[:, 1:2], [key + (1,)], [key + (2,)])


def phase_prenorm(P, C, L, src, hnT):
    P.begin_phase()
    gb = P.sb("gpre", [128, D], F32)
    P.dma(gb[:], C.pre_norm[L:L + 1, :].partition_broadcast(128), writes=["gpre"])
    ht = [P.sb("ht%d" % i, [128, D], F32) for i in range(2)]
    junk = P.sb("junk", [128, D], BF16)
    hnb = [P.sb("hnb%d" % i, [128, D], BF16) for i in range(2)]
    st = [P.sb("st%d" % i, [128, 4], F32) for i in range(2)]
    psT = C.ps[0][:].bitcast(BF16)
    psT2 = C.ps[1][:].bitcast(BF16)
    def stage1(tt):
        s = tt % 2
        P.dma(ht[s][:], src[tt * 128:(tt + 1) * 128, :], writes=[("ht", s)])
        P.act(junk[:], ht[s][:], AF.Square, [("ht", s)], ["junk", ("st", s, 0)], accum_out=st[s][:, 0:1])
        rstd_from_ssq(P, C, st[s], D, ("st", s))
        P.stt("dve", hnb[s][:], ht[s][:], st[s][:, 2:3], gb[:], ALU.mult, ALU.mult, [("ht", s), ("st", s, 2), "gpre"], [("hnb", s)])

    def stage2(tt):
        s = tt % 2
        pst = psT if s == 0 else psT2
        for kf in range(8):
            P.tr(pst[:, kf * 128:(kf + 1) * 128], hnb[s][:, kf * 128:(kf + 1) * 128], C.ident[:], [("hnb", s), "ident"], [("psT", s)])
        P.copy("act" if s == 0 else "dve", hnT[:, :, tt * 128:(tt + 1) * 128], pst[:, 0:1024].rearrange("p (k t) -> p k t", k=8), [("psT", s)], [("hnT", tt)])

    stage1(0)
    for tt in range(NT):
        if tt + 1 < NT:
            stage1(tt + 1)
        stage2(tt)
    P.end_phase()


def phase_out(P, C, L, ymT, src, dst):
    P.begin_phase()
    wout = P.sb("wout", [128, 8, D], BF16)
    wpg = P.sb("wpg", [128, 8, D], BF16)
    wpp = P.sb("wpp", [128, 2, D], BF16)
    load_w_bf16(P, C, wout, C.w_out[L], 8, D, "wout")
    load_w_bf16(P, C, wpg, C.ple_gate[L], 8, D, "wpg")
    load_w_bf16(P, C, wpp, C.ple_proj[L], 2, D, "wpp")
    gb = P.sb("gpost", [128, D], F32)
    P.dma(gb[:], C.post_norm[L:L + 1, :].partition_broadcast(128), writes=["gpost"])
    ht = [P.sb("ht%d" % i, [128, D], F32) for i in range(2)]
    pt = [P.sb("pt%d" % i, [128, 256], F32) for i in range(2)]
    ptb = [P.sb("ptb%d" % i, [128, 256], BF16) for i in range(2)]
    pT = [P.sb("pT%d" % i, [128, 2, 128], BF16) for i in range(2)]
    junk = P.sb("junk", [128, D], BF16)
    t1 = [P.sb("t1%d" % i, [128, D], F32) for i in range(2)]
    hm = [P.sb("hm%d" % i, [128, D], F32) for i in range(2)]
    hmb = [P.sb("hmb%d" % i, [128, D], BF16) for i in range(2)]
    hmT = [P.sb("hmT%d" % i, [128, 8, 128], BF16) for i in range(2)]
    sg = [P.sb("sg%d" % i, [128, D], F32) for i in range(2)]
    hn = [P.sb("hnw%d" % i, [128, D], F32) for i in range(2)]
    st = [P.sb("st%d" % i, [128, 4], F32) for i in range(2)]
    wkeys_out = [("w", "wout", k) for k in range(8)]
    wkeys_pg = [("w", "wpg", k) for k in range(8)]
    wkeys_pp = [("w", "wpp", k) for k in range(2)]
    def stage1(tt):
        s = tt % 2
        tsl = slice(tt * 128, (tt + 1) * 128)
        P.dma(ht[s][:], src[tsl, :], writes=[("ht", s)])
        P.dma(pt[s][:], C.p[L, tsl, :], writes=[("pt", s)])
        for hf in range(2):
            for kf in range(8):
                P.mm(C.ps[hf][:], ymT[:, kf, tsl], wout[:, kf, hf * 512:(hf + 1) * 512], kf == 0, kf == 7,
                     [("ymT", kf)] + wkeys_out, [("ps", hf)])
        for hf in range(2):
            P.act(junk[:, hf * 512:(hf + 1) * 512], C.ps[hf][:], AF.Square, [("ps", hf)], ["junk", ("st", s, 0, hf)],
                  accum_out=st[s][:, hf:hf + 1])
        P.tt("dve", st[s][:, 0:1], st[s][:, 0:1], st[s][:, 1:2], ALU.add, [("st", s, 0, 0), ("st", s, 0, 1)], [("st", s, 0)])
        rstd_from_ssq(P, C, st[s], D, ("st", s))
        for hf in range(2):
            hs = slice(hf * 512, (hf + 1) * 512)
            P.stt("dve", t1[s][:, hs], C.ps[hf][:], st[s][:, 2:3], gb[:, hs], ALU.mult, ALU.mult,
                  [("ps", hf), ("st", s, 2), "gpost"], [("t1", s, hf)])
            P.tt("dve", hm[s][:, hs], t1[s][:, hs], ht[s][:, hs], ALU.add, [("t1", s, hf), ("ht", s)], [("hm", s, hf)])
            P.copy("act", hmb[s][:, hs], hm[s][:, hs], [("hm", s, hf)], [("hmb", s, hf)])
        P.copy("act", ptb[s][:], pt[s][:], [("pt", s)], [("ptb", s)])

    def stage2(tt):
        s = tt % 2
        tsl = slice(tt * 128, (tt + 1) * 128)
        psT = C.ps[2][:].bitcast(BF16)
        for kf in range(8):
            P.tr(psT[:, kf * 128:(kf + 1) * 128], hmb[s][:, kf * 128:(kf + 1) * 128], C.ident[:],
                 [("hmb", s, kf // 4), "ident"], [("ps", 2)])
        P.copy("dve", hmT[s][:], psT[:, 0:1024].rearrange("p (k t) -> p k t", k=8), [("ps", 2)], [("hmT", s)])
        psT3 = C.ps[3][:].bitcast(BF16)
        for j in range(2):
            P.tr(psT3[:, j * 128:(j + 1) * 128], ptb[s][:, j * 128:(j + 1) * 128], C.ident[:], [("ptb", s), "ident"], [("ps", 3)])
        P.copy("dve", pT[s][:], psT3[:, 0:256].rearrange("p (k t) -> p k t", k=2), [("ps", 3)], [("pT", s)])
        for hf in range(2):
            hs = slice(hf * 512, (hf + 1) * 512)
            for kf in range(8):
                P.mm(C.ps[4 + hf][:], hmT[s][:, kf, :], wpg[:, kf, hs], kf == 0, kf == 7, [("hmT", s)] + wkeys_pg, [("ps", 4 + hf)])
            for j in range(2):
                P.mm(C.ps[6 + hf][:], pT[s][:, j, :], wpp[:, j, hs], j == 0, j == 1, [("pT", s)] + wkeys_pp, [("ps", 6 + hf)])
            P.act(sg[s][:, hs], C.ps[4 + hf][:], AF.Sigmoid, [("ps", 4 + hf)], [("sg", s, hf)])
            P.tt("dve", sg[s][:, hs], sg[s][:, hs], C.ps[6 + hf][:], ALU.mult, [("sg", s, hf), ("ps", 6 + hf)], [("sg", s, hf)])
            P.tt("dve", hn[s][:, hs], sg[s][:, hs], hm[s][:, hs], ALU.add, [("sg", s, hf), ("hm", s, hf)], [("hn", s, hf)])
        P.dma(dst[tsl, :], hn[s][:], reads=[("hn", s, 0), ("hn", s, 1)], writes=[("dst", tt % 4)], q="sp")

    stage1(0)
    for tt in range(NT):
        if tt + 1 < NT:
            stage1(tt + 1)
        stage2(tt)
    P.end_phase()


def phase_even_proj(P, C, j, hnT, uT, sgaT, sgbT, hcpad):
    P.begin_phase()
    wst = [P.sb("wst%d" % i, [128, 8, 128], F32) for i in range(2)]
    wbf = [P.sb("wbf%d" % i, [128, 8, 128], BF16) for i in range(2)]
    aT = P.sb("aT", [128, 4, S], BF16)
    sig = [P.sb("sig%d" % i, [128, 512], BF16) for i in range(2)]
    w_in = C.ev_w_in[j]
    P.memset("pool", hcpad[:, :, 0:30], 0.0, ["hcpad0"])
    n = 0
    for sl in range(20):
        s = sl % 2
        P.dma(wst[s][:], w_in[:, sl * 128:(sl + 1) * 128].rearrange("(k p) c -> p k c", p=128), writes=[("wst", s)])
        P.copy("pool", wbf[s][:], wst[s][:], [("wst", s)], [("wbf", s)])
        for c in range(4):
            cs = slice(c * 512, (c + 1) * 512)
            b = n % 4
            n += 1
            pb = C.ps[b]
            for kf in range(8):
                P.mm(pb[:], wbf[s][:, kf, :], hnT[:, kf, cs], kf == 0, kf == 7, [("wbf", s)], [("ps", b)])
            if sl < 4:
                P.copy("act", uT[:, sl, cs], pb[:], [("ps", b)], [("uT", sl, c)])
            elif sl < 8:
                P.act(sgaT[:, sl - 4, cs], pb[:], AF.Silu, [("ps", b)], [("sgaT", sl - 4, c)])
            elif sl < 12:
                P.copy("dve", aT[:, sl - 8, cs], pb[:], [("ps", b)], [("aT", sl - 8, c)])
            elif sl < 16:
                q = n % 2
                P.act(sig[q][:], pb[:], AF.Sigmoid, [("ps", b)], [("sig", q)])
                P.tt("dve", hcpad[:, sl - 12, 30 + c * 512:30 + (c + 1) * 512], aT[:, sl - 12, cs], sig[q][:], ALU.mult,
                     [("aT", sl - 12, c), ("sig", q)], [("hcpad", sl - 12, c)])
            else:
                P.act(sgbT[:, sl - 16, cs], pb[:], AF.Silu, [("ps", b)], [("sgbT", sl - 16, c)])
    P.end_phase()


def phase_conv(P, C, j, hcpad, sgbT, ymT):
    P.begin_phase()
    evp = P.sb("evp", [128, 4, 40], F32)
    P.dma(evp[:], C.evp[j], writes=["evp"])
    wpw = P.sb("wpw", [128, 4, 512], BF16)
    load_w_bf16(P, C, wpw, C.cv_w_pw[j], 4, 512, "wpw")
    wpw_keys = [("w", "wpw", k) for k in range(4)]
    dg = P.sb("dg", [128, 4, 31, 128], BF16)
    for ft in range(4):
        for k in range(31):
            P.ts("dve", dg[:, ft, k, :], C.identf[:], evp[:, ft, k:k + 1], None, ALU.mult, None,
                 ["evp", "identf"], [("dg", ft)])
    cv1 = P.sb("cv1", [128, 4, 512], F32)
    sq = P.sb("sq", [128, 4, 512], F32)
    mu = P.sb("mu", [128, 512], F32)
    m2 = P.sb("m2", [128, 512], F32)
    rs = P.sb("rs", [128, 512], F32)
    xn = P.sb("xn", [128, 4, 512], F32)
    cvn = P.sb("cvn", [128, 4, 512], BF16)
    for c in range(4):
        cs = slice(c * 512, (c + 1) * 512)
        for ft in range(4):
            for k in range(31):
                P.mm(C.ps[ft][:], dg[:, ft, k, :], hcpad[:, ft, c * 512 + k:c * 512 + k + 512], k == 0, k == 30,
                     [("dg", ft)], [("ps", ft)])
            P.act(cv1[:, ft, :], C.ps[ft][:], AF.Identity, [("ps", ft), "evp"], [("cv1", ft)], bias=evp[:, ft, 31:32])
            P.act(sq[:, ft, :], cv1[:, ft, :], AF.Square, [("cv1", ft)], [("sq", ft)])
        for ft in range(4):
            P.mm(C.ps[4][:], C.onesf[:], cv1[:, ft, :], ft == 0, ft == 3, [("cv1", ft), "onesf"], [("ps", 4)])
        for ft in range(4):
            P.mm(C.ps[5][:], C.onesf[:], sq[:, ft, :], ft == 0, ft == 3, [("sq", ft), "onesf"], [("ps", 5)])
        P.act(mu[:], C.ps[4][:], AF.Copy, [("ps", 4)], ["mu"], scale=1.0 / 512)
        P.tt("dve", m2[:], mu[:], mu[:], ALU.mult, ["mu"], ["m2"])
        P.stt("dve", m2[:], C.ps[5][:], 1.0 / 512, m2[:], ALU.mult, ALU.subtract, [("ps", 5), "m2"], ["m2"])
        P.act(rs[:], m2[:], AF.Ln, ["m2"], ["rs"], bias=C.epsb[:])
        P.act(rs[:], rs[:], AF.Exp, ["rs"], ["rs"], scale=-0.5)
        for ft in range(4):
            eng = "dve"
            P.tt(eng, xn[:, ft, :], cv1[:, ft, :], mu[:], ALU.subtract, [("cv1", ft), "mu"], [("xn", ft)])
            P.tt(eng, xn[:, ft, :], xn[:, ft, :], rs[:], ALU.mult, [("xn", ft), "rs"], [("xn", ft)])
            P.act(cvn[:, ft, :], xn[:, ft, :], AF.Silu, [("xn", ft), "evp"], [("cvn", ft)],
                  scale=evp[:, ft, 32:33], bias=evp[:, ft, 33:34])
        for ot in range(4):
            b = 6 + (ot % 2)
            for ft in range(4):
                P.mm(C.ps[b][:], wpw[:, ft, ot * 128:(ot + 1) * 128], cvn[:, ft, :], ft == 0, ft == 3,
                     [("cvn", ft)] + wpw_keys, [("ps", b)])
            P.tt("dve", ymT[:, 4 + ot, cs], C.ps[b][:], sgbT[:, ot, cs], ALU.mult, [("ps", b)], [("ymT", 4 + ot, c)])
    P.end_phase()


def phase_s5_setup(P, C, j, BtR, BtI, CtR, CtI, sc2):
    P.begin_phase()
    sc = P.sb("sc", [128, 16, 3], F32)
    P.dma(sc[:], C.s5sc[j], writes=["sc"])
    w = P.sb("w", [128, 16, 16], F32)

    def col(i):
        return w[:, :, i]
    lr, li, ls = sc[:, :, 0], sc[:, :, 1], sc[:, :, 2]
    k = ["w%d" % i for i in range(16)]
    P.act(col(0), ls, AF.Exp, ["sc"], [k[0]])
    P.tt("dve", col(1), lr, col(0), ALU.mult, ["sc", k[0]], [k[1]])
    P.tt("dve", sc2[:, :, 0], li, col(0), ALU.mult, ["sc", k[0]], ["th"])
    P.ts("dve", sc2[:, :, 1], sc2[:, :, 0], 1.0 / TWO_PI, None, ALU.mult, None, ["th"], ["thq"])
    P.act(sc2[:, :, 2], col(1), AF.Exp, [k[1]], ["rho"])
    ki = P.sb("ki", [128, 16], I32)
    P.copy("dve", ki[:], sc2[:, :, 1], ["thq"], ["ki"])
    P.stt("dve", col(2), ki[:], -TWO_PI, sc2[:, :, 0], ALU.mult, ALU.add, ["ki", "th"], [k[2]])
    P.ts("dve", col(2), col(2), math.pi, -math.pi, ALU.min, ALU.max, [k[2]], [k[2]])
    P.act(col(3), col(2), AF.Abs, [k[2]], [k[3]])
    P.act(col(4), col(2), AF.Sin, [k[2]], [k[4]])
    P.act(col(5), col(3), AF.Sin, [k[3]], [k[5]], scale=-1.0, bias=C.hpib[:])
    P.tt("dve", col(6), sc2[:, :, 2], col(5), ALU.mult, ["rho", k[5]], [k[6]])
    P.tt("dve", col(7), sc2[:, :, 2], col(4), ALU.mult, ["rho", k[4]], [k[7]])
    P.ts("dve", col(6), col(6), -1.0, None, ALU.add, None, [k[6]], [k[6]])
    P.tt("dve", col(8), lr, lr, ALU.mult, ["sc"], [k[8]])
    P.tt("dve", col(9), li, li, ALU.mult, ["sc"], [k[9]])
    P.tt("dve", col(8), col(8), col(9), ALU.add, [k[8], k[9]], [k[8]])
    P.recip(col(8), col(8), [k[8]], [k[8]])
    P.tt("dve", col(9), col(6), lr, ALU.mult, [k[6], "sc"], [k[9]])
    P.tt("dve", col(10), col(7), li, ALU.mult, [k[7], "sc"], [k[10]])
    P.tt("dve", col(9), col(9), col(10), ALU.add, [k[9], k[10]], [k[9]])
    P.tt("dve", col(11), col(9), col(8), ALU.mult, [k[9], k[8]], [k[11]])
    P.tt("dve", col(9), col(7), lr, ALU.mult, [k[7], "sc"], [k[9]])
    P.tt("dve", col(10), col(6), li, ALU.mult, [k[6], "sc"], [k[10]])
    P.tt("dve", col(9), col(9), col(10), ALU.subtract, [k[9], k[10]], [k[9]])
    P.tt("dve", col(12), col(9), col(8), ALU.mult, [k[9], k[8]], [k[12]])
    for ri, dst in ((0, BtR), (1, BtI)):
        stg = P.sb("bst%d" % ri, [128, 16, 128], F32)
        P.dma(stg[:], C.s5bT[j, ri], writes=[("bst", ri)])
        P.copy("pool", dst[:], stg[:], [("bst", ri)], [("Bt", ri)])
    cre = P.sb("cre", [128, 16, 128], F32)
    cim = P.sb("cim", [128, 16, 128], F32)
    t1 = P.sb("t1", [128, 16, 128], F32)
    t2 = P.sb("t2", [128, 16, 128], F32)
    P.dma(cre[:], C.s5cP[j, 0], writes=["cre"])
    P.dma(cim[:], C.s5cP[j, 1], writes=["cim"])
    fre = w[:, :, 11:12].to_broadcast([128, 16, 128])
    fim = w[:, :, 12:13].to_broadcast([128, 16, 128])
    P.tt("dve", t1[:], cre[:], fre, ALU.mult, ["cre", k[11]], ["t1"])
    P.tt("pool", t2[:], cim[:], fim, ALU.mult, ["cim", k[12]], ["t2"])
    P.tt("dve", CtR[:], t1[:], t2[:], ALU.subtract, ["t1", "t2"], ["CtR"])
    P.tt("dve", t1[:], cre[:], fim, ALU.mult, ["cre", k[12]], ["t1"])
    P.tt("pool", t2[:], cim[:], fre, ALU.mult, ["cim", k[11]], ["t2"])
    P.stt("dve", CtI[:], t1[:], -1.0, t2[:], ALU.mult, ALU.subtract, ["t1", "t2"], ["CtI"])
    P.end_phase()


def phase_s5(P, C, j, uT, sgaT, ymT, BtR, BtI, CtR, CtI, sc2):
    P.begin_phase()
    TH = 1024
    evp = P.sb("evp", [128, 4, 40], F32)
    P.dma(evp[:], C.evp[j], writes=["evp"])
    wglu = P.sb("wglu", [128, 4, 512], BF16)
    load_w_bf16(P, C, wglu, C.s5_w_glu[j], 4, 512, "wglu")
    wglu_keys = [("w", "wglu", k) for k in range(4)]
    iot = P.sb("iot", [128, S], F32)
    P.op("pool", lambda e: e.iota(iot[:], pattern=[[1, S]], base=0, channel_multiplier=0,
                                  allow_small_or_imprecise_dtypes=True), (), ["iot"])
    ki = P.sb("ki", [128, TH], I32)
    rr = P.sb("rr", [128, TH], F32)
    ra = P.sb("ra", [128, TH], F32)
    cs_ = P.sb("cos", [128, TH], BF16)
    sn_ = P.sb("sin", [128, TH], BF16)
    bpR = P.sb("bpR", [128, TH], BF16)
    bpI = P.sb("bpI", [128, TH], BF16)
    wR = P.sb("wR", [128, TH], BF16)
    wI = P.sb("wI", [128, TH], BF16)
    xR = P.sb("xR", [128, TH], BF16)
    xI = P.sb("xI", [128, TH], BF16)
    buR = [P.sb("buR%d" % q, [128, 512], BF16) for q in range(2)]
    buI = [P.sb("buI%d" % q, [128, 512], BF16) for q in range(2)]
    mt = [[P.sb("m%d_%d" % (i, q), [128, 512], BF16) for i in range(4)] for q in range(2)]
    mo = [P.sb("mo%d" % i, [128, TH], BF16) for i in range(4)]
    carry = P.sb("carry", [128, 16, 2], F32)
    P.memset("pool", carry[:], 0.0, ["carry"])
    ygT = P.sb("ygT", [128, 4, S], BF16)
    yv = P.sb("yv", [128, 512], F32)
    y2 = P.sb("y2", [128, 512], F32)
    sgm = P.sb("sgm", [128, 512], F32)
    it = 0
    for ft in range(4):
        for hf in range(2):
            t0 = hf * TH
            for pl in range(4):
                pr = 4 * ft + pl
                th = sc2[:, pr, 0:1]
                thq = sc2[:, pr, 1:2]
                P.ts("dve", ki[:], iot[:, t0:t0 + TH], thq, None, ALU.mult, None, ["iot"], ["ki"])
                P.act(ra[:], iot[:, t0:t0 + TH], AF.Copy, ["iot"], ["ra"], scale=th)
                P.stt("dve", rr[:], ki[:], -TWO_PI, ra[:], ALU.mult, ALU.add, ["ki", "ra"], ["rr"])
                P.ts("dve", rr[:], rr[:], math.pi, -math.pi, ALU.min, ALU.max, ["rr"], ["rr"])
                P.act(sn_[:], rr[:], AF.Sin, ["rr"], ["sin"])
                P.act(ra[:], rr[:], AF.Abs, ["rr"], ["ra"])
                P.act(cs_[:], ra[:], AF.Sin, ["ra"], ["cos"], scale=-1.0, bias=C.hpib[:])
                for c in range(2):
                    cl = slice(c * 512, (c + 1) * 512)
                    cg = slice(t0 + c * 512, t0 + (c + 1) * 512)
                    q = it % 2
                    it += 1
                    bA, bB = C.ps[2 * q], C.ps[2 * q + 1]
                    P.mm(bA[:], BtR[:, pr, :], uT[:, ft, cg], True, True, [], [("ps", 2 * q)])
                    P.mm(bB[:], BtI[:, pr, :], uT[:, ft, cg], True, True, [], [("ps", 2 * q + 1)])
                    P.copy("act", buR[q][:], bA[:], [("ps", 2 * q)], [("buR", q)])
                    P.copy("act", buI[q][:], bB[:], [("ps", 2 * q + 1)], [("buI", q)])
                    m = mt[q]
                    P.tt("dve", m[0][:], buR[q][:], cs_[:, cl], ALU.mult, [("buR", q), "cos"], [("m", q, 0)])
                    P.tt("dve", m[1][:], buI[q][:], sn_[:, cl], ALU.mult, [("buI", q), "sin"], [("m", q, 1)])
                    P.tt("dve", m[2][:], buI[q][:], cs_[:, cl], ALU.mult, [("buI", q), "cos"], [("m", q, 2)])
                    P.tt("dve", m[3][:], buR[q][:], sn_[:, cl], ALU.mult, [("buR", q), "sin"], [("m", q, 3)])
                    P.tt("dve", bpR[:, cl], m[0][:], m[1][:], ALU.add, [("m", q, 0), ("m", q, 1)], [("bpR", c)])
                    P.tt("dve", bpI[:, cl], m[2][:], m[3][:], ALU.subtract, [("m", q, 2), ("m", q, 3)], [("bpI", c)])
                P.op("dve", lambda e, pr=pr: e.tensor_tensor_scan(out=wR[:], data0=sc2[:, pr, 2:3].to_broadcast([128, TH]), data1=bpR[:],
                                                                 initial=carry[:, pr, 0:1], op0=ALU.mult, op1=ALU.add),
                     [("bpR", 0), ("bpR", 1), "carry"], ["wR"])
                P.op("dve", lambda e, pr=pr: e.tensor_tensor_scan(out=wI[:], data0=sc2[:, pr, 2:3].to_broadcast([128, TH]), data1=bpI[:],
                                                                 initial=carry[:, pr, 1:2], op0=ALU.mult, op1=ALU.add),
                     [("bpI", 0), ("bpI", 1), "carry"], ["wI"])
                if hf == 0:
                    P.copy("dve", carry[:, pr, 0:1], wR[:, TH - 1:TH], ["wR"], ["carry"])
                    P.copy("dve", carry[:, pr, 1:2], wI[:, TH - 1:TH], ["wI"], ["carry"])
                P.tt("dve", mo[0][:], wR[:], cs_[:], ALU.mult, ["wR", "cos"], ["mo0"])
                P.tt("dve", mo[1][:], wI[:], sn_[:], ALU.mult, ["wI", "sin"], ["mo1"])
                P.tt("dve", mo[2][:], wI[:], cs_[:], ALU.mult, ["wI", "cos"], ["mo2"])
                P.tt("dve", mo[3][:], wR[:], sn_[:], ALU.mult, ["wR", "sin"], ["mo3"])
                P.tt("dve", xR[:], mo[0][:], mo[1][:], ALU.subtract, ["mo0", "mo1"], ["xR"])
                P.tt("dve", xI[:], mo[2][:], mo[3][:], ALU.add, ["mo2", "mo3"], ["xI"])
                for c in range(2):
                    cl = slice(c * 512, (c + 1) * 512)
                    P.mm(C.ps[4 + c][:], CtR[:, pr, :], xR[:, cl], pl == 0, False, ["xR"], [("ps", 4 + c)])
                    P.mm(C.ps[4 + c][:], CtI[:, pr, :], xI[:, cl], False, pl == 3, ["xI"], [("ps", 4 + c)])
            for c in range(2):
                cg = slice(t0 + c * 512, t0 + (c + 1) * 512)
                P.stt("dve", yv[:], uT[:, ft, cg], evp[:, ft, 34:35], C.ps[4 + c][:], ALU.mult, ALU.add,
                      [("ps", 4 + c), "evp"], ["yv"])
                P.act(y2[:], yv[:], AF.Square, ["yv"], ["y2"])
                P.ts("dve", y2[:], y2[:], 0.044715, 1.0, ALU.mult, ALU.add, ["y2"], ["y2"])
                P.tt("dve", y2[:], y2[:], yv[:], ALU.mult, ["y2", "yv"], ["y2"])
                P.act(sgm[:], y2[:], AF.Sigmoid, ["y2"], ["sgm"], scale=1.5957691216057308)
                P.tt("pool", ygT[:, ft, cg], sgm[:], yv[:], ALU.mult, ["sgm", "yv"], [("ygT", ft, hf, c)])
    gk = [("ygT", ft, hf, c) for ft in range(4) for hf in range(2) for c in range(2)]
    n = 0
    for ot in range(4):
        for c in range(4):
            cs = slice(c * 512, (c + 1) * 512)
            b = n % 4
            n += 1
            for ft in range(4):
                P.mm(C.ps[b][:], wglu[:, ft, ot * 128:(ot + 1) * 128], ygT[:, ft, cs], ft == 0, ft == 3, gk + wglu_keys, [("ps", b)])
            P.act(sgm[:], C.ps[b][:], AF.Sigmoid, [("ps", b), "evp"], ["sgm"], bias=evp[:, ot, 35:36])
            P.tt("dve", sgm[:], sgm[:], ygT[:, ot, cs], ALU.mult, ["sgm"] + gk, ["sgm"])
            P.tt("dve", ymT[:, ot, cs], sgm[:], sgaT[:, ot, cs], ALU.mult, ["sgm"], [("ymT", ot, c)])
    P.end_phase()


def even_layer(P, C, L, src, dst):
    j = L // 2
    P.push_scope()
    ymT = P.sb("ymT", [128, 8, S], BF16)
    P.push_scope()
    uT = P.sb("uT", [128, 4, S], BF16)
    sgaT = P.sb("sgaT", [128, 4, S], BF16)
    P.push_scope()
    sgbT = P.sb("sgbT", [128, 4, S], BF16)
    hcpad = P.sb("hcpad", [128, 4, S + 30], BF16)
    P.push_scope()
    hnT = P.sb("hnT", [128, 8, S], BF16)
    phase_prenorm(P, C, L, src, hnT)
    phase_even_proj(P, C, j, hnT, uT, sgaT, sgbT, hcpad)
    P.pop_scope()
    phase_conv(P, C, j, hcpad, sgbT, ymT)
    P.pop_scope()
    P.push_scope()
    BtR = P.sb("BtR", [128, 16, 128], BF16)
    BtI = P.sb("BtI", [128, 16, 128], BF16)
    CtR = P.sb("CtR", [128, 16, 128], BF16)
    CtI = P.sb("CtI", [128, 16, 128], BF16)
    sc2 = P.sb("sc2", [128, 16, 3], F32)
    phase_s5_setup(P, C, j, BtR, BtI, CtR, CtI, sc2)
    phase_s5(P, C, j, uT, sgaT, ymT, BtR, BtI, CtR, CtI, sc2)
    P.pop_scope()
    P.pop_scope()
    phase_out(P, C, L, ymT, src, dst)
    P.pop_scope()


W_SHAPES = {
    "pre_norm": [4, D], "post_norm": [4, D], "ple_gate": [4, D, D], "ple_proj": [4, 256, D],
    "ev_w_in": [2, D, 2560], "s5_w_glu": [2, 512, 512], "cv_w_pw": [2, 512, 512], "w_out": [4, D, D],
    "evp": [2, 128, 4, 40], "s5sc": [2, 128, 16, 3], "s5bT": [2, 2, 128, 16, 128], "s5cP": [2, 2, 128, 16, 128],
    "od_w_in": [2, D, 2744], "mla_w_uq": [2, 256, 768], "mla_w_ukv": [2, 128, 1024], "mlap": [2, 128, 3],
    "ropef": [128, 2],
    "nsa_ck_w1": [2, 2048, 256], "nsa_ck_w2": [2, 256, 64], "nsa_cv_w1": [2, 2048, 256], "nsa_cv_w2": [2, 256, 64],
    "nsapos2": [2, 128, 2, 16], "selc": [128, NT, 2, 32],
}
W_BF16 = {"qaug": [4, 8, S], "kaugc": [36, S], "kaugcmp": [4, 128], "addmask": [128, S], "ovl": [128, 64], "gsel": [32, 3, 8, 128]}


def build(n_layers=4, dbg=False, odd_kw=None):
    nc = bass.Bass("TRN2", target_bir_lowering=False)
    C = Ctx()
    odd_kw = odd_kw or {}

    def din(name, shape, dt=F32):
        return nc.dram_tensor(name, list(shape), dt, kind="ExternalInput").ap()
    C.x = din("x", [S, D])
    C.p = din("p", [4, S, 256])
    C.positions = din("positions", [1, S], I32)
    C.dbg = nc.dram_tensor("dbg", [8, 128, S], F32, kind="ExternalOutput").ap() if dbg else None
    for k, shp in W_SHAPES.items():
        setattr(C, k, din(k, shp))
    for k, shp in W_BF16.items():
        setattr(C, k, din(k, shp, BF16))
    out = nc.dram_tensor("out", [S, D], F32, kind="ExternalOutput").ap()
    hbuf = nc.dram_tensor("hbuf", [S, D], F32, kind="Internal").ap()
    P = Prog(nc)
    C.ps = [P.gps("ps%d" % i, [128, 512]) for i in range(8)]
    C.ident = P.gsb("ident", [128, 128], BF16)
    C.identf = P.gsb("identf", [128, 128], F32)
    C.onesf = P.gsb("onesf", [128, 128], F32)
    C.epsb = P.gsb("epsb", [128, 1], F32)
    C.tri = P.gsb("tri", [128, 128], BF16)
    C.wmask = P.gsb("wmask", [128, 128], BF16)
    C.ropef_sb = P.gsb("ropef_sb", [128, 2], F32)
    C.hpib = P.gsb("hpib", [128, 1], F32)
    P.begin_phase()
    io = P.sb("io", [128, 128], F32)
    P.op("pool", lambda e: e.iota(io[:], pattern=[[1, 128]], base=0, channel_multiplier=-1,
                                  allow_small_or_imprecise_dtypes=True), (), ["io"])
    P.op("dve", lambda e: e.tensor_single_scalar(out=C.identf[:], in_=io[:], scalar=0.0, op=ALU.is_equal), ["io"], ["identf"])
    P.copy("dve", C.ident[:], C.identf[:], ["identf"], ["ident"])
    P.memset("pool", C.onesf[:], 1.0, ["onesf"])
    P.op("dve", lambda e: e.tensor_single_scalar(out=C.tri[:], in_=io[:], scalar=0.0, op=ALU.is_ge), ["io"], ["tri"])
    P.op("dve", lambda e: e.tensor_single_scalar(out=C.wmask[:], in_=io[:], scalar=0.0, op=ALU.is_lt), ["io"], ["wmask"])
    P.dma(C.ropef_sb[:], C.ropef[:, :], writes=["ropef_sb"])
    P.memset("pool", C.epsb[:], EPS, ["epsb"])
    P.memset("pool", C.hpib[:], math.pi / 2, ["hpib"])
    P.end_phase()
    C.ropef_dram = C.ropef
    C.ropef = C.ropef_sb
    for L in range(n_layers):
        src = C.x if L == 0 else hbuf
        dst = out if L == n_layers - 1 else hbuf
        if L % 2 == 0:
            even_layer(P, C, L, src, dst)
        else:
            odd_layer(P, C, L, src, dst, **odd_kw)
    C.ropef = C.ropef_dram
    P.close()
    return nc, P


def nsa_constants():
    import ml_dtypes
    bf = ml_dtypes.bfloat16
    t = np.arange(S)
    a_t, b_t = (t // 64).astype(np.float32), (t % 64).astype(np.float32)
    slopes = np.array([2.0 ** (-(i + 1)) for i in range(8)], np.float32)
    qaug = np.zeros((4, 8, S), np.float32)
    for h in range(8):
        qaug[0, h] = -slopes[h] * 64.0 * a_t
        qaug[1, h] = -slopes[h] * b_t
        qaug[2, h] = slopes[h] * 64.0
        qaug[3, h] = slopes[h]
    kaugc = np.zeros((36, S), np.float32)
    kaugc[t // 64, t] = 1.0
    kaugc[32] = 1.0
    kaugc[33] = 1.0
    kaugc[34] = a_t
    kaugc[35] = b_t
    c = np.arange(128)
    pc = 16 * c + 31
    kaugcmp = np.stack([np.ones(128), np.ones(128), pc // 64, pc % 64]).astype(np.float32)
    addmask = np.where((t[None, :] >= pc[:, None]) & (c[:, None] <= 126), 0.0, -30000.0).astype(np.float32)
    sb = np.arange(32)
    cs_ = c[:, None] * 16
    overlap = ((cs_ < (sb[None] + 1) * 64) & (cs_ + 32 > sb[None] * 64) & (c[:, None] <= 126)).astype(np.float32)
    ovl = np.concatenate([overlap, np.ones((128, 32), np.float32)], axis=1)
    cur = t[:, None] // 64
    forced = (sb[None] == 0) | (sb[None] == cur) | (sb[None] == cur - 1)
    causal = sb[None] * 64 <= t[:, None]
    mul = (forced | causal).astype(np.float32)
    add = np.where(forced, 1e4, np.where(causal, 0.0, -1e4)).astype(np.float32)
    selc = np.stack([mul, add], axis=1).reshape(NT, 128, 2, 32).transpose(1, 0, 2, 3)
    gsel = np.zeros((32, 3, 8, 128), np.float32)
    for b in range(3):
        for h in range(8):
            gsel[b * 8 + h, b, h, (h % 2) * 64:(h % 2) * 64 + 64] = 1.0
    return {"qaug": qaug.astype(bf), "kaugc": kaugc.astype(bf), "kaugcmp": kaugcmp.astype(bf), "addmask": addmask.astype(bf),
            "ovl": ovl.astype(bf), "gsel": gsel.astype(bf), "selc": np.ascontiguousarray(selc)}


def host_layout(inputs):
    f = lambda k: np.asarray(inputs[k], np.float32)
    ne = 2
    evp = np.zeros((ne, 128, 4, 40), np.float32)
    wdw = f("cv_w_dw")
    evp[:, :, :, 0:31] = wdw.reshape(ne, 31, 4, 128).transpose(0, 3, 2, 1)
    for col, key in ((31, "cv_b_dw"), (32, "cv_ln_g"), (33, "cv_ln_b"), (34, "s5_d"), (35, "s5_b_glu")):
        evp[:, :, :, col] = f(key).reshape(ne, 4, 128).transpose(0, 2, 1)
    s5sc = np.zeros((ne, 128, 16, 3), np.float32)
    for i, key in enumerate(("s5_lam_re", "s5_lam_im")):
        a = f(key).reshape(ne, 16, 2, 64)
        s5sc[:, :, :, i] = a.transpose(0, 2, 3, 1).reshape(ne, 128, 16)
    ls = f("s5_log_step").reshape(ne, 16, 2)
    s5sc[:, :, :, 2] = np.repeat(ls.transpose(0, 2, 1)[:, :, None, :], 64, axis=2).reshape(ne, 128, 16)
    s5bT = np.zeros((ne, 2, 128, 16, 128), np.float32)
    s5cP = np.zeros((ne, 2, 128, 16, 128), np.float32)
    for ri, (kb, kc) in enumerate((("s5_b_re", "s5_c_re"), ("s5_b_im", "s5_c_im"))):
        b = f(kb)
        c = f(kc)
        for pr in range(16):
            for gl in range(2):
                g = 2 * pr + gl
                k0 = (pr % 4) * 32 + gl * 16
                s5bT[:, ri, k0:k0 + 16, pr, gl * 64:(gl + 1) * 64] = b[:, g].transpose(0, 2, 1)
                s5cP[:, ri, gl * 64:(gl + 1) * 64, pr, k0:k0 + 16] = c[:, g].transpose(0, 2, 1)
    w_out = np.stack([f("ev_w_out")[0], f("od_w_out")[0], f("ev_w_out")[1], f("od_w_out")[1]])
    no = 2
    mlap = np.zeros((no, 128, 3), np.float32)
    mlap[:, :, 0:2] = f("mla_q_norm").reshape(no, 2, 128).transpose(0, 2, 1)
    mlap[:, :, 2] = f("mla_kv_norm")
    ropef = np.zeros((128, 2), np.float32)
    fr = (10000.0 ** (-np.arange(16, dtype=np.float32) / 16)).astype(np.float32)
    ropef[64:96, 0] = np.tile(fr, 2)
    ropef[:, 1] = ropef[:, 0] / np.float32(TWO_PI)
    rep = {"evp": evp, "s5sc": s5sc, "s5bT": s5bT, "s5cP": s5cP, "w_out": w_out, "mlap": mlap, "ropef": ropef}
    rep.update(nsa_constants())
    pos2 = np.zeros((no, 128, 2, 16), np.float32)
    for kv, key in enumerate(("nsa_pos_k", "nsa_pos_v")):
        a = f(key).reshape(no, 16, 2, 64)
        pos2[:, :, kv, :] = a.transpose(0, 2, 3, 1).reshape(no, 128, 16)
    rep["nsapos2"] = pos2
    for k in ("nsa_ck_w1", "nsa_ck_w2", "nsa_cv_w1", "nsa_cv_w2"):
        rep[k] = np.ascontiguousarray(f(k))
    for k in ("pre_norm", "post_norm", "ple_gate", "ple_proj", "ev_w_in", "s5_w_glu", "cv_w_pw", "od_w_in", "mla_w_uq", "mla_w_ukv"):
        rep[k] = np.ascontiguousarray(f(k))
    return rep


def kernel(**inputs):
    n = 8
    rep = host_layout(inputs)
    x = np.asarray(inputs["x"], np.float32)
    p = np.asarray(inputs["p"], np.float32)
    nc, _ = build(4)
    in_maps = []
    for b in range(n):
        m = dict(rep)
        m["x"] = np.ascontiguousarray(x[b])
        m["p"] = np.ascontiguousarray(p[:, b])
        m["positions"] = np.ascontiguousarray(np.asarray(inputs["positions"])[b:b + 1]).astype(np.int32)
        in_maps.append(m)
    res = run_bass_kernel_spmd(nc, in_maps, core_ids=list(range(n)))
    return np.stack([r["out"] for r in res.results], axis=0).astype(np.float32)


OD = {"q": 0, "kv": 512, "gl": 1280, "gn": 1304, "cq": 1816, "ckv": 2072, "kr": 2200, "gm": 2232}


def load_slab(P, C, W, c0, n, tag, slot, eng="pool"):
    stg = P.sb("sl_st_%s" % tag, [128, 8, n], F32)
    dst = P.sb("sl_bf_%s" % tag, [128, 8, n], BF16)
    P.dma(stg[:], W[:, c0:c0 + n].rearrange("(k p) c -> p k c", p=128), writes=[("slst", tag)])
    P.copy(eng, dst[:], stg[:], [("slst", tag)], [("slab", tag)])
    return dst


def angle_tables(P, C, ang_in, fq, f, cosT, sinT, n, tag):
    ki = P.sb("ki_" + tag, [128, n], I32)
    rr = P.sb("rr_" + tag, [128, n], F32)
    ra = P.sb("ra_" + tag, [128, n], F32)
    P.ts("dve", ki[:], ang_in, fq, None, ALU.mult, None, [tag + "in"], [tag + "ki"])
    P.ts("pool", rr[:], ki[:], -TWO_PI, None, ALU.mult, None, [tag + "ki"], [tag + "rr"])
    P.ts("pool", ra[:], ang_in, f, None, ALU.mult, None, [tag + "in"], [tag + "ra"])
    P.tt("pool", rr[:], rr[:], ra[:], ALU.add, [tag + "rr", tag + "ra"], [tag + "rr"])
    P.ts("pool", rr[:], rr[:], math.pi, -math.pi, ALU.min, ALU.max, [tag + "rr"], [tag + "rr"])
    P.act(sinT, rr[:], AF.Sin, [tag + "rr"], [tag + "sin"])
    P.act(ra[:], rr[:], AF.Abs, [tag + "rr"], [tag + "ra"])
    P.act(cosT, ra[:], AF.Sin, [tag + "ra"], [tag + "cos"], scale=-1.0, bias=C.hpib[:])


def softmax_pv_finish(P, C, ob, par, dst_rows, rec, tmpf, extra_mul, ekey, okey, wkey, first, clamp=False, gate_ps=None, gkey=None):
    lo = slice(par * 64, par * 64 + 64)
    hi = slice((1 - par) * 64, (1 - par) * 64 + 64)
    if clamp:
        P.ts("dve", rec[lo, :], ob[hi, :], 1e-18, None, ALU.max, None, [okey], [("rec", par)])
        P.act(rec[lo, :], rec[lo, :], AF.Ln, [("rec", par)], [("rec", par)])
    else:
        P.act(rec[lo, :], ob[hi, :], AF.Ln, [okey], [("rec", par)])
    P.act(rec[lo, :], rec[lo, :], AF.Exp, [("rec", par)], [("rec", par)], scale=-1.0)
    P.tt("dve", tmpf[lo, :], ob[lo, :], rec[lo, :], ALU.mult, [okey, ("rec", par)], [("tmpf", par)])
    if gate_ps is not None:
        P.tt("dve", tmpf[lo, :], tmpf[lo, :], gate_ps[lo, :], ALU.mult, [("tmpf", par), gkey], [("tmpf", par)])
    if not first:
        P.tt("dve", tmpf[lo, :], tmpf[lo, :], dst_rows, ALU.add, [("tmpf", par), wkey], [("tmpf", par)])
    if extra_mul is not None:
        P.tt("dve", dst_rows, tmpf[lo, :], extra_mul, ALU.mult, [("tmpf", par), ekey], [wkey])
    else:
        P.copy("act", dst_rows, tmpf[lo, :], [("tmpf", par)], [wkey])


def run_attention(P, C, groups, PT, depth=3, mask_eng="dve", sbanks=(0, 1), tick=None):
    flat = [(g, i) for g, (steps, fin) in enumerate(groups) for i in range(len(steps))]
    issued = 0
    for idx in range(len(flat)):
        while issued < min(len(flat), idx + depth):
            g2, i2 = flat[issued]
            st2 = groups[g2][0][i2]
            bank = sbanks[issued % len(sbanks)]
            P.mm(C.ps[bank][:, st2["c0"]:st2["c1"]], st2["lhsK"], st2["rhsQ"], True, True, st2.get("rk", []), [("ps", bank)])
            issued += 1
        g, i = flat[idx]
        steps, fin = groups[g]
        st = steps[i]
        bank = sbanks[idx % len(sbanks)]
        c0, c1 = st["c0"], st["c1"]
        pt = PT[idx % len(PT)]
        pk = ("PT", idx % len(PT))
        if st.get("scale") is not None:
            P.act(pt[:, c0:c1], C.ps[bank][:, c0:c1], AF.Exp, [("ps", bank)], [pk], scale=st["scale"])
        else:
            P.act(pt[:, c0:c1], C.ps[bank][:, c0:c1], AF.Exp, [("ps", bank)], [pk])
        for (a, b, m) in st["masks"]:
            P.tt(mask_eng, pt[:, a:b], pt[:, a:b], m, ALU.mult, [pk], [pk])
        P.mm(C.ps[st["ob"]][:, c0:c1], st["lhsV"], pt[:, c0:c1], i == 0, i == len(steps) - 1, [pk] + st.get("rv", []), [("ps", st["ob"])],
             skip=True)
        if i == len(steps) - 1:
            fin()
        if tick is not None:
            tick(idx)


def phase_mla_prep(P, C, L, hnT, cosT, sinT, cqnT, ckvnT, kropeT):
    j = L // 2
    W = C.od_w_in[j]
    P.begin_phase()
    mlap = P.sb("mlap", [128, 3], F32)
    P.dma(mlap[:], C.mlap[j], writes=["mlap"])
    posi = [P.sb("posi%d" % i, [128, 512], I32) for i in range(2)]
    posf = [P.sb("posf%d" % i, [128, 512], F32) for i in range(2)]
    ki = P.sb("rki", [128, 512], I32)
    rr = P.sb("rrr", [128, 512], F32)
    ra = P.sb("rra", [128, 512], F32)
    for c in range(4):
        cs = slice(c * 512, (c + 1) * 512)
        s = c % 2
        P.dma(posi[s][:], C.positions[0:1, cs].partition_broadcast(128), writes=[("posi", s)])
        P.copy("dve", posf[s][:], posi[s][:], [("posi", s)], [("posf", s)])
        P.ts("dve", ki[:], posf[s][:], C.ropef[:, 1:2], None, ALU.mult, None, [("posf", s)], ["ki"])
        P.ts("dve", rr[:], ki[:], -TWO_PI, None, ALU.mult, None, ["ki"], ["rr"])
        P.ts("dve", ra[:], posf[s][:], C.ropef[:, 0:1], None, ALU.mult, None, [("posf", s)], ["ra"])
        P.tt("dve", rr[:], rr[:], ra[:], ALU.add, ["rr", "ra"], ["rr"])
        P.ts("dve", rr[:], rr[:], math.pi, -math.pi, ALU.min, ALU.max, ["rr"], ["rr"])
        P.act(sinT[:, cs], rr[:], AF.Sin, ["rr"], [("sinT", c)])
        P.act(ra[:], rr[:], AF.Abs, ["rr"], ["ra"])
        P.act(cosT[:, cs], ra[:], AF.Sin, ["ra"], [("cosT", c)], scale=-1.0, bias=C.hpib[:])
    wcq = load_slab(P, C, W, OD["cq"], 256, "cq", 0, eng="act")
    wckv = load_slab(P, C, W, OD["ckv"], 128, "ckv", 0, eng="dve")
    wkr = load_slab(P, C, W, OD["kr"], 32, "kr", 0, eng="dve")
    wkrA = P.sb("wkrA", [128, 8, 96], BF16)
    wkrR = P.sb("wkrR", [128, 8, 96], BF16)
    P.memset("pool", wkrA[:], 0.0, ["wkrA"])
    P.memset("pool", wkrR[:], 0.0, ["wkrR"])
    P.copy("dve", wkrA[:, :, 64:96], wkr[:], [("slab", "kr"), "wkrA"], ["wkrA"])
    P.ts("dve", wkrR[:, :, 64:80], wkr[:, :, 16:32], -1.0, None, ALU.mult, None, [("slab", "kr"), "wkrR"], ["wkrR"])
    P.copy("dve", wkrR[:, :, 80:96], wkr[:, :, 0:16], [("slab", "kr"), "wkrR"], ["wkrR"])
    cqf = [P.sb("cqf%d" % i, [128, 512], F32) for i in range(3)]
    sq = [P.sb("sqm%d" % i, [128, 512], F32) for i in range(3)]
    rs = P.sb("rsm", [128, 512], F32)
    m1 = P.sb("m1", [128, 512], F32)
    m2 = P.sb("m2", [128, 512], F32)
    for c in range(4):
        cs = slice(c * 512, (c + 1) * 512)
        for t in range(3):
            for kf in range(8):
                lhs = wcq[:, kf, t * 128:(t + 1) * 128] if t < 2 else wckv[:, kf, :]
                P.mm(C.ps[4 + t][:], lhs, hnT[:, kf, cs], kf == 0, kf == 7, [("slab", "cq"), ("slab", "ckv")], [("ps", 4 + t)])
            P.copy("act", cqf[t][:], C.ps[4 + t][:], [("ps", 4 + t)], [("cqf", t)])
            P.act(sq[t][:], C.ps[4 + t][:], AF.Square, [("ps", 4 + t)], [("sq", t)])
        for tiles, nfeat in (((0, 1), 256), ((2,), 128)):
            for i, t in enumerate(tiles):
                P.mm(C.ps[7][:], C.onesf[:], sq[t][:], i == 0, i == len(tiles) - 1, [("sq", t)], [("ps", 7)])
            P.act(rs[:], C.ps[7][:], AF.Ln, [("ps", 7)], ["rs"], scale=1.0 / nfeat, bias=C.epsb[:])
            P.act(rs[:], rs[:], AF.Exp, ["rs"], ["rs"], scale=-0.5)
            for t in tiles:
                dst = cqnT[:, t, cs] if t < 2 else ckvnT[:, cs]
                P.stt("dve", dst, cqf[t][:], mlap[:, t:t + 1], rs[:], ALU.mult, ALU.mult, [("cqf", t), "rs", "mlap"], [("cn", t, c)])
        for kf in range(8):
            P.mm(C.ps[0][:96, :], wkrA[:, kf, :], hnT[:, kf, cs], kf == 0, kf == 7, ["wkrA"], [("ps", 0)])
        for kf in range(8):
            P.mm(C.ps[1][:96, :], wkrR[:, kf, :], hnT[:, kf, cs], kf == 0, kf == 7, ["wkrR"], [("ps", 1)])
        P.tt("dve", m1[64:96, :], C.ps[0][64:96, :], cosT[64:96, cs], ALU.mult, [("ps", 0), ("cosT", c)], ["m1"])
        P.tt("dve", m2[64:96, :], C.ps[1][64:96, :], sinT[64:96, cs], ALU.mult, [("ps", 1), ("sinT", c)], ["m2"])
        P.tt("dve", kropeT[64:96, cs], m1[64:96, :], m2[64:96, :], ALU.add, ["m1", "m2"], [("krope", c)])
    P.end_phase()


def phase_mla(P, C, L, hnT, ymT, cosT, sinT, cqnT, ckvnT, kropeT):
    j = L // 2
    W = C.od_w_in[j]
    SC = 96 ** -0.5
    P.begin_phase()
    wuq = P.sb("wuq", [128, 2, 768], BF16)
    load_w_bf16(P, C, wuq, C.mla_w_uq[j], 2, 768, "wuq")
    wuq_keys = [("w", "wuq", k) for k in range(2)]
    wuqR = P.sb("wuqR", [128, 2, 8, 96], BF16)
    P.memset("pool", wuqR[:], 0.0, ["wuqR"])
    wuq4 = wuq[:].rearrange("p k (h c) -> p k h c", h=8)
    for t in range(2):
        P.ts("dve", wuqR[:, t, :, 64:80], wuq4[:, t, :, 80:96], -1.0, None, ALU.mult, None, wuq_keys + ["wuqR"], ["wuqR"])
        P.copy("dve", wuqR[:, t, :, 80:96], wuq4[:, t, :, 64:80], wuq_keys + ["wuqR"], ["wuqR"])
    wukv = P.sb("wukv", [128, 1, 1024], BF16)
    load_w_bf16(P, C, wukv, C.mla_w_ukv[j], 1, 1024, "wukv")
    wukv_k = [("w", "wukv", 0)]
    m1s = [P.sb("m1_%d" % i, [128, 512], F32) for i in range(2)]
    m2s = [P.sb("m2_%d" % i, [128, 512], F32) for i in range(2)]
    QT = [P.sb("QT%d" % i, [128, 2, S], BF16) for i in range(2)]
    KT = [P.sb("KT%d" % i, [128, 2, S], BF16) for i in range(2)]
    VA = [P.sb("VA%d" % i, [128, NT, 2, 128], BF16) for i in range(2)]
    sgm = [P.sb("sgmT%d" % i, [128, S], BF16) for i in range(2)]
    PT = [P.sb("PT%d" % i, [128, 512], BF16) for i in range(4)]
    rec = P.sb("rec", [128, 512], F32)
    tmpf = P.sb("tmpf", [128, 512], F32)
    gst = [P.sb("gst%d" % i, [128, 8, 128], F32) for i in range(2)]
    gbf = [P.sb("gbf%d" % i, [128, 8, 128], BF16) for i in range(2)]
    wukv3 = wukv[:, 0, :].rearrange("p (h c) -> p h c", h=8)
    for i in range(2):
        P.memset("dve" if i == 0 else "pool", VA[i][:], 1.0, [("VA", i)])
    cnt = {"ps": 0, "m": 0}

    def pbank():
        b = (4, 5, 7)[cnt["ps"] % 3]
        cnt["ps"] += 1
        return b

    def proj_units(hp, sl):
        for par in range(2):
            h = 2 * hp + par
            for c in range(4):
                cs = slice(c * 512, (c + 1) * 512)
                b = pbank()
                P.mm(C.ps[b][:64, :], wukv[:, 0, h * 128:h * 128 + 64], ckvnT[:, cs], True, True, wukv_k, [("ps", b)])
                P.copy("dve", KT[sl][0:64, par, cs], C.ps[b][:64, :], [("ps", b)], [("KT", sl, par)])
                bA = pbank()
                bR = pbank()
                for t in range(2):
                    P.mm(C.ps[bA][:96, :], wuq[:, t, h * 96:(h + 1) * 96], cqnT[:, t, cs], t == 0, t == 1, wuq_keys, [("ps", bA)])
                for t in range(2):
                    P.mm(C.ps[bR][:96, :], wuqR[:, t, h, :], cqnT[:, t, cs], t == 0, t == 1, ["wuqR"], [("ps", bR)])
                P.copy("dve", QT[sl][0:64, par, cs], C.ps[bA][0:64, :], [("ps", bA)], [("QT", sl, par)])
                mi = cnt["m"] % 2
                cnt["m"] += 1
                m1, m2 = m1s[mi], m2s[mi]
                P.tt("dve", m1[64:96, :], C.ps[bA][64:96, :], cosT[64:96, cs], ALU.mult, [("ps", bA)], [("m1", mi)])
                P.tt("dve", m2[64:96, :], C.ps[bR][64:96, :], sinT[64:96, cs], ALU.mult, [("ps", bR)], [("m2", mi)])
                P.tt("dve", QT[sl][64:96, par, cs], m1[64:96, :], m2[64:96, :], ALU.add, [("m1", mi), ("m2", mi)], [("QT", sl, par)])
                yield
            P.copy("dve", KT[sl][64:96, par, :], kropeT[64:96, :], [], [("KT", sl, par)])
        for kt in range(NT):
            b = pbank()
            P.mm(C.ps[b][:, 0:128], ckvnT[:, kt * 128:(kt + 1) * 128], wukv3[:, 2 * hp:2 * hp + 2, 64:128], True, True,
                 wukv_k, [("ps", b)])
            P.copy("dve", VA[sl][:, kt, 0, 0:64], C.ps[b][:, 0:64], [("ps", b), ("VA", sl)], [("VA", sl)])
            P.copy("dve", VA[sl][:, kt, 1, 64:128], C.ps[b][:, 64:128], [("ps", b), ("VA", sl)], [("VA", sl)])
            if kt % 2 == 1:
                yield
        c0 = OD["gm"] + hp * 128
        P.dma(gst[sl][:], W[:, c0:c0 + 128].rearrange("(k p) c -> p k c", p=128), writes=[("gst", sl)])
        P.copy("pool", gbf[sl][:], gst[sl][:], [("gst", sl)], [("gbf", sl)])
        for c in range(4):
            cs = slice(c * 512, (c + 1) * 512)
            b = pbank()
            for kf in range(8):
                P.mm(C.ps[b][:], gbf[sl][:, kf, :], hnT[:, kf, cs], kf == 0, kf == 7, [("gbf", sl)], [("ps", b)])
            P.act(sgm[sl][:, cs], C.ps[b][:], AF.Silu, [("ps", b)], [("sgm", sl, c)])
            yield

    for _ in proj_units(0, 0):
        pass
    for hp in range(4):
        sl = hp % 2
        tile_i = 4 + hp
        gen = proj_units(hp + 1, 1 - sl) if hp < 3 else None
        groups = []
        for par in range(2):
            for qc in range(4):
                ob = 2 + (qc % 2)
                nk = 4 * qc + 4
                steps = []
                for kt in range(nk):
                    o = max(0, kt - 4 * qc)
                    q0 = qc * 512 + o * 128
                    ncol = 512 - o * 128
                    steps.append(dict(c0=o * 128, c1=512, lhsK=KT[sl][0:96, par, kt * 128:(kt + 1) * 128], rhsQ=QT[sl][0:96, par, q0:q0 + ncol],
                                      rk=[("KT", sl, par), ("QT", sl, par)], scale=SC,
                                      masks=[(o * 128, (o + 1) * 128, C.tri[:])] if kt >= 4 * qc else [],
                                      lhsV=VA[sl][:, kt, par, :], ob=ob, rv=[("VA", sl)]))

                def fin(par=par, qc=qc, ob=ob, tile_i=tile_i, sl=sl):
                    lo = slice(par * 64, par * 64 + 64)
                    qs = slice(qc * 512, (qc + 1) * 512)
                    softmax_pv_finish(P, C, C.ps[ob], par, ymT[lo, tile_i, qs], rec, tmpf, sgm[sl][lo, qs], ("sgm", sl, qc),
                                      ("ps", ob), ("ymT", tile_i, par, qc), True)
                groups.append((steps, fin))

        def tick(idx, gen=gen):
            if gen is not None and idx % 2 == 1:
                next(gen, None)
        run_attention(P, C, groups, PT, sbanks=(0, 1, 6), tick=tick, mask_eng="pool")
        if gen is not None:
            for _ in gen:
                pass
    P.end_phase()


def odd_layer(P, C, L, src, dst, do_nsa=True, do_mla=True):
    P.push_scope()
    ymT = P.sb("ymT", [128, 8, S], BF16)
    hnT = P.sb("hnT", [128, 8, S], BF16)
    phase_prenorm(P, C, L, src, hnT)
    if not (do_nsa and do_mla):
        P.begin_phase()
        P.memset("pool", ymT[:], 0.0, ["ymT"])
        P.end_phase()
    if do_nsa:
        nsa_mixer(P, C, L, hnT, ymT)
    if do_mla:
        P.push_scope()
        cosT = P.sb("cosT", [128, S], F32)
        sinT = P.sb("sinT", [128, S], F32)
        cqnT = P.sb("cqnT", [128, 2, S], BF16)
        ckvnT = P.sb("ckvnT", [128, S], BF16)
        kropeT = P.sb("kropeT", [128, S], BF16)
        phase_mla_prep(P, C, L, hnT, cosT, sinT, cqnT, ckvnT, kropeT)
        phase_mla(P, C, L, hnT, ymT, cosT, sinT, cqnT, ckvnT, kropeT)
        P.pop_scope()
    if C.dbg is not None:
        P.begin_phase()
        cv = [P.sb("dbgc%d" % i, [128, S], F32) for i in range(2)]
        for t in range(8):
            P.copy("dve", cv[t % 2][:], ymT[:, t, :], [], [("cv", t % 2)])
            P.dma(C.dbg[t], cv[t % 2][:], reads=[("cv", t % 2)], writes=[("dbg", t % 2)])
        P.end_phase()
    phase_out(P, C, L, ymT, src, dst)
    P.pop_scope()


class SlabLoader:
    def __init__(self, P, tag):
        self.P = P
        self.tag = tag
        self.st = [P.sb("sls_%s%d" % (tag, i), [128, 8, 128], F32) for i in range(2)]
        self.bf = [P.sb("slb_%s%d" % (tag, i), [128, 8, 128], BF16) for i in range(2)]
        self.n = 0

    def load(self, W, c0, n, eng="pool"):
        s = self.n % 2
        self.n += 1
        P = self.P
        P.dma(self.st[s][:, :, 0:n], W[:, c0:c0 + n].rearrange("(k p) c -> p k c", p=128), writes=[("sls", self.tag, s)])
        P.copy(eng, self.bf[s][:, :, 0:n], self.st[s][:, :, 0:n], [("sls", self.tag, s)], [("slb", self.tag, s)])
        return self.bf[s], ("slb", self.tag, s)


def proj_fm(P, C, slab, skey, n, hnT, consume, banks=(6, 7)):
    for c in range(4):
        cs = slice(c * 512, (c + 1) * 512)
        b = banks[c % len(banks)]
        for kf in range(8):
            P.mm(C.ps[b][0:n, :], slab[:, kf, 0:n], hnT[:, kf, cs], kf == 0, kf == 7, [skey], [("ps", b)])
        consume(c, cs, C.ps[b], ("ps", b))


def gelu_tanh(P, x_ps, xkey, out, okey, y2, t1, sg, tag):
    P.act(y2, x_ps, AF.Square, [xkey], [tag + "y2"])
    P.ts("dve", y2, y2, 0.044715, 1.0, ALU.mult, ALU.add, [tag + "y2"], [tag + "y2"])
    P.tt("dve", t1, y2, x_ps, ALU.mult, [tag + "y2", xkey], [tag + "t1"])
    P.act(sg, t1, AF.Sigmoid, [tag + "t1"], [tag + "sg"], scale=1.5957691216057308)
    P.tt("dve", out, sg, x_ps, ALU.mult, [tag + "sg", xkey], [okey])


def nsa_mixer(P, C, L, hnT, ymT):
    j = L // 2
    W = C.od_w_in[j]
    P.push_scope()
    QA = P.sb("QaugT", [128, 8, S], BF16)
    KS = P.sb("KselA", [128, 2, S], BF16)
    KW = P.sb("KwinA", [128, 2, S], BF16)
    SgT = P.sb("SgT", [32, S], BF16)
    KcA = P.sb("KcA", [128, 2, 128], BF16)
    VcA = P.sb("VcA", [128, 2, 2, 128], BF16)
    gsel = P.sb("gsel", [32, 3, 8, 128], BF16)

    P.begin_phase()
    SL = SlabLoader(P, "a")
    P.memset("pool", QA[64:96, :, :], 0.0, ["QAmask"])
    P.dma(QA[96:100, :, :], C.qaug[:, :, :], writes=["QAaug"])
    P.dma(gsel[:], C.gsel[:, :, :, :], writes=["gsel"])
    for y in range(2):
        P.dma(KS[64:96, y, :], C.kaugc[0:32, :], writes=[("KSe", y)])
        P.dma(KS[96:100, y, :], C.kaugc[32:36, :], writes=[("KSa", y)])
        P.dma(KW[96:100, y, :], C.kaugc[32:36, :], writes=[("KWa", y)])
    P.memset("pool", KW[64:96, :, :], 0.0, ["KWz"])
    P.memset("pool", SgT[:], 0.0, ["SgT0"])
    import os
    part = int(os.environ.get("NSA_PART", "9"))
    for hp in range(4 if part >= 1 else 0):
        slab, sk = SL.load(W, OD["q"] + hp * 128, 128)

        def cons_q(c, cs, ps, pk, hp=hp):
            for par in range(2):
                P.act(QA[0:64, 2 * hp + par, cs], ps[par * 64:(par + 1) * 64, :], AF.Copy, [pk], [("QA", 2 * hp + par, c)], scale=0.125)
        proj_fm(P, C, slab, sk, 128, hnT, cons_q)
    for (i, dst, nm) in (((2, KS, "KS"), (4, KW, "KW")) if part >= 2 else ()):
        slab, sk = SL.load(W, OD["kv"] + i * 128, 128)

        def cons_k(c, cs, ps, pk, dst=dst, nm=nm):
            for y in range(2):
                P.copy("act" if y == 0 else "dve", dst[0:64, y, cs], ps[y * 64:(y + 1) * 64, :], [pk], [(nm, y, c)])
        proj_fm(P, C, slab, sk, 128, hnT, cons_k)
    def cons_g(c, cs, ps, pk):
        P.act(SgT[0:32, cs], ps[0:32, :], AF.Sigmoid, [pk, "SgT0"], [("SgT", c)])
    if part >= 3:
        slab, sk = SL.load(W, OD["gl"], 32)
        proj_fm(P, C, slab, sk, 32, hnT, cons_g)
    P.end_phase()
    stop = os.environ.get("NSA_STOP", "")
    if stop == "a":
        P.pop_scope()
        return

    P.begin_phase()
    SL = SlabLoader(P, "b")
    P.memset("pool", VcA[:], 1.0, ["VcA"])
    P.memset("pool", KcA[64:96, :, :], 0.0, ["KcAz"])
    for y in range(2):
        P.dma(KcA[96:100, y, :], C.kaugcmp[:, :], writes=[("KcAa", y)])
    K2 = [P.sb("K2_%d" % y, [128, S], BF16) for y in range(2)]
    G = P.sb("Gc", [128, 16, 128], BF16)
    GT = P.sb("GTc", [128, 2, 128], BF16)
    w1st = P.sb("w1st", [128, 16, 256], F32)
    w1b = P.sb("w1b", [128, 16, 256], BF16)
    w2st = P.sb("w2st", [128, 2, 64], F32)
    w2b = P.sb("w2b", [128, 2, 64], BF16)
    pos2 = P.sb("pos2", [128, 2, 16], F32)
    P.dma(pos2[:], C.nsapos2[j], writes=["pos2"])
    y2 = P.sb("gy2", [128, 128], F32)
    t1 = P.sb("gt1", [128, 128], F32)
    sg = P.sb("gsg", [128, 128], F32)
    P.memset("pool", GT[:], 0.0, ["GT0"])
    for kv in range(2):
        w1 = (C.nsa_ck_w1 if kv == 0 else C.nsa_cv_w1)[j]
        w2 = (C.nsa_ck_w2 if kv == 0 else C.nsa_cv_w2)[j]
        P.dma(w1st[:], w1.rearrange("(j p) c -> p j c", p=128), writes=["w1st"])
        P.copy("dve", w1b[:, 0:8, :], w1st[:, 0:8, :], ["w1st"], ["w1b_a"])
        P.copy("act", w1b[:, 8:16, :], w1st[:, 8:16, :], ["w1st"], ["w1b_b"])
        P.dma(w2st[:], w2.rearrange("(k p) c -> p k c", p=128), writes=["w2st"])
        P.copy("dve", w2b[:], w2st[:], ["w2st"], ["w2b"])
        slab, sk = SL.load(W, OD["kv"] + kv * 128, 128, eng="dve")
        for y in range(2):
            P.memset("pool", K2[y][64:128, S - 1:S], 0.0, [("K2z", y)])

        def cons_c(c, cs, ps, pk):
            for y in range(2):
                src = ps[y * 64:(y + 1) * 64, :]
                eng = "act" if y == 0 else "dve"
                P.copy(eng, K2[y][0:64, cs], src, [pk], [("K2a", y, c)])
                if c == 0:
                    P.copy(eng, K2[y][64:128, 0:511], ps[y * 64:(y + 1) * 64, 1:512], [pk], [("K2b", y, c)])
                else:
                    P.copy(eng, K2[y][64:128, c * 512 - 1:c * 512 + 511], src, [pk, ("K2z", y)], [("K2b", y, c)])
        if part >= 1:
            proj_fm(P, C, slab, sk, 128, hnT, cons_c)
        k2keys = [[("K2a", y, c) for c in range(4)] + [("K2b", y, c) for c in range(4)] + [("K2z", y)] for y in range(2)]
        for y in range(2 if part >= 2 else 0):
            for jj in range(16):
                P.ts("dve", G[:, jj, 0:127], K2[y][:, 2 * jj:2 * jj + 2017:16], pos2[:, kv, jj:jj + 1], None, ALU.add, None,
                     k2keys[y] + ["pos2"], [("G", jj)])
            if part < 3:
                continue
            for ht in range(2):
                b = 4 + ht
                for jj in range(16):
                    P.mm(C.ps[b][:, 0:127], w1b[:, jj, ht * 128:(ht + 1) * 128], G[:, jj, 0:127], jj == 0, jj == 15, [("G", jj), "w1b_a", "w1b_b"], [("ps", b)])
                gelu_tanh(P, C.ps[b][:, 0:127], ("ps", b), GT[:, ht, 0:127], ("GT", ht), y2[:, 0:127], t1[:, 0:127], sg[:, 0:127], "g")
            if part < 4:
                continue
            if kv == 0:
                for ht in range(2):
                    P.mm(C.ps[2][0:64, 0:128], w2b[:, ht, :], GT[:, ht, :], ht == 0, ht == 1, [("GT", ht), "GT0", "w2b"], [("ps", 2)])
                P.copy("act", KcA[0:64, y, :], C.ps[2][0:64, 0:128], [("ps", 2)], [("KcA", y)])
            elif part >= 5:
                for ht in range(2):
                    P.mm(C.ps[3][:, 0:64], GT[:, ht, :], w2b[:, ht, :], ht == 0, ht == 1, [("GT", ht), "GT0", "w2b"], [("ps", 3)])
                if part >= 6:
                    P.copy("dve", VcA[:, y, 0, 0:64], C.ps[3][:, 0:64], [("ps", 3), "VcA"], [("VcAw", y, 0)])
                    P.copy("dve", VcA[:, y, 1, 64:128], C.ps[3][:, 0:64], [("ps", 3), "VcA"], [("VcAw", y, 1)])
    P.end_phase()
    if stop == "b":
        P.pop_scope()
        return

    P.begin_phase()
    addm = P.sb("addm", [128, S], BF16)
    P.dma(addm[:], C.addmask[:, :], writes=["addm"])
    ovl = P.sb("ovl", [128, 64], BF16)
    P.dma(ovl[:], C.ovl[:, :], writes=["ovl"])
    selc = P.sb("selc", [128, NT, 2, 32], F32)
    P.dma(selc[:], C.selc[:, :, :, :], writes=["selc"])
    pslc = P.sb("pslcT", [32, 2, S], F32)
    sm = [P.sb("smc%d" % i, [128, 512], F32) for i in range(2)]
    PT = [P.sb("PTc%d" % i, [128, 512], BF16) for i in range(2)]
    rec = P.sb("rec", [128, 512], F32)
    tmpf = P.sb("tmpf", [128, 512], F32)
    rec2 = P.sb("rec2", [32, 512], F32)
    tmp2 = P.sb("tmp2", [32, 512], F32)
    items = [(h, qc) for h in range(8) for qc in range(4)]

    def cmpA(i):
        h, qc = items[i]
        y = h // 4
        qs = slice(qc * 512, (qc + 1) * 512)
        s_ = i % 2
        P.mm(C.ps[s_][:], KcA[0:100, y, :], QA[0:100, h, qs], True, True, [], [("ps", s_)])
        P.tt("dve", sm[s_][:], C.ps[s_][:], addm[:, qs], ALU.add, [("ps", s_), "addm"], [("sm", s_)])
        P.act(PT[s_][:], sm[s_][:], AF.Exp, [("sm", s_)], [("PT", s_)])

    def cmpB(i):
        h, qc = items[i]
        y, par, tile_i, hh = h // 4, h % 2, h // 2, h % 4
        lo = slice(par * 64, par * 64 + 64)
        qs = slice(qc * 512, (qc + 1) * 512)
        s_ = i % 2
        ob = 2 + s_
        P.mm(C.ps[ob][:], VcA[:, y, par, :], PT[s_][:], True, True, [("PT", s_)], [("ps", ob)])
        P.mm(C.ps[5][0:64, :], ovl[:], PT[s_][:], True, True, [("PT", s_), "ovl"], [("ps", 5)])
        P.mm(C.ps[4][:], gsel[0:32, 0, h, :], SgT[0:32, qs], True, True, ["gsel"], [("ps", 4)])
        softmax_pv_finish(P, C, C.ps[ob], par, ymT[lo, tile_i, qs], rec, tmpf, None, None, ("ps", ob), ("ymT", tile_i, par, qc), True,
                          clamp=True, gate_ps=C.ps[4], gkey=("ps", 4))
        P.ts("dve", rec2[:], C.ps[5][32:64, :], 1e-18, None, ALU.max, None, [("ps", 5)], ["rec2"])
        P.act(rec2[:], rec2[:], AF.Ln, ["rec2"], ["rec2"])
        P.act(rec2[:], rec2[:], AF.Exp, ["rec2"], ["rec2"], scale=-1.0)
        if hh == 0:
            P.tt("dve", pslc[:, y, qs], C.ps[5][0:32, :], rec2[:], ALU.mult, [("ps", 5), "rec2"], [("pslc", y, qc)])
        else:
            P.tt("dve", tmp2[:], C.ps[5][0:32, :], rec2[:], ALU.mult, [("ps", 5), "rec2"], ["tmp2"])
            P.tt("dve", pslc[:, y, qs], pslc[:, y, qs], tmp2[:], ALU.add, ["tmp2", ("pslc", y, qc)], [("pslc", y, qc)])

    cmpA(0)
    for i in range(len(items)):
        if i + 1 < len(items):
            cmpA(i + 1)
        cmpB(i)
    sc = [P.sb("scs%d" % i, [128, 32], F32) for i in range(2)]
    m8 = [P.sb("m8s%d" % i, [128, 8], F32) for i in range(2)]
    ng = [P.sb("ngs%d" % i, [128, 32], F32) for i in range(2)]
    sitems = [(y, qt) for y in range(2) for qt in range(NT)]

    def selA(i):
        y, qt = sitems[i]
        s_ = i % 2
        ts_ = slice(qt * 128, (qt + 1) * 128)
        P.tr(C.ps[6 + s_][:, 0:32], pslc[:, y, ts_], C.identf[0:32, 0:32], [("pslc", y, qt // 4)], [("ps", 6 + s_)])
        P.tt("dve", sc[s_][:], C.ps[6 + s_][:, 0:32], selc[:, qt, 0, :], ALU.mult, [("ps", 6 + s_), "selc"], [("sc", s_)])
        P.tt("dve", sc[s_][:], sc[s_][:], selc[:, qt, 1, :], ALU.add, [("sc", s_), "selc"], [("sc", s_)])
        P.op("dve", lambda e, s_=s_: e.max(out=m8[s_][:], in_=sc[s_][:]), [("sc", s_)], [("m8", s_)])
        P.ts("dve", ng[s_][:], sc[s_][:], m8[s_][:, 7:8], 30000.0, ALU.is_ge, ALU.mult, [("sc", s_), ("m8", s_)], [("ng", s_)])
        P.ts("dve", ng[s_][:], ng[s_][:], -30000.0, None, ALU.add, None, [("ng", s_)], [("ng", s_)])

    def selB(i):
        y, qt = sitems[i]
        s_ = i % 2
        ts_ = slice(qt * 128, (qt + 1) * 128)
        P.tr(C.ps[4 + s_][0:32, 0:128], ng[s_][:], C.identf[:], [("ng", s_)], [("ps", 4 + s_)])
        for hh in range(4):
            eng = "act" if s_ == 0 else "dve"
            P.copy(eng, QA[64:96, 4 * y + hh, ts_], C.ps[4 + s_][0:32, 0:128], [("ps", 4 + s_), "QAmask"], [("QAm", 4 * y + hh, qt)])

    selA(0)
    for i in range(len(sitems)):
        if i + 1 < len(sitems):
            selA(i + 1)
        selB(i)
    P.end_phase()
    if stop == "c":
        P.pop_scope()
        return

    for br in (1, 2):
        P.begin_phase()
        SL = SlabLoader(P, "v%d" % br)
        VA = P.sb("VAn", [128, NT, 2, 2, 128], BF16)
        P.memset("dve", VA[:], 1.0, ["VA"])
        slab, sk = SL.load(W, OD["kv"] + (3 if br == 1 else 5) * 128, 128)
        for kt in range(NT):
            b = 6 + (kt % 2)
            for kf in range(8):
                P.mm(C.ps[b][:, 0:128], hnT[:, kf, kt * 128:(kt + 1) * 128], slab[:, kf, :], kf == 0, kf == 7, [sk], [("ps", b)])
            pv = C.ps[b][:, 0:128].rearrange("p (y c) -> p y c", y=2)
            eng = "act" if kt % 2 == 0 else "dve"
            P.copy(eng, VA[:, kt, :, 0, 0:64], pv, [("ps", b), "VA"], [("VAw", kt, 0)])
            P.copy(eng, VA[:, kt, :, 1, 64:128], pv, [("ps", b), "VA"], [("VAw", kt, 1)])
        sgn = None
        if br == 2:
            sgn = P.sb("sgn", [128, 4, S], BF16)
            for hp in range(4):
                slab2, sk2 = SL.load(W, OD["gn"] + hp * 128, 128, eng="dve")

                def cons_gn(c, cs, ps, pk, hp=hp):
                    P.act(sgn[:, hp, cs], ps[:], AF.Silu, [pk], [("sgn", hp, c)])
                proj_fm(P, C, slab2, sk2, 128, hnT, cons_gn)
        KA = KS if br == 1 else KW
        PT = [P.sb("PTn%d" % i, [128, 512], BF16) for i in range(6)]
        rec = P.sb("rec", [128, 512], F32)
        tmpf = P.sb("tmpf", [128, 512], F32)
        groups = []
        for h in range(8):
            y, par, tile_i = h // 4, h % 2, h // 2
            for qc in range(4):
                ob = 2 + (qc % 2)
                kts = list(range(0, 4 * qc + 4)) if br == 1 else list(range(max(0, 4 * qc - 4), 4 * qc + 4))
                steps = []
                for kt in kts:
                    o = kt - 4 * qc
                    rlo = max(o, 0)
                    rhi = 3 if br == 1 else min(o + 4, 3)
                    c0, c1 = rlo * 128, (rhi + 1) * 128
                    masks = []
                    if o >= 0:
                        masks.append((o * 128, (o + 1) * 128, C.tri[:]))
                    if br == 2 and o <= -1:
                        masks.append(((o + 4) * 128, (o + 5) * 128, C.wmask[:]))
                    steps.append(dict(c0=c0, c1=c1, lhsK=KA[0:100, y, kt * 128:(kt + 1) * 128],
                                      rhsQ=QA[0:100, h, qc * 512 + c0:qc * 512 + c1], scale=None, masks=masks,
                                      lhsV=VA[:, kt, y, par, :], ob=ob, rv=[("VAw", kt, par)]))

                def fin(h=h, par=par, qc=qc, ob=ob, tile_i=tile_i):
                    lo = slice(par * 64, par * 64 + 64)
                    qs = slice(qc * 512, (qc + 1) * 512)
                    P.mm(C.ps[4][:], gsel[0:32, br, h, :], SgT[0:32, qs], True, True, [], [("ps", 4)])
                    softmax_pv_finish(P, C, C.ps[ob], par, ymT[lo, tile_i, qs], rec, tmpf,
                                      sgn[lo, tile_i, qs] if br == 2 else None, ("sgn", tile_i, qc) if br == 2 else None,
                                      ("ps", ob), ("ymT", tile_i, par, qc), False, clamp=False, gate_ps=C.ps[4], gkey=("ps", 4))
                groups.append((steps, fin))
        run_attention(P, C, groups, PT, sbanks=(0, 1, 5, 7), depth=4)
        P.end_phase()
    P.pop_scope()
```
